# Optimizing a Trainium2 kernel written in Bass

```python
import jax
import jax.numpy as jnp
from jax import lax
import numpy as np

D_MODEL = 1024
BATCH = 4
SEQ = 4096
DEPTH = 1
DEC_BATCH = 32
DEC_SEQ = 16
PAST_LEN = 2048

CHUNK = 64
N_PAST_CHUNKS = 8
A_REACH = N_PAST_CHUNKS * CHUNK
BAND = A_REACH + CHUNK
H_A = 8
DH_A = 64
H_B = 8
DH_B = 64
H_M = 4
DH_M = 128
N_MEM = 256
REL_CLIP = 128
N_REL = 2 * REL_CLIP + 1
D_FF = 4 * D_MODEL
Q_BLOCK = 128
EPS = 1e-6
W_A = H_A * DH_A
W_B = H_B * DH_B
W_M = H_M * DH_M
N_BRANCH = 3
IN_SIZES = (W_A, W_A, W_A, W_B, W_B, W_B, H_B, W_M, N_BRANCH * D_MODEL)
IN_SPLITS = tuple(int(s) for s in np.cumsum(IN_SIZES)[:-1])
D_IN = int(sum(IN_SIZES))

kernel_name = 'chunk_streaming_hybrid_encoder_step'


def rmsnorm(x, g):
    xf = x.astype(jnp.float32)
    y = xf * lax.rsqrt(jnp.mean(xf * xf, axis=-1, keepdims=True) + EPS)
    return (y * g.astype(jnp.float32)).astype(x.dtype)


def project(xn, w_in, b_f, g_qa, g_ka, g_qb, g_kb, g_qm):
    b, t, _ = xn.shape
    z = xn @ w_in
    qa, ka, va, qb, kb, vb, fl, qm, gates = jnp.split(z, IN_SPLITS, axis=-1)
    qa = rmsnorm(qa.reshape(b, t, H_A, DH_A), g_qa)
    ka = rmsnorm(ka.reshape(b, t, H_A, DH_A), g_ka)
    va = va.reshape(b, t, H_A, DH_A)
    qb = rmsnorm(qb.reshape(b, t, H_B, DH_B), g_qb)
    kb = rmsnorm(kb.reshape(b, t, H_B, DH_B), g_kb)
    vb = vb.reshape(b, t, H_B, DH_B)
    logf = jax.nn.log_sigmoid(fl.astype(jnp.float32) + b_f.astype(jnp.float32))
    qm = rmsnorm(qm.reshape(b, t, H_M, DH_M), g_qm)
    return qa, ka, va, qb, kb, vb, logf, qm, gates


def rel_bias_lookup(rel_bias, dist):
    bias = rel_bias[jnp.clip(dist, -REL_CLIP, REL_CLIP) + REL_CLIP].astype(jnp.float32)
    return jnp.moveaxis(bias, -1, 0)


def band_attn_prompt(q, k, v, rel_bias):
    b, t, h, d = q.shape
    nc = t // CHUNK
    qc = q.reshape(b, nc, CHUNK, h, d)
    pad = ((0, 0), (A_REACH, 0), (0, 0), (0, 0))
    kp = jnp.pad(k, pad)
    vp = jnp.pad(v, pad)
    rows = jnp.arange(nc)[:, None] * CHUNK + jnp.arange(BAND)[None, :]
    kband = kp[:, rows]
    vband = vp[:, rows]
    s = jnp.einsum('bcqhd,bckhd->bchqk', qc, kband).astype(jnp.float32) * (d ** -0.5)
    dist = jnp.arange(CHUNK)[:, None] + A_REACH - jnp.arange(BAND)[None, :]
    s = s + rel_bias_lookup(rel_bias, dist)
    valid = rows >= A_REACH
    s = jnp.where(valid[None, :, None, None, :], s, -jnp.inf)
    p = jax.nn.softmax(s, axis=-1).astype(v.dtype)
    o = jnp.einsum('bchqk,bckhd->bcqhd', p, vband)
    return o.reshape(b, t, h * d)


def band_attn_sample(q, k_new, v_new, k_cache, v_cache, rel_bias, past_len):
    b, s_len, h, d = q.shape
    n_cache = k_cache.shape[1]
    kk = jnp.concatenate([k_cache, k_new], axis=1)
    vv = jnp.concatenate([v_cache, v_new], axis=1)
    q_pos = past_len + jnp.arange(s_len)
    k_pos = jnp.concatenate([past_len - n_cache + jnp.arange(n_cache), q_pos])
    dist = q_pos[:, None] - k_pos[None, :]
    s = jnp.einsum('bqhd,bkhd->bhqk', q, kk).astype(jnp.float32) * (d ** -0.5)
    s = s + rel_bias_lookup(rel_bias, dist)
    p = jax.nn.softmax(s, axis=-1).astype(vv.dtype)
    o = jnp.einsum('bhqk,bkhd->bqhd', p, vv)
    return o.reshape(b, s_len, h * d)


def forget_attn_prompt(q, k, v, logf):
    b, t, h, d = q.shape
    nb = t // Q_BLOCK
    c = jnp.cumsum(logf, axis=1).transpose(0, 2, 1)
    qblk = q.reshape(b, nb, Q_BLOCK, h, d).transpose(1, 0, 2, 3, 4)
    cblk = c.reshape(b, h, nb, Q_BLOCK).transpose(2, 0, 1, 3)
    k_pos = jnp.arange(t)
    scale = d ** -0.5

    def one_block(args):
        i, qi, ci = args
        s = jnp.einsum('bqhd,bkhd->bhqk', qi, k).astype(jnp.float32) * scale
        s = s + ci[..., :, None] - c[..., None, :]
        q_pos = i * Q_BLOCK + jnp.arange(Q_BLOCK)
        s = jnp.where(k_pos[None, :] <= q_pos[:, None], s, -jnp.inf)
        p = jax.nn.softmax(s, axis=-1).astype(v.dtype)
        return jnp.einsum('bhqk,bkhd->bqhd', p, v)

    o = lax.map(one_block, (jnp.arange(nb), qblk, cblk))
    return o.transpose(1, 0, 2, 3, 4).reshape(b, t, h * d)


def forget_attn_sample(q, k_new, v_new, logf_new, k_cache, v_cache, logf_cache):
    b, s_len, h, d = q.shape
    p_len = k_cache.shape[1]
    kk = jnp.concatenate([k_cache, k_new], axis=1)
    vv = jnp.concatenate([v_cache, v_new], axis=1)
    lf = jnp.concatenate([logf_cache.astype(jnp.float32), logf_new], axis=1)
    c = jnp.cumsum(lf, axis=1).transpose(0, 2, 1)
    cq = c[..., p_len:]
    s = jnp.einsum('bqhd,bkhd->bhqk', q, kk).astype(jnp.float32) * (d ** -0.5)
    s = s + cq[..., :, None] - c[..., None, :]
    q_pos = p_len + jnp.arange(s_len)
    k_pos = jnp.arange(p_len + s_len)
    s = jnp.where(k_pos[None, :] <= q_pos[:, None], s, -jnp.inf)
    p = jax.nn.softmax(s, axis=-1).astype(vv.dtype)
    o = jnp.einsum('bhqk,bkhd->bqhd', p, vv)
    return o.reshape(b, s_len, h * d)


def mem_kv(mem, g_mem, w_mkv, g_km):
    b, n, _ = mem.shape
    mk, mv = jnp.split(rmsnorm(mem, g_mem) @ w_mkv, 2, axis=-1)
    return rmsnorm(mk.reshape(b, n, H_M, DH_M), g_km), mv.reshape(b, n, H_M, DH_M)


def mem_attn(q, mk, mv):
    b, t, h, d = q.shape
    s = jnp.einsum('bqhd,bmhd->bhqm', q, mk).astype(jnp.float32) * (d ** -0.5)
    p = jax.nn.softmax(s, axis=-1).astype(mv.dtype)
    o = jnp.einsum('bhqm,bmhd->bqhd', p, mv)
    return o.reshape(b, t, h * d)


def gated_merge(o_a, o_b, o_m, gates, w_pa, w_pb, w_pm, w_o):
    g_a, g_b, g_m = jnp.split(jax.nn.sigmoid(gates), N_BRANCH, axis=-1)
    h = g_a * (o_a @ w_pa) + g_b * (o_b @ w_pb) + g_m * (o_m @ w_pm)
    return h @ w_o


def sq_relu_mlp(x, g_ffn, w_up, w_down):
    h = jax.nn.relu(rmsnorm(x, g_ffn) @ w_up)
    return (h * h) @ w_down


def setup_inputs(seed: int = 0) -> dict:
    key = jax.random.key(seed)
    ks = iter(jax.random.split(key, 40))
    f32 = jnp.float32
    L = DEPTH
    a_cache = min(A_REACH, PAST_LEN)

    def nrm(shape, scale=1.0):
        return scale * jax.random.normal(next(ks), shape, f32)

    def gain(n):
        return 1.0 + 0.05 * nrm((L, n))

    return {
        'x_prompt': nrm((BATCH, SEQ, D_MODEL)),
        'x_sample': nrm((DEC_BATCH, DEC_SEQ, D_MODEL)),
        'mem_prompt': nrm((BATCH, N_MEM, D_MODEL)),
        'cache_a_k': nrm((L, DEC_BATCH, a_cache, H_A, DH_A)),
        'cache_a_v': nrm((L, DEC_BATCH, a_cache, H_A, DH_A)),
        'cache_b_k': nrm((L, DEC_BATCH, PAST_LEN, H_B, DH_B)),
        'cache_b_v': nrm((L, DEC_BATCH, PAST_LEN, H_B, DH_B)),
        'cache_b_logf': jax.nn.log_sigmoid(3.0 + 1.5 * nrm((L, DEC_BATCH, PAST_LEN, H_B))),
        'cache_mem_k': nrm((L, DEC_BATCH, N_MEM, H_M, DH_M)),
        'cache_mem_v': nrm((L, DEC_BATCH, N_MEM, H_M, DH_M)),
        'g_mix': gain(D_MODEL),
        'w_in': nrm((L, D_MODEL, D_IN), D_MODEL ** -0.5),
        'b_f': 1.0 + 5.0 * jax.random.uniform(next(ks), (L, H_B), f32),
        'g_qa': gain(DH_A),
        'g_ka': gain(DH_A),
        'g_qb': gain(DH_B),
        'g_kb': gain(DH_B),
        'g_qm': gain(DH_M),
        'g_km': gain(DH_M),
        'rel_bias': nrm((L, N_REL, H_A), 0.5),
        'g_mem': gain(D_MODEL),
        'w_mkv': nrm((L, D_MODEL, 2 * W_M), D_MODEL ** -0.5),
        'w_pa': nrm((L, W_A, D_MODEL), W_A ** -0.5),
        'w_pb': nrm((L, W_B, D_MODEL), W_B ** -0.5),
        'w_pm': nrm((L, W_M, D_MODEL), W_M ** -0.5),
        'w_o': nrm((L, D_MODEL, D_MODEL), D_MODEL ** -0.5),
        'g_ffn': gain(D_MODEL),
        'w_up': nrm((L, D_MODEL, D_FF), D_MODEL ** -0.5),
        'w_down': nrm((L, D_FF, D_MODEL), D_FF ** -0.5),
    }


def reference(x_prompt, x_sample, mem_prompt, cache_a_k, cache_a_v, cache_b_k, cache_b_v, cache_b_logf,
              cache_mem_k, cache_mem_v, g_mix, w_in, b_f, g_qa, g_ka, g_qb, g_kb, g_qm, g_km, rel_bias,
              g_mem, w_mkv, w_pa, w_pb, w_pm, w_o, g_ffn, w_up, w_down):
    past_len = cache_b_k.shape[2]
    keep_p = min(A_REACH, x_prompt.shape[1])
    xp = x_prompt
    xs = x_sample
    ak_p, av_p, bk_p, bv_p, blf_p, mk_p, mv_p = [], [], [], [], [], [], []
    ak_s, av_s, bk_s, bv_s, blf_s = [], [], [], [], []
    for l in range(DEPTH):
        xn = rmsnorm(xp, g_mix[l])
        qa, ka, va, qb, kb, vb, lf, qm, gates = project(xn, w_in[l], b_f[l], g_qa[l], g_ka[l], g_qb[l], g_kb[l], g_qm[l])
        mk, mv = mem_kv(mem_prompt, g_mem[l], w_mkv[l], g_km[l])
        o_a = band_attn_prompt(qa, ka, va, rel_bias[l])
        o_b = forget_attn_prompt(qb, kb, vb, lf)
        o_m = mem_attn(qm, mk, mv)
        xp = xp + gated_merge(o_a, o_b, o_m, gates, w_pa[l], w_pb[l], w_pm[l], w_o[l])
        xp = xp + sq_relu_mlp(xp, g_ffn[l], w_up[l], w_down[l])
        ak_p.append(ka[:, -keep_p:])
        av_p.append(va[:, -keep_p:])
        bk_p.append(kb)
        bv_p.append(vb)
        blf_p.append(lf)
        mk_p.append(mk)
        mv_p.append(mv)
        xn = rmsnorm(xs, g_mix[l])
        qa, ka, va, qb, kb, vb, lf, qm, gates = project(xn, w_in[l], b_f[l], g_qa[l], g_ka[l], g_qb[l], g_kb[l], g_qm[l])
        o_a = band_attn_sample(qa, ka, va, cache_a_k[l], cache_a_v[l], rel_bias[l], past_len)
        o_b = forget_attn_sample(qb, kb, vb, lf, cache_b_k[l], cache_b_v[l], cache_b_logf[l])
        o_m = mem_attn(qm, cache_mem_k[l], cache_mem_v[l])
        xs = xs + gated_merge(o_a, o_b, o_m, gates, w_pa[l], w_pb[l], w_pm[l], w_o[l])
        xs = xs + sq_relu_mlp(xs, g_ffn[l], w_up[l], w_down[l])
        ak_s.append(ka)
        av_s.append(va)
        bk_s.append(kb)
        bv_s.append(vb)
        blf_s.append(lf)
    new_a_k_prompt = jnp.stack(ak_p)
    new_a_v_prompt = jnp.stack(av_p)
    new_b_k_prompt = jnp.stack(bk_p)
    new_b_v_prompt = jnp.stack(bv_p)
    new_b_logf_prompt = jnp.stack(blf_p)
    new_mem_k_prompt = jnp.stack(mk_p)
    new_mem_v_prompt = jnp.stack(mv_p)
    new_a_k_sample = jnp.stack(ak_s)
    new_a_v_sample = jnp.stack(av_s)
    new_b_k_sample = jnp.stack(bk_s)
    new_b_v_sample = jnp.stack(bv_s)
    new_b_logf_sample = jnp.stack(blf_s)
    return (xp, xs, new_a_k_prompt, new_a_v_prompt, new_b_k_prompt, new_b_v_prompt, new_b_logf_prompt,
            new_mem_k_prompt, new_mem_v_prompt, new_a_k_sample, new_a_v_sample, new_b_k_sample,
            new_b_v_sample, new_b_logf_sample)
```

```python
import contextlib
import numpy as np
import concourse.bass as bass
import concourse.mybir as mybir
from concourse.bass_utils import run_bass_kernel_spmd

F32 = mybir.dt.float32
BF16 = mybir.dt.bfloat16
ALU = mybir.AluOpType
AF = mybir.ActivationFunctionType
AX = mybir.AxisListType

ENGS = ("pe", "act", "dve", "pool", "sp")
NEG = -30000.0
EPS = 1e-6
LREL = 768


ALLBUFS = []


class Buf:
    __slots__ = ("name", "w", "r", "excl")

    def __init__(self, name="", excl=False):
        self.name = name
        self.w = None
        self.r = []
        self.excl = excl
        ALLBUFS.append(self)


class Prog:
    def __init__(self, nc):
        self.nc = nc
        self.ops = {e: [] for e in ENGS}
        self.n_dma_sem = 0

    def new_dma_sem(self):
        self.n_dma_sem += 1
        return {"id": self.n_dma_sem - 1, "count": 0}

    def _deps_for(self, reads, writes):
        deps = []
        for b in reads:
            if b.w is not None:
                deps.append(b.w)
        for b in writes:
            if b.w is not None:
                deps.append(b.w)
            deps.extend(b.r)
        return deps

    def _post(self, tok, reads, writes):
        for b in writes:
            b.w = tok
            b.r = []
        for b in reads:
            if b not in writes:
                if len(b.r) > 64:
                    last = {}
                    for t in b.r:
                        last[(t[0], t[1])] = t
                    b.r = list(last.values())
                b.r.append(tok)

    def op(self, eng, fn, reads=(), writes=()):
        writes = list(writes) + [b for b in reads if b.excl and b not in writes]
        deps = self._deps_for(reads, writes)
        o = {"eng": eng, "fn": fn, "deps": deps, "kind": "c", "signal": False,
             "idx": len(self.ops[eng])}
        self.ops[eng].append(o)
        self._post(("e", eng, o["idx"]), reads, writes)
        return o

    def dma(self, eng, fn, sem, reads=(), writes=()):
        deps = self._deps_for(reads, writes)
        sem["count"] += 16
        o = {"eng": eng, "fn": fn, "deps": deps, "kind": "d", "sem": sem,
             "idx": len(self.ops[eng])}
        self.ops[eng].append(o)
        self._post(("d", sem["id"], sem["count"]), reads, writes)
        return o

    def barrier(self, bufs):
        deps = []
        for b in bufs:
            if b.w is not None:
                deps.append(b.w)
            deps.extend(b.r)
        for e in ENGS:
            o = {"eng": e, "fn": None, "deps": list(deps), "kind": "n",
                 "idx": len(self.ops[e])}
            self.ops[e].append(o)

    def emit(self, final_waits=()):
        nc = self.nc
        for e in ENGS:
            for o in self.ops[e]:
                for d in o["deps"]:
                    if d[0] == "e":
                        self.ops[d[1]][d[2]]["signal"] = True
        for e in ENGS:
            c = 0
            for o in self.ops[e]:
                if o["kind"] == "c" and o["signal"]:
                    c += 1
                    o["sigval"] = c
        with contextlib.ExitStack() as st:
            esem = {e: st.enter_context(nc.semaphore("s_" + e)) for e in ENGS}
            dsem = [st.enter_context(nc.semaphore("d%d" % i)) for i in range(self.n_dma_sem)]
            block = st.enter_context(nc.Block())
            prog = self

            def run(ename, eng):
                seen_e = {e: 0 for e in ENGS}
                seen_d = {}
                for o in prog.ops[ename]:
                    need_e = {}
                    need_d = {}
                    for d in o["deps"]:
                        if d[0] == "e":
                            v = prog.ops[d[1]][d[2]]["sigval"]
                            if v > need_e.get(d[1], 0):
                                need_e[d[1]] = v
                        else:
                            if d[2] > need_d.get(d[1], 0):
                                need_d[d[1]] = d[2]
                    for e2, v in need_e.items():
                        if v > seen_e[e2]:
                            eng.wait_ge(esem[e2], v)
                            seen_e[e2] = v
                    for s, v in need_d.items():
                        if v > seen_d.get(s, 0):
                            eng.wait_ge(dsem[s], v)
                            seen_d[s] = v
                    if o["kind"] == "c":
                        ins = o["fn"](eng)
                        if o["signal"]:
                            ins.then_inc(esem[ename], 1)
                    elif o["kind"] == "d":
                        ins = o["fn"](eng)
                        ins.then_inc(dsem[o["sem"]["id"]], 16)
                if ename == "sp":
                    for s in final_waits:
                        if s["count"] > 0:
                            eng.wait_ge(dsem[s["id"]], s["count"])

            @block.tensor
            def _(eng):
                run("pe", eng)

            @block.scalar
            def _(eng):
                run("act", eng)

            @block.vector
            def _(eng):
                run("dve", eng)

            @block.gpsimd
            def _(eng):
                run("pool", eng)

            @block.sync
            def _(eng):
                run("sp", eng)


def bcl(ap, m):
    return bass.AP(tensor=ap.tensor, offset=ap.offset,
                   ap=[list(x) for x in ap.ap] + [[0, m]])


def bcm(ap, k):
    a = [list(x) for x in ap.ap]
    return bass.AP(tensor=ap.tensor, offset=ap.offset, ap=[a[0], [0, k]] + a[1:])


class Rot:
    def __init__(self, items):
        self.items = items
        self.i = 0

    def next(self):
        it = self.items[self.i % len(self.items)]
        self.i += 1
        return it


C_QA, C_KA, C_VA, C_QB, C_KB, C_VB, C_FL, C_QM, C_G = 0, 512, 1024, 1536, 2048, 2560, 3072, 3080, 3592

NPRE = 16
NOWN = 16
NT = NPRE + NOWN
NS = 64
ARENA_W = 53100


def build_nc():
    nc = bass.Bass("TRN2", target_bir_lowering=False)
    P = Prog(nc)

    def din(name, shape):
        return nc.dram_tensor(name, list(shape), F32, kind="ExternalInput").ap()

    def dout(name, shape):
        return nc.dram_tensor(name, list(shape), F32, kind="ExternalOutput").ap()

    xpre = din("xpre", [2048, 1024]); xown = din("xown", [2048, 1024]); xs = din("xs", [NS, 1024])
    mem = din("mem", [256, 1024])
    cak = din("cak", [4, 512, 512]); cav = din("cav", [4, 512, 512])
    cbk = din("cbk", [4, 2048, 512]); cbv = din("cbv", [4, 2048, 512]); cblf = din("cblf", [4, 2048, 8])
    cmk = din("cmk", [4, 256, 512]); cmv = din("cmv", [4, 256, 512])
    g_mix = din("g_mix", [1, 1024]); w_in = din("w_in", [1024, 6664]); b_f = din("b_f", [1, 8])
    g_qa = din("g_qa", [1, 64]); g_ka = din("g_ka", [1, 64]); g_qb = din("g_qb", [1, 64]); g_kb = din("g_kb", [1, 64])
    g_qm = din("g_qm", [1, 128]); g_km = din("g_km", [1, 128]); rel_bias = din("rel_bias", [257, 8])
    g_mem = din("g_mem", [1, 1024]); w_mkv = din("w_mkv", [1024, 1024])
    w_pa = din("w_pa", [512, 1024]); w_pb = din("w_pb", [512, 1024]); w_pm = din("w_pm", [512, 1024])
    w_o = din("w_o", [1024, 1024]); g_ffn = din("g_ffn", [1, 1024])
    w_up = din("w_up", [1024, 4096]); w_down = din("w_down", [4096, 1024])
    cmat = din("cmat", [128, 4 * 128]); ident = din("ident", [128, 128]); flag = din("flag", [1, 1])

    y_p = dout("y_p", [2048, 1024]); y_s = dout("y_s", [NS, 1024])
    akp = dout("akp", [512, 512]); avp = dout("avp", [512, 512])
    bkp = dout("bkp", [2048, 512]); bvp = dout("bvp", [2048, 512]); blfp = dout("blfp", [2048, 8])
    mkp = dout("mkp", [256, 512]); mvp = dout("mvp", [256, 512])
    aks = dout("aks", [NS, 512]); avs = dout("avs", [NS, 512])
    bks = dout("bks", [NS, 512]); bvs = dout("bvs", [NS, 512]); blfs = dout("blfs", [NS, 8])
    relE = nc.dram_tensor("relE", [8, LREL], F32, kind="Internal").ap()
    relD = nc.dram_tensor("relD", [128, 8 * LREL], F32, kind="Internal").ap()

    st = contextlib.ExitStack()
    arena = st.enter_context(nc.sbuf_tensor("arena", [128, ARENA_W], F32))
    psum = [st.enter_context(nc.psum_tensor("ps%d" % i, [128, 512], F32)) for i in range(8)]

    class Arena:
        def __init__(self):
            self.top = 0

        def f32(self, dims, parts=128):
            n = int(np.prod(dims))
            o = self.top
            self.top += n
            assert self.top <= ARENA_W, ("arena overflow", self.top)
            ap = arena[0:parts, o:o + n]
            return self._view(ap, dims)

        def bf16(self, dims, parts=128):
            n = int(np.prod(dims))
            w = (n + 1) // 2
            o = self.top
            self.top += w
            assert self.top <= ARENA_W, ("arena overflow", self.top)
            ap = arena[0:parts, o:o + w].bitcast(BF16)[:, 0:n]
            return self._view(ap, dims)

        @staticmethod
        def _view(ap, dims):
            if len(dims) == 1:
                return ap
            if len(dims) == 2:
                return ap.rearrange("p (a b) -> p a b", a=dims[0])
            if len(dims) == 3:
                return ap.rearrange("p (a b c) -> p a b c", a=dims[0], b=dims[1])
            if len(dims) == 4:
                return ap.rearrange("p (a b c d) -> p a b c d", a=dims[0], b=dims[1], c=dims[2])
            raise ValueError

    A = Arena()
    out_sems = [P.new_dma_sem() for _ in range(4)]
    osr = Rot(out_sems)

    def mm(out, pairs, reads, writes):
        pairs = list(pairs)

        def f(e):
            n = len(pairs)
            for k, (l, r) in enumerate(pairs):
                i = e.matmul(out, lhsT=l, rhs=r, start=(k == 0), stop=(k == n - 1))
            return i
        P.op("pe", f, reads, writes)

    def transposes(items, reads, writes):
        items = list(items)

        def f(e):
            for (o, i_, idn) in items:
                ins = e.transpose(out=o, in_=i_, identity=idn)
            return ins
        P.op("pe", f, reads, writes)

    def act(out, in_, func, reads, writes, bias=None, scale=None, accum=None, eng="act"):
        kw = {}
        if bias is not None:
            kw["bias"] = bias
        if scale is not None:
            kw["scale"] = scale
        if accum is not None:
            kw["accum_out"] = accum
        P.op(eng, lambda e: e.activation(out=out, in_=in_, func=func, **kw), reads, writes)

    def tt(out, in0, in1, op, reads, writes, eng="dve"):
        P.op(eng, lambda e: e.tensor_tensor(out=out, in0=in0, in1=in1, op=op), reads, writes)

    def ts(out, in0, s1, s2, op0, op1, reads, writes, eng="dve"):
        if s2 is None:
            P.op(eng, lambda e: e.tensor_scalar(out=out, in0=in0, scalar1=s1, scalar2=None, op0=op0), reads, writes)
        else:
            P.op(eng, lambda e: e.tensor_scalar(out=out, in0=in0, scalar1=s1, scalar2=s2, op0=op0, op1=op1), reads, writes)

    def stt(out, in0, scalar, in1, op0, op1, reads, writes, eng="dve"):
        P.op(eng, lambda e: e.scalar_tensor_tensor(out=out, in0=in0, scalar=scalar, in1=in1, op0=op0, op1=op1), reads, writes)

    def cp(out, in_, reads, writes, eng="dve"):
        P.op(eng, lambda e: e.tensor_copy(out=out, in_=in_), reads, writes)

    def recip(out, in_, reads, writes):
        P.op("dve", lambda e: e.reciprocal(out=out, in_=in_), reads, writes)

    def red(out, in_, reads, writes):
        P.op("dve", lambda e: e.tensor_reduce(out=out, in_=in_, axis=AX.X, op=ALU.add), reads, writes)

    def memset(ap, val, writes, eng="dve"):
        P.op(eng, lambda e: e.memset(ap, val), (), writes)

    BUFSEM = {}

    def sem_of(b):
        if id(b) not in BUFSEM:
            BUFSEM[id(b)] = P.new_dma_sem()
        return BUFSEM[id(b)]

    STORE_SEMS = []

    def ld(out, in_, sem, writes, reads=(), eng="sp", slow=False):
        sem = sem_of(writes[0])
        if slow:
            P.dma(eng, lambda e: e.dma_start(out=out, in_=in_, allow_slow_non_contiguous=True), sem, reads, writes)
        else:
            P.dma(eng, lambda e: e.dma_start(out=out, in_=in_), sem, reads, writes)

    def store(out, in_, reads):
        sem = sem_of(reads[0])
        if sem not in STORE_SEMS:
            STORE_SEMS.append(sem)
        P.dma("sp", lambda e: e.dma_start(out=out, in_=in_), sem, reads, ())

    def rstd_from_ss(ap, n, bufs):
        act(ap, ap, AF.Sqrt, list(bufs) + [b_eps], bufs, bias=EPSB[0:ap.shape[0], 0:1], scale=1.0 / n)
        recip(ap, ap, bufs, bufs)

    csem = P.new_dma_sem()
    identb = A.bf16([128]); b_id = Buf()
    ld(identb, ident, csem, [b_id], eng="pool")
    trif = A.f32([4, 128]); b_cm = Buf()
    ld(trif, cmat.rearrange("p (a b) -> p a b", a=4), csem, [b_cm])
    TRI, ONES, SELMID, SEL0 = trif[:, 0, :], trif[:, 1, :], trif[:, 2, :], trif[:, 3, :]
    trib = A.bf16([128]); b_trib = Buf()
    cp(trib, TRI, [b_cm], [b_trib])
    gmix = A.f32([1024]); b_g = Buf()
    ld(gmix, g_mix.partition_broadcast(128), csem, [b_g])
    gv = {}
    for nm, src, n in (("qa", g_qa, 64), ("ka", g_ka, 64), ("qb", g_qb, 64), ("kb", g_kb, 64),
                       ("qm", g_qm, 128), ("km", g_km, 128)):
        gv[nm] = A.f32([n])
        ld(gv[nm], src.partition_broadcast(128), csem, [b_g])
    bfb = A.f32([8])
    ld(bfb, b_f.partition_broadcast(128), csem, [b_g])
    EPSB = A.f32([1]); b_eps = Buf()
    memset(EPSB, EPS, [b_eps])
    negflag = A.f32([1]); b_nf = Buf()
    ld(negflag, flag.partition_broadcast(128), csem, [b_nf])
    ts(negflag, negflag, -1.0, -NEG, ALU.add, ALU.mult, [b_nf], [b_nf])

    xt_r = Rot([(A.f32([1024]), Buf(), P.new_dma_sem()) for _ in range(2)])
    sq_r = Rot([(A.bf16([1024]), Buf()) for _ in range(1)])
    xn_r = Rot([(A.bf16([1024]), Buf()) for _ in range(2)])
    st_r = Rot([(A.f32([512]), Buf()) for _ in range(3)])
    sb_r = Rot([(A.bf16([512]), Buf()) for _ in range(2)])
    sm_r = Rot([(A.f32([16]), Buf()) for _ in range(4)])
    pt_r = Rot([(A.bf16([128]), Buf()) for _ in range(12)])
    sa_r = Rot([(A.f32([128]), Buf()) for _ in range(12)])
    fin_r = Rot([(A.bf16([512]), Buf()) for _ in range(2)])
    bias_r = Rot([(A.f32([8]), Buf()) for _ in range(4)])
    w_r = Rot([(A.bf16([4224]), Buf(), P.new_dma_sem()) for _ in range(2)])
    PB = [Buf(excl=True) for _ in range(8)]
    pz_r = Rot([(psum[0], PB[0]), (psum[1], PB[1])])
    pT_r = Rot([(psum[2].bitcast(BF16), PB[2]), (psum[3].bitcast(BF16), PB[3])])
    pS_r = Rot([(psum[4], PB[4]), (psum[5], PB[5])])
    pO = [(psum[6], PB[6]), (psum[7], PB[7])]

    PIPEQ = []
    PST = {"s": 0, "r": 0}
    PFN = {}
    PDEF = []

    def defer(fn, delay):
        PDEF.append([PST["r"] + delay, fn])

    def run_deferred(all_=False):
        while True:
            todo = [d for d in PDEF if all_ or d[0] <= PST["r"]]
            if not todo:
                break
            for d in todo:
                PDEF.remove(d)
            for d in todo:
                d[1]()

    def pipe_pump(drain):
        nb = len(pS_r.items)
        while True:
            if PST["s"] < len(PIPEQ) and PST["s"] - PST["r"] < nb:
                ent = PIPEQ[PST["s"]]
                ent["bank"] = PFN["S"](ent["q"])
                PST["s"] += 1
                continue
            if PST["r"] < PST["s"] and (drain or PST["s"] - PST["r"] >= nb):
                ent = PIPEQ[PST["r"]]
                PFN["R"](ent["q"], *ent["bank"])
                for cb in ent["after"]:
                    cb()
                ent["after"] = []
                PST["r"] += 1
                run_deferred()
                continue
            break
        if drain:
            run_deferred(True)

    def pipe_drain():
        pipe_pump(True)

    def psum_mode(attn):
        pipe_drain()
        if attn:
            pz_r.items = [(psum[0], PB[0])]
            pT_r.items = [(psum[2].bitcast(BF16), PB[2])]
            pS_r.items = [(psum[1], PB[1]), (psum[3], PB[3]), (psum[4], PB[4]), (psum[5], PB[5])]
        else:
            pz_r.items = [(psum[0], PB[0]), (psum[1], PB[1]), (psum[4], PB[4]), (psum[5], PB[5])]
            pT_r.items = [(psum[2].bitcast(BF16), PB[2]), (psum[3].bitcast(BF16), PB[3])]
            pS_r.items = [(psum[4], PB[4]), (psum[5], PB[5])]

    psum_mode(False)

    base_top = A.top

    def wload(src, kc, n):
        wt, wb, wsem = w_r.next()
        v = wt[:, 0:kc * n].rearrange("p (c n) -> p c n", c=kc)
        ld(v, src.rearrange("(c p) n -> p c n", p=128), wsem, [wb], eng="pool")
        return v, wb

    def norm_T(xrows, nt, gbc, dstT, dstbuf, x_keep=None):
        if isinstance(xrows, tuple):
            xt, xb = xrows
        else:
            xt, xb, xsem = xt_r.next()
            ld(xt[0:nt], xrows, xsem, [xb])
        sq, sqb = sq_r.next()
        sm, smb = sm_r.next()
        act(sq[0:nt], xt[0:nt], AF.Square, [xb], [sqb, smb], accum=sm[0:nt, 0:1])
        rstd_from_ss(sm[0:nt, 0:1], 1024.0, [smb])
        xn, xnb = xn_r.next()
        stt(xn[0:nt], xt[0:nt], sm[0:nt, 0:1], gbc[0:nt], ALU.mult, ALU.mult, [xb, smb, b_g], [xnb])
        pT, pTb = pT_r.next()
        transposes([(pT[:, c * 128:c * 128 + nt], xn[0:nt, c * 128:(c + 1) * 128], identb[0:nt, 0:nt])
                    for c in range(8)], [xnb, b_id], [pTb])
        cp(dstT, pT.rearrange("p (c t) -> p c t", c=8)[:, :, 0:nt], [pTb], [dstbuf])

    def norm_T_staged(xrows, nt, gbc, dstT, dstbuf):
        xt, xb, xsem = xt_r.next()
        ld(xt[0:nt], xrows, xsem, [xb])
        sq, sqb = sq_r.next()
        sm, smb = sm_r.next()
        hold = {}

        def sA():
            act(sq[0:nt], xt[0:nt], AF.Square, [xb], [sqb, smb], accum=sm[0:nt, 0:1])
            act(sm[0:nt, 0:1], sm[0:nt, 0:1], AF.Sqrt, [smb, b_eps], [smb], bias=EPSB[0:nt, 0:1], scale=1.0 / 1024.0)

        def sB():
            recip(sm[0:nt, 0:1], sm[0:nt, 0:1], [smb], [smb])
            xn, xnb = xn_r.next()
            stt(xn[0:nt], xt[0:nt], sm[0:nt, 0:1], gbc[0:nt], ALU.mult, ALU.mult, [xb, smb, b_g], [xnb])
            hold["pT"] = pT_r.next()
            pT, pTb = hold["pT"]
            transposes([(pT[:, c * 128:c * 128 + nt], xn[0:nt, c * 128:(c + 1) * 128], identb[0:nt, 0:nt])
                        for c in range(8)], [xnb, b_id], [pTb])

        def sC():
            pT, pTb = hold["pT"]
            cp(dstT, pT.rearrange("p (c t) -> p c t", c=8)[:, :, 0:nt], [pTb], [dstbuf])
        return sA, sB, sC

    def proj_tok(xT, xTb, nt, wv, wb, ncols, c0=0):
        pz, pzb = pz_r.next()
        mm(pz[0:nt, 0:ncols], [(xT[:, c, :], wv[:, c, c0:c0 + ncols]) for c in range(8)], [xTb, wb], [pzb])
        return pz, pzb

    hn_sm_r = Rot([(A.f32([16]), Buf()) for _ in range(3)])

    def head_norm(pz, pzb, nt, nh, dh, g, out_dram, dstT=None, dstTb=None, pair=True, defer_T=0, staged=False):
        n = nh * dh
        sq, sqb = sq_r.next()
        sm, smb = hn_sm_r.next() if (defer_T or staged) else sm_r.next()
        stg, stb = st_r.next()
        sbf, sbb = (None, None)
        if dstT is not None:
            sbf, sbb = sb_r.next()
        v3 = lambda a: a[0:nt, 0:n].rearrange("p (h d) -> p h d", h=nh)

        def stage_act():
            def fsq(e):
                for h in range(nh):
                    ins = e.activation(out=sq[0:nt, h * dh:(h + 1) * dh], in_=pz[0:nt, h * dh:(h + 1) * dh],
                                       func=AF.Square, accum_out=sm[0:nt, h:h + 1])
                return ins
            P.op("act", fsq, [pzb], [sqb, smb])
            act(sm[0:nt, 0:nh], sm[0:nt, 0:nh], AF.Sqrt, [smb, b_eps], [smb], bias=EPSB[0:nt, 0:1], scale=1.0 / dh)

        def stage_dve():
            recip(sm[0:nt, 0:nh], sm[0:nt, 0:nh], [smb], [smb])
            tt(v3(stg), v3(pz), bcl(sm[0:nt, 0:nh], dh), ALU.mult, [pzb, smb], [stb])
            if out_dram is not None:
                tt(v3(stg), v3(stg), bcm(g[0:nt], nh), ALU.mult, [stb, b_g], [stb])
                store(out_dram, stg[0:nt, 0:n], [stb])
                if dstT is not None:
                    cp(sbf[0:nt, 0:n], stg[0:nt, 0:n], [stb], [sbb])
            else:
                tt(v3(sbf), v3(stg), bcm(g[0:nt], nh), ALU.mult, [stb, b_g], [sbb])

        def stage_pe():
            if dstT is None:
                return
            pT, pTb = pT_r.next()
            nch = n // 128
            transposes([(pT[:, c * 128:c * 128 + nt], sbf[0:nt, c * 128:(c + 1) * 128], identb[0:nt, 0:nt])
                        for c in range(nch)], [sbb, b_id], [pTb])
            cp(dstT, pT[:, 0:nch * 128].rearrange("p (c t) -> p c t", c=nch)[:, :, 0:nt], [pTb], [dstTb])

        if staged:
            return stage_act, stage_dve, stage_pe
        if defer_T:
            defer(stage_act, defer_T)
            defer(stage_dve, 2 * defer_T)
            defer(stage_pe, 3 * defer_T)
        else:
            stage_act()
            stage_dve()
            stage_pe()

    def pipelined(n_tiles, make):
        st_ = {}
        for it in range(n_tiles + 2):
            if it < n_tiles:
                st_[it] = make(it)
                st_[it][0]()
            if 0 <= it - 1 < n_tiles:
                st_[it - 1][1]()
            if 0 <= it - 2 < n_tiles:
                st_.pop(it - 2)[2]()

    def v_store(pz, pzb, nt, nh, dh, out_dram, vaug, vb):
        n = nh * dh
        stg, stb = st_r.next()
        cp(stg[0:nt, 0:n], pz[0:nt, 0:n], [pzb], [stb])
        if out_dram is not None:
            store(out_dram, stg[0:nt, 0:n], [stb])
        act(vaug[0:nt, :, 0:dh], stg[0:nt, 0:n].rearrange("p (h d) -> p h d", h=nh), AF.Copy, [stb], [vb])

    def logf_a(xT, xTb, nt, wv, wb, c0, LFdst, lfb):
        pz, pzb = pz_r.next()
        mm(pz[0:nt, 0:8], [(xT[:, c, :], wv[:, c, c0:c0 + 8]) for c in range(8)], [xTb, wb], [pzb])
        tt(LFdst, pz[0:nt, 0:8], bfb[0:nt], ALU.add, [pzb, b_g], [lfb])

    def logf_b(LFdst, lfb):
        act(LFdst, LFdst, AF.Exp, [lfb], [lfb], scale=-1.0)
        act(LFdst, LFdst, AF.Ln, [lfb], [lfb], bias=1.0)

    def logf_c(LFdst, lfb, out_dram):
        ts(LFdst, LFdst, -1.0, None, ALU.mult, None, [lfb], [lfb])
        if out_dram is not None:
            store(out_dram, LFdst, [lfb])

    def logf_tile(xT, xTb, nt, wv, wb, c0, LFdst, lfb, out_dram):
        logf_a(xT, xTb, nt, wv, wb, c0, LFdst, lfb)
        logf_b(LFdst, lfb)
        logf_c(LFdst, lfb, out_dram)

    def cumsum_tile(LF, lfb, nt, R, rb, Cdst, cb, first):
        pz, pzb = pz_r.next()
        if first:
            mm(pz[0:nt, 0:8], [(TRI[0:nt, 0:nt], LF)], [lfb, b_cm], [pzb])
        else:
            mm(pz[0:nt, 0:8], [(TRI[0:nt, 0:nt], LF), (ONES[:, 0:nt], R)], [lfb, b_cm, rb], [pzb])
        cp(Cdst, pz[0:nt, 0:8], [pzb], [cb])
        if first:
            memset(R, 0.0, [rb])
        tt(R[0:nt], R[0:nt], LF, ALU.add, [rb, lfb], [rb])

    def run_sums(LFs_list, Rall, rb, first_is_zero):
        if first_is_zero:
            memset(Rall[:, 0, :], 0.0, [rb])
        for t, (lf, lfb, nt) in enumerate(LFs_list):
            if nt < 128:
                continue
            tt(Rall[:, t + 1, :], Rall[:, t, :], lf, ALU.add, [rb, lfb], [rb])

    def cumsum_par(LF, lfb, nt, Rt, rb, Cdst, cb):
        pz, pzb = pz_r.next()
        mm(pz[0:nt, 0:8], [(TRI[0:nt, 0:nt], LF), (ONES[:, 0:nt], Rt)], [lfb, b_cm, rb], [pzb])
        cp(Cdst, pz[0:nt, 0:8], [pzb], [cb])

    def bcast_row(src, sb_, nrows, SEL, dst, db):
        pz, pzb = pz_r.next()
        mm(pz[:, 0:8], [(SEL[0:nrows, :], src)], [sb_, b_cm], [pzb])
        cp(dst, pz[:, 0:8], [pzb], [db])

    def attn_head(qT, nq, ktiles, po_ap, pob, scale, first_reads):
        n = len(ktiles)
        PENDING.append([dict(kt=kt, qT=qT, nq=nq, po_ap=po_ap, pob=pob, scale=scale,
                             first_reads=list(first_reads), start=(idx == 0), stop=(idx == n - 1))
                        for idx, kt in enumerate(ktiles)])

    PENDING = []
    LA = 4

    def attn_flush():
        calls = list(PENDING)
        del PENDING[:]
        steps = [sp_ for c_ in calls for sp_ in c_]
        quads = [steps[i:i + 4] for i in range(0, len(steps), 4)]

        def uniq(bl):
            out = []
            for b in bl:
                if b not in out:
                    out.append(b)
            return out

        def emit_S(q):
            bank, bb = pS_r.next()
            q = list(q)

            def f(e):
                for k, sp_ in enumerate(q):
                    kt = sp_["kt"]
                    ins = e.matmul(bank[0:kt["nk"], k * 128:k * 128 + sp_["nq"]], lhsT=kt["kT"], rhs=sp_["qT"],
                                   start=True, stop=True)
                return ins
            rds = []
            for sp_ in q:
                rds += list(sp_["kt"]["rd"]) + sp_["first_reads"]
            P.op("pe", f, uniq(rds), [bb])
            return bank, bb

        def emit_rest(q, bank, bb):
            q = list(q)
            pts = [pt_r.next() for _ in q]
            sas = {}
            pre_items = []
            for k, sp_ in enumerate(q):
                if sp_["kt"].get("pre") is not None or sp_["kt"].get("prebias") is not None:
                    sas[k] = sa_r.next()
                    pre_items.append(k)
            if pre_items:
                def fd(e):
                    for k in pre_items:
                        sp_ = q[k]
                        kt = sp_["kt"]
                        nk, nq = kt["nk"], sp_["nq"]
                        if kt.get("pre") is not None:
                            ins = e.scalar_tensor_tensor(out=sas[k][0][0:nk, 0:nq], in0=bank[0:nk, k * 128:k * 128 + nq],
                                                         scalar=sp_["scale"], in1=kt["pre"][0], op0=ALU.mult, op1=ALU.add)
                        else:
                            ins = e.tensor_scalar(out=sas[k][0][0:nk, 0:nq], in0=bank[0:nk, k * 128:k * 128 + nq],
                                                  scalar1=sp_["scale"], scalar2=kt["prebias"][0],
                                                  op0=ALU.mult, op1=ALU.add)
                    return ins
                rds = [bb]
                for k in pre_items:
                    rds += list((q[k]["kt"].get("pre") or q[k]["kt"].get("prebias"))[1])
                P.op("dve", fd, uniq(rds), [sas[k][1] for k in pre_items])

            def fa(e):
                for k, sp_ in enumerate(q):
                    kt = sp_["kt"]
                    nk, nq = kt["nk"], sp_["nq"]
                    if k in sas:
                        ins = e.activation(out=pts[k][0][0:nk, 0:nq], in_=sas[k][0][0:nk, 0:nq], func=AF.Exp,
                                           bias=(0.0 if kt.get("prebias") is not None else kt.get("bias", 0.0)))
                    else:
                        ins = e.activation(out=pts[k][0][0:nk, 0:nq], in_=bank[0:nk, k * 128:k * 128 + nq], func=AF.Exp,
                                           bias=kt.get("bias", 0.0), scale=sp_["scale"])
                return ins
            rds = [bb] + [sas[k][1] for k in pre_items]
            for sp_ in q:
                rds += list(sp_["kt"].get("brd", []))
            P.op("act", fa, uniq(rds), [p_[1] for p_ in pts])
            for k, sp_ in enumerate(q):
                kt = sp_["kt"]
                if kt.get("mask") is not None:
                    nk, nq = kt["nk"], sp_["nq"]
                    tt(pts[k][0][0:nk, 0:nq], pts[k][0][0:nk, 0:nq], kt["mask"][0:nk, 0:nq], ALU.mult,
                       [pts[k][1], b_trib, b_mask0], [pts[k][1]], eng="pool")
            def fp(e):
                for k, sp_ in enumerate(q):
                    kt = sp_["kt"]
                    nk, nq = kt["nk"], sp_["nq"]
                    ins = e.matmul(sp_["po_ap"], lhsT=pts[k][0][0:nk, 0:nq], rhs=kt["v"],
                                   start=sp_["start"], stop=sp_["stop"])
                return ins
            rds = [p_[1] for p_ in pts]
            for sp_ in q:
                rds += list(sp_["kt"]["rd"])
            P.op("pe", fp, uniq(rds), uniq([sp_["pob"] for sp_ in q]))

        for q in quads:
            PIPEQ.append({"q": q, "bank": None, "after": []})
        PFN["S"], PFN["R"] = emit_S, emit_rest
        pipe_pump(False)


    def mm_acc(out, l, r, start, stop, reads, writes):
        P.op("pe", lambda e: e.matmul(out, lhsT=l, rhs=r, start=start, stop=stop), reads, writes)

    FIN = {}

    def attn_finish_group(nq, nh, dh, half):
        attn_flush()
        PIPEQ[-1]["after"].append(lambda: fin_group(nq, nh, dh, half))
        if PST["r"] >= len(PIPEQ):
            raise RuntimeError("pipeline ran ahead of finish")

    def fin_group(nq, nh, dh, half):
        hpb = nh // 2
        if half == 0:
            FIN["sbf"] = fin_r.next()
        sbf, sbb = FIN["sbf"]
        po, pob = pO[half]
        pv = po[0:nq, 0:hpb * (dh + 1)].rearrange("p (h d) -> p h d", h=hpb)
        sm, smb = sm_r.next()
        recip(sm[0:nq, 0:hpb], pv[:, :, dh], [pob], [smb])
        if dh == 64:
            ov_ = sbf[0:nq, 0:512].rearrange("p (hp two d) -> p hp two d", two=2, d=64)[:, :, half, :]
        else:
            ov_ = sbf[0:nq, half * hpb * dh:(half + 1) * hpb * dh].rearrange("p (h d) -> p h d", h=hpb)
        tt(ov_, pv[:, :, 0:dh], bcl(sm[0:nq, 0:hpb], dh), ALU.mult, [pob, smb], [sbb])

    def attn_finish(nq, nh, dh, dstT, dstTb):
        PIPEQ[-1]["after"].append(lambda: fin_tile(nq, dstT, dstTb))

    def fin_tile(nq, dstT, dstTb):
        sbf, sbb = FIN["sbf"]

        def part2():
            pT, pTb = pT_r.next()
            transposes([(pT[:, c * 128:c * 128 + nq], sbf[0:nq, c * 128:(c + 1) * 128], identb[0:nq, 0:nq])
                        for c in range(4)], [sbb, b_id], [pTb])
            cp(dstT, pT[:, 0:512].rearrange("p (c t) -> p c t", c=4)[:, :, 0:nq], [pTb], [dstTb])
        defer(part2, 2)

    def po_slot(h, nh, dh):
        hpb = nh // 2
        if dh == 64:
            po, pob = pO[h % 2]
            hh = h // 2
        else:
            po, pob = pO[h // hpb]
            hh = h % hpb
        return po, pob, hh * (dh + 1)

    BiasA = A.f32([8, 2, 128]); b_bias = Buf()
    relc = A.f32([8]); relcnf = A.f32([8])
    ld(relc, rel_bias[256:257, :].partition_broadcast(128), csem, [b_bias])
    ts(relcnf, relc, negflag[:, 0:1], None, ALU.add, None, [b_bias, b_nf], [b_bias])
    mask0 = A.bf16([128]); b_mask0 = Buf()
    memset(mask0, 1.0, [b_mask0])
    memset(mask0[0:64, 64:128], 0.0, [b_mask0])
    mark0 = A.top
    relx = A.f32([LREL], parts=8); b_relx = Buf()
    rsem = P.new_dma_sem()
    memset(relx, 0.0, [b_relx])
    sm8 = A.f32([1], parts=8)
    ld(sm8, rel_bias[256:257, :].rearrange("i h -> h i"), rsem, [b_relx], slow=True)
    ts(relx, relx, sm8[:, 0:1], None, ALU.add, None, [b_relx], [b_relx])
    ld(relx[:, 0:257], rel_bias.rearrange("i h -> h i"), rsem, [b_relx], slow=True)
    b_relE = Buf()
    ld(relE, relx, rsem, [b_relE], reads=[b_relx])
    relrep = A.f32([8, LREL]); b_rr = Buf()
    ld(relrep, relE.partition_broadcast(128), rsem, [b_rr], reads=[b_relE])
    b_relD = Buf()
    ld(relD, relrep.rearrange("p a b -> p (a b)"), rsem, [b_relD], reads=[b_rr])
    for t in (3, 4):
        src = bass.AP(tensor=relD.tensor, offset=640 - 128 * t,
                      ap=[[8 * LREL - 1, 128], [LREL, 8], [1, 128]])
        ld(BiasA[:, :, t - 3, :], src, rsem, [b_bias], reads=[b_relD])
    memset(BiasA[64:128, :, 1, 0:64], NEG, [b_bias])
    P.barrier(list(ALLBUFS))
    A.top = mark0

    mkT = A.bf16([4, 256]); b_mkT = Buf()
    mvA = A.bf16([2, 4, 129]); b_mv = Buf()
    memset(mvA[:, :, :, 128:129], 1.0, [b_mv])
    mark1 = A.top
    gmem = A.f32([1024]); b_gm = Buf()
    ld(gmem, g_mem.partition_broadcast(128), csem, [b_g])
    xmT = A.bf16([8, 256]); b_xmT = Buf()
    for t in range(2):
        norm_T(mem[t * 128:(t + 1) * 128, :], 128, gmem, xmT[:, :, t * 128:(t + 1) * 128], b_xmT)
    wv, wb = wload(w_mkv[:, 0:512], 8, 512)
    for t in range(2):
        pz, pzb = proj_tok(xmT[:, :, t * 128:(t + 1) * 128], b_xmT, 128, wv, wb, 512)
        head_norm(pz, pzb, 128, 4, 128, gv["km"], mkp[t * 128:(t + 1) * 128, :],
                  mkT[:, :, t * 128:(t + 1) * 128], b_mkT)
    wv, wb = wload(w_mkv[:, 512:1024], 8, 512)
    for t in range(2):
        pz, pzb = proj_tok(xmT[:, :, t * 128:(t + 1) * 128], b_xmT, 128, wv, wb, 512)
        v_store(pz, pzb, 128, 4, 128, mvp[t * 128:(t + 1) * 128, :], mvA[:, t], b_mv)
    P.barrier(list(ALLBUFS))
    A.top = mark1

    def full_barrier():
        pipe_drain()
        P.barrier(list(ALLBUFS))

    mark_x = A.top
    xnT = A.bf16([8, 2048]); b_xnT = [Buf() for _ in range(NOWN)]
    xsT = A.bf16([8, NS]); b_xsT = Buf()
    osT = {k: A.bf16([4, NS]) for k in "bam"}
    b_osT = {k: Buf() for k in "bam"}
    oT = {}
    b_oT = {k: Buf() for k in "bam"}
    KsT_new = A.bf16([4, NS]); b_KsN = Buf()
    QsT = A.bf16([4, NS]); b_QsT = Buf()
    VsN = A.bf16([4, 8, 65], parts=16); b_VsN = Buf()
    LFs = A.f32([4, 8], parts=16); b_LFs = Buf()
    memset(VsN[:, :, :, 64:65], 1.0, [b_VsN])
    oT["b"] = A.bf16([4, 2048])

    def xT_of(i):
        return xnT[:, :, i * 128:(i + 1) * 128], b_xnT[i]

    markB = A.top
    KbT = A.bf16([4, NT * 128]); b_KbT = [Buf() for _ in range(NT)]
    Vb = A.bf16([NT, 8, 65]); b_Vb = [Buf() for _ in range(NT)]
    Qb_r = Rot([(A.bf16([4, 128]), Buf()) for _ in range(2)])
    LF = A.f32([NT, 8]); b_LF = [Buf() for _ in range(NT)]
    Cc = A.f32([NT, 8]); b_C = [Buf() for _ in range(NT)]
    crefs = A.f32([NOWN, 8]); b_cref = [Buf() for _ in range(NOWN)]
    Rall = A.f32([NT + 1, 8]); b_R = Buf()
    biasT_r = Rot([(A.f32([NT + 1, 8]), Buf()) for _ in range(3)])
    memset(Vb[:, :, :, 64:65], 1.0, b_Vb)
    def mk_nt_pre(i):
        xv, xb_ = xT_of(i)
        return norm_T_staged(xpre[i * 128:(i + 1) * 128, :], 128, gmix, xv, xb_)
    pipelined(NPRE, mk_nt_pre)
    wv, wb = wload(w_in[:, C_KB:C_KB + 512], 8, 512)
    def mk_kb_pre(i):
        xv, xb_ = xT_of(i)
        pz, pzb = proj_tok(xv, xb_, 128, wv, wb, 512)
        return head_norm(pz, pzb, 128, 8, 64, gv["kb"], None, KbT[:, :, i * 128:(i + 1) * 128], b_KbT[i], staged=True)
    pipelined(NPRE, mk_kb_pre)
    wv, wb = wload(w_in[:, C_VB:C_VB + 520], 8, 520)
    for i in range(NPRE):
        xv, xb_ = xT_of(i)
        pz, pzb = proj_tok(xv, xb_, 128, wv, wb, 512)
        v_store(pz, pzb, 128, 8, 64, None, Vb[:, i], b_Vb[i])
        logf_a(xv, xb_, 128, wv, wb, 512, LF[:, i, :], b_LF[i])
    for i in range(NPRE):
        logf_b(LF[:, i, :], b_LF[i])
    for i in range(NPRE):
        logf_c(LF[:, i, :], b_LF[i], None)
    run_sums([(LF[:, i, :], b_LF[i], 128) for i in range(NPRE)], Rall, b_R, True)
    for i in range(NPRE):
        cumsum_par(LF[:, i, :], b_LF[i], 128, Rall[:, i, :], b_R, Cc[:, i, :], b_C[i])
    def mk_nt_own(i):
        xv, xb_ = xT_of(i)
        return norm_T_staged(xown[i * 128:(i + 1) * 128, :], 128, gmix, xv, xb_)
    pipelined(NOWN, mk_nt_own)
    norm_T(xs, NS, gmix, xsT, b_xsT)
    wv, wb = wload(w_in[:, C_KB:C_KB + 512], 8, 512)
    def mk_kb_own(i):
        xv, xb_ = xT_of(i)
        pz, pzb = proj_tok(xv, xb_, 128, wv, wb, 512)
        return head_norm(pz, pzb, 128, 8, 64, gv["kb"], bkp[i * 128:(i + 1) * 128, :],
                         KbT[:, :, (NPRE + i) * 128:(NPRE + i + 1) * 128], b_KbT[NPRE + i], staged=True)
    pipelined(NOWN, mk_kb_own)
    pz, pzb = proj_tok(xsT, b_xsT, NS, wv, wb, 512)
    head_norm(pz, pzb, NS, 8, 64, gv["kb"], bks, KsT_new, b_KsN)
    wv, wb = wload(w_in[:, C_VB:C_VB + 520], 8, 520)
    for i in range(NOWN):
        xv, xb_ = xT_of(i)
        j = NPRE + i
        pz, pzb = proj_tok(xv, xb_, 128, wv, wb, 512)
        v_store(pz, pzb, 128, 8, 64, bvp[i * 128:(i + 1) * 128, :], Vb[:, j], b_Vb[j])
        logf_a(xv, xb_, 128, wv, wb, 512, LF[:, j, :], b_LF[j])
    for i in range(NOWN):
        logf_b(LF[:, NPRE + i, :], b_LF[NPRE + i])
    for i in range(NOWN):
        logf_c(LF[:, NPRE + i, :], b_LF[NPRE + i], blfp[i * 128:(i + 1) * 128, :])
    for i in range(NOWN):
        j = NPRE + i
        tt(Rall[:, j + 1, :], Rall[:, j, :], LF[:, j, :], ALU.add, [b_R, b_LF[j]], [b_R])
    for i in range(NOWN):
        j = NPRE + i
        cumsum_par(LF[:, j, :], b_LF[j], 128, Rall[:, j, :], b_R, Cc[:, j, :], b_C[j])
    for i in range(NOWN):
        j = NPRE + i
        bcast_row(Cc[:, j, :], b_C[j], 128, SELMID, crefs[:, i, :], b_cref[i])
    for bb in range(4):
        xv = xsT[:, :, bb * 16:(bb + 1) * 16]
        pz, pzb = proj_tok(xv, b_xsT, 16, wv, wb, 512)
        v_store(pz, pzb, 16, 8, 64, bvs[bb * 16:(bb + 1) * 16, :], VsN[:, bb], b_VsN)
        logf_tile(xv, b_xsT, 16, wv, wb, 512, LFs[:, bb, :], b_LFs, blfs[bb * 16:(bb + 1) * 16, :])
    wv, wb = wload(w_in[:, C_QB:C_QB + 512], 8, 512)
    pz, pzb = proj_tok(xsT, b_xsT, NS, wv, wb, 512)
    head_norm(pz, pzb, NS, 8, 64, gv["qb"], None, QsT, b_QsT)

    psum_mode(True)
    def prep_b(i):
        gi = NPRE + i
        xv, xb_ = xT_of(i)
        pz, pzb = proj_tok(xv, xb_, 128, wv, wb, 512)
        qv, qvb = Qb_r.next()
        head_norm(pz, pzb, 128, 8, 64, gv["qb"], None, qv, qvb, defer_T=2)
        bt_all, b_bt = biasT_r.next()
        for j in range(gi + 1):
            tt(bt_all[:, j, :], crefs[:, i, :], Cc[:, j, :], ALU.subtract, [b_cref[i], b_C[j]], [b_bt])
        ts(bt_all[:, 0:NPRE, :], bt_all[:, 0:NPRE, :], negflag[:, 0:1], None, ALU.add, None, [b_bt, b_nf], [b_bt])
        return qv, qvb, bt_all, b_bt

    nxt = prep_b(0)
    for i in range(NOWN):
        gi = NPRE + i
        run_deferred(True)
        qv, qvb, bt_all, b_bt = nxt
        if i + 1 < NOWN:
            nxt = prep_b(i + 1)
        for h in (0, 2, 4, 6, 1, 3, 5, 7):
            hp, hr = h // 2, (h % 2) * 64
            po, pob, off = po_slot(h, 8, 64)
            kts = []
            for j in range(gi + 1):
                d_ = dict(kT=KbT[hr:hr + 64, hp, j * 128:(j + 1) * 128], nk=128,
                          v=Vb[:, j, h, :], rd=[b_KbT[j], b_Vb[j]],
                          bias=bt_all[:, j, h:h + 1], brd=[b_bt],
                          mask=(trib if j == gi else None))
                if j % 4 != 3:
                    d_["prebias"] = (bt_all[:, j, h:h + 1], [b_bt])
                kts.append(d_)
            attn_head(qv[hr:hr + 64, hp, :], 128, kts, po[:, off:off + 65], pob,
                      0.125, [qvb])
            if h >= 6:
                attn_finish_group(128, 8, 64, h % 2)
        attn_finish(128, 8, 64, oT["b"][:, :, i * 128:(i + 1) * 128], b_oT["b"])
    full_barrier()
    A.top = markB

    Kc = A.bf16([16, 512]); b_Kc = [Buf() for _ in range(4)]; kcsem = None
    KsT = A.bf16([4, 2048]); b_KsT = Buf()
    Vs = A.bf16([16, 8, 65]); b_Vs = [Buf() for _ in range(16)]
    LFc = A.f32([16, 8]); b_LFc = Buf()
    Cs = A.f32([16, 8]); b_Cs = Buf()
    CsN = A.f32([8], parts=16); b_CsN = Buf()
    crefS = A.f32([8]); b_crefS = Buf()
    memset(Vs[:, :, :, 64:65], 1.0, b_Vs)
    for bb in range(4):
        for q4 in range(4):
            ld(Kc[:, 4 * q4:4 * q4 + 4, :], cbk[bb, 512 * q4:512 * (q4 + 1), :].rearrange("(t p) n -> p t n", p=128),
               kcsem, [b_Kc[q4]], eng="pool")
        for t in range(16):
            ld(Vs[:, t, :, 0:64], cbv[bb, 128 * t:128 * (t + 1), :].rearrange("p (h d) -> p h d", h=8),
               kcsem, [b_Vs[t]], eng="pool")
        ld(LFc, cblf[bb].rearrange("(t p) h -> p t h", p=128), kcsem, [b_LFc])
        for t in range(16):
            pT, pTb = pT_r.next()
            transposes([(pT[:, c * 128:(c + 1) * 128], Kc[:, t, c * 128:(c + 1) * 128], identb) for c in range(4)],
                       [b_Kc[t // 4], b_id], [pTb])
            cp(KsT[:, :, t * 128:(t + 1) * 128], pT[:, 0:512].rearrange("p (c t) -> p c t", c=4), [pTb], [b_KsT])
        run_sums([(LFc[:, t, :], b_LFc, 128) for t in range(16)], Rall, b_R, True)
        for t in range(16):
            cumsum_par(LFc[:, t, :], b_LFc, 128, Rall[:, t, :], b_R, Cs[:, t, :], b_Cs)
        cumsum_par(LFs[:, bb, :], b_LFs, 16, Rall[:, 16, :], b_R, CsN, b_CsN)
        bcast_row(CsN, b_CsN, 16, SEL0, crefS, b_crefS)
        bt_all, b_bt = biasT_r.next()
        for t in range(16):
            tt(bt_all[:, t, :], crefS, Cs[:, t, :], ALU.subtract, [b_crefS, b_Cs], [b_bt])
        tt(bt_all[0:16, 16, :], crefS[0:16], CsN, ALU.subtract, [b_crefS, b_CsN], [b_bt])
        for h in (0, 2, 4, 6, 1, 3, 5, 7):
            hp, hr = h // 2, (h % 2) * 64
            po, pob, off = po_slot(h, 8, 64)
            kts = []
            for t in range(16):
                kts.append(dict(kT=KsT[hr:hr + 64, hp, t * 128:(t + 1) * 128], nk=128, v=Vs[:, t, h, :],
                                rd=[b_KsT, b_Vs[t]], bias=bt_all[:, t, h:h + 1], brd=[b_bt], mask=None))
            kts.append(dict(kT=KsT_new[hr:hr + 64, hp, bb * 16:(bb + 1) * 16], nk=16, v=VsN[:, bb, h, :],
                            rd=[b_KsN, b_VsN], bias=bt_all[0:16, 16, h:h + 1], brd=[b_bt], mask=trib))
            attn_head(QsT[hr:hr + 64, hp, bb * 16:(bb + 1) * 16], 16, kts, po[0:16, off:off + 65], pob,
                      0.125, [b_QsT])
            if h >= 6:
                attn_finish_group(16, 8, 64, h % 2)
        attn_finish(16, 8, 64, osT["b"][:, :, bb * 16:(bb + 1) * 16], b_osT["b"])
        pipe_drain()
    full_barrier()
    A.top = markB

    psum_mode(False)
    oT["a"] = A.bf16([4, 2048])
    markAA = A.top
    NA = 4 + NOWN
    KaT = A.bf16([4, NA * 128]); b_KaT = [Buf() for _ in range(NA)]
    Va = A.bf16([NA, 8, 65]); b_Va = [Buf() for _ in range(NA)]
    Qa_r = Rot([(A.bf16([4, 128]), Buf()) for _ in range(2)])
    KsaN = A.bf16([4, NS]); b_KsaN = Buf()
    QsaT = A.bf16([4, NS]); b_QsaT = Buf()
    VsaN = A.bf16([4, 8, 65], parts=16); b_VsaN = Buf()
    memset(Va[:, :, :, 64:65], 1.0, b_Va)
    memset(VsaN[:, :, :, 64:65], 1.0, [b_VsaN])

    mark_xp4 = A.top
    xpT4 = A.bf16([8, 512]); b_xp4 = Buf()
    for i in range(4):
        norm_T(xpre[(12 + i) * 128:(13 + i) * 128, :], 128, gmix, xpT4[:, :, i * 128:(i + 1) * 128], b_xp4)

    def xa_of(idx):
        if idx < 4:
            return xpT4[:, :, idx * 128:(idx + 1) * 128], b_xp4
        return xT_of(idx - 4)

    wv, wb = wload(w_in[:, C_KA:C_KA + 512], 8, 512)
    def mk_ka(idx):
        xv, xb_ = xa_of(idx)
        pz, pzb = proj_tok(xv, xb_, 128, wv, wb, 512)
        od = akp[(idx - 16) * 128:(idx - 15) * 128, :] if idx >= 16 else None
        return head_norm(pz, pzb, 128, 8, 64, gv["ka"], od, KaT[:, :, idx * 128:(idx + 1) * 128], b_KaT[idx], staged=True)
    pipelined(NA, mk_ka)
    pz, pzb = proj_tok(xsT, b_xsT, NS, wv, wb, 512)
    head_norm(pz, pzb, NS, 8, 64, gv["ka"], aks, KsaN, b_KsaN)
    wv, wb = wload(w_in[:, C_VA:C_VA + 512], 8, 512)
    for idx in range(NA):
        xv, xb_ = xa_of(idx)
        pz, pzb = proj_tok(xv, xb_, 128, wv, wb, 512)
        od = avp[(idx - 16) * 128:(idx - 15) * 128, :] if idx >= 16 else None
        v_store(pz, pzb, 128, 8, 64, od, Va[:, idx], b_Va[idx])
    for bb in range(4):
        pz, pzb = proj_tok(xsT[:, :, bb * 16:(bb + 1) * 16], b_xsT, 16, wv, wb, 512)
        v_store(pz, pzb, 16, 8, 64, avs[bb * 16:(bb + 1) * 16, :], VsaN[:, bb], b_VsaN)
    wv, wb = wload(w_in[:, C_QA:C_QA + 512], 8, 512)
    pz, pzb = proj_tok(xsT, b_xsT, NS, wv, wb, 512)
    head_norm(pz, pzb, NS, 8, 64, gv["qa"], None, QsaT, b_QsaT)
    def prep_a(i):
        xv, xb_ = xT_of(i)
        pz, pzb = proj_tok(xv, xb_, 128, wv, wb, 512)
        qv, qvb = Qa_r.next()
        head_norm(pz, pzb, 128, 8, 64, gv["qa"], None, qv, qvb, defer_T=2)
        return qv, qvb

    psum_mode(True)
    nxt = prep_a(0)
    for i in range(NOWN):
        run_deferred(True)
        qv, qvb = nxt
        if i + 1 < NOWN:
            nxt = prep_a(i + 1)
        for h in (0, 2, 4, 6, 1, 3, 5, 7):
            hp, hr = h // 2, (h % 2) * 64
            po, pob, off = po_slot(h, 8, 64)
            kts = []
            for t in range(5):
                idx = i + t
                pre_ = idx < 4
                d = dict(kT=KaT[hr:hr + 64, hp, idx * 128:(idx + 1) * 128], nk=128, v=Va[:, idx, h, :],
                         rd=[b_KaT[idx], b_Va[idx]], mask=None, brd=[b_bias, b_nf])
                if t < 3:
                    d["bias"] = (relcnf if pre_ else relc)[:, h:h + 1]
                    if t == 0:
                        d["mask"] = mask0
                else:
                    d["pre"] = (BiasA[:, h, t - 3, :], [b_bias])
                    d["bias"] = negflag[:, 0:1] if pre_ else 0.0
                kts.append(d)
            attn_head(qv[hr:hr + 64, hp, :], 128, kts, po[:, off:off + 65], pob,
                      0.125, [qvb])
            if h >= 6:
                attn_finish_group(128, 8, 64, h % 2)
        attn_finish(128, 8, 64, oT["a"][:, :, i * 128:(i + 1) * 128], b_oT["a"])
    full_barrier()
    A.top = mark_xp4
    Kca = A.bf16([4, 512]); b_Kca = Buf()
    KsaT = A.bf16([4, 512]); b_KsaT = Buf()
    Vsa = A.bf16([4, 8, 65]); b_Vsa = Buf()
    memset(Vsa[:, :, :, 64:65], 1.0, [b_Vsa])
    for bb in range(4):
        ld(Kca, cak[bb].rearrange("(t p) n -> p t n", p=128), kcsem, [b_Kca], eng="pool")
        for t in range(4):
            ld(Vsa[:, t, :, 0:64], cav[bb, 128 * t:128 * (t + 1), :].rearrange("p (h d) -> p h d", h=8),
               kcsem, [b_Vsa], eng="pool")
        for t in range(4):
            pT, pTb = pT_r.next()
            transposes([(pT[:, c * 128:(c + 1) * 128], Kca[:, t, c * 128:(c + 1) * 128], identb) for c in range(4)],
                       [b_Kca, b_id], [pTb])
            cp(KsaT[:, :, t * 128:(t + 1) * 128], pT[:, 0:512].rearrange("p (c t) -> p c t", c=4), [pTb], [b_KsaT])
        for h in (0, 2, 4, 6, 1, 3, 5, 7):
            hp, hr = h // 2, (h % 2) * 64
            po, pob, off = po_slot(h, 8, 64)
            kts = []
            for t in range(4):
                d = dict(kT=KsaT[hr:hr + 64, hp, t * 128:(t + 1) * 128], nk=128, v=Vsa[:, t, h, :],
                         rd=[b_KsaT, b_Vsa], mask=None, brd=[b_bias])
                if t < 3:
                    d["bias"] = relc[:, h:h + 1]
                else:
                    d["pre"] = (BiasA[:, h, 0, 0:16], [b_bias])
                    d["bias"] = 0.0
                kts.append(d)
            kts.append(dict(kT=KsaN[hr:hr + 64, hp, bb * 16:(bb + 1) * 16], nk=16, v=VsaN[:, bb, h, :],
                            rd=[b_KsaN, b_VsaN], mask=None, brd=[b_bias],
                            pre=(BiasA[0:16, h, 1, 0:16], [b_bias]), bias=0.0))
            attn_head(QsaT[hr:hr + 64, hp, bb * 16:(bb + 1) * 16], 16, kts, po[0:16, off:off + 65], pob,
                      0.125, [b_QsaT])
            if h >= 6:
                attn_finish_group(16, 8, 64, h % 2)
        attn_finish(16, 8, 64, osT["a"][:, :, bb * 16:(bb + 1) * 16], b_osT["a"])
        pipe_drain()
    full_barrier()
    A.top = markAA

    psum_mode(False)
    oT["m"] = A.bf16([4, 2048])
    markM = A.top
    Qm_r = Rot([(A.bf16([4, 128]), Buf()) for _ in range(2)])
    QsmT = A.bf16([4, NS]); b_QsmT = Buf()
    wv, wb = wload(w_in[:, C_QM:C_QM + 512], 8, 512)
    pz, pzb = proj_tok(xsT, b_xsT, NS, wv, wb, 512)
    head_norm(pz, pzb, NS, 4, 128, gv["qm"], None, QsmT, b_QsmT)
    sc_m = float(128 ** -0.5)
    def prep_m(i):
        xv, xb_ = xT_of(i)
        pz, pzb = proj_tok(xv, xb_, 128, wv, wb, 512)
        qv, qvb = Qm_r.next()
        head_norm(pz, pzb, 128, 4, 128, gv["qm"], None, qv, qvb, defer_T=1)
        return qv, qvb

    psum_mode(True)
    nxt = prep_m(0)
    for i in range(NOWN):
        run_deferred(True)
        qv, qvb = nxt
        if i + 1 < NOWN:
            nxt = prep_m(i + 1)
        for h in range(4):
            po, pob, off = po_slot(h, 4, 128)
            kts = [dict(kT=mkT[:, h, m * 128:(m + 1) * 128], nk=128, v=mvA[:, m, h, :], rd=[b_mkT, b_mv],
                        mask=None, bias=0.0) for m in range(2)]
            attn_head(qv[:, h, :], 128, kts, po[:, off:off + 129], pob, sc_m, [qvb])
            if (h + 1) % (4 // 2) == 0:
                attn_finish_group(128, 4, 128, h // (4 // 2))
        attn_finish(128, 4, 128, oT["m"][:, :, i * 128:(i + 1) * 128], b_oT["m"])
    Kcm = A.bf16([2, 512]); b_Kcm = Buf()
    mksT = A.bf16([4, 256]); b_mksT = Buf()
    mvs = A.bf16([2, 4, 129]); b_mvs = Buf()
    memset(mvs[:, :, :, 128:129], 1.0, [b_mvs])
    for bb in range(4):
        ld(Kcm, cmk[bb].rearrange("(t p) n -> p t n", p=128), kcsem, [b_Kcm], eng="pool")
        for t in range(2):
            ld(mvs[:, t, :, 0:128], cmv[bb, 128 * t:128 * (t + 1), :].rearrange("p (h d) -> p h d", h=4),
               kcsem, [b_mvs], eng="pool")
        for t in range(2):
            pT, pTb = pT_r.next()
            transposes([(pT[:, c * 128:(c + 1) * 128], Kcm[:, t, c * 128:(c + 1) * 128], identb) for c in range(4)],
                       [b_Kcm, b_id], [pTb])
            cp(mksT[:, :, t * 128:(t + 1) * 128], pT[:, 0:512].rearrange("p (c t) -> p c t", c=4), [pTb], [b_mksT])
        for h in range(4):
            po, pob, off = po_slot(h, 4, 128)
            kts = [dict(kT=mksT[:, h, m * 128:(m + 1) * 128], nk=128, v=mvs[:, m, h, :], rd=[b_mksT, b_mvs],
                        mask=None, bias=0.0) for m in range(2)]
            attn_head(QsmT[:, h, bb * 16:(bb + 1) * 16], 16, kts, po[0:16, off:off + 129], pob, sc_m, [b_QsmT])
            if (h + 1) % (4 // 2) == 0:
                attn_finish_group(16, 4, 128, h // (4 // 2))
        attn_finish(16, 4, 128, osT["m"][:, :, bb * 16:(bb + 1) * 16], b_osT["m"])
        pipe_drain()
    full_barrier()
    A.top = markM

    psum_mode(False)
    hT_start = A.top
    hT = A.bf16([8, 2048]); b_hT = [Buf() for _ in range(4)]
    hsT = A.bf16([8, NS]); b_hsT = Buf()
    mark_h_end = A.top
    acc_r = Rot([(A.f32([512]), Buf()) for _ in range(2)])
    wp_r = Rot([(A.bf16([4 * 384]), Buf(), P.new_dma_sem()) for _ in range(2)])
    wps = {"a": w_pa, "b": w_pb, "m": w_pm}
    BR = "abm"
    groups = [(tg * 512, 512, b_xnT[4 * tg:4 * tg + 4], b_hT[tg]) for tg in range(4)]
    for nch in range(8):
        wt, wgb, wsem = w_r.next()
        wg = wt[:, 0:8 * 384].rearrange("p (c b n) -> p c b n", c=8, b=3)
        wpt, wpb, wpsem = wp_r.next()
        wp = wpt.rearrange("p (c b n) -> p c b n", c=4, b=3)
        for bi, br in enumerate(BR):
            c0 = C_G + bi * 1024 + nch * 128
            ld(wg[:, :, bi, :], w_in[:, c0:c0 + 128].rearrange("(c p) n -> p c n", p=128), wsem, [wgb], eng="pool")
            ld(wp[:, :, bi, :], wps[br][:, nch * 128:(nch + 1) * 128].rearrange("(c p) n -> p c n", p=128),
               wpsem, [wpb], eng="pool")
        for gidx in range(5):
            if gidx < 4:
                t0, ntok, xbufs, hb = groups[gidx]
                xv = xnT[:, :, t0:t0 + ntok]
                ov = {k: oT[k][:, :, t0:t0 + ntok] for k in BR}
                obufs = b_oT
                hdst = hT[:, nch, t0:t0 + ntok]
            else:
                ntok, xbufs, hb = NS, [b_xsT], b_hsT
                xv = xsT
                ov = osT
                obufs = b_osT
                hdst = hsT[:, nch, :]
            acc, accb = acc_r.next()
            for bi, br in enumerate(BR):
                pz, pzb = pz_r.next()
                mm(pz[:, 0:ntok], [(wg[:, c, bi, :], xv[:, c, :]) for c in range(8)], [wgb] + list(xbufs), [pzb])
                sg, sgb = st_r.next()
                act(sg[:, 0:ntok], pz[:, 0:ntok], AF.Sigmoid, [pzb], [sgb])
                pz2, pz2b = pz_r.next()
                mm(pz2[:, 0:ntok], [(wp[:, c, bi, :], ov[br][:, c, :]) for c in range(4)], [wpb, obufs[br]], [pz2b])
                if bi == 0:
                    tt(acc[:, 0:ntok], sg[:, 0:ntok], pz2[:, 0:ntok], ALU.mult, [sgb, pz2b], [accb])
                else:
                    tt(sg[:, 0:ntok], sg[:, 0:ntok], pz2[:, 0:ntok], ALU.mult, [sgb, pz2b], [sgb])
                    tt(acc[:, 0:ntok], acc[:, 0:ntok], sg[:, 0:ntok], ALU.add, [accb, sgb], [accb], eng="pool")
            act(hdst, acc[:, 0:ntok], AF.Copy, [accb], [hb])
    full_barrier()

    A.top = mark_x
    x1 = A.f32([NOWN, 1024]); b_x1 = [Buf() for _ in range(NOWN)]
    x1s = A.f32([1024]); b_x1s = Buf()
    assert A.top <= hT_start, ("x1 overlaps hT", A.top, hT_start)
    mark_x1_end = A.top
    A.top = mark_h_end
    wo0, wo0b = wload(w_o[:, 0:512], 8, 512)
    wo1, wo1b = wload(w_o[:, 512:1024], 8, 512)
    for i in range(NOWN + 1):
        if i < NOWN:
            nt_, src, hv, hb, dst, db = 128, xown[i * 128:(i + 1) * 128, :], hT[:, :, i * 128:(i + 1) * 128], b_hT[i // 4], x1[:, i, :], b_x1[i]
        else:
            nt_, src, hv, hb, dst, db = NS, xs, hsT, b_hsT, x1s, b_x1s
        xt, xb, xsem = xt_r.next()
        ld(xt[0:nt_], src, xsem, [xb])
        for half, (wo, wob) in enumerate(((wo0, wo0b), (wo1, wo1b))):
            pz, pzb = pz_r.next()
            mm(pz[0:nt_, :], [(hv[:, c, :], wo[:, c, :]) for c in range(8)], [hb, wob], [pzb])
            tt(dst[0:nt_, half * 512:(half + 1) * 512], xt[0:nt_, half * 512:(half + 1) * 512], pz[0:nt_, :],
               ALU.add, [xb, pzb], [db])
    full_barrier()

    A.top = mark_x1_end
    gffn = A.f32([1024])
    ld(gffn, g_ffn.partition_broadcast(128), csem, [b_g])
    xn2T = A.bf16([8, 512]); b_xn2T = Buf()
    hdnT = A.bf16([32, 512]); b_hdn = Buf()
    wd_r = Rot([(A.bf16([4, 1024]), Buf(), P.new_dma_sem()) for _ in range(2)])
    for stg in range(5):
        if stg < 4:
            ntok = 512
            tiles = [(x1[:, 4 * stg + k, :], b_x1[4 * stg + k], 128, y_p[(4 * stg + k) * 128:(4 * stg + k + 1) * 128, :]) for k in range(4)]
        else:
            ntok = NS
            tiles = [(x1s, b_x1s, NS, y_s)]
        for k, (xa, xab, nt_, _) in enumerate(tiles):
            norm_T((xa, xab), nt_, gffn, xn2T[:, :, k * 128:k * 128 + nt_], b_xn2T)
        for ffb in range(8):
            wv, wb = wload(w_up[:, ffb * 512:(ffb + 1) * 512], 8, 512)
            for fc in range(4):
                pz, pzb = pz_r.next()
                mm(pz[:, 0:ntok], [(wv[:, c, fc * 128:(fc + 1) * 128], xn2T[:, c, 0:ntok]) for c in range(8)],
                   [wb, b_xn2T], [pzb])
                rl, rlb = st_r.next()
                act(rl[:, 0:ntok], pz[:, 0:ntok], AF.Relu, [pzb], [rlb])
                tt(hdnT[:, ffb * 4 + fc, 0:ntok], rl[:, 0:ntok], rl[:, 0:ntok], ALU.mult, [rlb], [b_hdn])
        for blk in range(8):
            wd, wdb, wdsem = wd_r.next()
            ld(wd, w_down[blk * 512:(blk + 1) * 512, :].rearrange("(c p) n -> p c n", p=128), wdsem, [wdb], eng="pool")
            for k, (xa, xab, nt_, _) in enumerate(tiles):
                for half in range(2):
                    pz, pzb = pz_r.next()
                    mm(pz[0:nt_, :], [(hdnT[:, blk * 4 + c, k * 128:k * 128 + nt_], wd[:, c, half * 512:(half + 1) * 512])
                                      for c in range(4)], [b_hdn, wdb], [pzb])
                    tt(xa[0:nt_, half * 512:(half + 1) * 512], xa[0:nt_, half * 512:(half + 1) * 512], pz[0:nt_, :],
                       ALU.add, [xab, pzb], [xab])
        for k, (xa, xab, nt_, yd) in enumerate(tiles):
            store(yd, xa[0:nt_], [xab])

    P.emit(final_waits=STORE_SEMS)
    st.close()
    return nc


_NC = None


def _consts():
    tri = np.triu(np.ones((128, 128), np.float32))
    ones = np.ones((128, 128), np.float32)
    selmid = np.zeros((128, 128), np.float32); selmid[64, :] = 1.0
    sel0 = np.zeros((128, 128), np.float32); sel0[0, :] = 1.0
    cm = np.stack([tri, ones, selmid, sel0], axis=1).reshape(128, 512)
    return np.ascontiguousarray(cm), np.eye(128, dtype=np.float32)


def kernel(**inp):
    global _NC
    if _NC is None:
        _NC = build_nc()
    nc = _NC
    f = lambda a: np.ascontiguousarray(np.asarray(a, dtype=np.float32))
    cm, ident = _consts()
    xp = f(inp["x_prompt"]); xsm = f(inp["x_sample"]); memp = f(inp["mem_prompt"])
    shared = {
        "g_mix": f(inp["g_mix"]), "w_in": f(inp["w_in"][0]), "b_f": f(inp["b_f"]),
        "g_qa": f(inp["g_qa"]), "g_ka": f(inp["g_ka"]), "g_qb": f(inp["g_qb"]), "g_kb": f(inp["g_kb"]),
        "g_qm": f(inp["g_qm"]), "g_km": f(inp["g_km"]), "rel_bias": f(inp["rel_bias"][0]),
        "g_mem": f(inp["g_mem"]), "w_mkv": f(inp["w_mkv"][0]),
        "w_pa": f(inp["w_pa"][0]), "w_pb": f(inp["w_pb"][0]), "w_pm": f(inp["w_pm"][0]),
        "w_o": f(inp["w_o"][0]), "g_ffn": f(inp["g_ffn"]), "w_up": f(inp["w_up"][0]), "w_down": f(inp["w_down"][0]),
        "cmat": cm, "ident": ident,
    }
    in_maps = []
    for c in range(8):
        b, hf = c // 2, c % 2
        sb = slice(4 * c, 4 * c + 4)
        m = dict(shared)
        m["xpre"] = f(xp[b, 0:2048])
        m["xown"] = f(xp[b, hf * 2048:(hf + 1) * 2048])
        m["xs"] = f(xsm[sb].reshape(64, 1024))
        m["mem"] = f(memp[b])
        m["cak"] = f(inp["cache_a_k"][0, sb].reshape(4, 512, 512))
        m["cav"] = f(inp["cache_a_v"][0, sb].reshape(4, 512, 512))
        m["cbk"] = f(inp["cache_b_k"][0, sb].reshape(4, 2048, 512))
        m["cbv"] = f(inp["cache_b_v"][0, sb].reshape(4, 2048, 512))
        m["cblf"] = f(inp["cache_b_logf"][0, sb])
        m["cmk"] = f(inp["cache_mem_k"][0, sb].reshape(4, 256, 512))
        m["cmv"] = f(inp["cache_mem_v"][0, sb].reshape(4, 256, 512))
        m["flag"] = np.full((1, 1), float(hf), np.float32)
        in_maps.append(m)
    res = run_bass_kernel_spmd(nc, in_maps, core_ids=list(range(8)))
    R = res.results
    y_p = np.zeros((4, 4096, 1024), np.float32)
    y_s = np.zeros((32, 16, 1024), np.float32)
    akp = np.zeros((1, 4, 512, 8, 64), np.float32); avp = np.zeros_like(akp)
    bkp = np.zeros((1, 4, 4096, 8, 64), np.float32); bvp = np.zeros_like(bkp)
    blfp = np.zeros((1, 4, 4096, 8), np.float32)
    mkp = np.zeros((1, 4, 256, 4, 128), np.float32); mvp = np.zeros_like(mkp)
    aks = np.zeros((1, 32, 16, 8, 64), np.float32); avs = np.zeros_like(aks)
    bks = np.zeros_like(aks); bvs = np.zeros_like(aks)
    blfs = np.zeros((1, 32, 16, 8), np.float32)
    for c in range(8):
        b, hf = c // 2, c % 2
        r = R[c]
        s = slice(hf * 2048, (hf + 1) * 2048)
        y_p[b, s] = r["y_p"]
        y_s[4 * c:4 * c + 4] = r["y_s"].reshape(4, 16, 1024)
        bkp[0, b, s] = r["bkp"].reshape(2048, 8, 64)
        bvp[0, b, s] = r["bvp"].reshape(2048, 8, 64)
        blfp[0, b, s] = r["blfp"]
        if hf == 1:
            akp[0, b] = r["akp"].reshape(512, 8, 64)
            avp[0, b] = r["avp"].reshape(512, 8, 64)
        else:
            mkp[0, b] = r["mkp"].reshape(256, 4, 128)
            mvp[0, b] = r["mvp"].reshape(256, 4, 128)
        aks[0, 4 * c:4 * c + 4] = r["aks"].reshape(4, 16, 8, 64)
        avs[0, 4 * c:4 * c + 4] = r["avs"].reshape(4, 16, 8, 64)
        bks[0, 4 * c:4 * c + 4] = r["bks"].reshape(4, 16, 8, 64)
        bvs[0, 4 * c:4 * c + 4] = r["bvs"].reshape(4, 16, 8, 64)
        blfs[0, 4 * c:4 * c + 4] = r["blfs"].reshape(4, 16, 8)
    return (y_p, y_s, akp, avp, bkp, bvp, blfp, mkp, mvp, aks, avs, bks, bvs, blfs)
```

```python
import contextlib
import numpy as np
import concourse.bass as bass
import concourse.mybir as mybir
from concourse.bass_utils import run_bass_kernel_spmd

F32 = mybir.dt.float32
BF16 = mybir.dt.bfloat16
ALU = mybir.AluOpType
AF = mybir.ActivationFunctionType
AX = mybir.AxisListType

ENGS = ("pe", "act", "dve", "pool", "sp")
NEG = -30000.0
EPS = 1e-6
LREL = 768


ALLBUFS = []


class Buf:
    __slots__ = ("name", "w", "r", "excl")

    def __init__(self, name="", excl=False):
        self.name = name
        self.w = None
        self.r = []
        self.excl = excl
        ALLBUFS.append(self)


class Prog:
    def __init__(self, nc):
        self.nc = nc
        self.ops = {e: [] for e in ENGS}
        self.n_dma_sem = 0

    def new_dma_sem(self):
        self.n_dma_sem += 1
        return {"id": self.n_dma_sem - 1, "count": 0}

    def _deps_for(self, reads, writes):
        deps = []
        for b in reads:
            if b.w is not None:
                deps.append(b.w)
        for b in writes:
            if b.w is not None:
                deps.append(b.w)
            deps.extend(b.r)
        return deps

    def _post(self, tok, reads, writes):
        for b in writes:
            b.w = tok
            b.r = []
        for b in reads:
            if b not in writes:
                if len(b.r) > 64:
                    last = {}
                    for t in b.r:
                        last[(t[0], t[1])] = t
                    b.r = list(last.values())
                b.r.append(tok)

    def op(self, eng, fn, reads=(), writes=()):
        writes = list(writes) + [b for b in reads if b.excl and b not in writes]
        deps = self._deps_for(reads, writes)
        o = {"eng": eng, "fn": fn, "deps": deps, "kind": "c", "signal": False,
             "idx": len(self.ops[eng])}
        self.ops[eng].append(o)
        self._post(("e", eng, o["idx"]), reads, writes)
        return o

    def dma(self, eng, fn, sem, reads=(), writes=()):
        deps = self._deps_for(reads, writes)
        sem["count"] += 16
        o = {"eng": eng, "fn": fn, "deps": deps, "kind": "d", "sem": sem,
             "idx": len(self.ops[eng])}
        self.ops[eng].append(o)
        self._post(("d", sem["id"], sem["count"]), reads, writes)
        return o

    def barrier(self, bufs):
        deps = []
        for b in bufs:
            if b.w is not None:
                deps.append(b.w)
            deps.extend(b.r)
        for e in ENGS:
            o = {"eng": e, "fn": None, "deps": list(deps), "kind": "n",
                 "idx": len(self.ops[e])}
            self.ops[e].append(o)

    def emit(self, final_waits=()):
        nc = self.nc
        for e in ENGS:
            for o in self.ops[e]:
                for d in o["deps"]:
                    if d[0] == "e":
                        self.ops[d[1]][d[2]]["signal"] = True
        for e in ENGS:
            c = 0
            for o in self.ops[e]:
                if o["kind"] == "c" and o["signal"]:
                    c += 1
                    o["sigval"] = c
        with contextlib.ExitStack() as st:
            esem = {e: st.enter_context(nc.semaphore("s_" + e)) for e in ENGS}
            dsem = [st.enter_context(nc.semaphore("d%d" % i)) for i in range(self.n_dma_sem)]
            block = st.enter_context(nc.Block())
            prog = self

            def run(ename, eng):
                seen_e = {e: 0 for e in ENGS}
                seen_d = {}
                for o in prog.ops[ename]:
                    need_e = {}
                    need_d = {}
                    for d in o["deps"]:
                        if d[0] == "e":
                            v = prog.ops[d[1]][d[2]]["sigval"]
                            if v > need_e.get(d[1], 0):
                                need_e[d[1]] = v
                        else:
                            if d[2] > need_d.get(d[1], 0):
                                need_d[d[1]] = d[2]
                    for e2, v in need_e.items():
                        if v > seen_e[e2]:
                            eng.wait_ge(esem[e2], v)
                            seen_e[e2] = v
                    for s, v in need_d.items():
                        if v > seen_d.get(s, 0):
                            eng.wait_ge(dsem[s], v)
                            seen_d[s] = v
                    if o["kind"] == "c":
                        ins = o["fn"](eng)
                        if o["signal"]:
                            ins.then_inc(esem[ename], 1)
                    elif o["kind"] == "d":
                        ins = o["fn"](eng)
                        ins.then_inc(dsem[o["sem"]["id"]], 16)
                if ename == "sp":
                    for s in final_waits:
                        if s["count"] > 0:
                            eng.wait_ge(dsem[s["id"]], s["count"])

            @block.tensor
            def _(eng):
                run("pe", eng)

            @block.scalar
            def _(eng):
                run("act", eng)

            @block.vector
            def _(eng):
                run("dve", eng)

            @block.gpsimd
            def _(eng):
                run("pool", eng)

            @block.sync
            def _(eng):
                run("sp", eng)


def bcl(ap, m):
    return bass.AP(tensor=ap.tensor, offset=ap.offset,
                   ap=[list(x) for x in ap.ap] + [[0, m]])


def bcm(ap, k):
    a = [list(x) for x in ap.ap]
    return bass.AP(tensor=ap.tensor, offset=ap.offset, ap=[a[0], [0, k]] + a[1:])


class Rot:
    def __init__(self, items):
        self.items = items
        self.i = 0

    def next(self):
        it = self.items[self.i % len(self.items)]
        self.i += 1
        return it


C_QA, C_KA, C_VA, C_QB, C_KB, C_VB, C_FL, C_QM, C_G = 0, 512, 1024, 1536, 2048, 2560, 3072, 3080, 3592

NPRE = 16
NOWN = 16
NT = NPRE + NOWN
NS = 64
ARENA_W = 53100


def build_nc():
    nc = bass.Bass("TRN2", target_bir_lowering=False)
    P = Prog(nc)

    def din(name, shape):
        return nc.dram_tensor(name, list(shape), F32, kind="ExternalInput").ap()

    def dout(name, shape):
        return nc.dram_tensor(name, list(shape), F32, kind="ExternalOutput").ap()

    xpre = din("xpre", [2048, 1024]); xown = din("xown", [2048, 1024]); xs = din("xs", [NS, 1024])
    mem = din("mem", [256, 1024])
    cak = din("cak", [4, 512, 512]); cav = din("cav", [4, 512, 512])
    cbk = din("cbk", [4, 2048, 512]); cbv = din("cbv", [4, 2048, 512]); cblf = din("cblf", [4, 2048, 8])
    cmk = din("cmk", [4, 256, 512]); cmv = din("cmv", [4, 256, 512])
    g_mix = din("g_mix", [1, 1024]); w_in = din("w_in", [1024, 6664]); b_f = din("b_f", [1, 8])
    g_qa = din("g_qa", [1, 64]); g_ka = din("g_ka", [1, 64]); g_qb = din("g_qb", [1, 64]); g_kb = din("g_kb", [1, 64])
    g_qm = din("g_qm", [1, 128]); g_km = din("g_km", [1, 128]); rel_bias = din("rel_bias", [257, 8])
    g_mem = din("g_mem", [1, 1024]); w_mkv = din("w_mkv", [1024, 1024])
    w_pa = din("w_pa", [512, 1024]); w_pb = din("w_pb", [512, 1024]); w_pm = din("w_pm", [512, 1024])
    w_o = din("w_o", [1024, 1024]); g_ffn = din("g_ffn", [1, 1024])
    w_up = din("w_up", [1024, 4096]); w_down = din("w_down", [4096, 1024])
    cmat = din("cmat", [128, 4 * 128]); ident = din("ident", [128, 128]); flag = din("flag", [1, 1])

    y_p = dout("y_p", [2048, 1024]); y_s = dout("y_s", [NS, 1024])
    akp = dout("akp", [512, 512]); avp = dout("avp", [512, 512])
    bkp = dout("bkp", [2048, 512]); bvp = dout("bvp", [2048, 512]); blfp = dout("blfp", [2048, 8])
    mkp = dout("mkp", [256, 512]); mvp = dout("mvp", [256, 512])
    aks = dout("aks", [NS, 512]); avs = dout("avs", [NS, 512])
    bks = dout("bks", [NS, 512]); bvs = dout("bvs", [NS, 512]); blfs = dout("blfs", [NS, 8])
    relE = nc.dram_tensor("relE", [8, LREL], F32, kind="Internal").ap()
    relD = nc.dram_tensor("relD", [128, 8 * LREL], F32, kind="Internal").ap()

    st = contextlib.ExitStack()
    arena = st.enter_context(nc.sbuf_tensor("arena", [128, ARENA_W], F32))
    psum = [st.enter_context(nc.psum_tensor("ps%d" % i, [128, 512], F32)) for i in range(8)]

    class Arena:
        def __init__(self):
            self.top = 0

        def f32(self, dims, parts=128):
            n = int(np.prod(dims))
            o = self.top
            self.top += n
            assert self.top <= ARENA_W, ("arena overflow", self.top)
            ap = arena[0:parts, o:o + n]
            return self._view(ap, dims)

        def bf16(self, dims, parts=128):
            n = int(np.prod(dims))
            w = (n + 1) // 2
            o = self.top
            self.top += w
            assert self.top <= ARENA_W, ("arena overflow", self.top)
            ap = arena[0:parts, o:o + w].bitcast(BF16)[:, 0:n]
            return self._view(ap, dims)

        @staticmethod
        def _view(ap, dims):
            if len(dims) == 1:
                return ap
            if len(dims) == 2:
                return ap.rearrange("p (a b) -> p a b", a=dims[0])
            if len(dims) == 3:
                return ap.rearrange("p (a b c) -> p a b c", a=dims[0], b=dims[1])
            if len(dims) == 4:
                return ap.rearrange("p (a b c d) -> p a b c d", a=dims[0], b=dims[1], c=dims[2])
            raise ValueError

    A = Arena()
    out_sems = [P.new_dma_sem() for _ in range(4)]
    osr = Rot(out_sems)

    def mm(out, pairs, reads, writes):
        pairs = list(pairs)

        def f(e):
            n = len(pairs)
            for k, (l, r) in enumerate(pairs):
                i = e.matmul(out, lhsT=l, rhs=r, start=(k == 0), stop=(k == n - 1))
            return i
        P.op("pe", f, reads, writes)

    def transposes(items, reads, writes):
        items = list(items)

        def f(e):
            for (o, i_, idn) in items:
                ins = e.transpose(out=o, in_=i_, identity=idn)
            return ins
        P.op("pe", f, reads, writes)

    def act(out, in_, func, reads, writes, bias=None, scale=None, accum=None, eng="act"):
        kw = {}
        if bias is not None:
            kw["bias"] = bias
        if scale is not None:
            kw["scale"] = scale
        if accum is not None:
            kw["accum_out"] = accum
        P.op(eng, lambda e: e.activation(out=out, in_=in_, func=func, **kw), reads, writes)

    def tt(out, in0, in1, op, reads, writes, eng="dve"):
        P.op(eng, lambda e: e.tensor_tensor(out=out, in0=in0, in1=in1, op=op), reads, writes)

    def ts(out, in0, s1, s2, op0, op1, reads, writes, eng="dve"):
        if s2 is None:
            P.op(eng, lambda e: e.tensor_scalar(out=out, in0=in0, scalar1=s1, scalar2=None, op0=op0), reads, writes)
        else:
            P.op(eng, lambda e: e.tensor_scalar(out=out, in0=in0, scalar1=s1, scalar2=s2, op0=op0, op1=op1), reads, writes)

    def stt(out, in0, scalar, in1, op0, op1, reads, writes, eng="dve"):
        P.op(eng, lambda e: e.scalar_tensor_tensor(out=out, in0=in0, scalar=scalar, in1=in1, op0=op0, op1=op1), reads, writes)

    def cp(out, in_, reads, writes, eng="dve"):
        P.op(eng, lambda e: e.tensor_copy(out=out, in_=in_), reads, writes)

    def recip(out, in_, reads, writes):
        P.op("dve", lambda e: e.reciprocal(out=out, in_=in_), reads, writes)

    def red(out, in_, reads, writes):
        P.op("dve", lambda e: e.tensor_reduce(out=out, in_=in_, axis=AX.X, op=ALU.add), reads, writes)

    def memset(ap, val, writes, eng="dve"):
        P.op(eng, lambda e: e.memset(ap, val), (), writes)

    BUFSEM = {}

    def sem_of(b):
        if id(b) not in BUFSEM:
            BUFSEM[id(b)] = P.new_dma_sem()
        return BUFSEM[id(b)]

    STORE_SEMS = []

    def ld(out, in_, sem, writes, reads=(), eng="sp", slow=False):
        sem = sem_of(writes[0])
        if slow:
            P.dma(eng, lambda e: e.dma_start(out=out, in_=in_, allow_slow_non_contiguous=True), sem, reads, writes)
        else:
            P.dma(eng, lambda e: e.dma_start(out=out, in_=in_), sem, reads, writes)

    def store(out, in_, reads):
        sem = sem_of(reads[0])
        if sem not in STORE_SEMS:
            STORE_SEMS.append(sem)
        P.dma("sp", lambda e: e.dma_start(out=out, in_=in_), sem, reads, ())

    def rstd_from_ss(ap, n, bufs):
        act(ap, ap, AF.Sqrt, list(bufs) + [b_eps], bufs, bias=EPSB[0:ap.shape[0], 0:1], scale=1.0 / n)
        recip(ap, ap, bufs, bufs)

    csem = P.new_dma_sem()
    identb = A.bf16([128]); b_id = Buf()
    ld(identb, ident, csem, [b_id], eng="pool")
    trif = A.f32([4, 128]); b_cm = Buf()
    ld(trif, cmat.rearrange("p (a b) -> p a b", a=4), csem, [b_cm])
    TRI, ONES, SELMID, SEL0 = trif[:, 0, :], trif[:, 1, :], trif[:, 2, :], trif[:, 3, :]
    trib = A.bf16([128]); b_trib = Buf()
    cp(trib, TRI, [b_cm], [b_trib])
    gmix = A.f32([1024]); b_g = Buf()
    ld(gmix, g_mix.partition_broadcast(128), csem, [b_g])
    gv = {}
    for nm, src, n in (("qa", g_qa, 64), ("ka", g_ka, 64), ("qb", g_qb, 64), ("kb", g_kb, 64),
                       ("qm", g_qm, 128), ("km", g_km, 128)):
        gv[nm] = A.f32([n])
        ld(gv[nm], src.partition_broadcast(128), csem, [b_g])
    bfb = A.f32([8])
    ld(bfb, b_f.partition_broadcast(128), csem, [b_g])
    EPSB = A.f32([1]); b_eps = Buf()
    memset(EPSB, EPS, [b_eps])
    negflag = A.f32([1]); b_nf = Buf()
    ld(negflag, flag.partition_broadcast(128), csem, [b_nf])
    ts(negflag, negflag, -1.0, -NEG, ALU.add, ALU.mult, [b_nf], [b_nf])

    xt_r = Rot([(A.f32([1024]), Buf(), P.new_dma_sem()) for _ in range(2)])
    sq_r = Rot([(A.bf16([1024]), Buf()) for _ in range(1)])
    xn_r = Rot([(A.bf16([1024]), Buf()) for _ in range(2)])
    st_r = Rot([(A.f32([512]), Buf()) for _ in range(3)])
    sb_r = Rot([(A.bf16([512]), Buf()) for _ in range(2)])
    sm_r = Rot([(A.f32([16]), Buf()) for _ in range(4)])
    pt_r = Rot([(A.bf16([128]), Buf()) for _ in range(12)])
    sa_r = Rot([(A.f32([128]), Buf()) for _ in range(4)])
    fin_r = Rot([(A.bf16([512]), Buf()) for _ in range(2)])
    bias_r = Rot([(A.f32([8]), Buf()) for _ in range(4)])
    w_r = Rot([(A.bf16([4224]), Buf(), P.new_dma_sem()) for _ in range(2)])
    PB = [Buf(excl=True) for _ in range(8)]
    pz_r = Rot([(psum[0], PB[0]), (psum[1], PB[1])])
    pT_r = Rot([(psum[2].bitcast(BF16), PB[2]), (psum[3].bitcast(BF16), PB[3])])
    pS_r = Rot([(psum[4], PB[4]), (psum[5], PB[5])])
    pO = [(psum[6], PB[6]), (psum[7], PB[7])]

    PIPEQ = []
    PST = {"s": 0, "r": 0}
    PFN = {}
    PDEF = []

    def defer(fn, delay):
        PDEF.append([PST["r"] + delay, fn])

    def run_deferred(all_=False):
        while True:
            todo = [d for d in PDEF if all_ or d[0] <= PST["r"]]
            if not todo:
                break
            for d in todo:
                PDEF.remove(d)
            for d in todo:
                d[1]()

    def pipe_pump(drain):
        nb = len(pS_r.items)
        while True:
            if PST["s"] < len(PIPEQ) and PST["s"] - PST["r"] < nb:
                ent = PIPEQ[PST["s"]]
                ent["bank"] = PFN["S"](ent["q"])
                PST["s"] += 1
                continue
            if PST["r"] < PST["s"] and (drain or PST["s"] - PST["r"] >= nb):
                ent = PIPEQ[PST["r"]]
                PFN["R"](ent["q"], *ent["bank"])
                for cb in ent["after"]:
                    cb()
                ent["after"] = []
                PST["r"] += 1
                run_deferred()
                continue
            break
        if drain:
            run_deferred(True)

    def pipe_drain():
        pipe_pump(True)

    def psum_mode(attn):
        pipe_drain()
        if attn:
            pz_r.items = [(psum[0], PB[0])]
            pT_r.items = [(psum[2].bitcast(BF16), PB[2])]
            pS_r.items = [(psum[1], PB[1]), (psum[3], PB[3]), (psum[4], PB[4]), (psum[5], PB[5])]
        else:
            pz_r.items = [(psum[0], PB[0]), (psum[1], PB[1]), (psum[4], PB[4]), (psum[5], PB[5])]
            pT_r.items = [(psum[2].bitcast(BF16), PB[2]), (psum[3].bitcast(BF16), PB[3])]
            pS_r.items = [(psum[4], PB[4]), (psum[5], PB[5])]

    psum_mode(False)

    base_top = A.top

    def wload(src, kc, n):
        wt, wb, wsem = w_r.next()
        v = wt[:, 0:kc * n].rearrange("p (c n) -> p c n", c=kc)
        ld(v, src.rearrange("(c p) n -> p c n", p=128), wsem, [wb], eng="pool")
        return v, wb

    def norm_T(xrows, nt, gbc, dstT, dstbuf, x_keep=None):
        if isinstance(xrows, tuple):
            xt, xb = xrows
        else:
            xt, xb, xsem = xt_r.next()
            ld(xt[0:nt], xrows, xsem, [xb])
        sq, sqb = sq_r.next()
        sm, smb = sm_r.next()
        act(sq[0:nt], xt[0:nt], AF.Square, [xb], [sqb, smb], accum=sm[0:nt, 0:1])
        rstd_from_ss(sm[0:nt, 0:1], 1024.0, [smb])
        xn, xnb = xn_r.next()
        stt(xn[0:nt], xt[0:nt], sm[0:nt, 0:1], gbc[0:nt], ALU.mult, ALU.mult, [xb, smb, b_g], [xnb])
        pT, pTb = pT_r.next()
        transposes([(pT[:, c * 128:c * 128 + nt], xn[0:nt, c * 128:(c + 1) * 128], identb[0:nt, 0:nt])
                    for c in range(8)], [xnb, b_id], [pTb])
        cp(dstT, pT.rearrange("p (c t) -> p c t", c=8)[:, :, 0:nt], [pTb], [dstbuf])

    def norm_T_staged(xrows, nt, gbc, dstT, dstbuf):
        xt, xb, xsem = xt_r.next()
        ld(xt[0:nt], xrows, xsem, [xb])
        sq, sqb = sq_r.next()
        sm, smb = sm_r.next()
        hold = {}

        def sA():
            act(sq[0:nt], xt[0:nt], AF.Square, [xb], [sqb, smb], accum=sm[0:nt, 0:1])
            act(sm[0:nt, 0:1], sm[0:nt, 0:1], AF.Sqrt, [smb, b_eps], [smb], bias=EPSB[0:nt, 0:1], scale=1.0 / 1024.0)

        def sB():
            recip(sm[0:nt, 0:1], sm[0:nt, 0:1], [smb], [smb])
            xn, xnb = xn_r.next()
            stt(xn[0:nt], xt[0:nt], sm[0:nt, 0:1], gbc[0:nt], ALU.mult, ALU.mult, [xb, smb, b_g], [xnb])
            hold["pT"] = pT_r.next()
            pT, pTb = hold["pT"]
            transposes([(pT[:, c * 128:c * 128 + nt], xn[0:nt, c * 128:(c + 1) * 128], identb[0:nt, 0:nt])
                        for c in range(8)], [xnb, b_id], [pTb])

        def sC():
            pT, pTb = hold["pT"]
            cp(dstT, pT.rearrange("p (c t) -> p c t", c=8)[:, :, 0:nt], [pTb], [dstbuf])
        return sA, sB, sC

    def proj_tok(xT, xTb, nt, wv, wb, ncols, c0=0):
        pz, pzb = pz_r.next()
        mm(pz[0:nt, 0:ncols], [(xT[:, c, :], wv[:, c, c0:c0 + ncols]) for c in range(8)], [xTb, wb], [pzb])
        return pz, pzb

    hn_sm_r = Rot([(A.f32([16]), Buf()) for _ in range(3)])

    def head_norm(pz, pzb, nt, nh, dh, g, out_dram, dstT=None, dstTb=None, pair=True, defer_T=0, staged=False):
        n = nh * dh
        sq, sqb = sq_r.next()
        sm, smb = hn_sm_r.next() if (defer_T or staged) else sm_r.next()
        stg, stb = st_r.next()
        sbf, sbb = (None, None)
        if dstT is not None:
            sbf, sbb = sb_r.next()
        v3 = lambda a: a[0:nt, 0:n].rearrange("p (h d) -> p h d", h=nh)

        def stage_act():
            def fsq(e):
                for h in range(nh):
                    ins = e.activation(out=sq[0:nt, h * dh:(h + 1) * dh], in_=pz[0:nt, h * dh:(h + 1) * dh],
                                       func=AF.Square, accum_out=sm[0:nt, h:h + 1])
                return ins
            P.op("act", fsq, [pzb], [sqb, smb])
            act(sm[0:nt, 0:nh], sm[0:nt, 0:nh], AF.Sqrt, [smb, b_eps], [smb], bias=EPSB[0:nt, 0:1], scale=1.0 / dh)

        def stage_dve():
            recip(sm[0:nt, 0:nh], sm[0:nt, 0:nh], [smb], [smb])
            tt(v3(stg), v3(pz), bcl(sm[0:nt, 0:nh], dh), ALU.mult, [pzb, smb], [stb])
            if out_dram is not None:
                tt(v3(stg), v3(stg), bcm(g[0:nt], nh), ALU.mult, [stb, b_g], [stb])
                store(out_dram, stg[0:nt, 0:n], [stb])
                if dstT is not None:
                    cp(sbf[0:nt, 0:n], stg[0:nt, 0:n], [stb], [sbb])
            else:
                tt(v3(sbf), v3(stg), bcm(g[0:nt], nh), ALU.mult, [stb, b_g], [sbb])

        def stage_pe():
            if dstT is None:
                return
            pT, pTb = pT_r.next()
            nch = n // 128
            transposes([(pT[:, c * 128:c * 128 + nt], sbf[0:nt, c * 128:(c + 1) * 128], identb[0:nt, 0:nt])
                        for c in range(nch)], [sbb, b_id], [pTb])
            cp(dstT, pT[:, 0:nch * 128].rearrange("p (c t) -> p c t", c=nch)[:, :, 0:nt], [pTb], [dstTb])

        if staged:
            return stage_act, stage_dve, stage_pe
        if defer_T:
            defer(stage_act, defer_T)
            defer(stage_dve, 2 * defer_T)
            defer(stage_pe, 3 * defer_T)
        else:
            stage_act()
            stage_dve()
            stage_pe()

    def pipelined(n_tiles, make):
        st_ = {}
        for it in range(n_tiles + 2):
            if it < n_tiles:
                st_[it] = make(it)
                st_[it][0]()
            if 0 <= it - 1 < n_tiles:
                st_[it - 1][1]()
            if 0 <= it - 2 < n_tiles:
                st_.pop(it - 2)[2]()

    def v_store(pz, pzb, nt, nh, dh, out_dram, vaug, vb):
        n = nh * dh
        stg, stb = st_r.next()
        cp(stg[0:nt, 0:n], pz[0:nt, 0:n], [pzb], [stb])
        if out_dram is not None:
            store(out_dram, stg[0:nt, 0:n], [stb])
        act(vaug[0:nt, :, 0:dh], stg[0:nt, 0:n].rearrange("p (h d) -> p h d", h=nh), AF.Copy, [stb], [vb])

    def logf_a(xT, xTb, nt, wv, wb, c0, LFdst, lfb):
        pz, pzb = pz_r.next()
        mm(pz[0:nt, 0:8], [(xT[:, c, :], wv[:, c, c0:c0 + 8]) for c in range(8)], [xTb, wb], [pzb])
        tt(LFdst, pz[0:nt, 0:8], bfb[0:nt], ALU.add, [pzb, b_g], [lfb])

    def logf_b(LFdst, lfb):
        act(LFdst, LFdst, AF.Exp, [lfb], [lfb], scale=-1.0)
        act(LFdst, LFdst, AF.Ln, [lfb], [lfb], bias=1.0)

    def logf_c(LFdst, lfb, out_dram):
        ts(LFdst, LFdst, -1.0, None, ALU.mult, None, [lfb], [lfb])
        if out_dram is not None:
            store(out_dram, LFdst, [lfb])

    def logf_tile(xT, xTb, nt, wv, wb, c0, LFdst, lfb, out_dram):
        logf_a(xT, xTb, nt, wv, wb, c0, LFdst, lfb)
        logf_b(LFdst, lfb)
        logf_c(LFdst, lfb, out_dram)

    def cumsum_tile(LF, lfb, nt, R, rb, Cdst, cb, first):
        pz, pzb = pz_r.next()
        if first:
            mm(pz[0:nt, 0:8], [(TRI[0:nt, 0:nt], LF)], [lfb, b_cm], [pzb])
        else:
            mm(pz[0:nt, 0:8], [(TRI[0:nt, 0:nt], LF), (ONES[:, 0:nt], R)], [lfb, b_cm, rb], [pzb])
        cp(Cdst, pz[0:nt, 0:8], [pzb], [cb])
        if first:
            memset(R, 0.0, [rb])
        tt(R[0:nt], R[0:nt], LF, ALU.add, [rb, lfb], [rb])

    def run_sums(LFs_list, Rall, rb, first_is_zero):
        if first_is_zero:
            memset(Rall[:, 0, :], 0.0, [rb])
        for t, (lf, lfb, nt) in enumerate(LFs_list):
            if nt < 128:
                continue
            tt(Rall[:, t + 1, :], Rall[:, t, :], lf, ALU.add, [rb, lfb], [rb])

    def cumsum_par(LF, lfb, nt, Rt, rb, Cdst, cb):
        pz, pzb = pz_r.next()
        mm(pz[0:nt, 0:8], [(TRI[0:nt, 0:nt], LF), (ONES[:, 0:nt], Rt)], [lfb, b_cm, rb], [pzb])
        cp(Cdst, pz[0:nt, 0:8], [pzb], [cb])

    def bcast_row(src, sb_, nrows, SEL, dst, db):
        pz, pzb = pz_r.next()
        mm(pz[:, 0:8], [(SEL[0:nrows, :], src)], [sb_, b_cm], [pzb])
        cp(dst, pz[:, 0:8], [pzb], [db])

    def attn_head(qT, nq, ktiles, po_ap, pob, scale, first_reads):
        n = len(ktiles)
        PENDING.append([dict(kt=kt, qT=qT, nq=nq, po_ap=po_ap, pob=pob, scale=scale,
                             first_reads=list(first_reads), start=(idx == 0), stop=(idx == n - 1))
                        for idx, kt in enumerate(ktiles)])

    PENDING = []
    LA = 4

    def attn_flush():
        calls = list(PENDING)
        del PENDING[:]
        steps = [sp_ for c_ in calls for sp_ in c_]
        quads = [steps[i:i + 4] for i in range(0, len(steps), 4)]

        def uniq(bl):
            out = []
            for b in bl:
                if b not in out:
                    out.append(b)
            return out

        def emit_S(q):
            bank, bb = pS_r.next()
            q = list(q)

            def f(e):
                for k, sp_ in enumerate(q):
                    kt = sp_["kt"]
                    ins = e.matmul(bank[0:kt["nk"], k * 128:k * 128 + sp_["nq"]], lhsT=kt["kT"], rhs=sp_["qT"],
                                   start=True, stop=True)
                return ins
            rds = []
            for sp_ in q:
                rds += list(sp_["kt"]["rd"]) + sp_["first_reads"]
            P.op("pe", f, uniq(rds), [bb])
            return bank, bb

        def emit_rest(q, bank, bb):
            q = list(q)
            pts = [pt_r.next() for _ in q]
            sas = {}
            pre_items = []
            for k, sp_ in enumerate(q):
                if sp_["kt"].get("pre") is not None:
                    sas[k] = sa_r.next()
                    pre_items.append(k)
            if pre_items:
                def fd(e):
                    for k in pre_items:
                        sp_ = q[k]
                        kt = sp_["kt"]
                        nk, nq = kt["nk"], sp_["nq"]
                        ins = e.scalar_tensor_tensor(out=sas[k][0][0:nk, 0:nq], in0=bank[0:nk, k * 128:k * 128 + nq],
                                                     scalar=sp_["scale"], in1=kt["pre"][0], op0=ALU.mult, op1=ALU.add)
                    return ins
                rds = [bb]
                for k in pre_items:
                    rds += list(q[k]["kt"]["pre"][1])
                P.op("dve", fd, uniq(rds), [sas[k][1] for k in pre_items])

            def fa(e):
                for k, sp_ in enumerate(q):
                    kt = sp_["kt"]
                    nk, nq = kt["nk"], sp_["nq"]
                    if k in sas:
                        ins = e.activation(out=pts[k][0][0:nk, 0:nq], in_=sas[k][0][0:nk, 0:nq], func=AF.Exp,
                                           bias=kt.get("bias", 0.0))
                    else:
                        ins = e.activation(out=pts[k][0][0:nk, 0:nq], in_=bank[0:nk, k * 128:k * 128 + nq], func=AF.Exp,
                                           bias=kt.get("bias", 0.0), scale=sp_["scale"])
                return ins
            rds = [bb] + [sas[k][1] for k in pre_items]
            for sp_ in q:
                rds += list(sp_["kt"].get("brd", []))
            P.op("act", fa, uniq(rds), [p_[1] for p_ in pts])
            for k, sp_ in enumerate(q):
                kt = sp_["kt"]
                if kt.get("mask") is not None:
                    nk, nq = kt["nk"], sp_["nq"]
                    tt(pts[k][0][0:nk, 0:nq], pts[k][0][0:nk, 0:nq], kt["mask"][0:nk, 0:nq], ALU.mult,
                       [pts[k][1], b_trib, b_mask0], [pts[k][1]], eng="pool")
            def fp(e):
                for k, sp_ in enumerate(q):
                    kt = sp_["kt"]
                    nk, nq = kt["nk"], sp_["nq"]
                    ins = e.matmul(sp_["po_ap"], lhsT=pts[k][0][0:nk, 0:nq], rhs=kt["v"],
                                   start=sp_["start"], stop=sp_["stop"])
                return ins
            rds = [p_[1] for p_ in pts]
            for sp_ in q:
                rds += list(sp_["kt"]["rd"])
            P.op("pe", fp, uniq(rds), uniq([sp_["pob"] for sp_ in q]))

        for q in quads:
            PIPEQ.append({"q": q, "bank": None, "after": []})
        PFN["S"], PFN["R"] = emit_S, emit_rest
        pipe_pump(False)


    def mm_acc(out, l, r, start, stop, reads, writes):
        P.op("pe", lambda e: e.matmul(out, lhsT=l, rhs=r, start=start, stop=stop), reads, writes)

    FIN = {}

    def attn_finish_group(nq, nh, dh, half):
        attn_flush()
        PIPEQ[-1]["after"].append(lambda: fin_group(nq, nh, dh, half))
        if PST["r"] >= len(PIPEQ):
            raise RuntimeError("pipeline ran ahead of finish")

    def fin_group(nq, nh, dh, half):
        hpb = nh // 2
        if half == 0:
            FIN["sbf"] = fin_r.next()
        sbf, sbb = FIN["sbf"]
        po, pob = pO[half]
        pv = po[0:nq, 0:hpb * (dh + 1)].rearrange("p (h d) -> p h d", h=hpb)
        sm, smb = sm_r.next()
        recip(sm[0:nq, 0:hpb], pv[:, :, dh], [pob], [smb])
        if dh == 64:
            ov_ = sbf[0:nq, 0:512].rearrange("p (hp two d) -> p hp two d", two=2, d=64)[:, :, half, :]
        else:
            ov_ = sbf[0:nq, half * hpb * dh:(half + 1) * hpb * dh].rearrange("p (h d) -> p h d", h=hpb)
        tt(ov_, pv[:, :, 0:dh], bcl(sm[0:nq, 0:hpb], dh), ALU.mult, [pob, smb], [sbb])

    def attn_finish(nq, nh, dh, dstT, dstTb):
        PIPEQ[-1]["after"].append(lambda: fin_tile(nq, dstT, dstTb))

    def fin_tile(nq, dstT, dstTb):
        sbf, sbb = FIN["sbf"]

        def part2():
            pT, pTb = pT_r.next()
            transposes([(pT[:, c * 128:c * 128 + nq], sbf[0:nq, c * 128:(c + 1) * 128], identb[0:nq, 0:nq])
                        for c in range(4)], [sbb, b_id], [pTb])
            cp(dstT, pT[:, 0:512].rearrange("p (c t) -> p c t", c=4)[:, :, 0:nq], [pTb], [dstTb])
        defer(part2, 2)

    def po_slot(h, nh, dh):
        hpb = nh // 2
        if dh == 64:
            po, pob = pO[h % 2]
            hh = h // 2
        else:
            po, pob = pO[h // hpb]
            hh = h % hpb
        return po, pob, hh * (dh + 1)

    BiasA = A.f32([8, 2, 128]); b_bias = Buf()
    relc = A.f32([8]); relcnf = A.f32([8])
    ld(relc, rel_bias[256:257, :].partition_broadcast(128), csem, [b_bias])
    ts(relcnf, relc, negflag[:, 0:1], None, ALU.add, None, [b_bias, b_nf], [b_bias])
    mask0 = A.bf16([128]); b_mask0 = Buf()
    memset(mask0, 1.0, [b_mask0])
    memset(mask0[0:64, 64:128], 0.0, [b_mask0])
    mark0 = A.top
    relx = A.f32([LREL], parts=8); b_relx = Buf()
    rsem = P.new_dma_sem()
    memset(relx, 0.0, [b_relx])
    sm8 = A.f32([1], parts=8)
    ld(sm8, rel_bias[256:257, :].rearrange("i h -> h i"), rsem, [b_relx], slow=True)
    ts(relx, relx, sm8[:, 0:1], None, ALU.add, None, [b_relx], [b_relx])
    ld(relx[:, 0:257], rel_bias.rearrange("i h -> h i"), rsem, [b_relx], slow=True)
    b_relE = Buf()
    ld(relE, relx, rsem, [b_relE], reads=[b_relx])
    relrep = A.f32([8, LREL]); b_rr = Buf()
    ld(relrep, relE.partition_broadcast(128), rsem, [b_rr], reads=[b_relE])
    b_relD = Buf()
    ld(relD, relrep.rearrange("p a b -> p (a b)"), rsem, [b_relD], reads=[b_rr])
    for t in (3, 4):
        src = bass.AP(tensor=relD.tensor, offset=640 - 128 * t,
                      ap=[[8 * LREL - 1, 128], [LREL, 8], [1, 128]])
        ld(BiasA[:, :, t - 3, :], src, rsem, [b_bias], reads=[b_relD])
    memset(BiasA[64:128, :, 1, 0:64], NEG, [b_bias])
    P.barrier(list(ALLBUFS))
    A.top = mark0

    mkT = A.bf16([4, 256]); b_mkT = Buf()
    mvA = A.bf16([2, 4, 129]); b_mv = Buf()
    memset(mvA[:, :, :, 128:129], 1.0, [b_mv])
    mark1 = A.top
    gmem = A.f32([1024]); b_gm = Buf()
    ld(gmem, g_mem.partition_broadcast(128), csem, [b_g])
    xmT = A.bf16([8, 256]); b_xmT = Buf()
    for t in range(2):
        norm_T(mem[t * 128:(t + 1) * 128, :], 128, gmem, xmT[:, :, t * 128:(t + 1) * 128], b_xmT)
    wv, wb = wload(w_mkv[:, 0:512], 8, 512)
    for t in range(2):
        pz, pzb = proj_tok(xmT[:, :, t * 128:(t + 1) * 128], b_xmT, 128, wv, wb, 512)
        head_norm(pz, pzb, 128, 4, 128, gv["km"], mkp[t * 128:(t + 1) * 128, :],
                  mkT[:, :, t * 128:(t + 1) * 128], b_mkT)
    wv, wb = wload(w_mkv[:, 512:1024], 8, 512)
    for t in range(2):
        pz, pzb = proj_tok(xmT[:, :, t * 128:(t + 1) * 128], b_xmT, 128, wv, wb, 512)
        v_store(pz, pzb, 128, 4, 128, mvp[t * 128:(t + 1) * 128, :], mvA[:, t], b_mv)
    P.barrier(list(ALLBUFS))
    A.top = mark1

    def full_barrier():
        pipe_drain()
        P.barrier(list(ALLBUFS))

    mark_x = A.top
    xnT = A.bf16([8, 2048]); b_xnT = [Buf() for _ in range(NOWN)]
    xpT4 = A.bf16([8, 512]); b_xp4 = Buf()
    xsT = A.bf16([8, NS]); b_xsT = Buf()
    osT = {k: A.bf16([4, NS]) for k in "bam"}
    b_osT = {k: Buf() for k in "bam"}
    oT = {}
    b_oT = {k: Buf() for k in "bam"}
    KsT_new = A.bf16([4, NS]); b_KsN = Buf()
    QsT = A.bf16([4, NS]); b_QsT = Buf()
    VsN = A.bf16([4, 8, 65], parts=16); b_VsN = Buf()
    LFs = A.f32([4, 8], parts=16); b_LFs = Buf()
    memset(VsN[:, :, :, 64:65], 1.0, [b_VsN])
    oT["b"] = A.bf16([4, 2048])

    def xT_of(i):
        return xnT[:, :, i * 128:(i + 1) * 128], b_xnT[i]

    markB = A.top
    KbT = A.bf16([4, NT * 128]); b_KbT = [Buf() for _ in range(NT)]
    Vb = A.bf16([NT, 8, 65]); b_Vb = [Buf() for _ in range(NT)]
    Qb_r = Rot([(A.bf16([4, 128]), Buf()) for _ in range(2)])
    LF = A.f32([NT, 8]); b_LF = [Buf() for _ in range(NT)]
    Cc = A.f32([NT, 8]); b_C = [Buf() for _ in range(NT)]
    crefs = A.f32([NOWN, 8]); b_cref = [Buf() for _ in range(NOWN)]
    Rall = A.f32([NT + 1, 8]); b_R = Buf()
    biasT_r = Rot([(A.f32([NT + 1, 8]), Buf()) for _ in range(3)])
    memset(Vb[:, :, :, 64:65], 1.0, b_Vb)
    def mk_nt_pre(i):
        xv, xb_ = xT_of(i)
        return norm_T_staged(xpre[i * 128:(i + 1) * 128, :], 128, gmix, xv, xb_)
    pipelined(NPRE, mk_nt_pre)
    for i in range(4):
        xv, xb_ = xT_of(12 + i)
        cp(xpT4[:, :, i * 128:(i + 1) * 128], xv, [xb_], [b_xp4])
    wv, wb = wload(w_in[:, C_KB:C_KB + 512], 8, 512)
    def mk_kb_pre(i):
        xv, xb_ = xT_of(i)
        pz, pzb = proj_tok(xv, xb_, 128, wv, wb, 512)
        return head_norm(pz, pzb, 128, 8, 64, gv["kb"], None, KbT[:, :, i * 128:(i + 1) * 128], b_KbT[i], staged=True)
    pipelined(NPRE, mk_kb_pre)
    wv, wb = wload(w_in[:, C_VB:C_VB + 520], 8, 520)
    for i in range(NPRE):
        xv, xb_ = xT_of(i)
        pz, pzb = proj_tok(xv, xb_, 128, wv, wb, 512)
        v_store(pz, pzb, 128, 8, 64, None, Vb[:, i], b_Vb[i])
        logf_a(xv, xb_, 128, wv, wb, 512, LF[:, i, :], b_LF[i])
    for i in range(NPRE):
        logf_b(LF[:, i, :], b_LF[i])
    for i in range(NPRE):
        logf_c(LF[:, i, :], b_LF[i], None)
    run_sums([(LF[:, i, :], b_LF[i], 128) for i in range(NPRE)], Rall, b_R, True)
    for i in range(NPRE):
        cumsum_par(LF[:, i, :], b_LF[i], 128, Rall[:, i, :], b_R, Cc[:, i, :], b_C[i])
    def mk_nt_own(i):
        xv, xb_ = xT_of(i)
        return norm_T_staged(xown[i * 128:(i + 1) * 128, :], 128, gmix, xv, xb_)
    pipelined(NOWN, mk_nt_own)
    norm_T(xs, NS, gmix, xsT, b_xsT)
    wv, wb = wload(w_in[:, C_KB:C_KB + 512], 8, 512)
    def mk_kb_own(i):
        xv, xb_ = xT_of(i)
        pz, pzb = proj_tok(xv, xb_, 128, wv, wb, 512)
        return head_norm(pz, pzb, 128, 8, 64, gv["kb"], bkp[i * 128:(i + 1) * 128, :],
                         KbT[:, :, (NPRE + i) * 128:(NPRE + i + 1) * 128], b_KbT[NPRE + i], staged=True)
    pipelined(NOWN, mk_kb_own)
    pz, pzb = proj_tok(xsT, b_xsT, NS, wv, wb, 512)
    head_norm(pz, pzb, NS, 8, 64, gv["kb"], bks, KsT_new, b_KsN)
    wv, wb = wload(w_in[:, C_VB:C_VB + 520], 8, 520)
    for i in range(NOWN):
        xv, xb_ = xT_of(i)
        j = NPRE + i
        pz, pzb = proj_tok(xv, xb_, 128, wv, wb, 512)
        v_store(pz, pzb, 128, 8, 64, bvp[i * 128:(i + 1) * 128, :], Vb[:, j], b_Vb[j])
        logf_a(xv, xb_, 128, wv, wb, 512, LF[:, j, :], b_LF[j])
    for i in range(NOWN):
        logf_b(LF[:, NPRE + i, :], b_LF[NPRE + i])
    for i in range(NOWN):
        logf_c(LF[:, NPRE + i, :], b_LF[NPRE + i], blfp[i * 128:(i + 1) * 128, :])
    for i in range(NOWN):
        j = NPRE + i
        tt(Rall[:, j + 1, :], Rall[:, j, :], LF[:, j, :], ALU.add, [b_R, b_LF[j]], [b_R])
    for i in range(NOWN):
        j = NPRE + i
        cumsum_par(LF[:, j, :], b_LF[j], 128, Rall[:, j, :], b_R, Cc[:, j, :], b_C[j])
    for i in range(NOWN):
        j = NPRE + i
        bcast_row(Cc[:, j, :], b_C[j], 128, SELMID, crefs[:, i, :], b_cref[i])
    for bb in range(4):
        xv = xsT[:, :, bb * 16:(bb + 1) * 16]
        pz, pzb = proj_tok(xv, b_xsT, 16, wv, wb, 512)
        v_store(pz, pzb, 16, 8, 64, bvs[bb * 16:(bb + 1) * 16, :], VsN[:, bb], b_VsN)
        logf_tile(xv, b_xsT, 16, wv, wb, 512, LFs[:, bb, :], b_LFs, blfs[bb * 16:(bb + 1) * 16, :])
    wv, wb = wload(w_in[:, C_QB:C_QB + 512], 8, 512)
    pz, pzb = proj_tok(xsT, b_xsT, NS, wv, wb, 512)
    head_norm(pz, pzb, NS, 8, 64, gv["qb"], None, QsT, b_QsT)

    psum_mode(True)
    def prep_b(i):
        gi = NPRE + i
        xv, xb_ = xT_of(i)
        pz, pzb = proj_tok(xv, xb_, 128, wv, wb, 512)
        qv, qvb = Qb_r.next()
        head_norm(pz, pzb, 128, 8, 64, gv["qb"], None, qv, qvb, defer_T=2)
        bt_all, b_bt = biasT_r.next()
        for j in range(gi + 1):
            tt(bt_all[:, j, :], crefs[:, i, :], Cc[:, j, :], ALU.subtract, [b_cref[i], b_C[j]], [b_bt])
        ts(bt_all[:, 0:NPRE, :], bt_all[:, 0:NPRE, :], negflag[:, 0:1], None, ALU.add, None, [b_bt, b_nf], [b_bt])
        return qv, qvb, bt_all, b_bt

    nxt = prep_b(0)
    for i in range(NOWN):
        gi = NPRE + i
        run_deferred(True)
        qv, qvb, bt_all, b_bt = nxt
        if i + 1 < NOWN:
            nxt = prep_b(i + 1)
        for h in (0, 2, 4, 6, 1, 3, 5, 7):
            hp, hr = h // 2, (h % 2) * 64
            po, pob, off = po_slot(h, 8, 64)
            kts = []
            for j in range(gi + 1):
                kts.append(dict(kT=KbT[hr:hr + 64, hp, j * 128:(j + 1) * 128], nk=128,
                                v=Vb[:, j, h, :], rd=[b_KbT[j], b_Vb[j]],
                                bias=bt_all[:, j, h:h + 1], brd=[b_bt],
                                mask=(trib if j == gi else None)))
            attn_head(qv[hr:hr + 64, hp, :], 128, kts, po[:, off:off + 65], pob,
                      0.125, [qvb])
            if h >= 6:
                attn_finish_group(128, 8, 64, h % 2)
        attn_finish(128, 8, 64, oT["b"][:, :, i * 128:(i + 1) * 128], b_oT["b"])
    full_barrier()
    A.top = markB

    Kc = A.bf16([16, 512]); b_Kc = [Buf() for _ in range(4)]; kcsem = None
    KsT = A.bf16([4, 2048]); b_KsT = Buf()
    Vs = A.bf16([16, 8, 65]); b_Vs = [Buf() for _ in range(16)]
    LFc = A.f32([16, 8]); b_LFc = Buf()
    Cs = A.f32([16, 8]); b_Cs = Buf()
    CsN = A.f32([8], parts=16); b_CsN = Buf()
    crefS = A.f32([8]); b_crefS = Buf()
    memset(Vs[:, :, :, 64:65], 1.0, b_Vs)
    for bb in range(4):
        for q4 in range(4):
            ld(Kc[:, 4 * q4:4 * q4 + 4, :], cbk[bb, 512 * q4:512 * (q4 + 1), :].rearrange("(t p) n -> p t n", p=128),
               kcsem, [b_Kc[q4]], eng="pool")
        for t in range(16):
            ld(Vs[:, t, :, 0:64], cbv[bb, 128 * t:128 * (t + 1), :].rearrange("p (h d) -> p h d", h=8),
               kcsem, [b_Vs[t]], eng="pool")
        ld(LFc, cblf[bb].rearrange("(t p) h -> p t h", p=128), kcsem, [b_LFc])
        for t in range(16):
            pT, pTb = pT_r.next()
            transposes([(pT[:, c * 128:(c + 1) * 128], Kc[:, t, c * 128:(c + 1) * 128], identb) for c in range(4)],
                       [b_Kc[t // 4], b_id], [pTb])
            cp(KsT[:, :, t * 128:(t + 1) * 128], pT[:, 0:512].rearrange("p (c t) -> p c t", c=4), [pTb], [b_KsT])
        run_sums([(LFc[:, t, :], b_LFc, 128) for t in range(16)], Rall, b_R, True)
        for t in range(16):
            cumsum_par(LFc[:, t, :], b_LFc, 128, Rall[:, t, :], b_R, Cs[:, t, :], b_Cs)
        cumsum_par(LFs[:, bb, :], b_LFs, 16, Rall[:, 16, :], b_R, CsN, b_CsN)
        bcast_row(CsN, b_CsN, 16, SEL0, crefS, b_crefS)
        bt_all, b_bt = biasT_r.next()
        for t in range(16):
            tt(bt_all[:, t, :], crefS, Cs[:, t, :], ALU.subtract, [b_crefS, b_Cs], [b_bt])
        tt(bt_all[0:16, 16, :], crefS[0:16], CsN, ALU.subtract, [b_crefS, b_CsN], [b_bt])
        for h in (0, 2, 4, 6, 1, 3, 5, 7):
            hp, hr = h // 2, (h % 2) * 64
            po, pob, off = po_slot(h, 8, 64)
            kts = []
            for t in range(16):
                kts.append(dict(kT=KsT[hr:hr + 64, hp, t * 128:(t + 1) * 128], nk=128, v=Vs[:, t, h, :],
                                rd=[b_KsT, b_Vs[t]], bias=bt_all[:, t, h:h + 1], brd=[b_bt], mask=None))
            kts.append(dict(kT=KsT_new[hr:hr + 64, hp, bb * 16:(bb + 1) * 16], nk=16, v=VsN[:, bb, h, :],
                            rd=[b_KsN, b_VsN], bias=bt_all[0:16, 16, h:h + 1], brd=[b_bt], mask=trib))
            attn_head(QsT[hr:hr + 64, hp, bb * 16:(bb + 1) * 16], 16, kts, po[0:16, off:off + 65], pob,
                      0.125, [b_QsT])
            if h >= 6:
                attn_finish_group(16, 8, 64, h % 2)
        attn_finish(16, 8, 64, osT["b"][:, :, bb * 16:(bb + 1) * 16], b_osT["b"])
        pipe_drain()
    full_barrier()
    A.top = markB

    psum_mode(False)
    oT["a"] = A.bf16([4, 2048])
    markAA = A.top
    NA = 4 + NOWN
    KaT = A.bf16([4, NA * 128]); b_KaT = [Buf() for _ in range(NA)]
    Va = A.bf16([NA, 8, 65]); b_Va = [Buf() for _ in range(NA)]
    Qa_r = Rot([(A.bf16([4, 128]), Buf()) for _ in range(2)])
    KsaN = A.bf16([4, NS]); b_KsaN = Buf()
    QsaT = A.bf16([4, NS]); b_QsaT = Buf()
    VsaN = A.bf16([4, 8, 65], parts=16); b_VsaN = Buf()
    memset(Va[:, :, :, 64:65], 1.0, b_Va)
    memset(VsaN[:, :, :, 64:65], 1.0, [b_VsaN])

    def xa_of(idx):
        if idx < 4:
            return xpT4[:, :, idx * 128:(idx + 1) * 128], b_xp4
        return xT_of(idx - 4)

    wv, wb = wload(w_in[:, C_KA:C_KA + 512], 8, 512)
    def mk_ka(idx):
        xv, xb_ = xa_of(idx)
        pz, pzb = proj_tok(xv, xb_, 128, wv, wb, 512)
        od = akp[(idx - 16) * 128:(idx - 15) * 128, :] if idx >= 16 else None
        return head_norm(pz, pzb, 128, 8, 64, gv["ka"], od, KaT[:, :, idx * 128:(idx + 1) * 128], b_KaT[idx], staged=True)
    pipelined(NA, mk_ka)
    pz, pzb = proj_tok(xsT, b_xsT, NS, wv, wb, 512)
    head_norm(pz, pzb, NS, 8, 64, gv["ka"], aks, KsaN, b_KsaN)
    wv, wb = wload(w_in[:, C_VA:C_VA + 512], 8, 512)
    for idx in range(NA):
        xv, xb_ = xa_of(idx)
        pz, pzb = proj_tok(xv, xb_, 128, wv, wb, 512)
        od = avp[(idx - 16) * 128:(idx - 15) * 128, :] if idx >= 16 else None
        v_store(pz, pzb, 128, 8, 64, od, Va[:, idx], b_Va[idx])
    for bb in range(4):
        pz, pzb = proj_tok(xsT[:, :, bb * 16:(bb + 1) * 16], b_xsT, 16, wv, wb, 512)
        v_store(pz, pzb, 16, 8, 64, avs[bb * 16:(bb + 1) * 16, :], VsaN[:, bb], b_VsaN)
    wv, wb = wload(w_in[:, C_QA:C_QA + 512], 8, 512)
    pz, pzb = proj_tok(xsT, b_xsT, NS, wv, wb, 512)
    head_norm(pz, pzb, NS, 8, 64, gv["qa"], None, QsaT, b_QsaT)
    def prep_a(i):
        xv, xb_ = xT_of(i)
        pz, pzb = proj_tok(xv, xb_, 128, wv, wb, 512)
        qv, qvb = Qa_r.next()
        head_norm(pz, pzb, 128, 8, 64, gv["qa"], None, qv, qvb, defer_T=2)
        return qv, qvb

    psum_mode(True)
    nxt = prep_a(0)
    for i in range(NOWN):
        run_deferred(True)
        qv, qvb = nxt
        if i + 1 < NOWN:
            nxt = prep_a(i + 1)
        for h in (0, 2, 4, 6, 1, 3, 5, 7):
            hp, hr = h // 2, (h % 2) * 64
            po, pob, off = po_slot(h, 8, 64)
            kts = []
            for t in range(5):
                idx = i + t
                pre_ = idx < 4
                d = dict(kT=KaT[hr:hr + 64, hp, idx * 128:(idx + 1) * 128], nk=128, v=Va[:, idx, h, :],
                         rd=[b_KaT[idx], b_Va[idx]], mask=None, brd=[b_bias, b_nf])
                if t < 3:
                    d["bias"] = (relcnf if pre_ else relc)[:, h:h + 1]
                    if t == 0:
                        d["mask"] = mask0
                else:
                    d["pre"] = (BiasA[:, h, t - 3, :], [b_bias])
                    d["bias"] = negflag[:, 0:1] if pre_ else 0.0
                kts.append(d)
            attn_head(qv[hr:hr + 64, hp, :], 128, kts, po[:, off:off + 65], pob,
                      0.125, [qvb])
            if h >= 6:
                attn_finish_group(128, 8, 64, h % 2)
        attn_finish(128, 8, 64, oT["a"][:, :, i * 128:(i + 1) * 128], b_oT["a"])
    Kca = A.bf16([4, 512]); b_Kca = Buf()
    KsaT = A.bf16([4, 512]); b_KsaT = Buf()
    Vsa = A.bf16([4, 8, 65]); b_Vsa = Buf()
    memset(Vsa[:, :, :, 64:65], 1.0, [b_Vsa])
    for bb in range(4):
        ld(Kca, cak[bb].rearrange("(t p) n -> p t n", p=128), kcsem, [b_Kca], eng="pool")
        for t in range(4):
            ld(Vsa[:, t, :, 0:64], cav[bb, 128 * t:128 * (t + 1), :].rearrange("p (h d) -> p h d", h=8),
               kcsem, [b_Vsa], eng="pool")
        for t in range(4):
            pT, pTb = pT_r.next()
            transposes([(pT[:, c * 128:(c + 1) * 128], Kca[:, t, c * 128:(c + 1) * 128], identb) for c in range(4)],
                       [b_Kca, b_id], [pTb])
            cp(KsaT[:, :, t * 128:(t + 1) * 128], pT[:, 0:512].rearrange("p (c t) -> p c t", c=4), [pTb], [b_KsaT])
        for h in (0, 2, 4, 6, 1, 3, 5, 7):
            hp, hr = h // 2, (h % 2) * 64
            po, pob, off = po_slot(h, 8, 64)
            kts = []
            for t in range(4):
                d = dict(kT=KsaT[hr:hr + 64, hp, t * 128:(t + 1) * 128], nk=128, v=Vsa[:, t, h, :],
                         rd=[b_KsaT, b_Vsa], mask=None, brd=[b_bias])
                if t < 3:
                    d["bias"] = relc[:, h:h + 1]
                else:
                    d["pre"] = (BiasA[:, h, 0, 0:16], [b_bias])
                    d["bias"] = 0.0
                kts.append(d)
            kts.append(dict(kT=KsaN[hr:hr + 64, hp, bb * 16:(bb + 1) * 16], nk=16, v=VsaN[:, bb, h, :],
                            rd=[b_KsaN, b_VsaN], mask=None, brd=[b_bias],
                            pre=(BiasA[0:16, h, 1, 0:16], [b_bias]), bias=0.0))
            attn_head(QsaT[hr:hr + 64, hp, bb * 16:(bb + 1) * 16], 16, kts, po[0:16, off:off + 65], pob,
                      0.125, [b_QsaT])
            if h >= 6:
                attn_finish_group(16, 8, 64, h % 2)
        attn_finish(16, 8, 64, osT["a"][:, :, bb * 16:(bb + 1) * 16], b_osT["a"])
        pipe_drain()
    full_barrier()
    A.top = markAA

    psum_mode(False)
    oT["m"] = A.bf16([4, 2048])
    markM = A.top
    Qm_r = Rot([(A.bf16([4, 128]), Buf()) for _ in range(2)])
    QsmT = A.bf16([4, NS]); b_QsmT = Buf()
    wv, wb = wload(w_in[:, C_QM:C_QM + 512], 8, 512)
    pz, pzb = proj_tok(xsT, b_xsT, NS, wv, wb, 512)
    head_norm(pz, pzb, NS, 4, 128, gv["qm"], None, QsmT, b_QsmT)
    sc_m = float(128 ** -0.5)
    def prep_m(i):
        xv, xb_ = xT_of(i)
        pz, pzb = proj_tok(xv, xb_, 128, wv, wb, 512)
        qv, qvb = Qm_r.next()
        head_norm(pz, pzb, 128, 4, 128, gv["qm"], None, qv, qvb, defer_T=1)
        return qv, qvb

    psum_mode(True)
    nxt = prep_m(0)
    for i in range(NOWN):
        run_deferred(True)
        qv, qvb = nxt
        if i + 1 < NOWN:
            nxt = prep_m(i + 1)
        for h in range(4):
            po, pob, off = po_slot(h, 4, 128)
            kts = [dict(kT=mkT[:, h, m * 128:(m + 1) * 128], nk=128, v=mvA[:, m, h, :], rd=[b_mkT, b_mv],
                        mask=None, bias=0.0) for m in range(2)]
            attn_head(qv[:, h, :], 128, kts, po[:, off:off + 129], pob, sc_m, [qvb])
            if (h + 1) % (4 // 2) == 0:
                attn_finish_group(128, 4, 128, h // (4 // 2))
        attn_finish(128, 4, 128, oT["m"][:, :, i * 128:(i + 1) * 128], b_oT["m"])
    Kcm = A.bf16([2, 512]); b_Kcm = Buf()
    mksT = A.bf16([4, 256]); b_mksT = Buf()
    mvs = A.bf16([2, 4, 129]); b_mvs = Buf()
    memset(mvs[:, :, :, 128:129], 1.0, [b_mvs])
    for bb in range(4):
        ld(Kcm, cmk[bb].rearrange("(t p) n -> p t n", p=128), kcsem, [b_Kcm], eng="pool")
        for t in range(2):
            ld(mvs[:, t, :, 0:128], cmv[bb, 128 * t:128 * (t + 1), :].rearrange("p (h d) -> p h d", h=4),
               kcsem, [b_mvs], eng="pool")
        for t in range(2):
            pT, pTb = pT_r.next()
            transposes([(pT[:, c * 128:(c + 1) * 128], Kcm[:, t, c * 128:(c + 1) * 128], identb) for c in range(4)],
                       [b_Kcm, b_id], [pTb])
            cp(mksT[:, :, t * 128:(t + 1) * 128], pT[:, 0:512].rearrange("p (c t) -> p c t", c=4), [pTb], [b_mksT])
        for h in range(4):
            po, pob, off = po_slot(h, 4, 128)
            kts = [dict(kT=mksT[:, h, m * 128:(m + 1) * 128], nk=128, v=mvs[:, m, h, :], rd=[b_mksT, b_mvs],
                        mask=None, bias=0.0) for m in range(2)]
            attn_head(QsmT[:, h, bb * 16:(bb + 1) * 16], 16, kts, po[0:16, off:off + 129], pob, sc_m, [b_QsmT])
            if (h + 1) % (4 // 2) == 0:
                attn_finish_group(16, 4, 128, h // (4 // 2))
        attn_finish(16, 4, 128, osT["m"][:, :, bb * 16:(bb + 1) * 16], b_osT["m"])
        pipe_drain()
    full_barrier()
    A.top = markM

    psum_mode(False)
    hT_start = A.top
    hT = A.bf16([8, 2048]); b_hT = [Buf() for _ in range(4)]
    hsT = A.bf16([8, NS]); b_hsT = Buf()
    mark_h_end = A.top
    acc_r = Rot([(A.f32([512]), Buf()) for _ in range(2)])
    wp_r = Rot([(A.bf16([4 * 384]), Buf(), P.new_dma_sem()) for _ in range(2)])
    wps = {"a": w_pa, "b": w_pb, "m": w_pm}
    BR = "abm"
    groups = [(tg * 512, 512, b_xnT[4 * tg:4 * tg + 4], b_hT[tg]) for tg in range(4)]
    for nch in range(8):
        wt, wgb, wsem = w_r.next()
        wg = wt[:, 0:8 * 384].rearrange("p (c b n) -> p c b n", c=8, b=3)
        wpt, wpb, wpsem = wp_r.next()
        wp = wpt.rearrange("p (c b n) -> p c b n", c=4, b=3)
        for bi, br in enumerate(BR):
            c0 = C_G + bi * 1024 + nch * 128
            ld(wg[:, :, bi, :], w_in[:, c0:c0 + 128].rearrange("(c p) n -> p c n", p=128), wsem, [wgb], eng="pool")
            ld(wp[:, :, bi, :], wps[br][:, nch * 128:(nch + 1) * 128].rearrange("(c p) n -> p c n", p=128),
               wpsem, [wpb], eng="pool")
        for gidx in range(5):
            if gidx < 4:
                t0, ntok, xbufs, hb = groups[gidx]
                xv = xnT[:, :, t0:t0 + ntok]
                ov = {k: oT[k][:, :, t0:t0 + ntok] for k in BR}
                obufs = b_oT
                hdst = hT[:, nch, t0:t0 + ntok]
            else:
                ntok, xbufs, hb = NS, [b_xsT], b_hsT
                xv = xsT
                ov = osT
                obufs = b_osT
                hdst = hsT[:, nch, :]
            acc, accb = acc_r.next()
            for bi, br in enumerate(BR):
                pz, pzb = pz_r.next()
                mm(pz[:, 0:ntok], [(wg[:, c, bi, :], xv[:, c, :]) for c in range(8)], [wgb] + list(xbufs), [pzb])
                sg, sgb = st_r.next()
                act(sg[:, 0:ntok], pz[:, 0:ntok], AF.Sigmoid, [pzb], [sgb])
                pz2, pz2b = pz_r.next()
                mm(pz2[:, 0:ntok], [(wp[:, c, bi, :], ov[br][:, c, :]) for c in range(4)], [wpb, obufs[br]], [pz2b])
                if bi == 0:
                    tt(acc[:, 0:ntok], sg[:, 0:ntok], pz2[:, 0:ntok], ALU.mult, [sgb, pz2b], [accb])
                elif bi == 1:
                    tt(sg[:, 0:ntok], sg[:, 0:ntok], pz2[:, 0:ntok], ALU.mult, [sgb, pz2b], [sgb])
                    tt(acc[:, 0:ntok], acc[:, 0:ntok], sg[:, 0:ntok], ALU.add, [accb, sgb], [accb], eng="pool")
                else:
                    tt(sg[:, 0:ntok], sg[:, 0:ntok], pz2[:, 0:ntok], ALU.mult, [sgb, pz2b], [sgb])
                    tt(hdst, acc[:, 0:ntok], sg[:, 0:ntok], ALU.add, [accb, sgb], [hb], eng="pool")
    full_barrier()

    A.top = mark_x
    x1 = A.f32([NOWN, 1024]); b_x1 = [Buf() for _ in range(NOWN)]
    x1s = A.f32([1024]); b_x1s = Buf()
    assert A.top <= hT_start, ("x1 overlaps hT", A.top, hT_start)
    mark_x1_end = A.top
    A.top = mark_h_end
    wo0, wo0b = wload(w_o[:, 0:512], 8, 512)
    wo1, wo1b = wload(w_o[:, 512:1024], 8, 512)
    for i in range(NOWN + 1):
        if i < NOWN:
            nt_, src, hv, hb, dst, db = 128, xown[i * 128:(i + 1) * 128, :], hT[:, :, i * 128:(i + 1) * 128], b_hT[i // 4], x1[:, i, :], b_x1[i]
        else:
            nt_, src, hv, hb, dst, db = NS, xs, hsT, b_hsT, x1s, b_x1s
        xt, xb, xsem = xt_r.next()
        ld(xt[0:nt_], src, xsem, [xb])
        for half, (wo, wob) in enumerate(((wo0, wo0b), (wo1, wo1b))):
            pz, pzb = pz_r.next()
            mm(pz[0:nt_, :], [(hv[:, c, :], wo[:, c, :]) for c in range(8)], [hb, wob], [pzb])
            tt(dst[0:nt_, half * 512:(half + 1) * 512], xt[0:nt_, half * 512:(half + 1) * 512], pz[0:nt_, :],
               ALU.add, [xb, pzb], [db])
    full_barrier()

    A.top = mark_x1_end
    gffn = A.f32([1024])
    ld(gffn, g_ffn.partition_broadcast(128), csem, [b_g])
    xn2T = A.bf16([8, 512]); b_xn2T = Buf()
    hdnT = A.bf16([32, 512]); b_hdn = Buf()
    wd_r = Rot([(A.bf16([4, 1024]), Buf(), P.new_dma_sem()) for _ in range(2)])
    for stg in range(5):
        if stg < 4:
            ntok = 512
            tiles = [(x1[:, 4 * stg + k, :], b_x1[4 * stg + k], 128, y_p[(4 * stg + k) * 128:(4 * stg + k + 1) * 128, :]) for k in range(4)]
        else:
            ntok = NS
            tiles = [(x1s, b_x1s, NS, y_s)]
        for k, (xa, xab, nt_, _) in enumerate(tiles):
            norm_T((xa, xab), nt_, gffn, xn2T[:, :, k * 128:k * 128 + nt_], b_xn2T)
        for ffb in range(8):
            wv, wb = wload(w_up[:, ffb * 512:(ffb + 1) * 512], 8, 512)
            for fc in range(4):
                pz, pzb = pz_r.next()
                mm(pz[:, 0:ntok], [(wv[:, c, fc * 128:(fc + 1) * 128], xn2T[:, c, 0:ntok]) for c in range(8)],
                   [wb, b_xn2T], [pzb])
                rl, rlb = st_r.next()
                act(rl[:, 0:ntok], pz[:, 0:ntok], AF.Relu, [pzb], [rlb])
                tt(hdnT[:, ffb * 4 + fc, 0:ntok], rl[:, 0:ntok], rl[:, 0:ntok], ALU.mult, [rlb], [b_hdn])
        for blk in range(8):
            wd, wdb, wdsem = wd_r.next()
            ld(wd, w_down[blk * 512:(blk + 1) * 512, :].rearrange("(c p) n -> p c n", p=128), wdsem, [wdb], eng="pool")
            for k, (xa, xab, nt_, _) in enumerate(tiles):
                for half in range(2):
                    pz, pzb = pz_r.next()
                    mm(pz[0:nt_, :], [(hdnT[:, blk * 4 + c, k * 128:k * 128 + nt_], wd[:, c, half * 512:(half + 1) * 512])
                                      for c in range(4)], [b_hdn, wdb], [pzb])
                    tt(xa[0:nt_, half * 512:(half + 1) * 512], xa[0:nt_, half * 512:(half + 1) * 512], pz[0:nt_, :],
                       ALU.add, [xab, pzb], [xab])
        for k, (xa, xab, nt_, yd) in enumerate(tiles):
            store(yd, xa[0:nt_], [xab])

    P.emit(final_waits=STORE_SEMS)
    st.close()
    return nc


_NC = None


def _consts():
    tri = np.triu(np.ones((128, 128), np.float32))
    ones = np.ones((128, 128), np.float32)
    selmid = np.zeros((128, 128), np.float32); selmid[64, :] = 1.0
    sel0 = np.zeros((128, 128), np.float32); sel0[0, :] = 1.0
    cm = np.stack([tri, ones, selmid, sel0], axis=1).reshape(128, 512)
    return np.ascontiguousarray(cm), np.eye(128, dtype=np.float32)


def kernel(**inp):
    global _NC
    if _NC is None:
        _NC = build_nc()
    nc = _NC
    f = lambda a: np.ascontiguousarray(np.asarray(a, dtype=np.float32))
    cm, ident = _consts()
    xp = f(inp["x_prompt"]); xsm = f(inp["x_sample"]); memp = f(inp["mem_prompt"])
    shared = {
        "g_mix": f(inp["g_mix"]), "w_in": f(inp["w_in"][0]), "b_f": f(inp["b_f"]),
        "g_qa": f(inp["g_qa"]), "g_ka": f(inp["g_ka"]), "g_qb": f(inp["g_qb"]), "g_kb": f(inp["g_kb"]),
        "g_qm": f(inp["g_qm"]), "g_km": f(inp["g_km"]), "rel_bias": f(inp["rel_bias"][0]),
        "g_mem": f(inp["g_mem"]), "w_mkv": f(inp["w_mkv"][0]),
        "w_pa": f(inp["w_pa"][0]), "w_pb": f(inp["w_pb"][0]), "w_pm": f(inp["w_pm"][0]),
        "w_o": f(inp["w_o"][0]), "g_ffn": f(inp["g_ffn"]), "w_up": f(inp["w_up"][0]), "w_down": f(inp["w_down"][0]),
        "cmat": cm, "ident": ident,
    }
    in_maps = []
    for c in range(8):
        b, hf = c // 2, c % 2
        sb = slice(4 * c, 4 * c + 4)
        m = dict(shared)
        m["xpre"] = f(xp[b, 0:2048])
        m["xown"] = f(xp[b, hf * 2048:(hf + 1) * 2048])
        m["xs"] = f(xsm[sb].reshape(64, 1024))
        m["mem"] = f(memp[b])
        m["cak"] = f(inp["cache_a_k"][0, sb].reshape(4, 512, 512))
        m["cav"] = f(inp["cache_a_v"][0, sb].reshape(4, 512, 512))
        m["cbk"] = f(inp["cache_b_k"][0, sb].reshape(4, 2048, 512))
        m["cbv"] = f(inp["cache_b_v"][0, sb].reshape(4, 2048, 512))
        m["cblf"] = f(inp["cache_b_logf"][0, sb])
        m["cmk"] = f(inp["cache_mem_k"][0, sb].reshape(4, 256, 512))
        m["cmv"] = f(inp["cache_mem_v"][0, sb].reshape(4, 256, 512))
        m["flag"] = np.full((1, 1), float(hf), np.float32)
        in_maps.append(m)
    res = run_bass_kernel_spmd(nc, in_maps, core_ids=list(range(8)))
    R = res.results
    y_p = np.zeros((4, 4096, 1024), np.float32)
    y_s = np.zeros((32, 16, 1024), np.float32)
    akp = np.zeros((1, 4, 512, 8, 64), np.float32); avp = np.zeros_like(akp)
    bkp = np.zeros((1, 4, 4096, 8, 64), np.float32); bvp = np.zeros_like(bkp)
    blfp = np.zeros((1, 4, 4096, 8), np.float32)
    mkp = np.zeros((1, 4, 256, 4, 128), np.float32); mvp = np.zeros_like(mkp)
    aks = np.zeros((1, 32, 16, 8, 64), np.float32); avs = np.zeros_like(aks)
    bks = np.zeros_like(aks); bvs = np.zeros_like(aks)
    blfs = np.zeros((1, 32, 16, 8), np.float32)
    for c in range(8):
        b, hf = c // 2, c % 2
        r = R[c]
        s = slice(hf * 2048, (hf + 1) * 2048)
        y_p[b, s] = r["y_p"]
        y_s[4 * c:4 * c + 4] = r["y_s"].reshape(4, 16, 1024)
        bkp[0, b, s] = r["bkp"].reshape(2048, 8, 64)
        bvp[0, b, s] = r["bvp"].reshape(2048, 8, 64)
        blfp[0, b, s] = r["blfp"]
        if hf == 1:
            akp[0, b] = r["akp"].reshape(512, 8, 64)
            avp[0, b] = r["avp"].reshape(512, 8, 64)
        else:
            mkp[0, b] = r["mkp"].reshape(256, 4, 128)
            mvp[0, b] = r["mvp"].reshape(256, 4, 128)
        aks[0, 4 * c:4 * c + 4] = r["aks"].reshape(4, 16, 8, 64)
        avs[0, 4 * c:4 * c + 4] = r["avs"].reshape(4, 16, 8, 64)
        bks[0, 4 * c:4 * c + 4] = r["bks"].reshape(4, 16, 8, 64)
        bvs[0, 4 * c:4 * c + 4] = r["bvs"].reshape(4, 16, 8, 64)
        blfs[0, 4 * c:4 * c + 4] = r["blfs"].reshape(4, 16, 8)
    return (y_p, y_s, akp, avp, bkp, bvp, blfp, mkp, mvp, aks, avs, bks, bvs, blfs)
```

```python
import contextlib
import numpy as np
import concourse.bass as bass
import concourse.mybir as mybir
from concourse.bass_utils import run_bass_kernel_spmd

F32 = mybir.dt.float32
BF16 = mybir.dt.bfloat16
ALU = mybir.AluOpType
AF = mybir.ActivationFunctionType
AX = mybir.AxisListType

ENGS = ("pe", "act", "dve", "pool", "sp")
NEG = -30000.0
EPS = 1e-6
LREL = 768


ALLBUFS = []


class Buf:
    __slots__ = ("name", "w", "r", "excl")

    def __init__(self, name="", excl=False):
        self.name = name
        self.w = None
        self.r = []
        self.excl = excl
        ALLBUFS.append(self)


class Prog:
    def __init__(self, nc):
        self.nc = nc
        self.ops = {e: [] for e in ENGS}
        self.n_dma_sem = 0

    def new_dma_sem(self):
        self.n_dma_sem += 1
        return {"id": self.n_dma_sem - 1, "count": 0}

    def _deps_for(self, reads, writes):
        deps = []
        for b in reads:
            if b.w is not None:
                deps.append(b.w)
        for b in writes:
            if b.w is not None:
                deps.append(b.w)
            deps.extend(b.r)
        return deps

    def _post(self, tok, reads, writes):
        for b in writes:
            b.w = tok
            b.r = []
        for b in reads:
            if b not in writes:
                if len(b.r) > 64:
                    last = {}
                    for t in b.r:
                        last[(t[0], t[1])] = t
                    b.r = list(last.values())
                b.r.append(tok)

    def op(self, eng, fn, reads=(), writes=()):
        writes = list(writes) + [b for b in reads if b.excl and b not in writes]
        deps = self._deps_for(reads, writes)
        o = {"eng": eng, "fn": fn, "deps": deps, "kind": "c", "signal": False,
             "idx": len(self.ops[eng])}
        self.ops[eng].append(o)
        self._post(("e", eng, o["idx"]), reads, writes)
        return o

    def dma(self, eng, fn, sem, reads=(), writes=()):
        deps = self._deps_for(reads, writes)
        sem["count"] += 16
        o = {"eng": eng, "fn": fn, "deps": deps, "kind": "d", "sem": sem,
             "idx": len(self.ops[eng])}
        self.ops[eng].append(o)
        self._post(("d", sem["id"], sem["count"]), reads, writes)
        return o

    def barrier(self, bufs):
        deps = []
        for b in bufs:
            if b.w is not None:
                deps.append(b.w)
            deps.extend(b.r)
        for e in ENGS:
            o = {"eng": e, "fn": None, "deps": list(deps), "kind": "n",
                 "idx": len(self.ops[e])}
            self.ops[e].append(o)

    def emit(self, final_waits=()):
        nc = self.nc
        for e in ENGS:
            for o in self.ops[e]:
                for d in o["deps"]:
                    if d[0] == "e":
                        self.ops[d[1]][d[2]]["signal"] = True
        for e in ENGS:
            c = 0
            for o in self.ops[e]:
                if o["kind"] == "c" and o["signal"]:
                    c += 1
                    o["sigval"] = c
        with contextlib.ExitStack() as st:
            esem = {e: st.enter_context(nc.semaphore("s_" + e)) for e in ENGS}
            dsem = [st.enter_context(nc.semaphore("d%d" % i)) for i in range(self.n_dma_sem)]
            block = st.enter_context(nc.Block())
            prog = self

            def run(ename, eng):
                seen_e = {e: 0 for e in ENGS}
                seen_d = {}
                for o in prog.ops[ename]:
                    need_e = {}
                    need_d = {}
                    for d in o["deps"]:
                        if d[0] == "e":
                            v = prog.ops[d[1]][d[2]]["sigval"]
                            if v > need_e.get(d[1], 0):
                                need_e[d[1]] = v
                        else:
                            if d[2] > need_d.get(d[1], 0):
                                need_d[d[1]] = d[2]
                    for e2, v in need_e.items():
                        if v > seen_e[e2]:
                            eng.wait_ge(esem[e2], v)
                            seen_e[e2] = v
                    for s, v in need_d.items():
                        if v > seen_d.get(s, 0):
                            eng.wait_ge(dsem[s], v)
                            seen_d[s] = v
                    if o["kind"] == "c":
                        ins = o["fn"](eng)
                        if o["signal"]:
                            ins.then_inc(esem[ename], 1)
                    elif o["kind"] == "d":
                        ins = o["fn"](eng)
                        ins.then_inc(dsem[o["sem"]["id"]], 16)
                if ename == "sp":
                    for s in final_waits:
                        if s["count"] > 0:
                            eng.wait_ge(dsem[s["id"]], s["count"])

            @block.tensor
            def _(eng):
                run("pe", eng)

            @block.scalar
            def _(eng):
                run("act", eng)

            @block.vector
            def _(eng):
                run("dve", eng)

            @block.gpsimd
            def _(eng):
                run("pool", eng)

            @block.sync
            def _(eng):
                run("sp", eng)


def bcl(ap, m):
    return bass.AP(tensor=ap.tensor, offset=ap.offset,
                   ap=[list(x) for x in ap.ap] + [[0, m]])


def bcm(ap, k):
    a = [list(x) for x in ap.ap]
    return bass.AP(tensor=ap.tensor, offset=ap.offset, ap=[a[0], [0, k]] + a[1:])


class Rot:
    def __init__(self, items):
        self.items = items
        self.i = 0

    def next(self):
        it = self.items[self.i % len(self.items)]
        self.i += 1
        return it


C_QA, C_KA, C_VA, C_QB, C_KB, C_VB, C_FL, C_QM, C_G = 0, 512, 1024, 1536, 2048, 2560, 3072, 3080, 3592

NPRE = 16
NOWN = 16
NT = NPRE + NOWN
NS = 64
ARENA_W = 53100


def build_nc():
    nc = bass.Bass("TRN2", target_bir_lowering=False)
    P = Prog(nc)

    def din(name, shape):
        return nc.dram_tensor(name, list(shape), F32, kind="ExternalInput").ap()

    def dout(name, shape):
        return nc.dram_tensor(name, list(shape), F32, kind="ExternalOutput").ap()

    xpre = din("xpre", [2048, 1024]); xown = din("xown", [2048, 1024]); xs = din("xs", [NS, 1024])
    mem = din("mem", [256, 1024])
    cak = din("cak", [4, 512, 512]); cav = din("cav", [4, 512, 512])
    cbk = din("cbk", [4, 2048, 512]); cbv = din("cbv", [4, 2048, 512]); cblf = din("cblf", [4, 2048, 8])
    cmk = din("cmk", [4, 256, 512]); cmv = din("cmv", [4, 256, 512])
    g_mix = din("g_mix", [1, 1024]); w_in = din("w_in", [1024, 6664]); b_f = din("b_f", [1, 8])
    g_qa = din("g_qa", [1, 64]); g_ka = din("g_ka", [1, 64]); g_qb = din("g_qb", [1, 64]); g_kb = din("g_kb", [1, 64])
    g_qm = din("g_qm", [1, 128]); g_km = din("g_km", [1, 128]); rel_bias = din("rel_bias", [257, 8])
    g_mem = din("g_mem", [1, 1024]); w_mkv = din("w_mkv", [1024, 1024])
    w_pa = din("w_pa", [512, 1024]); w_pb = din("w_pb", [512, 1024]); w_pm = din("w_pm", [512, 1024])
    w_o = din("w_o", [1024, 1024]); g_ffn = din("g_ffn", [1, 1024])
    w_up = din("w_up", [1024, 4096]); w_down = din("w_down", [4096, 1024])
    cmat = din("cmat", [128, 4 * 128]); ident = din("ident", [128, 128]); flag = din("flag", [1, 1])

    y_p = dout("y_p", [2048, 1024]); y_s = dout("y_s", [NS, 1024])
    akp = dout("akp", [512, 512]); avp = dout("avp", [512, 512])
    bkp = dout("bkp", [2048, 512]); bvp = dout("bvp", [2048, 512]); blfp = dout("blfp", [2048, 8])
    mkp = dout("mkp", [256, 512]); mvp = dout("mvp", [256, 512])
    aks = dout("aks", [NS, 512]); avs = dout("avs", [NS, 512])
    bks = dout("bks", [NS, 512]); bvs = dout("bvs", [NS, 512]); blfs = dout("blfs", [NS, 8])
    relE = nc.dram_tensor("relE", [8, LREL], F32, kind="Internal").ap()
    relD = nc.dram_tensor("relD", [128, 8 * LREL], F32, kind="Internal").ap()

    st = contextlib.ExitStack()
    arena = st.enter_context(nc.sbuf_tensor("arena", [128, ARENA_W], F32))
    psum = [st.enter_context(nc.psum_tensor("ps%d" % i, [128, 512], F32)) for i in range(8)]

    class Arena:
        def __init__(self):
            self.top = 0

        def f32(self, dims, parts=128):
            n = int(np.prod(dims))
            o = self.top
            self.top += n
            assert self.top <= ARENA_W, ("arena overflow", self.top)
            ap = arena[0:parts, o:o + n]
            return self._view(ap, dims)

        def bf16(self, dims, parts=128):
            n = int(np.prod(dims))
            w = (n + 1) // 2
            o = self.top
            self.top += w
            assert self.top <= ARENA_W, ("arena overflow", self.top)
            ap = arena[0:parts, o:o + w].bitcast(BF16)[:, 0:n]
            return self._view(ap, dims)

        @staticmethod
        def _view(ap, dims):
            if len(dims) == 1:
                return ap
            if len(dims) == 2:
                return ap.rearrange("p (a b) -> p a b", a=dims[0])
            if len(dims) == 3:
                return ap.rearrange("p (a b c) -> p a b c", a=dims[0], b=dims[1])
            if len(dims) == 4:
                return ap.rearrange("p (a b c d) -> p a b c d", a=dims[0], b=dims[1], c=dims[2])
            raise ValueError

    A = Arena()
    out_sems = [P.new_dma_sem() for _ in range(4)]
    osr = Rot(out_sems)

    def mm(out, pairs, reads, writes):
        pairs = list(pairs)

        def f(e):
            n = len(pairs)
            for k, (l, r) in enumerate(pairs):
                i = e.matmul(out, lhsT=l, rhs=r, start=(k == 0), stop=(k == n - 1))
            return i
        P.op("pe", f, reads, writes)

    def transposes(items, reads, writes):
        items = list(items)

        def f(e):
            for (o, i_, idn) in items:
                ins = e.transpose(out=o, in_=i_, identity=idn)
            return ins
        P.op("pe", f, reads, writes)

    def act(out, in_, func, reads, writes, bias=None, scale=None, accum=None, eng="act"):
        kw = {}
        if bias is not None:
            kw["bias"] = bias
        if scale is not None:
            kw["scale"] = scale
        if accum is not None:
            kw["accum_out"] = accum
        P.op(eng, lambda e: e.activation(out=out, in_=in_, func=func, **kw), reads, writes)

    def tt(out, in0, in1, op, reads, writes, eng="dve"):
        P.op(eng, lambda e: e.tensor_tensor(out=out, in0=in0, in1=in1, op=op), reads, writes)

    def ts(out, in0, s1, s2, op0, op1, reads, writes, eng="dve"):
        if s2 is None:
            P.op(eng, lambda e: e.tensor_scalar(out=out, in0=in0, scalar1=s1, scalar2=None, op0=op0), reads, writes)
        else:
            P.op(eng, lambda e: e.tensor_scalar(out=out, in0=in0, scalar1=s1, scalar2=s2, op0=op0, op1=op1), reads, writes)

    def stt(out, in0, scalar, in1, op0, op1, reads, writes, eng="dve"):
        P.op(eng, lambda e: e.scalar_tensor_tensor(out=out, in0=in0, scalar=scalar, in1=in1, op0=op0, op1=op1), reads, writes)

    def cp(out, in_, reads, writes, eng="dve"):
        P.op(eng, lambda e: e.tensor_copy(out=out, in_=in_), reads, writes)

    def recip(out, in_, reads, writes):
        P.op("dve", lambda e: e.reciprocal(out=out, in_=in_), reads, writes)

    def red(out, in_, reads, writes):
        P.op("dve", lambda e: e.tensor_reduce(out=out, in_=in_, axis=AX.X, op=ALU.add), reads, writes)

    def memset(ap, val, writes, eng="dve"):
        P.op(eng, lambda e: e.memset(ap, val), (), writes)

    BUFSEM = {}

    def sem_of(b):
        if id(b) not in BUFSEM:
            BUFSEM[id(b)] = P.new_dma_sem()
        return BUFSEM[id(b)]

    STORE_SEMS = []

    def ld(out, in_, sem, writes, reads=(), eng="sp", slow=False):
        sem = sem_of(writes[0])
        if slow:
            P.dma(eng, lambda e: e.dma_start(out=out, in_=in_, allow_slow_non_contiguous=True), sem, reads, writes)
        else:
            P.dma(eng, lambda e: e.dma_start(out=out, in_=in_), sem, reads, writes)

    def store(out, in_, reads):
        sem = sem_of(reads[0])
        if sem not in STORE_SEMS:
            STORE_SEMS.append(sem)
        P.dma("sp", lambda e: e.dma_start(out=out, in_=in_), sem, reads, ())

    def rstd_from_ss(ap, n, bufs):
        act(ap, ap, AF.Sqrt, list(bufs) + [b_eps], bufs, bias=EPSB[0:ap.shape[0], 0:1], scale=1.0 / n)
        recip(ap, ap, bufs, bufs)

    csem = P.new_dma_sem()
    identb = A.bf16([128]); b_id = Buf()
    ld(identb, ident, csem, [b_id], eng="pool")
    trif = A.f32([4, 128]); b_cm = Buf()
    ld(trif, cmat.rearrange("p (a b) -> p a b", a=4), csem, [b_cm])
    TRI, ONES, SELMID, SEL0 = trif[:, 0, :], trif[:, 1, :], trif[:, 2, :], trif[:, 3, :]
    trib = A.bf16([128]); b_trib = Buf()
    cp(trib, TRI, [b_cm], [b_trib])
    gmix = A.f32([1024]); b_g = Buf()
    ld(gmix, g_mix.partition_broadcast(128), csem, [b_g])
    gv = {}
    for nm, src, n in (("qa", g_qa, 64), ("ka", g_ka, 64), ("qb", g_qb, 64), ("kb", g_kb, 64),
                       ("qm", g_qm, 128), ("km", g_km, 128)):
        gv[nm] = A.f32([n])
        ld(gv[nm], src.partition_broadcast(128), csem, [b_g])
    bfb = A.f32([8])
    ld(bfb, b_f.partition_broadcast(128), csem, [b_g])
    EPSB = A.f32([1]); b_eps = Buf()
    memset(EPSB, EPS, [b_eps])
    negflag = A.f32([1]); b_nf = Buf()
    ld(negflag, flag.partition_broadcast(128), csem, [b_nf])
    ts(negflag, negflag, -1.0, -NEG, ALU.add, ALU.mult, [b_nf], [b_nf])

    xt_r = Rot([(A.f32([1024]), Buf(), P.new_dma_sem()) for _ in range(2)])
    sq_r = Rot([(A.bf16([1024]), Buf()) for _ in range(1)])
    xn_r = Rot([(A.bf16([1024]), Buf()) for _ in range(2)])
    st_r = Rot([(A.f32([512]), Buf()) for _ in range(3)])
    sb_r = Rot([(A.bf16([512]), Buf()) for _ in range(2)])
    sm_r = Rot([(A.f32([16]), Buf()) for _ in range(4)])
    pt_r = Rot([(A.bf16([128]), Buf()) for _ in range(12)])
    sa_r = Rot([(A.f32([128]), Buf()) for _ in range(4)])
    fin_r = Rot([(A.bf16([512]), Buf()) for _ in range(2)])
    bias_r = Rot([(A.f32([8]), Buf()) for _ in range(4)])
    w_r = Rot([(A.bf16([4224]), Buf(), P.new_dma_sem()) for _ in range(2)])
    PB = [Buf(excl=True) for _ in range(8)]
    pz_r = Rot([(psum[0], PB[0]), (psum[1], PB[1])])
    pT_r = Rot([(psum[2].bitcast(BF16), PB[2]), (psum[3].bitcast(BF16), PB[3])])
    pS_r = Rot([(psum[4], PB[4]), (psum[5], PB[5])])
    pO = [(psum[6], PB[6]), (psum[7], PB[7])]

    PIPEQ = []
    PST = {"s": 0, "r": 0}
    PFN = {}
    PDEF = []

    def defer(fn, delay):
        PDEF.append([PST["r"] + delay, fn])

    def run_deferred(all_=False):
        while True:
            todo = [d for d in PDEF if all_ or d[0] <= PST["r"]]
            if not todo:
                break
            for d in todo:
                PDEF.remove(d)
            for d in todo:
                d[1]()

    def pipe_pump(drain):
        nb = len(pS_r.items)
        while True:
            if PST["s"] < len(PIPEQ) and PST["s"] - PST["r"] < nb:
                ent = PIPEQ[PST["s"]]
                ent["bank"] = PFN["S"](ent["q"])
                PST["s"] += 1
                continue
            if PST["r"] < PST["s"] and (drain or PST["s"] - PST["r"] >= nb):
                ent = PIPEQ[PST["r"]]
                PFN["R"](ent["q"], *ent["bank"])
                for cb in ent["after"]:
                    cb()
                ent["after"] = []
                PST["r"] += 1
                run_deferred()
                continue
            break
        if drain:
            run_deferred(True)

    def pipe_drain():
        pipe_pump(True)

    def psum_mode(attn):
        pipe_drain()
        if attn:
            pz_r.items = [(psum[0], PB[0])]
            pT_r.items = [(psum[2].bitcast(BF16), PB[2])]
            pS_r.items = [(psum[1], PB[1]), (psum[3], PB[3]), (psum[4], PB[4]), (psum[5], PB[5])]
        else:
            pz_r.items = [(psum[0], PB[0]), (psum[1], PB[1]), (psum[4], PB[4]), (psum[5], PB[5])]
            pT_r.items = [(psum[2].bitcast(BF16), PB[2]), (psum[3].bitcast(BF16), PB[3])]
            pS_r.items = [(psum[4], PB[4]), (psum[5], PB[5])]

    psum_mode(False)

    base_top = A.top

    def wload(src, kc, n):
        wt, wb, wsem = w_r.next()
        v = wt[:, 0:kc * n].rearrange("p (c n) -> p c n", c=kc)
        ld(v, src.rearrange("(c p) n -> p c n", p=128), wsem, [wb], eng="pool")
        return v, wb

    def norm_T(xrows, nt, gbc, dstT, dstbuf, x_keep=None):
        if isinstance(xrows, tuple):
            xt, xb = xrows
        else:
            xt, xb, xsem = xt_r.next()
            ld(xt[0:nt], xrows, xsem, [xb])
        sq, sqb = sq_r.next()
        sm, smb = sm_r.next()
        act(sq[0:nt], xt[0:nt], AF.Square, [xb], [sqb, smb], accum=sm[0:nt, 0:1])
        rstd_from_ss(sm[0:nt, 0:1], 1024.0, [smb])
        xn, xnb = xn_r.next()
        stt(xn[0:nt], xt[0:nt], sm[0:nt, 0:1], gbc[0:nt], ALU.mult, ALU.mult, [xb, smb, b_g], [xnb])
        pT, pTb = pT_r.next()
        transposes([(pT[:, c * 128:c * 128 + nt], xn[0:nt, c * 128:(c + 1) * 128], identb[0:nt, 0:nt])
                    for c in range(8)], [xnb, b_id], [pTb])
        cp(dstT, pT.rearrange("p (c t) -> p c t", c=8)[:, :, 0:nt], [pTb], [dstbuf])

    def norm_T_staged(xrows, nt, gbc, dstT, dstbuf):
        xt, xb, xsem = xt_r.next()
        ld(xt[0:nt], xrows, xsem, [xb])
        sq, sqb = sq_r.next()
        sm, smb = sm_r.next()
        hold = {}

        def sA():
            act(sq[0:nt], xt[0:nt], AF.Square, [xb], [sqb, smb], accum=sm[0:nt, 0:1])
            act(sm[0:nt, 0:1], sm[0:nt, 0:1], AF.Sqrt, [smb, b_eps], [smb], bias=EPSB[0:nt, 0:1], scale=1.0 / 1024.0)

        def sB():
            recip(sm[0:nt, 0:1], sm[0:nt, 0:1], [smb], [smb])
            xn, xnb = xn_r.next()
            stt(xn[0:nt], xt[0:nt], sm[0:nt, 0:1], gbc[0:nt], ALU.mult, ALU.mult, [xb, smb, b_g], [xnb])
            hold["pT"] = pT_r.next()
            pT, pTb = hold["pT"]
            transposes([(pT[:, c * 128:c * 128 + nt], xn[0:nt, c * 128:(c + 1) * 128], identb[0:nt, 0:nt])
                        for c in range(8)], [xnb, b_id], [pTb])

        def sC():
            pT, pTb = hold["pT"]
            cp(dstT, pT.rearrange("p (c t) -> p c t", c=8)[:, :, 0:nt], [pTb], [dstbuf])
        return sA, sB, sC

    def proj_tok(xT, xTb, nt, wv, wb, ncols, c0=0):
        pz, pzb = pz_r.next()
        mm(pz[0:nt, 0:ncols], [(xT[:, c, :], wv[:, c, c0:c0 + ncols]) for c in range(8)], [xTb, wb], [pzb])
        return pz, pzb

    hn_sm_r = Rot([(A.f32([16]), Buf()) for _ in range(3)])

    def head_norm(pz, pzb, nt, nh, dh, g, out_dram, dstT=None, dstTb=None, pair=True, defer_T=0, staged=False):
        n = nh * dh
        sq, sqb = sq_r.next()
        sm, smb = hn_sm_r.next() if (defer_T or staged) else sm_r.next()
        stg, stb = st_r.next()
        sbf, sbb = (None, None)
        if dstT is not None:
            sbf, sbb = sb_r.next()
        v3 = lambda a: a[0:nt, 0:n].rearrange("p (h d) -> p h d", h=nh)

        def stage_act():
            def fsq(e):
                for h in range(nh):
                    ins = e.activation(out=sq[0:nt, h * dh:(h + 1) * dh], in_=pz[0:nt, h * dh:(h + 1) * dh],
                                       func=AF.Square, accum_out=sm[0:nt, h:h + 1])
                return ins
            P.op("act", fsq, [pzb], [sqb, smb])
            act(sm[0:nt, 0:nh], sm[0:nt, 0:nh], AF.Sqrt, [smb, b_eps], [smb], bias=EPSB[0:nt, 0:1], scale=1.0 / dh)

        def stage_dve():
            recip(sm[0:nt, 0:nh], sm[0:nt, 0:nh], [smb], [smb])
            tt(v3(stg), v3(pz), bcl(sm[0:nt, 0:nh], dh), ALU.mult, [pzb, smb], [stb])
            if out_dram is not None:
                tt(v3(stg), v3(stg), bcm(g[0:nt], nh), ALU.mult, [stb, b_g], [stb])
                store(out_dram, stg[0:nt, 0:n], [stb])
                if dstT is not None:
                    cp(sbf[0:nt, 0:n], stg[0:nt, 0:n], [stb], [sbb])
            else:
                tt(v3(sbf), v3(stg), bcm(g[0:nt], nh), ALU.mult, [stb, b_g], [sbb])

        def stage_pe():
            if dstT is None:
                return
            pT, pTb = pT_r.next()
            nch = n // 128
            transposes([(pT[:, c * 128:c * 128 + nt], sbf[0:nt, c * 128:(c + 1) * 128], identb[0:nt, 0:nt])
                        for c in range(nch)], [sbb, b_id], [pTb])
            cp(dstT, pT[:, 0:nch * 128].rearrange("p (c t) -> p c t", c=nch)[:, :, 0:nt], [pTb], [dstTb])

        if staged:
            return stage_act, stage_dve, stage_pe
        if defer_T:
            defer(stage_act, defer_T)
            defer(stage_dve, 2 * defer_T)
            defer(stage_pe, 3 * defer_T)
        else:
            stage_act()
            stage_dve()
            stage_pe()

    def pipelined(n_tiles, make):
        st_ = {}
        for it in range(n_tiles + 2):
            if it < n_tiles:
                st_[it] = make(it)
                st_[it][0]()
            if 0 <= it - 1 < n_tiles:
                st_[it - 1][1]()
            if 0 <= it - 2 < n_tiles:
                st_.pop(it - 2)[2]()

    def v_store(pz, pzb, nt, nh, dh, out_dram, vaug, vb):
        n = nh * dh
        stg, stb = st_r.next()
        cp(stg[0:nt, 0:n], pz[0:nt, 0:n], [pzb], [stb])
        if out_dram is not None:
            store(out_dram, stg[0:nt, 0:n], [stb])
        act(vaug[0:nt, :, 0:dh], stg[0:nt, 0:n].rearrange("p (h d) -> p h d", h=nh), AF.Copy, [stb], [vb])

    def logf_a(xT, xTb, nt, wv, wb, c0, LFdst, lfb):
        pz, pzb = pz_r.next()
        mm(pz[0:nt, 0:8], [(xT[:, c, :], wv[:, c, c0:c0 + 8]) for c in range(8)], [xTb, wb], [pzb])
        tt(LFdst, pz[0:nt, 0:8], bfb[0:nt], ALU.add, [pzb, b_g], [lfb])

    def logf_b(LFdst, lfb):
        act(LFdst, LFdst, AF.Exp, [lfb], [lfb], scale=-1.0)
        act(LFdst, LFdst, AF.Ln, [lfb], [lfb], bias=1.0)

    def logf_c(LFdst, lfb, out_dram):
        ts(LFdst, LFdst, -1.0, None, ALU.mult, None, [lfb], [lfb])
        if out_dram is not None:
            store(out_dram, LFdst, [lfb])

    def logf_tile(xT, xTb, nt, wv, wb, c0, LFdst, lfb, out_dram):
        logf_a(xT, xTb, nt, wv, wb, c0, LFdst, lfb)
        logf_b(LFdst, lfb)
        logf_c(LFdst, lfb, out_dram)

    def cumsum_tile(LF, lfb, nt, R, rb, Cdst, cb, first):
        pz, pzb = pz_r.next()
        if first:
            mm(pz[0:nt, 0:8], [(TRI[0:nt, 0:nt], LF)], [lfb, b_cm], [pzb])
        else:
            mm(pz[0:nt, 0:8], [(TRI[0:nt, 0:nt], LF), (ONES[:, 0:nt], R)], [lfb, b_cm, rb], [pzb])
        cp(Cdst, pz[0:nt, 0:8], [pzb], [cb])
        if first:
            memset(R, 0.0, [rb])
        tt(R[0:nt], R[0:nt], LF, ALU.add, [rb, lfb], [rb])

    def run_sums(LFs_list, Rall, rb, first_is_zero):
        if first_is_zero:
            memset(Rall[:, 0, :], 0.0, [rb])
        for t, (lf, lfb, nt) in enumerate(LFs_list):
            if nt < 128:
                continue
            tt(Rall[:, t + 1, :], Rall[:, t, :], lf, ALU.add, [rb, lfb], [rb])

    def cumsum_par(LF, lfb, nt, Rt, rb, Cdst, cb):
        pz, pzb = pz_r.next()
        mm(pz[0:nt, 0:8], [(TRI[0:nt, 0:nt], LF), (ONES[:, 0:nt], Rt)], [lfb, b_cm, rb], [pzb])
        cp(Cdst, pz[0:nt, 0:8], [pzb], [cb])

    def bcast_row(src, sb_, nrows, SEL, dst, db):
        pz, pzb = pz_r.next()
        mm(pz[:, 0:8], [(SEL[0:nrows, :], src)], [sb_, b_cm], [pzb])
        cp(dst, pz[:, 0:8], [pzb], [db])

    def attn_head(qT, nq, ktiles, po_ap, pob, scale, first_reads):
        n = len(ktiles)
        PENDING.append([dict(kt=kt, qT=qT, nq=nq, po_ap=po_ap, pob=pob, scale=scale,
                             first_reads=list(first_reads), start=(idx == 0), stop=(idx == n - 1))
                        for idx, kt in enumerate(ktiles)])

    PENDING = []
    LA = 4

    def attn_flush():
        calls = list(PENDING)
        del PENDING[:]
        steps = [sp_ for c_ in calls for sp_ in c_]
        quads = [steps[i:i + 4] for i in range(0, len(steps), 4)]

        def uniq(bl):
            out = []
            for b in bl:
                if b not in out:
                    out.append(b)
            return out

        def emit_S(q):
            bank, bb = pS_r.next()
            q = list(q)

            def f(e):
                for k, sp_ in enumerate(q):
                    kt = sp_["kt"]
                    ins = e.matmul(bank[0:kt["nk"], k * 128:k * 128 + sp_["nq"]], lhsT=kt["kT"], rhs=sp_["qT"],
                                   start=True, stop=True)
                return ins
            rds = []
            for sp_ in q:
                rds += list(sp_["kt"]["rd"]) + sp_["first_reads"]
            P.op("pe", f, uniq(rds), [bb])
            return bank, bb

        def emit_rest(q, bank, bb):
            q = list(q)
            pts = [pt_r.next() for _ in q]
            sas = {}
            pre_items = []
            for k, sp_ in enumerate(q):
                if sp_["kt"].get("pre") is not None:
                    sas[k] = sa_r.next()
                    pre_items.append(k)
            if pre_items:
                def fd(e):
                    for k in pre_items:
                        sp_ = q[k]
                        kt = sp_["kt"]
                        nk, nq = kt["nk"], sp_["nq"]
                        ins = e.scalar_tensor_tensor(out=sas[k][0][0:nk, 0:nq], in0=bank[0:nk, k * 128:k * 128 + nq],
                                                     scalar=sp_["scale"], in1=kt["pre"][0], op0=ALU.mult, op1=ALU.add)
                    return ins
                rds = [bb]
                for k in pre_items:
                    rds += list(q[k]["kt"]["pre"][1])
                P.op("dve", fd, uniq(rds), [sas[k][1] for k in pre_items])

            def fa(e):
                for k, sp_ in enumerate(q):
                    kt = sp_["kt"]
                    nk, nq = kt["nk"], sp_["nq"]
                    if k in sas:
                        ins = e.activation(out=pts[k][0][0:nk, 0:nq], in_=sas[k][0][0:nk, 0:nq], func=AF.Exp,
                                           bias=kt.get("bias", 0.0))
                    else:
                        ins = e.activation(out=pts[k][0][0:nk, 0:nq], in_=bank[0:nk, k * 128:k * 128 + nq], func=AF.Exp,
                                           bias=kt.get("bias", 0.0), scale=sp_["scale"])
                return ins
            rds = [bb] + [sas[k][1] for k in pre_items]
            for sp_ in q:
                rds += list(sp_["kt"].get("brd", []))
            P.op("act", fa, uniq(rds), [p_[1] for p_ in pts])
            for k, sp_ in enumerate(q):
                kt = sp_["kt"]
                if kt.get("mask") is not None:
                    nk, nq = kt["nk"], sp_["nq"]
                    tt(pts[k][0][0:nk, 0:nq], pts[k][0][0:nk, 0:nq], kt["mask"][0:nk, 0:nq], ALU.mult,
                       [pts[k][1], b_trib, b_mask0], [pts[k][1]], eng="pool")
            def fp(e):
                for k, sp_ in enumerate(q):
                    kt = sp_["kt"]
                    nk, nq = kt["nk"], sp_["nq"]
                    ins = e.matmul(sp_["po_ap"], lhsT=pts[k][0][0:nk, 0:nq], rhs=kt["v"],
                                   start=sp_["start"], stop=sp_["stop"])
                return ins
            rds = [p_[1] for p_ in pts]
            for sp_ in q:
                rds += list(sp_["kt"]["rd"])
            P.op("pe", fp, uniq(rds), uniq([sp_["pob"] for sp_ in q]))

        for q in quads:
            PIPEQ.append({"q": q, "bank": None, "after": []})
        PFN["S"], PFN["R"] = emit_S, emit_rest
        pipe_pump(False)


    def mm_acc(out, l, r, start, stop, reads, writes):
        P.op("pe", lambda e: e.matmul(out, lhsT=l, rhs=r, start=start, stop=stop), reads, writes)

    FIN = {}

    def attn_finish_group(nq, nh, dh, half):
        attn_flush()
        PIPEQ[-1]["after"].append(lambda: fin_group(nq, nh, dh, half))
        if PST["r"] >= len(PIPEQ):
            raise RuntimeError("pipeline ran ahead of finish")

    def fin_group(nq, nh, dh, half):
        hpb = nh // 2
        if half == 0:
            FIN["sbf"] = fin_r.next()
        sbf, sbb = FIN["sbf"]
        po, pob = pO[half]
        pv = po[0:nq, 0:hpb * (dh + 1)].rearrange("p (h d) -> p h d", h=hpb)
        sm, smb = sm_r.next()
        recip(sm[0:nq, 0:hpb], pv[:, :, dh], [pob], [smb])
        if dh == 64:
            ov_ = sbf[0:nq, 0:512].rearrange("p (hp two d) -> p hp two d", two=2, d=64)[:, :, half, :]
        else:
            ov_ = sbf[0:nq, half * hpb * dh:(half + 1) * hpb * dh].rearrange("p (h d) -> p h d", h=hpb)
        tt(ov_, pv[:, :, 0:dh], bcl(sm[0:nq, 0:hpb], dh), ALU.mult, [pob, smb], [sbb])

    def attn_finish(nq, nh, dh, dstT, dstTb):
        PIPEQ[-1]["after"].append(lambda: fin_tile(nq, dstT, dstTb))

    def fin_tile(nq, dstT, dstTb):
        sbf, sbb = FIN["sbf"]

        def part2():
            pT, pTb = pT_r.next()
            transposes([(pT[:, c * 128:c * 128 + nq], sbf[0:nq, c * 128:(c + 1) * 128], identb[0:nq, 0:nq])
                        for c in range(4)], [sbb, b_id], [pTb])
            cp(dstT, pT[:, 0:512].rearrange("p (c t) -> p c t", c=4)[:, :, 0:nq], [pTb], [dstTb])
        defer(part2, 2)

    def po_slot(h, nh, dh):
        hpb = nh // 2
        if dh == 64:
            po, pob = pO[h % 2]
            hh = h // 2
        else:
            po, pob = pO[h // hpb]
            hh = h % hpb
        return po, pob, hh * (dh + 1)

    BiasA = A.f32([8, 2, 128]); b_bias = Buf()
    relc = A.f32([8]); relcnf = A.f32([8])
    ld(relc, rel_bias[256:257, :].partition_broadcast(128), csem, [b_bias])
    ts(relcnf, relc, negflag[:, 0:1], None, ALU.add, None, [b_bias, b_nf], [b_bias])
    mask0 = A.bf16([128]); b_mask0 = Buf()
    memset(mask0, 1.0, [b_mask0])
    memset(mask0[0:64, 64:128], 0.0, [b_mask0])
    mark0 = A.top
    relx = A.f32([LREL], parts=8); b_relx = Buf()
    rsem = P.new_dma_sem()
    memset(relx, 0.0, [b_relx])
    sm8 = A.f32([1], parts=8)
    ld(sm8, rel_bias[256:257, :].rearrange("i h -> h i"), rsem, [b_relx], slow=True)
    ts(relx, relx, sm8[:, 0:1], None, ALU.add, None, [b_relx], [b_relx])
    ld(relx[:, 0:257], rel_bias.rearrange("i h -> h i"), rsem, [b_relx], slow=True)
    b_relE = Buf()
    ld(relE, relx, rsem, [b_relE], reads=[b_relx])
    relrep = A.f32([8, LREL]); b_rr = Buf()
    ld(relrep, relE.partition_broadcast(128), rsem, [b_rr], reads=[b_relE])
    b_relD = Buf()
    ld(relD, relrep.rearrange("p a b -> p (a b)"), rsem, [b_relD], reads=[b_rr])
    for t in (3, 4):
        src = bass.AP(tensor=relD.tensor, offset=640 - 128 * t,
                      ap=[[8 * LREL - 1, 128], [LREL, 8], [1, 128]])
        ld(BiasA[:, :, t - 3, :], src, rsem, [b_bias], reads=[b_relD])
    memset(BiasA[64:128, :, 1, 0:64], NEG, [b_bias])
    P.barrier(list(ALLBUFS))
    A.top = mark0

    mkT = A.bf16([4, 256]); b_mkT = Buf()
    mvA = A.bf16([2, 4, 129]); b_mv = Buf()
    memset(mvA[:, :, :, 128:129], 1.0, [b_mv])
    mark1 = A.top
    gmem = A.f32([1024]); b_gm = Buf()
    ld(gmem, g_mem.partition_broadcast(128), csem, [b_g])
    xmT = A.bf16([8, 256]); b_xmT = Buf()
    for t in range(2):
        norm_T(mem[t * 128:(t + 1) * 128, :], 128, gmem, xmT[:, :, t * 128:(t + 1) * 128], b_xmT)
    wv, wb = wload(w_mkv[:, 0:512], 8, 512)
    for t in range(2):
        pz, pzb = proj_tok(xmT[:, :, t * 128:(t + 1) * 128], b_xmT, 128, wv, wb, 512)
        head_norm(pz, pzb, 128, 4, 128, gv["km"], mkp[t * 128:(t + 1) * 128, :],
                  mkT[:, :, t * 128:(t + 1) * 128], b_mkT)
    wv, wb = wload(w_mkv[:, 512:1024], 8, 512)
    for t in range(2):
        pz, pzb = proj_tok(xmT[:, :, t * 128:(t + 1) * 128], b_xmT, 128, wv, wb, 512)
        v_store(pz, pzb, 128, 4, 128, mvp[t * 128:(t + 1) * 128, :], mvA[:, t], b_mv)
    P.barrier(list(ALLBUFS))
    A.top = mark1

    def full_barrier():
        pipe_drain()
        P.barrier(list(ALLBUFS))

    mark_x = A.top
    xnT = A.bf16([8, 2048]); b_xnT = [Buf() for _ in range(NOWN)]
    xpT4 = A.bf16([8, 512]); b_xp4 = Buf()
    xsT = A.bf16([8, NS]); b_xsT = Buf()
    osT = {k: A.bf16([4, NS]) for k in "bam"}
    b_osT = {k: Buf() for k in "bam"}
    oT = {}
    b_oT = {k: Buf() for k in "bam"}
    KsT_new = A.bf16([4, NS]); b_KsN = Buf()
    QsT = A.bf16([4, NS]); b_QsT = Buf()
    VsN = A.bf16([4, 8, 65], parts=16); b_VsN = Buf()
    LFs = A.f32([4, 8], parts=16); b_LFs = Buf()
    memset(VsN[:, :, :, 64:65], 1.0, [b_VsN])
    oT["b"] = A.bf16([4, 2048])

    def xT_of(i):
        return xnT[:, :, i * 128:(i + 1) * 128], b_xnT[i]

    markB = A.top
    KbT = A.bf16([4, NT * 128]); b_KbT = [Buf() for _ in range(NT)]
    Vb = A.bf16([NT, 8, 65]); b_Vb = [Buf() for _ in range(NT)]
    Qb_r = Rot([(A.bf16([4, 128]), Buf()) for _ in range(2)])
    LF = A.f32([NT, 8]); b_LF = [Buf() for _ in range(NT)]
    Cc = A.f32([NT, 8]); b_C = [Buf() for _ in range(NT)]
    crefs = A.f32([NOWN, 8]); b_cref = [Buf() for _ in range(NOWN)]
    Rall = A.f32([NT + 1, 8]); b_R = Buf()
    biasT_r = Rot([(A.f32([NT + 1, 8]), Buf()) for _ in range(3)])
    memset(Vb[:, :, :, 64:65], 1.0, b_Vb)
    def mk_nt_pre(i):
        xv, xb_ = xT_of(i)
        return norm_T_staged(xpre[i * 128:(i + 1) * 128, :], 128, gmix, xv, xb_)
    pipelined(NPRE, mk_nt_pre)
    for i in range(4):
        xv, xb_ = xT_of(12 + i)
        cp(xpT4[:, :, i * 128:(i + 1) * 128], xv, [xb_], [b_xp4])
    wv, wb = wload(w_in[:, C_KB:C_KB + 512], 8, 512)
    def mk_kb_pre(i):
        xv, xb_ = xT_of(i)
        pz, pzb = proj_tok(xv, xb_, 128, wv, wb, 512)
        return head_norm(pz, pzb, 128, 8, 64, gv["kb"], None, KbT[:, :, i * 128:(i + 1) * 128], b_KbT[i], staged=True)
    pipelined(NPRE, mk_kb_pre)
    wv, wb = wload(w_in[:, C_VB:C_VB + 520], 8, 520)
    for i in range(NPRE):
        xv, xb_ = xT_of(i)
        pz, pzb = proj_tok(xv, xb_, 128, wv, wb, 512)
        v_store(pz, pzb, 128, 8, 64, None, Vb[:, i], b_Vb[i])
        logf_a(xv, xb_, 128, wv, wb, 512, LF[:, i, :], b_LF[i])
    for i in range(NPRE):
        logf_b(LF[:, i, :], b_LF[i])
    for i in range(NPRE):
        logf_c(LF[:, i, :], b_LF[i], None)
    run_sums([(LF[:, i, :], b_LF[i], 128) for i in range(NPRE)], Rall, b_R, True)
    for i in range(NPRE):
        cumsum_par(LF[:, i, :], b_LF[i], 128, Rall[:, i, :], b_R, Cc[:, i, :], b_C[i])
    def mk_nt_own(i):
        xv, xb_ = xT_of(i)
        return norm_T_staged(xown[i * 128:(i + 1) * 128, :], 128, gmix, xv, xb_)
    pipelined(NOWN, mk_nt_own)
    norm_T(xs, NS, gmix, xsT, b_xsT)
    wv, wb = wload(w_in[:, C_KB:C_KB + 512], 8, 512)
    def mk_kb_own(i):
        xv, xb_ = xT_of(i)
        pz, pzb = proj_tok(xv, xb_, 128, wv, wb, 512)
        return head_norm(pz, pzb, 128, 8, 64, gv["kb"], bkp[i * 128:(i + 1) * 128, :],
                         KbT[:, :, (NPRE + i) * 128:(NPRE + i + 1) * 128], b_KbT[NPRE + i], staged=True)
    pipelined(NOWN, mk_kb_own)
    pz, pzb = proj_tok(xsT, b_xsT, NS, wv, wb, 512)
    head_norm(pz, pzb, NS, 8, 64, gv["kb"], bks, KsT_new, b_KsN)
    wv, wb = wload(w_in[:, C_VB:C_VB + 520], 8, 520)
    for i in range(NOWN):
        xv, xb_ = xT_of(i)
        j = NPRE + i
        pz, pzb = proj_tok(xv, xb_, 128, wv, wb, 512)
        v_store(pz, pzb, 128, 8, 64, bvp[i * 128:(i + 1) * 128, :], Vb[:, j], b_Vb[j])
        logf_a(xv, xb_, 128, wv, wb, 512, LF[:, j, :], b_LF[j])
    for i in range(NOWN):
        logf_b(LF[:, NPRE + i, :], b_LF[NPRE + i])
    for i in range(NOWN):
        logf_c(LF[:, NPRE + i, :], b_LF[NPRE + i], blfp[i * 128:(i + 1) * 128, :])
    for i in range(NOWN):
        j = NPRE + i
        tt(Rall[:, j + 1, :], Rall[:, j, :], LF[:, j, :], ALU.add, [b_R, b_LF[j]], [b_R])
    for i in range(NOWN):
        j = NPRE + i
        cumsum_par(LF[:, j, :], b_LF[j], 128, Rall[:, j, :], b_R, Cc[:, j, :], b_C[j])
    for i in range(NOWN):
        j = NPRE + i
        bcast_row(Cc[:, j, :], b_C[j], 128, SELMID, crefs[:, i, :], b_cref[i])
    for bb in range(4):
        xv = xsT[:, :, bb * 16:(bb + 1) * 16]
        pz, pzb = proj_tok(xv, b_xsT, 16, wv, wb, 512)
        v_store(pz, pzb, 16, 8, 64, bvs[bb * 16:(bb + 1) * 16, :], VsN[:, bb], b_VsN)
        logf_tile(xv, b_xsT, 16, wv, wb, 512, LFs[:, bb, :], b_LFs, blfs[bb * 16:(bb + 1) * 16, :])
    wv, wb = wload(w_in[:, C_QB:C_QB + 512], 8, 512)
    pz, pzb = proj_tok(xsT, b_xsT, NS, wv, wb, 512)
    head_norm(pz, pzb, NS, 8, 64, gv["qb"], None, QsT, b_QsT)

    psum_mode(True)
    def prep_b(i):
        gi = NPRE + i
        xv, xb_ = xT_of(i)
        pz, pzb = proj_tok(xv, xb_, 128, wv, wb, 512)
        qv, qvb = Qb_r.next()
        head_norm(pz, pzb, 128, 8, 64, gv["qb"], None, qv, qvb, defer_T=2)
        bt_all, b_bt = biasT_r.next()
        for j in range(gi + 1):
            tt(bt_all[:, j, :], crefs[:, i, :], Cc[:, j, :], ALU.subtract, [b_cref[i], b_C[j]], [b_bt])
        ts(bt_all[:, 0:NPRE, :], bt_all[:, 0:NPRE, :], negflag[:, 0:1], None, ALU.add, None, [b_bt, b_nf], [b_bt])
        return qv, qvb, bt_all, b_bt

    nxt = prep_b(0)
    for i in range(NOWN):
        gi = NPRE + i
        run_deferred(True)
        qv, qvb, bt_all, b_bt = nxt
        if i + 1 < NOWN:
            nxt = prep_b(i + 1)
        for h in (0, 2, 4, 6, 1, 3, 5, 7):
            hp, hr = h // 2, (h % 2) * 64
            po, pob, off = po_slot(h, 8, 64)
            kts = []
            for j in range(gi + 1):
                kts.append(dict(kT=KbT[hr:hr + 64, hp, j * 128:(j + 1) * 128], nk=128,
                                v=Vb[:, j, h, :], rd=[b_KbT[j], b_Vb[j]],
                                bias=bt_all[:, j, h:h + 1], brd=[b_bt],
                                mask=(trib if j == gi else None)))
            attn_head(qv[hr:hr + 64, hp, :], 128, kts, po[:, off:off + 65], pob,
                      0.125, [qvb])
            if h >= 6:
                attn_finish_group(128, 8, 64, h % 2)
        attn_finish(128, 8, 64, oT["b"][:, :, i * 128:(i + 1) * 128], b_oT["b"])
    full_barrier()
    A.top = markB

    Kc = A.bf16([16, 512]); b_Kc = [Buf() for _ in range(4)]; kcsem = None
    KsT = A.bf16([4, 2048]); b_KsT = Buf()
    Vs = A.bf16([16, 8, 65]); b_Vs = [Buf() for _ in range(16)]
    LFc = A.f32([16, 8]); b_LFc = Buf()
    Cs = A.f32([16, 8]); b_Cs = Buf()
    CsN = A.f32([8], parts=16); b_CsN = Buf()
    crefS = A.f32([8]); b_crefS = Buf()
    memset(Vs[:, :, :, 64:65], 1.0, b_Vs)
    for bb in range(4):
        for q4 in range(4):
            ld(Kc[:, 4 * q4:4 * q4 + 4, :], cbk[bb, 512 * q4:512 * (q4 + 1), :].rearrange("(t p) n -> p t n", p=128),
               kcsem, [b_Kc[q4]], eng="pool")
        for t in range(16):
            ld(Vs[:, t, :, 0:64], cbv[bb, 128 * t:128 * (t + 1), :].rearrange("p (h d) -> p h d", h=8),
               kcsem, [b_Vs[t]], eng="pool")
        ld(LFc, cblf[bb].rearrange("(t p) h -> p t h", p=128), kcsem, [b_LFc])
        for t in range(16):
            pT, pTb = pT_r.next()
            transposes([(pT[:, c * 128:(c + 1) * 128], Kc[:, t, c * 128:(c + 1) * 128], identb) for c in range(4)],
                       [b_Kc[t // 4], b_id], [pTb])
            cp(KsT[:, :, t * 128:(t + 1) * 128], pT[:, 0:512].rearrange("p (c t) -> p c t", c=4), [pTb], [b_KsT])
        run_sums([(LFc[:, t, :], b_LFc, 128) for t in range(16)], Rall, b_R, True)
        for t in range(16):
            cumsum_par(LFc[:, t, :], b_LFc, 128, Rall[:, t, :], b_R, Cs[:, t, :], b_Cs)
        cumsum_par(LFs[:, bb, :], b_LFs, 16, Rall[:, 16, :], b_R, CsN, b_CsN)
        bcast_row(CsN, b_CsN, 16, SEL0, crefS, b_crefS)
        bt_all, b_bt = biasT_r.next()
        for t in range(16):
            tt(bt_all[:, t, :], crefS, Cs[:, t, :], ALU.subtract, [b_crefS, b_Cs], [b_bt])
        tt(bt_all[0:16, 16, :], crefS[0:16], CsN, ALU.subtract, [b_crefS, b_CsN], [b_bt])
        for h in (0, 2, 4, 6, 1, 3, 5, 7):
            hp, hr = h // 2, (h % 2) * 64
            po, pob, off = po_slot(h, 8, 64)
            kts = []
            for t in range(16):
                kts.append(dict(kT=KsT[hr:hr + 64, hp, t * 128:(t + 1) * 128], nk=128, v=Vs[:, t, h, :],
                                rd=[b_KsT, b_Vs[t]], bias=bt_all[:, t, h:h + 1], brd=[b_bt], mask=None))
            kts.append(dict(kT=KsT_new[hr:hr + 64, hp, bb * 16:(bb + 1) * 16], nk=16, v=VsN[:, bb, h, :],
                            rd=[b_KsN, b_VsN], bias=bt_all[0:16, 16, h:h + 1], brd=[b_bt], mask=trib))
            attn_head(QsT[hr:hr + 64, hp, bb * 16:(bb + 1) * 16], 16, kts, po[0:16, off:off + 65], pob,
                      0.125, [b_QsT])
            if h >= 6:
                attn_finish_group(16, 8, 64, h % 2)
        attn_finish(16, 8, 64, osT["b"][:, :, bb * 16:(bb + 1) * 16], b_osT["b"])
        pipe_drain()
    full_barrier()
    A.top = markB

    psum_mode(False)
    oT["a"] = A.bf16([4, 2048])
    markAA = A.top
    NA = 4 + NOWN
    KaT = A.bf16([4, NA * 128]); b_KaT = [Buf() for _ in range(NA)]
    Va = A.bf16([NA, 8, 65]); b_Va = [Buf() for _ in range(NA)]
    Qa_r = Rot([(A.bf16([4, 128]), Buf()) for _ in range(2)])
    KsaN = A.bf16([4, NS]); b_KsaN = Buf()
    QsaT = A.bf16([4, NS]); b_QsaT = Buf()
    VsaN = A.bf16([4, 8, 65], parts=16); b_VsaN = Buf()
    memset(Va[:, :, :, 64:65], 1.0, b_Va)
    memset(VsaN[:, :, :, 64:65], 1.0, [b_VsaN])

    def xa_of(idx):
        if idx < 4:
            return xpT4[:, :, idx * 128:(idx + 1) * 128], b_xp4
        return xT_of(idx - 4)

    wv, wb = wload(w_in[:, C_KA:C_KA + 512], 8, 512)
    def mk_ka(idx):
        xv, xb_ = xa_of(idx)
        pz, pzb = proj_tok(xv, xb_, 128, wv, wb, 512)
        od = akp[(idx - 16) * 128:(idx - 15) * 128, :] if idx >= 16 else None
        return head_norm(pz, pzb, 128, 8, 64, gv["ka"], od, KaT[:, :, idx * 128:(idx + 1) * 128], b_KaT[idx], staged=True)
    pipelined(NA, mk_ka)
    pz, pzb = proj_tok(xsT, b_xsT, NS, wv, wb, 512)
    head_norm(pz, pzb, NS, 8, 64, gv["ka"], aks, KsaN, b_KsaN)
    wv, wb = wload(w_in[:, C_VA:C_VA + 512], 8, 512)
    for idx in range(NA):
        xv, xb_ = xa_of(idx)
        pz, pzb = proj_tok(xv, xb_, 128, wv, wb, 512)
        od = avp[(idx - 16) * 128:(idx - 15) * 128, :] if idx >= 16 else None
        v_store(pz, pzb, 128, 8, 64, od, Va[:, idx], b_Va[idx])
    for bb in range(4):
        pz, pzb = proj_tok(xsT[:, :, bb * 16:(bb + 1) * 16], b_xsT, 16, wv, wb, 512)
        v_store(pz, pzb, 16, 8, 64, avs[bb * 16:(bb + 1) * 16, :], VsaN[:, bb], b_VsaN)
    wv, wb = wload(w_in[:, C_QA:C_QA + 512], 8, 512)
    pz, pzb = proj_tok(xsT, b_xsT, NS, wv, wb, 512)
    head_norm(pz, pzb, NS, 8, 64, gv["qa"], None, QsaT, b_QsaT)
    def prep_a(i):
        xv, xb_ = xT_of(i)
        pz, pzb = proj_tok(xv, xb_, 128, wv, wb, 512)
        qv, qvb = Qa_r.next()
        head_norm(pz, pzb, 128, 8, 64, gv["qa"], None, qv, qvb, defer_T=2)
        return qv, qvb

    psum_mode(True)
    nxt = prep_a(0)
    for i in range(NOWN):
        run_deferred(True)
        qv, qvb = nxt
        if i + 1 < NOWN:
            nxt = prep_a(i + 1)
        for h in (0, 2, 4, 6, 1, 3, 5, 7):
            hp, hr = h // 2, (h % 2) * 64
            po, pob, off = po_slot(h, 8, 64)
            kts = []
            for t in range(5):
                idx = i + t
                pre_ = idx < 4
                d = dict(kT=KaT[hr:hr + 64, hp, idx * 128:(idx + 1) * 128], nk=128, v=Va[:, idx, h, :],
                         rd=[b_KaT[idx], b_Va[idx]], mask=None, brd=[b_bias, b_nf])
                if t < 3:
                    d["bias"] = (relcnf if pre_ else relc)[:, h:h + 1]
                    if t == 0:
                        d["mask"] = mask0
                else:
                    d["pre"] = (BiasA[:, h, t - 3, :], [b_bias])
                    d["bias"] = negflag[:, 0:1] if pre_ else 0.0
                kts.append(d)
            attn_head(qv[hr:hr + 64, hp, :], 128, kts, po[:, off:off + 65], pob,
                      0.125, [qvb])
            if h >= 6:
                attn_finish_group(128, 8, 64, h % 2)
        attn_finish(128, 8, 64, oT["a"][:, :, i * 128:(i + 1) * 128], b_oT["a"])
    Kca = A.bf16([4, 512]); b_Kca = Buf()
    KsaT = A.bf16([4, 512]); b_KsaT = Buf()
    Vsa = A.bf16([4, 8, 65]); b_Vsa = Buf()
    memset(Vsa[:, :, :, 64:65], 1.0, [b_Vsa])
    for bb in range(4):
        ld(Kca, cak[bb].rearrange("(t p) n -> p t n", p=128), kcsem, [b_Kca], eng="pool")
        for t in range(4):
            ld(Vsa[:, t, :, 0:64], cav[bb, 128 * t:128 * (t + 1), :].rearrange("p (h d) -> p h d", h=8),
               kcsem, [b_Vsa], eng="pool")
        for t in range(4):
            pT, pTb = pT_r.next()
            transposes([(pT[:, c * 128:(c + 1) * 128], Kca[:, t, c * 128:(c + 1) * 128], identb) for c in range(4)],
                       [b_Kca, b_id], [pTb])
            cp(KsaT[:, :, t * 128:(t + 1) * 128], pT[:, 0:512].rearrange("p (c t) -> p c t", c=4), [pTb], [b_KsaT])
        for h in (0, 2, 4, 6, 1, 3, 5, 7):
            hp, hr = h // 2, (h % 2) * 64
            po, pob, off = po_slot(h, 8, 64)
            kts = []
            for t in range(4):
                d = dict(kT=KsaT[hr:hr + 64, hp, t * 128:(t + 1) * 128], nk=128, v=Vsa[:, t, h, :],
                         rd=[b_KsaT, b_Vsa], mask=None, brd=[b_bias])
                if t < 3:
                    d["bias"] = relc[:, h:h + 1]
                else:
                    d["pre"] = (BiasA[:, h, 0, 0:16], [b_bias])
                    d["bias"] = 0.0
                kts.append(d)
            kts.append(dict(kT=KsaN[hr:hr + 64, hp, bb * 16:(bb + 1) * 16], nk=16, v=VsaN[:, bb, h, :],
                            rd=[b_KsaN, b_VsaN], mask=None, brd=[b_bias],
                            pre=(BiasA[0:16, h, 1, 0:16], [b_bias]), bias=0.0))
            attn_head(QsaT[hr:hr + 64, hp, bb * 16:(bb + 1) * 16], 16, kts, po[0:16, off:off + 65], pob,
                      0.125, [b_QsaT])
            if h >= 6:
                attn_finish_group(16, 8, 64, h % 2)
        attn_finish(16, 8, 64, osT["a"][:, :, bb * 16:(bb + 1) * 16], b_osT["a"])
        pipe_drain()
    full_barrier()
    A.top = markAA

    psum_mode(False)
    oT["m"] = A.bf16([4, 2048])
    markM = A.top
    Qm_r = Rot([(A.bf16([4, 128]), Buf()) for _ in range(2)])
    QsmT = A.bf16([4, NS]); b_QsmT = Buf()
    wv, wb = wload(w_in[:, C_QM:C_QM + 512], 8, 512)
    pz, pzb = proj_tok(xsT, b_xsT, NS, wv, wb, 512)
    head_norm(pz, pzb, NS, 4, 128, gv["qm"], None, QsmT, b_QsmT)
    sc_m = float(128 ** -0.5)
    def prep_m(i):
        xv, xb_ = xT_of(i)
        pz, pzb = proj_tok(xv, xb_, 128, wv, wb, 512)
        qv, qvb = Qm_r.next()
        head_norm(pz, pzb, 128, 4, 128, gv["qm"], None, qv, qvb, defer_T=1)
        return qv, qvb

    psum_mode(True)
    nxt = prep_m(0)
    for i in range(NOWN):
        run_deferred(True)
        qv, qvb = nxt
        if i + 1 < NOWN:
            nxt = prep_m(i + 1)
        for h in range(4):
            po, pob, off = po_slot(h, 4, 128)
            kts = [dict(kT=mkT[:, h, m * 128:(m + 1) * 128], nk=128, v=mvA[:, m, h, :], rd=[b_mkT, b_mv],
                        mask=None, bias=0.0) for m in range(2)]
            attn_head(qv[:, h, :], 128, kts, po[:, off:off + 129], pob, sc_m, [qvb])
            if (h + 1) % (4 // 2) == 0:
                attn_finish_group(128, 4, 128, h // (4 // 2))
        attn_finish(128, 4, 128, oT["m"][:, :, i * 128:(i + 1) * 128], b_oT["m"])
    Kcm = A.bf16([2, 512]); b_Kcm = Buf()
    mksT = A.bf16([4, 256]); b_mksT = Buf()
    mvs = A.bf16([2, 4, 129]); b_mvs = Buf()
    memset(mvs[:, :, :, 128:129], 1.0, [b_mvs])
    for bb in range(4):
        ld(Kcm, cmk[bb].rearrange("(t p) n -> p t n", p=128), kcsem, [b_Kcm], eng="pool")
        for t in range(2):
            ld(mvs[:, t, :, 0:128], cmv[bb, 128 * t:128 * (t + 1), :].rearrange("p (h d) -> p h d", h=4),
               kcsem, [b_mvs], eng="pool")
        for t in range(2):
            pT, pTb = pT_r.next()
            transposes([(pT[:, c * 128:(c + 1) * 128], Kcm[:, t, c * 128:(c + 1) * 128], identb) for c in range(4)],
                       [b_Kcm, b_id], [pTb])
            cp(mksT[:, :, t * 128:(t + 1) * 128], pT[:, 0:512].rearrange("p (c t) -> p c t", c=4), [pTb], [b_mksT])
        for h in range(4):
            po, pob, off = po_slot(h, 4, 128)
            kts = [dict(kT=mksT[:, h, m * 128:(m + 1) * 128], nk=128, v=mvs[:, m, h, :], rd=[b_mksT, b_mvs],
                        mask=None, bias=0.0) for m in range(2)]
            attn_head(QsmT[:, h, bb * 16:(bb + 1) * 16], 16, kts, po[0:16, off:off + 129], pob, sc_m, [b_QsmT])
            if (h + 1) % (4 // 2) == 0:
                attn_finish_group(16, 4, 128, h // (4 // 2))
        attn_finish(16, 4, 128, osT["m"][:, :, bb * 16:(bb + 1) * 16], b_osT["m"])
        pipe_drain()
    full_barrier()
    A.top = markM

    psum_mode(False)
    hT_start = A.top
    hT = A.bf16([8, 2048]); b_hT = [Buf() for _ in range(4)]
    hsT = A.bf16([8, NS]); b_hsT = Buf()
    mark_h_end = A.top
    acc_r = Rot([(A.f32([512]), Buf()) for _ in range(2)])
    wp_r = Rot([(A.bf16([4 * 384]), Buf(), P.new_dma_sem()) for _ in range(2)])
    wps = {"a": w_pa, "b": w_pb, "m": w_pm}
    BR = "abm"
    groups = [(tg * 512, 512, b_xnT[4 * tg:4 * tg + 4], b_hT[tg]) for tg in range(4)]
    for nch in range(8):
        wt, wgb, wsem = w_r.next()
        wg = wt[:, 0:8 * 384].rearrange("p (c b n) -> p c b n", c=8, b=3)
        wpt, wpb, wpsem = wp_r.next()
        wp = wpt.rearrange("p (c b n) -> p c b n", c=4, b=3)
        for bi, br in enumerate(BR):
            c0 = C_G + bi * 1024 + nch * 128
            ld(wg[:, :, bi, :], w_in[:, c0:c0 + 128].rearrange("(c p) n -> p c n", p=128), wsem, [wgb], eng="pool")
            ld(wp[:, :, bi, :], wps[br][:, nch * 128:(nch + 1) * 128].rearrange("(c p) n -> p c n", p=128),
               wpsem, [wpb], eng="pool")
        for gidx in range(5):
            if gidx < 4:
                t0, ntok, xbufs, hb = groups[gidx]
                xv = xnT[:, :, t0:t0 + ntok]
                ov = {k: oT[k][:, :, t0:t0 + ntok] for k in BR}
                obufs = b_oT
                hdst = hT[:, nch, t0:t0 + ntok]
            else:
                ntok, xbufs, hb = NS, [b_xsT], b_hsT
                xv = xsT
                ov = osT
                obufs = b_osT
                hdst = hsT[:, nch, :]
            acc, accb = acc_r.next()
            for bi, br in enumerate(BR):
                pz, pzb = pz_r.next()
                mm(pz[:, 0:ntok], [(wg[:, c, bi, :], xv[:, c, :]) for c in range(8)], [wgb] + list(xbufs), [pzb])
                sg, sgb = st_r.next()
                act(sg[:, 0:ntok], pz[:, 0:ntok], AF.Sigmoid, [pzb], [sgb])
                pz2, pz2b = pz_r.next()
                mm(pz2[:, 0:ntok], [(wp[:, c, bi, :], ov[br][:, c, :]) for c in range(4)], [wpb, obufs[br]], [pz2b])
                if bi == 0:
                    tt(acc[:, 0:ntok], sg[:, 0:ntok], pz2[:, 0:ntok], ALU.mult, [sgb, pz2b], [accb])
                elif bi == 1:
                    tt(sg[:, 0:ntok], sg[:, 0:ntok], pz2[:, 0:ntok], ALU.mult, [sgb, pz2b], [sgb])
                    tt(acc[:, 0:ntok], acc[:, 0:ntok], sg[:, 0:ntok], ALU.add, [accb, sgb], [accb])
                else:
                    tt(sg[:, 0:ntok], sg[:, 0:ntok], pz2[:, 0:ntok], ALU.mult, [sgb, pz2b], [sgb])
                    tt(hdst, acc[:, 0:ntok], sg[:, 0:ntok], ALU.add, [accb, sgb], [hb])
    full_barrier()

    A.top = mark_x
    x1 = A.f32([NOWN, 1024]); b_x1 = [Buf() for _ in range(NOWN)]
    x1s = A.f32([1024]); b_x1s = Buf()
    assert A.top <= hT_start, ("x1 overlaps hT", A.top, hT_start)
    mark_x1_end = A.top
    A.top = mark_h_end
    wo0, wo0b = wload(w_o[:, 0:512], 8, 512)
    wo1, wo1b = wload(w_o[:, 512:1024], 8, 512)
    for i in range(NOWN + 1):
        if i < NOWN:
            nt_, src, hv, hb, dst, db = 128, xown[i * 128:(i + 1) * 128, :], hT[:, :, i * 128:(i + 1) * 128], b_hT[i // 4], x1[:, i, :], b_x1[i]
        else:
            nt_, src, hv, hb, dst, db = NS, xs, hsT, b_hsT, x1s, b_x1s
        xt, xb, xsem = xt_r.next()
        ld(xt[0:nt_], src, xsem, [xb])
        for half, (wo, wob) in enumerate(((wo0, wo0b), (wo1, wo1b))):
            pz, pzb = pz_r.next()
            mm(pz[0:nt_, :], [(hv[:, c, :], wo[:, c, :]) for c in range(8)], [hb, wob], [pzb])
            tt(dst[0:nt_, half * 512:(half + 1) * 512], xt[0:nt_, half * 512:(half + 1) * 512], pz[0:nt_, :],
               ALU.add, [xb, pzb], [db])
    full_barrier()

    A.top = mark_x1_end
    gffn = A.f32([1024])
    ld(gffn, g_ffn.partition_broadcast(128), csem, [b_g])
    xn2T = A.bf16([8, 512]); b_xn2T = Buf()
    hdnT = A.bf16([32, 512]); b_hdn = Buf()
    wd_r = Rot([(A.bf16([4, 1024]), Buf(), P.new_dma_sem()) for _ in range(2)])
    for stg in range(5):
        if stg < 4:
            ntok = 512
            tiles = [(x1[:, 4 * stg + k, :], b_x1[4 * stg + k], 128, y_p[(4 * stg + k) * 128:(4 * stg + k + 1) * 128, :]) for k in range(4)]
        else:
            ntok = NS
            tiles = [(x1s, b_x1s, NS, y_s)]
        for k, (xa, xab, nt_, _) in enumerate(tiles):
            norm_T((xa, xab), nt_, gffn, xn2T[:, :, k * 128:k * 128 + nt_], b_xn2T)
        for ffb in range(8):
            wv, wb = wload(w_up[:, ffb * 512:(ffb + 1) * 512], 8, 512)
            for fc in range(4):
                pz, pzb = pz_r.next()
                mm(pz[:, 0:ntok], [(wv[:, c, fc * 128:(fc + 1) * 128], xn2T[:, c, 0:ntok]) for c in range(8)],
                   [wb, b_xn2T], [pzb])
                rl, rlb = st_r.next()
                act(rl[:, 0:ntok], pz[:, 0:ntok], AF.Relu, [pzb], [rlb])
                tt(hdnT[:, ffb * 4 + fc, 0:ntok], rl[:, 0:ntok], rl[:, 0:ntok], ALU.mult, [rlb], [b_hdn])
        for blk in range(8):
            wd, wdb, wdsem = wd_r.next()
            ld(wd, w_down[blk * 512:(blk + 1) * 512, :].rearrange("(c p) n -> p c n", p=128), wdsem, [wdb], eng="pool")
            for k, (xa, xab, nt_, _) in enumerate(tiles):
                for half in range(2):
                    pz, pzb = pz_r.next()
                    mm(pz[0:nt_, :], [(hdnT[:, blk * 4 + c, k * 128:k * 128 + nt_], wd[:, c, half * 512:(half + 1) * 512])
                                      for c in range(4)], [b_hdn, wdb], [pzb])
                    tt(xa[0:nt_, half * 512:(half + 1) * 512], xa[0:nt_, half * 512:(half + 1) * 512], pz[0:nt_, :],
                       ALU.add, [xab, pzb], [xab])
        for k, (xa, xab, nt_, yd) in enumerate(tiles):
            store(yd, xa[0:nt_], [xab])

    P.emit(final_waits=STORE_SEMS)
    st.close()
    return nc


_NC = None


def _consts():
    tri = np.triu(np.ones((128, 128), np.float32))
    ones = np.ones((128, 128), np.float32)
    selmid = np.zeros((128, 128), np.float32); selmid[64, :] = 1.0
    sel0 = np.zeros((128, 128), np.float32); sel0[0, :] = 1.0
    cm = np.stack([tri, ones, selmid, sel0], axis=1).reshape(128, 512)
    return np.ascontiguousarray(cm), np.eye(128, dtype=np.float32)


def kernel(**inp):
    global _NC
    if _NC is None:
        _NC = build_nc()
    nc = _NC
    f = lambda a: np.ascontiguousarray(np.asarray(a, dtype=np.float32))
    cm, ident = _consts()
    xp = f(inp["x_prompt"]); xsm = f(inp["x_sample"]); memp = f(inp["mem_prompt"])
    shared = {
        "g_mix": f(inp["g_mix"]), "w_in": f(inp["w_in"][0]), "b_f": f(inp["b_f"]),
        "g_qa": f(inp["g_qa"]), "g_ka": f(inp["g_ka"]), "g_qb": f(inp["g_qb"]), "g_kb": f(inp["g_kb"]),
        "g_qm": f(inp["g_qm"]), "g_km": f(inp["g_km"]), "rel_bias": f(inp["rel_bias"][0]),
        "g_mem": f(inp["g_mem"]), "w_mkv": f(inp["w_mkv"][0]),
        "w_pa": f(inp["w_pa"][0]), "w_pb": f(inp["w_pb"][0]), "w_pm": f(inp["w_pm"][0]),
        "w_o": f(inp["w_o"][0]), "g_ffn": f(inp["g_ffn"]), "w_up": f(inp["w_up"][0]), "w_down": f(inp["w_down"][0]),
        "cmat": cm, "ident": ident,
    }
    in_maps = []
    for c in range(8):
        b, hf = c // 2, c % 2
        sb = slice(4 * c, 4 * c + 4)
        m = dict(shared)
        m["xpre"] = f(xp[b, 0:2048])
        m["xown"] = f(xp[b, hf * 2048:(hf + 1) * 2048])
        m["xs"] = f(xsm[sb].reshape(64, 1024))
        m["mem"] = f(memp[b])
        m["cak"] = f(inp["cache_a_k"][0, sb].reshape(4, 512, 512))
        m["cav"] = f(inp["cache_a_v"][0, sb].reshape(4, 512, 512))
        m["cbk"] = f(inp["cache_b_k"][0, sb].reshape(4, 2048, 512))
        m["cbv"] = f(inp["cache_b_v"][0, sb].reshape(4, 2048, 512))
        m["cblf"] = f(inp["cache_b_logf"][0, sb])
        m["cmk"] = f(inp["cache_mem_k"][0, sb].reshape(4, 256, 512))
        m["cmv"] = f(inp["cache_mem_v"][0, sb].reshape(4, 256, 512))
        m["flag"] = np.full((1, 1), float(hf), np.float32)
        in_maps.append(m)
    res = run_bass_kernel_spmd(nc, in_maps, core_ids=list(range(8)))
    R = res.results
    y_p = np.zeros((4, 4096, 1024), np.float32)
    y_s = np.zeros((32, 16, 1024), np.float32)
    akp = np.zeros((1, 4, 512, 8, 64), np.float32); avp = np.zeros_like(akp)
    bkp = np.zeros((1, 4, 4096, 8, 64), np.float32); bvp = np.zeros_like(bkp)
    blfp = np.zeros((1, 4, 4096, 8), np.float32)
    mkp = np.zeros((1, 4, 256, 4, 128), np.float32); mvp = np.zeros_like(mkp)
    aks = np.zeros((1, 32, 16, 8, 64), np.float32); avs = np.zeros_like(aks)
    bks = np.zeros_like(aks); bvs = np.zeros_like(aks)
    blfs = np.zeros((1, 32, 16, 8), np.float32)
    for c in range(8):
        b, hf = c // 2, c % 2
        r = R[c]
        s = slice(hf * 2048, (hf + 1) * 2048)
        y_p[b, s] = r["y_p"]
        y_s[4 * c:4 * c + 4] = r["y_s"].reshape(4, 16, 1024)
        bkp[0, b, s] = r["bkp"].reshape(2048, 8, 64)
        bvp[0, b, s] = r["bvp"].reshape(2048, 8, 64)
        blfp[0, b, s] = r["blfp"]
        if hf == 1:
            akp[0, b] = r["akp"].reshape(512, 8, 64)
            avp[0, b] = r["avp"].reshape(512, 8, 64)
        else:
            mkp[0, b] = r["mkp"].reshape(256, 4, 128)
            mvp[0, b] = r["mvp"].reshape(256, 4, 128)
        aks[0, 4 * c:4 * c + 4] = r["aks"].reshape(4, 16, 8, 64)
        avs[0, 4 * c:4 * c + 4] = r["avs"].reshape(4, 16, 8, 64)
        bks[0, 4 * c:4 * c + 4] = r["bks"].reshape(4, 16, 8, 64)
        bvs[0, 4 * c:4 * c + 4] = r["bvs"].reshape(4, 16, 8, 64)
        blfs[0, 4 * c:4 * c + 4] = r["blfs"].reshape(4, 16, 8)
    return (y_p, y_s, akp, avp, bkp, bvp, blfp, mkp, mvp, aks, avs, bks, bvs, blfs)
```

```python
import contextlib
import numpy as np
import concourse.bass as bass
import concourse.mybir as mybir
from concourse.bass_utils import run_bass_kernel_spmd

F32 = mybir.dt.float32
BF16 = mybir.dt.bfloat16
ALU = mybir.AluOpType
AF = mybir.ActivationFunctionType
AX = mybir.AxisListType

ENGS = ("pe", "act", "dve", "pool", "sp")
NEG = -30000.0
EPS = 1e-6
LREL = 768


ALLBUFS = []


class Buf:
    __slots__ = ("name", "w", "r", "excl")

    def __init__(self, name="", excl=False):
        self.name = name
        self.w = None
        self.r = []
        self.excl = excl
        ALLBUFS.append(self)


class Prog:
    def __init__(self, nc):
        self.nc = nc
        self.ops = {e: [] for e in ENGS}
        self.n_dma_sem = 0

    def new_dma_sem(self):
        self.n_dma_sem += 1
        return {"id": self.n_dma_sem - 1, "count": 0}

    def _deps_for(self, reads, writes):
        deps = []
        for b in reads:
            if b.w is not None:
                deps.append(b.w)
        for b in writes:
            if b.w is not None:
                deps.append(b.w)
            deps.extend(b.r)
        return deps

    def _post(self, tok, reads, writes):
        for b in writes:
            b.w = tok
            b.r = []
        for b in reads:
            if b not in writes:
                if len(b.r) > 64:
                    last = {}
                    for t in b.r:
                        last[(t[0], t[1])] = t
                    b.r = list(last.values())
                b.r.append(tok)

    def op(self, eng, fn, reads=(), writes=()):
        writes = list(writes) + [b for b in reads if b.excl and b not in writes]
        deps = self._deps_for(reads, writes)
        o = {"eng": eng, "fn": fn, "deps": deps, "kind": "c", "signal": False,
             "idx": len(self.ops[eng])}
        self.ops[eng].append(o)
        self._post(("e", eng, o["idx"]), reads, writes)
        return o

    def dma(self, eng, fn, sem, reads=(), writes=()):
        deps = self._deps_for(reads, writes)
        sem["count"] += 16
        o = {"eng": eng, "fn": fn, "deps": deps, "kind": "d", "sem": sem,
             "idx": len(self.ops[eng])}
        self.ops[eng].append(o)
        self._post(("d", sem["id"], sem["count"]), reads, writes)
        return o

    def barrier(self, bufs):
        deps = []
        for b in bufs:
            if b.w is not None:
                deps.append(b.w)
            deps.extend(b.r)
        for e in ENGS:
            o = {"eng": e, "fn": None, "deps": list(deps), "kind": "n",
                 "idx": len(self.ops[e])}
            self.ops[e].append(o)

    def emit(self, final_waits=()):
        nc = self.nc
        for e in ENGS:
            for o in self.ops[e]:
                for d in o["deps"]:
                    if d[0] == "e":
                        self.ops[d[1]][d[2]]["signal"] = True
        for e in ENGS:
            c = 0
            for o in self.ops[e]:
                if o["kind"] == "c" and o["signal"]:
                    c += 1
                    o["sigval"] = c
        with contextlib.ExitStack() as st:
            esem = {e: st.enter_context(nc.semaphore("s_" + e)) for e in ENGS}
            dsem = [st.enter_context(nc.semaphore("d%d" % i)) for i in range(self.n_dma_sem)]
            block = st.enter_context(nc.Block())
            prog = self

            def run(ename, eng):
                seen_e = {e: 0 for e in ENGS}
                seen_d = {}
                for o in prog.ops[ename]:
                    need_e = {}
                    need_d = {}
                    for d in o["deps"]:
                        if d[0] == "e":
                            v = prog.ops[d[1]][d[2]]["sigval"]
                            if v > need_e.get(d[1], 0):
                                need_e[d[1]] = v
                        else:
                            if d[2] > need_d.get(d[1], 0):
                                need_d[d[1]] = d[2]
                    for e2, v in need_e.items():
                        if v > seen_e[e2]:
                            eng.wait_ge(esem[e2], v)
                            seen_e[e2] = v
                    for s, v in need_d.items():
                        if v > seen_d.get(s, 0):
                            eng.wait_ge(dsem[s], v)
                            seen_d[s] = v
                    if o["kind"] == "c":
                        ins = o["fn"](eng)
                        if o["signal"]:
                            ins.then_inc(esem[ename], 1)
                    elif o["kind"] == "d":
                        ins = o["fn"](eng)
                        ins.then_inc(dsem[o["sem"]["id"]], 16)
                if ename == "sp":
                    for s in final_waits:
                        if s["count"] > 0:
                            eng.wait_ge(dsem[s["id"]], s["count"])

            @block.tensor
            def _(eng):
                run("pe", eng)

            @block.scalar
            def _(eng):
                run("act", eng)

            @block.vector
            def _(eng):
                run("dve", eng)

            @block.gpsimd
            def _(eng):
                run("pool", eng)

            @block.sync
            def _(eng):
                run("sp", eng)


def bcl(ap, m):
    return bass.AP(tensor=ap.tensor, offset=ap.offset,
                   ap=[list(x) for x in ap.ap] + [[0, m]])


def bcm(ap, k):
    a = [list(x) for x in ap.ap]
    return bass.AP(tensor=ap.tensor, offset=ap.offset, ap=[a[0], [0, k]] + a[1:])


class Rot:
    def __init__(self, items):
        self.items = items
        self.i = 0

    def next(self):
        it = self.items[self.i % len(self.items)]
        self.i += 1
        return it


C_QA, C_KA, C_VA, C_QB, C_KB, C_VB, C_FL, C_QM, C_G = 0, 512, 1024, 1536, 2048, 2560, 3072, 3080, 3592

NPRE = 16
NOWN = 16
NT = NPRE + NOWN
NS = 64
ARENA_W = 53100


def build_nc():
    nc = bass.Bass("TRN2", target_bir_lowering=False)
    P = Prog(nc)

    def din(name, shape):
        return nc.dram_tensor(name, list(shape), F32, kind="ExternalInput").ap()

    def dout(name, shape):
        return nc.dram_tensor(name, list(shape), F32, kind="ExternalOutput").ap()

    xpre = din("xpre", [2048, 1024]); xown = din("xown", [2048, 1024]); xs = din("xs", [NS, 1024])
    mem = din("mem", [256, 1024])
    cak = din("cak", [4, 512, 512]); cav = din("cav", [4, 512, 512])
    cbk = din("cbk", [4, 2048, 512]); cbv = din("cbv", [4, 2048, 512]); cblf = din("cblf", [4, 2048, 8])
    cmk = din("cmk", [4, 256, 512]); cmv = din("cmv", [4, 256, 512])
    g_mix = din("g_mix", [1, 1024]); w_in = din("w_in", [1024, 6664]); b_f = din("b_f", [1, 8])
    g_qa = din("g_qa", [1, 64]); g_ka = din("g_ka", [1, 64]); g_qb = din("g_qb", [1, 64]); g_kb = din("g_kb", [1, 64])
    g_qm = din("g_qm", [1, 128]); g_km = din("g_km", [1, 128]); rel_bias = din("rel_bias", [257, 8])
    g_mem = din("g_mem", [1, 1024]); w_mkv = din("w_mkv", [1024, 1024])
    w_pa = din("w_pa", [512, 1024]); w_pb = din("w_pb", [512, 1024]); w_pm = din("w_pm", [512, 1024])
    w_o = din("w_o", [1024, 1024]); g_ffn = din("g_ffn", [1, 1024])
    w_up = din("w_up", [1024, 4096]); w_down = din("w_down", [4096, 1024])
    cmat = din("cmat", [128, 4 * 128]); ident = din("ident", [128, 128]); flag = din("flag", [1, 1])

    y_p = dout("y_p", [2048, 1024]); y_s = dout("y_s", [NS, 1024])
    akp = dout("akp", [512, 512]); avp = dout("avp", [512, 512])
    bkp = dout("bkp", [2048, 512]); bvp = dout("bvp", [2048, 512]); blfp = dout("blfp", [2048, 8])
    mkp = dout("mkp", [256, 512]); mvp = dout("mvp", [256, 512])
    aks = dout("aks", [NS, 512]); avs = dout("avs", [NS, 512])
    bks = dout("bks", [NS, 512]); bvs = dout("bvs", [NS, 512]); blfs = dout("blfs", [NS, 8])
    relE = nc.dram_tensor("relE", [8, LREL], F32, kind="Internal").ap()
    relD = nc.dram_tensor("relD", [128, 8 * LREL], F32, kind="Internal").ap()

    st = contextlib.ExitStack()
    arena = st.enter_context(nc.sbuf_tensor("arena", [128, ARENA_W], F32))
    psum = [st.enter_context(nc.psum_tensor("ps%d" % i, [128, 512], F32)) for i in range(8)]

    class Arena:
        def __init__(self):
            self.top = 0

        def f32(self, dims, parts=128):
            n = int(np.prod(dims))
            o = self.top
            self.top += n
            assert self.top <= ARENA_W, ("arena overflow", self.top)
            ap = arena[0:parts, o:o + n]
            return self._view(ap, dims)

        def bf16(self, dims, parts=128):
            n = int(np.prod(dims))
            w = (n + 1) // 2
            o = self.top
            self.top += w
            assert self.top <= ARENA_W, ("arena overflow", self.top)
            ap = arena[0:parts, o:o + w].bitcast(BF16)[:, 0:n]
            return self._view(ap, dims)

        @staticmethod
        def _view(ap, dims):
            if len(dims) == 1:
                return ap
            if len(dims) == 2:
                return ap.rearrange("p (a b) -> p a b", a=dims[0])
            if len(dims) == 3:
                return ap.rearrange("p (a b c) -> p a b c", a=dims[0], b=dims[1])
            if len(dims) == 4:
                return ap.rearrange("p (a b c d) -> p a b c d", a=dims[0], b=dims[1], c=dims[2])
            raise ValueError

    A = Arena()
    out_sems = [P.new_dma_sem() for _ in range(4)]
    osr = Rot(out_sems)

    def mm(out, pairs, reads, writes):
        pairs = list(pairs)

        def f(e):
            n = len(pairs)
            for k, (l, r) in enumerate(pairs):
                i = e.matmul(out, lhsT=l, rhs=r, start=(k == 0), stop=(k == n - 1))
            return i
        P.op("pe", f, reads, writes)

    def transposes(items, reads, writes):
        items = list(items)

        def f(e):
            for (o, i_, idn) in items:
                ins = e.transpose(out=o, in_=i_, identity=idn)
            return ins
        P.op("pe", f, reads, writes)

    def act(out, in_, func, reads, writes, bias=None, scale=None, accum=None, eng="act"):
        kw = {}
        if bias is not None:
            kw["bias"] = bias
        if scale is not None:
            kw["scale"] = scale
        if accum is not None:
            kw["accum_out"] = accum
        P.op(eng, lambda e: e.activation(out=out, in_=in_, func=func, **kw), reads, writes)

    def tt(out, in0, in1, op, reads, writes, eng="dve"):
        P.op(eng, lambda e: e.tensor_tensor(out=out, in0=in0, in1=in1, op=op), reads, writes)

    def ts(out, in0, s1, s2, op0, op1, reads, writes, eng="dve"):
        if s2 is None:
            P.op(eng, lambda e: e.tensor_scalar(out=out, in0=in0, scalar1=s1, scalar2=None, op0=op0), reads, writes)
        else:
            P.op(eng, lambda e: e.tensor_scalar(out=out, in0=in0, scalar1=s1, scalar2=s2, op0=op0, op1=op1), reads, writes)

    def stt(out, in0, scalar, in1, op0, op1, reads, writes, eng="dve"):
        P.op(eng, lambda e: e.scalar_tensor_tensor(out=out, in0=in0, scalar=scalar, in1=in1, op0=op0, op1=op1), reads, writes)

    def cp(out, in_, reads, writes, eng="dve"):
        P.op(eng, lambda e: e.tensor_copy(out=out, in_=in_), reads, writes)

    def recip(out, in_, reads, writes):
        P.op("dve", lambda e: e.reciprocal(out=out, in_=in_), reads, writes)

    def red(out, in_, reads, writes):
        P.op("dve", lambda e: e.tensor_reduce(out=out, in_=in_, axis=AX.X, op=ALU.add), reads, writes)

    def memset(ap, val, writes, eng="dve"):
        P.op(eng, lambda e: e.memset(ap, val), (), writes)

    BUFSEM = {}

    def sem_of(b):
        if id(b) not in BUFSEM:
            BUFSEM[id(b)] = P.new_dma_sem()
        return BUFSEM[id(b)]

    STORE_SEMS = []

    def ld(out, in_, sem, writes, reads=(), eng="sp", slow=False):
        sem = sem_of(writes[0])
        if slow:
            P.dma(eng, lambda e: e.dma_start(out=out, in_=in_, allow_slow_non_contiguous=True), sem, reads, writes)
        else:
            P.dma(eng, lambda e: e.dma_start(out=out, in_=in_), sem, reads, writes)

    def store(out, in_, reads):
        sem = sem_of(reads[0])
        if sem not in STORE_SEMS:
            STORE_SEMS.append(sem)
        P.dma("sp", lambda e: e.dma_start(out=out, in_=in_), sem, reads, ())

    def rstd_from_ss(ap, n, bufs):
        act(ap, ap, AF.Sqrt, list(bufs) + [b_eps], bufs, bias=EPSB[0:ap.shape[0], 0:1], scale=1.0 / n)
        recip(ap, ap, bufs, bufs)

    csem = P.new_dma_sem()
    identb = A.bf16([128]); b_id = Buf()
    ld(identb, ident, csem, [b_id], eng="pool")
    trif = A.f32([4, 128]); b_cm = Buf()
    ld(trif, cmat.rearrange("p (a b) -> p a b", a=4), csem, [b_cm])
    TRI, ONES, SELMID, SEL0 = trif[:, 0, :], trif[:, 1, :], trif[:, 2, :], trif[:, 3, :]
    trib = A.bf16([128]); b_trib = Buf()
    cp(trib, TRI, [b_cm], [b_trib])
    gmix = A.f32([1024]); b_g = Buf()
    ld(gmix, g_mix.partition_broadcast(128), csem, [b_g])
    gv = {}
    for nm, src, n in (("qa", g_qa, 64), ("ka", g_ka, 64), ("qb", g_qb, 64), ("kb", g_kb, 64),
                       ("qm", g_qm, 128), ("km", g_km, 128)):
        gv[nm] = A.f32([n])
        ld(gv[nm], src.partition_broadcast(128), csem, [b_g])
    bfb = A.f32([8])
    ld(bfb, b_f.partition_broadcast(128), csem, [b_g])
    EPSB = A.f32([1]); b_eps = Buf()
    memset(EPSB, EPS, [b_eps])
    negflag = A.f32([1]); b_nf = Buf()
    ld(negflag, flag.partition_broadcast(128), csem, [b_nf])
    ts(negflag, negflag, -1.0, -NEG, ALU.add, ALU.mult, [b_nf], [b_nf])

    xt_r = Rot([(A.f32([1024]), Buf(), P.new_dma_sem()) for _ in range(2)])
    sq_r = Rot([(A.bf16([1024]), Buf()) for _ in range(1)])
    xn_r = Rot([(A.bf16([1024]), Buf()) for _ in range(2)])
    st_r = Rot([(A.f32([512]), Buf()) for _ in range(3)])
    sb_r = Rot([(A.bf16([512]), Buf()) for _ in range(2)])
    sm_r = Rot([(A.f32([16]), Buf()) for _ in range(4)])
    pt_r = Rot([(A.bf16([128]), Buf()) for _ in range(12)])
    sa_r = Rot([(A.f32([128]), Buf()) for _ in range(4)])
    fin_r = Rot([(A.bf16([512]), Buf()) for _ in range(2)])
    bias_r = Rot([(A.f32([8]), Buf()) for _ in range(4)])
    w_r = Rot([(A.bf16([4224]), Buf(), P.new_dma_sem()) for _ in range(2)])
    PB = [Buf(excl=True) for _ in range(8)]
    pz_r = Rot([(psum[0], PB[0]), (psum[1], PB[1])])
    pT_r = Rot([(psum[2].bitcast(BF16), PB[2]), (psum[3].bitcast(BF16), PB[3])])
    pS_r = Rot([(psum[4], PB[4]), (psum[5], PB[5])])
    pO = [(psum[6], PB[6]), (psum[7], PB[7])]

    PIPEQ = []
    PST = {"s": 0, "r": 0}
    PFN = {}
    PDEF = []

    def defer(fn, delay):
        PDEF.append([PST["r"] + delay, fn])

    def run_deferred(all_=False):
        while True:
            todo = [d for d in PDEF if all_ or d[0] <= PST["r"]]
            if not todo:
                break
            for d in todo:
                PDEF.remove(d)
            for d in todo:
                d[1]()

    def pipe_pump(drain):
        nb = len(pS_r.items)
        while True:
            if PST["s"] < len(PIPEQ) and PST["s"] - PST["r"] < nb:
                ent = PIPEQ[PST["s"]]
                ent["bank"] = PFN["S"](ent["q"])
                PST["s"] += 1
                continue
            if PST["r"] < PST["s"] and (drain or PST["s"] - PST["r"] >= nb):
                ent = PIPEQ[PST["r"]]
                PFN["R"](ent["q"], *ent["bank"])
                for cb in ent["after"]:
                    cb()
                ent["after"] = []
                PST["r"] += 1
                run_deferred()
                continue
            break
        if drain:
            run_deferred(True)

    def pipe_drain():
        pipe_pump(True)

    def psum_mode(attn):
        pipe_drain()
        if attn:
            pz_r.items = [(psum[0], PB[0])]
            pT_r.items = [(psum[2].bitcast(BF16), PB[2])]
            pS_r.items = [(psum[1], PB[1]), (psum[3], PB[3]), (psum[4], PB[4]), (psum[5], PB[5])]
        else:
            pz_r.items = [(psum[0], PB[0]), (psum[1], PB[1]), (psum[4], PB[4]), (psum[5], PB[5])]
            pT_r.items = [(psum[2].bitcast(BF16), PB[2]), (psum[3].bitcast(BF16), PB[3])]
            pS_r.items = [(psum[4], PB[4]), (psum[5], PB[5])]

    psum_mode(False)

    base_top = A.top

    def wload(src, kc, n):
        wt, wb, wsem = w_r.next()
        v = wt[:, 0:kc * n].rearrange("p (c n) -> p c n", c=kc)
        ld(v, src.rearrange("(c p) n -> p c n", p=128), wsem, [wb], eng="pool")
        return v, wb

    def norm_T(xrows, nt, gbc, dstT, dstbuf, x_keep=None):
        if isinstance(xrows, tuple):
            xt, xb = xrows
        else:
            xt, xb, xsem = xt_r.next()
            ld(xt[0:nt], xrows, xsem, [xb])
        sq, sqb = sq_r.next()
        sm, smb = sm_r.next()
        act(sq[0:nt], xt[0:nt], AF.Square, [xb], [sqb, smb], accum=sm[0:nt, 0:1])
        rstd_from_ss(sm[0:nt, 0:1], 1024.0, [smb])
        xn, xnb = xn_r.next()
        stt(xn[0:nt], xt[0:nt], sm[0:nt, 0:1], gbc[0:nt], ALU.mult, ALU.mult, [xb, smb, b_g], [xnb])
        pT, pTb = pT_r.next()
        transposes([(pT[:, c * 128:c * 128 + nt], xn[0:nt, c * 128:(c + 1) * 128], identb[0:nt, 0:nt])
                    for c in range(8)], [xnb, b_id], [pTb])
        cp(dstT, pT.rearrange("p (c t) -> p c t", c=8)[:, :, 0:nt], [pTb], [dstbuf])

    def norm_T_staged(xrows, nt, gbc, dstT, dstbuf):
        xt, xb, xsem = xt_r.next()
        ld(xt[0:nt], xrows, xsem, [xb])
        sq, sqb = sq_r.next()
        sm, smb = sm_r.next()
        hold = {}

        def sA():
            act(sq[0:nt], xt[0:nt], AF.Square, [xb], [sqb, smb], accum=sm[0:nt, 0:1])
            act(sm[0:nt, 0:1], sm[0:nt, 0:1], AF.Sqrt, [smb, b_eps], [smb], bias=EPSB[0:nt, 0:1], scale=1.0 / 1024.0)

        def sB():
            recip(sm[0:nt, 0:1], sm[0:nt, 0:1], [smb], [smb])
            xn, xnb = xn_r.next()
            stt(xn[0:nt], xt[0:nt], sm[0:nt, 0:1], gbc[0:nt], ALU.mult, ALU.mult, [xb, smb, b_g], [xnb])
            hold["pT"] = pT_r.next()
            pT, pTb = hold["pT"]
            transposes([(pT[:, c * 128:c * 128 + nt], xn[0:nt, c * 128:(c + 1) * 128], identb[0:nt, 0:nt])
                        for c in range(8)], [xnb, b_id], [pTb])

        def sC():
            pT, pTb = hold["pT"]
            cp(dstT, pT.rearrange("p (c t) -> p c t", c=8)[:, :, 0:nt], [pTb], [dstbuf])
        return sA, sB, sC

    def proj_tok(xT, xTb, nt, wv, wb, ncols, c0=0):
        pz, pzb = pz_r.next()
        mm(pz[0:nt, 0:ncols], [(xT[:, c, :], wv[:, c, c0:c0 + ncols]) for c in range(8)], [xTb, wb], [pzb])
        return pz, pzb

    hn_sm_r = Rot([(A.f32([16]), Buf()) for _ in range(3)])

    def head_norm(pz, pzb, nt, nh, dh, g, out_dram, dstT=None, dstTb=None, pair=True, defer_T=0, staged=False):
        n = nh * dh
        sq, sqb = sq_r.next()
        sm, smb = hn_sm_r.next() if (defer_T or staged) else sm_r.next()
        stg, stb = st_r.next()
        sbf, sbb = (None, None)
        if dstT is not None:
            sbf, sbb = sb_r.next()
        v3 = lambda a: a[0:nt, 0:n].rearrange("p (h d) -> p h d", h=nh)

        def stage_act():
            def fsq(e):
                for h in range(nh):
                    ins = e.activation(out=sq[0:nt, h * dh:(h + 1) * dh], in_=pz[0:nt, h * dh:(h + 1) * dh],
                                       func=AF.Square, accum_out=sm[0:nt, h:h + 1])
                return ins
            P.op("act", fsq, [pzb], [sqb, smb])
            act(sm[0:nt, 0:nh], sm[0:nt, 0:nh], AF.Sqrt, [smb, b_eps], [smb], bias=EPSB[0:nt, 0:1], scale=1.0 / dh)

        def stage_dve():
            recip(sm[0:nt, 0:nh], sm[0:nt, 0:nh], [smb], [smb])
            tt(v3(stg), v3(pz), bcl(sm[0:nt, 0:nh], dh), ALU.mult, [pzb, smb], [stb])
            if out_dram is not None:
                tt(v3(stg), v3(stg), bcm(g[0:nt], nh), ALU.mult, [stb, b_g], [stb])
                store(out_dram, stg[0:nt, 0:n], [stb])
                if dstT is not None:
                    cp(sbf[0:nt, 0:n], stg[0:nt, 0:n], [stb], [sbb])
            else:
                tt(v3(sbf), v3(stg), bcm(g[0:nt], nh), ALU.mult, [stb, b_g], [sbb])

        def stage_pe():
            if dstT is None:
                return
            pT, pTb = pT_r.next()
            nch = n // 128
            transposes([(pT[:, c * 128:c * 128 + nt], sbf[0:nt, c * 128:(c + 1) * 128], identb[0:nt, 0:nt])
                        for c in range(nch)], [sbb, b_id], [pTb])
            cp(dstT, pT[:, 0:nch * 128].rearrange("p (c t) -> p c t", c=nch)[:, :, 0:nt], [pTb], [dstTb])

        if staged:
            return stage_act, stage_dve, stage_pe
        if defer_T:
            defer(stage_act, defer_T)
            defer(stage_dve, 2 * defer_T)
            defer(stage_pe, 3 * defer_T)
        else:
            stage_act()
            stage_dve()
            stage_pe()

    def pipelined(n_tiles, make):
        st_ = {}
        for it in range(n_tiles + 2):
            if it < n_tiles:
                st_[it] = make(it)
                st_[it][0]()
            if 0 <= it - 1 < n_tiles:
                st_[it - 1][1]()
            if 0 <= it - 2 < n_tiles:
                st_.pop(it - 2)[2]()

    def v_store(pz, pzb, nt, nh, dh, out_dram, vaug, vb):
        n = nh * dh
        stg, stb = st_r.next()
        cp(stg[0:nt, 0:n], pz[0:nt, 0:n], [pzb], [stb])
        if out_dram is not None:
            store(out_dram, stg[0:nt, 0:n], [stb])
        act(vaug[0:nt, :, 0:dh], stg[0:nt, 0:n].rearrange("p (h d) -> p h d", h=nh), AF.Copy, [stb], [vb])

    def logf_a(xT, xTb, nt, wv, wb, c0, LFdst, lfb):
        pz, pzb = pz_r.next()
        mm(pz[0:nt, 0:8], [(xT[:, c, :], wv[:, c, c0:c0 + 8]) for c in range(8)], [xTb, wb], [pzb])
        tt(LFdst, pz[0:nt, 0:8], bfb[0:nt], ALU.add, [pzb, b_g], [lfb])

    def logf_b(LFdst, lfb):
        act(LFdst, LFdst, AF.Exp, [lfb], [lfb], scale=-1.0)
        act(LFdst, LFdst, AF.Ln, [lfb], [lfb], bias=1.0)

    def logf_c(LFdst, lfb, out_dram):
        ts(LFdst, LFdst, -1.0, None, ALU.mult, None, [lfb], [lfb])
        if out_dram is not None:
            store(out_dram, LFdst, [lfb])

    def logf_tile(xT, xTb, nt, wv, wb, c0, LFdst, lfb, out_dram):
        logf_a(xT, xTb, nt, wv, wb, c0, LFdst, lfb)
        logf_b(LFdst, lfb)
        logf_c(LFdst, lfb, out_dram)

    def cumsum_tile(LF, lfb, nt, R, rb, Cdst, cb, first):
        pz, pzb = pz_r.next()
        if first:
            mm(pz[0:nt, 0:8], [(TRI[0:nt, 0:nt], LF)], [lfb, b_cm], [pzb])
        else:
            mm(pz[0:nt, 0:8], [(TRI[0:nt, 0:nt], LF), (ONES[:, 0:nt], R)], [lfb, b_cm, rb], [pzb])
        cp(Cdst, pz[0:nt, 0:8], [pzb], [cb])
        if first:
            memset(R, 0.0, [rb])
        tt(R[0:nt], R[0:nt], LF, ALU.add, [rb, lfb], [rb])

    def run_sums(LFs_list, Rall, rb, first_is_zero):
        if first_is_zero:
            memset(Rall[:, 0, :], 0.0, [rb])
        for t, (lf, lfb, nt) in enumerate(LFs_list):
            if nt < 128:
                continue
            tt(Rall[:, t + 1, :], Rall[:, t, :], lf, ALU.add, [rb, lfb], [rb])

    def cumsum_par(LF, lfb, nt, Rt, rb, Cdst, cb):
        pz, pzb = pz_r.next()
        mm(pz[0:nt, 0:8], [(TRI[0:nt, 0:nt], LF), (ONES[:, 0:nt], Rt)], [lfb, b_cm, rb], [pzb])
        cp(Cdst, pz[0:nt, 0:8], [pzb], [cb])

    def cumsum_batch(LFv, lfbufs, Rv, rb, Cv, cbufs, ncol):
        pz, pzb = pz_r.next()
        mm(pz[:, 0:ncol], [(TRI, LFv), (ONES, Rv)], list(lfbufs) + [b_cm, rb], [pzb])
        cp(Cv, pz[:, 0:ncol], [pzb], list(cbufs))

    def bcast_row(src, sb_, nrows, SEL, dst, db):
        pz, pzb = pz_r.next()
        mm(pz[:, 0:8], [(SEL[0:nrows, :], src)], [sb_, b_cm], [pzb])
        cp(dst, pz[:, 0:8], [pzb], [db])

    def attn_head(qT, nq, ktiles, po_ap, pob, scale, first_reads):
        n = len(ktiles)
        PENDING.append([dict(kt=kt, qT=qT, nq=nq, po_ap=po_ap, pob=pob, scale=scale,
                             first_reads=list(first_reads), start=(idx == 0), stop=(idx == n - 1))
                        for idx, kt in enumerate(ktiles)])

    PENDING = []
    LA = 4

    def attn_flush():
        calls = list(PENDING)
        del PENDING[:]
        steps = [sp_ for c_ in calls for sp_ in c_]
        quads = [steps[i:i + 4] for i in range(0, len(steps), 4)]

        def uniq(bl):
            out = []
            for b in bl:
                if b not in out:
                    out.append(b)
            return out

        def emit_S(q):
            bank, bb = pS_r.next()
            q = list(q)

            def f(e):
                for k, sp_ in enumerate(q):
                    kt = sp_["kt"]
                    ins = e.matmul(bank[0:kt["nk"], k * 128:k * 128 + sp_["nq"]], lhsT=kt["kT"], rhs=sp_["qT"],
                                   start=True, stop=True)
                return ins
            rds = []
            for sp_ in q:
                rds += list(sp_["kt"]["rd"]) + sp_["first_reads"]
            P.op("pe", f, uniq(rds), [bb])
            return bank, bb

        def emit_rest(q, bank, bb):
            q = list(q)
            pts = [pt_r.next() for _ in q]
            sas = {}
            pre_items = []
            for k, sp_ in enumerate(q):
                if sp_["kt"].get("pre") is not None:
                    sas[k] = sa_r.next()
                    pre_items.append(k)
            if pre_items:
                def fd(e):
                    for k in pre_items:
                        sp_ = q[k]
                        kt = sp_["kt"]
                        nk, nq = kt["nk"], sp_["nq"]
                        ins = e.scalar_tensor_tensor(out=sas[k][0][0:nk, 0:nq], in0=bank[0:nk, k * 128:k * 128 + nq],
                                                     scalar=sp_["scale"], in1=kt["pre"][0], op0=ALU.mult, op1=ALU.add)
                    return ins
                rds = [bb]
                for k in pre_items:
                    rds += list(q[k]["kt"]["pre"][1])
                P.op("dve", fd, uniq(rds), [sas[k][1] for k in pre_items])

            def fa(e):
                for k, sp_ in enumerate(q):
                    kt = sp_["kt"]
                    nk, nq = kt["nk"], sp_["nq"]
                    if k in sas:
                        ins = e.activation(out=pts[k][0][0:nk, 0:nq], in_=sas[k][0][0:nk, 0:nq], func=AF.Exp,
                                           bias=kt.get("bias", 0.0))
                    else:
                        ins = e.activation(out=pts[k][0][0:nk, 0:nq], in_=bank[0:nk, k * 128:k * 128 + nq], func=AF.Exp,
                                           bias=kt.get("bias", 0.0), scale=sp_["scale"])
                return ins
            rds = [bb] + [sas[k][1] for k in pre_items]
            for sp_ in q:
                rds += list(sp_["kt"].get("brd", []))
            P.op("act", fa, uniq(rds), [p_[1] for p_ in pts])
            for k, sp_ in enumerate(q):
                kt = sp_["kt"]
                if kt.get("mask") is not None:
                    nk, nq = kt["nk"], sp_["nq"]
                    tt(pts[k][0][0:nk, 0:nq], pts[k][0][0:nk, 0:nq], kt["mask"][0:nk, 0:nq], ALU.mult,
                       [pts[k][1], b_trib, b_mask0], [pts[k][1]], eng="pool")
            def fp(e):
                for k, sp_ in enumerate(q):
                    kt = sp_["kt"]
                    nk, nq = kt["nk"], sp_["nq"]
                    ins = e.matmul(sp_["po_ap"], lhsT=pts[k][0][0:nk, 0:nq], rhs=kt["v"],
                                   start=sp_["start"], stop=sp_["stop"])
                return ins
            rds = [p_[1] for p_ in pts]
            for sp_ in q:
                rds += list(sp_["kt"]["rd"])
            P.op("pe", fp, uniq(rds), uniq([sp_["pob"] for sp_ in q]))

        for q in quads:
            PIPEQ.append({"q": q, "bank": None, "after": []})
        PFN["S"], PFN["R"] = emit_S, emit_rest
        pipe_pump(False)


    def mm_acc(out, l, r, start, stop, reads, writes):
        P.op("pe", lambda e: e.matmul(out, lhsT=l, rhs=r, start=start, stop=stop), reads, writes)

    FIN = {}

    def attn_finish_group(nq, nh, dh, half):
        attn_flush()
        PIPEQ[-1]["after"].append(lambda: fin_group(nq, nh, dh, half))
        if PST["r"] >= len(PIPEQ):
            raise RuntimeError("pipeline ran ahead of finish")

    def fin_group(nq, nh, dh, half):
        hpb = nh // 2
        if half == 0:
            FIN["sbf"] = fin_r.next()
        sbf, sbb = FIN["sbf"]
        po, pob = pO[half]
        pv = po[0:nq, 0:hpb * (dh + 1)].rearrange("p (h d) -> p h d", h=hpb)
        sm, smb = sm_r.next()
        recip(sm[0:nq, 0:hpb], pv[:, :, dh], [pob], [smb])
        if dh == 64:
            ov_ = sbf[0:nq, 0:512].rearrange("p (hp two d) -> p hp two d", two=2, d=64)[:, :, half, :]
        else:
            ov_ = sbf[0:nq, half * hpb * dh:(half + 1) * hpb * dh].rearrange("p (h d) -> p h d", h=hpb)
        tt(ov_, pv[:, :, 0:dh], bcl(sm[0:nq, 0:hpb], dh), ALU.mult, [pob, smb], [sbb])

    def attn_finish(nq, nh, dh, dstT, dstTb):
        PIPEQ[-1]["after"].append(lambda: fin_tile(nq, dstT, dstTb))

    def fin_tile(nq, dstT, dstTb):
        sbf, sbb = FIN["sbf"]

        def part2():
            pT, pTb = pT_r.next()
            transposes([(pT[:, c * 128:c * 128 + nq], sbf[0:nq, c * 128:(c + 1) * 128], identb[0:nq, 0:nq])
                        for c in range(4)], [sbb, b_id], [pTb])
            cp(dstT, pT[:, 0:512].rearrange("p (c t) -> p c t", c=4)[:, :, 0:nq], [pTb], [dstTb])
        defer(part2, 2)

    def po_slot(h, nh, dh):
        hpb = nh // 2
        if dh == 64:
            po, pob = pO[h % 2]
            hh = h // 2
        else:
            po, pob = pO[h // hpb]
            hh = h % hpb
        return po, pob, hh * (dh + 1)

    BiasA = A.f32([8, 2, 128]); b_bias = Buf()
    relc = A.f32([8]); relcnf = A.f32([8])
    ld(relc, rel_bias[256:257, :].partition_broadcast(128), csem, [b_bias])
    ts(relcnf, relc, negflag[:, 0:1], None, ALU.add, None, [b_bias, b_nf], [b_bias])
    mask0 = A.bf16([128]); b_mask0 = Buf()
    memset(mask0, 1.0, [b_mask0])
    memset(mask0[0:64, 64:128], 0.0, [b_mask0])
    mark0 = A.top
    relx = A.f32([LREL], parts=8); b_relx = Buf()
    rsem = P.new_dma_sem()
    memset(relx, 0.0, [b_relx])
    sm8 = A.f32([1], parts=8)
    ld(sm8, rel_bias[256:257, :].rearrange("i h -> h i"), rsem, [b_relx], slow=True)
    ts(relx, relx, sm8[:, 0:1], None, ALU.add, None, [b_relx], [b_relx])
    ld(relx[:, 0:257], rel_bias.rearrange("i h -> h i"), rsem, [b_relx], slow=True)
    b_relE = Buf()
    ld(relE, relx, rsem, [b_relE], reads=[b_relx])
    relrep = A.f32([8, LREL]); b_rr = Buf()
    ld(relrep, relE.partition_broadcast(128), rsem, [b_rr], reads=[b_relE])
    b_relD = Buf()
    ld(relD, relrep.rearrange("p a b -> p (a b)"), rsem, [b_relD], reads=[b_rr])
    for t in (3, 4):
        src = bass.AP(tensor=relD.tensor, offset=640 - 128 * t,
                      ap=[[8 * LREL - 1, 128], [LREL, 8], [1, 128]])
        ld(BiasA[:, :, t - 3, :], src, rsem, [b_bias], reads=[b_relD])
    memset(BiasA[64:128, :, 1, 0:64], NEG, [b_bias])
    P.barrier(list(ALLBUFS))
    A.top = mark0

    mkT = A.bf16([4, 256]); b_mkT = Buf()
    mvA = A.bf16([2, 4, 129]); b_mv = Buf()
    memset(mvA[:, :, :, 128:129], 1.0, [b_mv])
    mark1 = A.top
    gmem = A.f32([1024]); b_gm = Buf()
    ld(gmem, g_mem.partition_broadcast(128), csem, [b_g])
    xmT = A.bf16([8, 256]); b_xmT = Buf()
    for t in range(2):
        norm_T(mem[t * 128:(t + 1) * 128, :], 128, gmem, xmT[:, :, t * 128:(t + 1) * 128], b_xmT)
    wv, wb = wload(w_mkv[:, 0:512], 8, 512)
    for t in range(2):
        pz, pzb = proj_tok(xmT[:, :, t * 128:(t + 1) * 128], b_xmT, 128, wv, wb, 512)
        head_norm(pz, pzb, 128, 4, 128, gv["km"], mkp[t * 128:(t + 1) * 128, :],
                  mkT[:, :, t * 128:(t + 1) * 128], b_mkT)
    wv, wb = wload(w_mkv[:, 512:1024], 8, 512)
    for t in range(2):
        pz, pzb = proj_tok(xmT[:, :, t * 128:(t + 1) * 128], b_xmT, 128, wv, wb, 512)
        v_store(pz, pzb, 128, 4, 128, mvp[t * 128:(t + 1) * 128, :], mvA[:, t], b_mv)
    P.barrier(list(ALLBUFS))
    A.top = mark1

    def full_barrier():
        pipe_drain()
        P.barrier(list(ALLBUFS))

    mark_x = A.top
    xnT = A.bf16([8, 2048]); b_xnT = [Buf() for _ in range(NOWN)]
    xpT4 = A.bf16([8, 512]); b_xp4 = Buf()
    xsT = A.bf16([8, NS]); b_xsT = Buf()
    osT = {k: A.bf16([4, NS]) for k in "bam"}
    b_osT = {k: Buf() for k in "bam"}
    oT = {}
    b_oT = {k: Buf() for k in "bam"}
    KsT_new = A.bf16([4, NS]); b_KsN = Buf()
    QsT = A.bf16([4, NS]); b_QsT = Buf()
    VsN = A.bf16([4, 8, 65], parts=16); b_VsN = Buf()
    LFs = A.f32([4, 8], parts=16); b_LFs = Buf()
    memset(VsN[:, :, :, 64:65], 1.0, [b_VsN])
    oT["b"] = A.bf16([4, 2048])

    def xT_of(i):
        return xnT[:, :, i * 128:(i + 1) * 128], b_xnT[i]

    markB = A.top
    KbT = A.bf16([4, NT * 128]); b_KbT = [Buf() for _ in range(NT)]
    Vb = A.bf16([NT, 8, 65]); b_Vb = [Buf() for _ in range(NT)]
    Qb_r = Rot([(A.bf16([4, 128]), Buf()) for _ in range(2)])
    LF = A.f32([NT, 8]); b_LF = [Buf() for _ in range(NT)]
    Cc = A.f32([NT, 8]); b_C = [Buf() for _ in range(NT)]
    crefs = A.f32([NOWN, 8]); b_cref = [Buf() for _ in range(NOWN)]
    Rall = A.f32([NT + 1, 8]); b_R = Buf()
    biasT_r = Rot([(A.f32([NT + 1, 8]), Buf()) for _ in range(3)])
    memset(Vb[:, :, :, 64:65], 1.0, b_Vb)
    def mk_nt_pre(i):
        xv, xb_ = xT_of(i)
        return norm_T_staged(xpre[i * 128:(i + 1) * 128, :], 128, gmix, xv, xb_)
    pipelined(NPRE, mk_nt_pre)
    for i in range(4):
        xv, xb_ = xT_of(12 + i)
        cp(xpT4[:, :, i * 128:(i + 1) * 128], xv, [xb_], [b_xp4])
    wv, wb = wload(w_in[:, C_KB:C_KB + 512], 8, 512)
    def mk_kb_pre(i):
        xv, xb_ = xT_of(i)
        pz, pzb = proj_tok(xv, xb_, 128, wv, wb, 512)
        return head_norm(pz, pzb, 128, 8, 64, gv["kb"], None, KbT[:, :, i * 128:(i + 1) * 128], b_KbT[i], staged=True)
    pipelined(NPRE, mk_kb_pre)
    wv, wb = wload(w_in[:, C_VB:C_VB + 520], 8, 520)
    for i in range(NPRE):
        xv, xb_ = xT_of(i)
        pz, pzb = proj_tok(xv, xb_, 128, wv, wb, 512)
        v_store(pz, pzb, 128, 8, 64, None, Vb[:, i], b_Vb[i])
        logf_a(xv, xb_, 128, wv, wb, 512, LF[:, i, :], b_LF[i])
    for i in range(NPRE):
        logf_b(LF[:, i, :], b_LF[i])
    for i in range(NPRE):
        logf_c(LF[:, i, :], b_LF[i], None)
    run_sums([(LF[:, i, :], b_LF[i], 128) for i in range(NPRE)], Rall, b_R, True)
    cumsum_batch(LF[:, 0:NPRE, :].rearrange("p t h -> p (t h)"), b_LF[0:NPRE],
                 Rall[:, 0:NPRE, :].rearrange("p t h -> p (t h)"), b_R,
                 Cc[:, 0:NPRE, :].rearrange("p t h -> p (t h)"), b_C[0:NPRE], NPRE * 8)
    def mk_nt_own(i):
        xv, xb_ = xT_of(i)
        return norm_T_staged(xown[i * 128:(i + 1) * 128, :], 128, gmix, xv, xb_)
    pipelined(NOWN, mk_nt_own)
    norm_T(xs, NS, gmix, xsT, b_xsT)
    wv, wb = wload(w_in[:, C_KB:C_KB + 512], 8, 512)
    def mk_kb_own(i):
        xv, xb_ = xT_of(i)
        pz, pzb = proj_tok(xv, xb_, 128, wv, wb, 512)
        return head_norm(pz, pzb, 128, 8, 64, gv["kb"], bkp[i * 128:(i + 1) * 128, :],
                         KbT[:, :, (NPRE + i) * 128:(NPRE + i + 1) * 128], b_KbT[NPRE + i], staged=True)
    pipelined(NOWN, mk_kb_own)
    pz, pzb = proj_tok(xsT, b_xsT, NS, wv, wb, 512)
    head_norm(pz, pzb, NS, 8, 64, gv["kb"], bks, KsT_new, b_KsN)
    wv, wb = wload(w_in[:, C_VB:C_VB + 520], 8, 520)
    for i in range(NOWN):
        xv, xb_ = xT_of(i)
        j = NPRE + i
        pz, pzb = proj_tok(xv, xb_, 128, wv, wb, 512)
        v_store(pz, pzb, 128, 8, 64, bvp[i * 128:(i + 1) * 128, :], Vb[:, j], b_Vb[j])
        logf_a(xv, xb_, 128, wv, wb, 512, LF[:, j, :], b_LF[j])
    for i in range(NOWN):
        logf_b(LF[:, NPRE + i, :], b_LF[NPRE + i])
    for i in range(NOWN):
        logf_c(LF[:, NPRE + i, :], b_LF[NPRE + i], blfp[i * 128:(i + 1) * 128, :])
    for i in range(NOWN):
        j = NPRE + i
        tt(Rall[:, j + 1, :], Rall[:, j, :], LF[:, j, :], ALU.add, [b_R, b_LF[j]], [b_R])
    cumsum_batch(LF[:, NPRE:NT, :].rearrange("p t h -> p (t h)"), b_LF[NPRE:NT],
                 Rall[:, NPRE:NT, :].rearrange("p t h -> p (t h)"), b_R,
                 Cc[:, NPRE:NT, :].rearrange("p t h -> p (t h)"), b_C[NPRE:NT], NOWN * 8)
    pz, pzb = pz_r.next()
    mm(pz[:, 0:NOWN * 8], [(SELMID, Cc[:, NPRE:NT, :].rearrange("p t h -> p (t h)"))], b_C[NPRE:NT] + [b_cm], [pzb])
    cp(crefs.rearrange("p t h -> p (t h)"), pz[:, 0:NOWN * 8], [pzb], b_cref)
    for bb in range(4):
        xv = xsT[:, :, bb * 16:(bb + 1) * 16]
        pz, pzb = proj_tok(xv, b_xsT, 16, wv, wb, 512)
        v_store(pz, pzb, 16, 8, 64, bvs[bb * 16:(bb + 1) * 16, :], VsN[:, bb], b_VsN)
        logf_tile(xv, b_xsT, 16, wv, wb, 512, LFs[:, bb, :], b_LFs, blfs[bb * 16:(bb + 1) * 16, :])
    wv, wb = wload(w_in[:, C_QB:C_QB + 512], 8, 512)
    pz, pzb = proj_tok(xsT, b_xsT, NS, wv, wb, 512)
    head_norm(pz, pzb, NS, 8, 64, gv["qb"], None, QsT, b_QsT)

    psum_mode(True)
    def prep_b(i):
        gi = NPRE + i
        xv, xb_ = xT_of(i)
        pz, pzb = proj_tok(xv, xb_, 128, wv, wb, 512)
        qv, qvb = Qb_r.next()
        head_norm(pz, pzb, 128, 8, 64, gv["qb"], None, qv, qvb, defer_T=2)
        bt_all, b_bt = biasT_r.next()
        for j in range(gi + 1):
            tt(bt_all[:, j, :], crefs[:, i, :], Cc[:, j, :], ALU.subtract, [b_cref[i], b_C[j]], [b_bt])
        ts(bt_all[:, 0:NPRE, :], bt_all[:, 0:NPRE, :], negflag[:, 0:1], None, ALU.add, None, [b_bt, b_nf], [b_bt])
        return qv, qvb, bt_all, b_bt

    nxt = prep_b(0)
    for i in range(NOWN):
        gi = NPRE + i
        run_deferred(True)
        qv, qvb, bt_all, b_bt = nxt
        if i + 1 < NOWN:
            nxt = prep_b(i + 1)
        for h in (0, 2, 4, 6, 1, 3, 5, 7):
            hp, hr = h // 2, (h % 2) * 64
            po, pob, off = po_slot(h, 8, 64)
            kts = []
            for j in range(gi + 1):
                kts.append(dict(kT=KbT[hr:hr + 64, hp, j * 128:(j + 1) * 128], nk=128,
                                v=Vb[:, j, h, :], rd=[b_KbT[j], b_Vb[j]],
                                bias=bt_all[:, j, h:h + 1], brd=[b_bt],
                                mask=(trib if j == gi else None)))
            attn_head(qv[hr:hr + 64, hp, :], 128, kts, po[:, off:off + 65], pob,
                      0.125, [qvb])
            if h >= 6:
                attn_finish_group(128, 8, 64, h % 2)
        attn_finish(128, 8, 64, oT["b"][:, :, i * 128:(i + 1) * 128], b_oT["b"])
    full_barrier()
    A.top = markB

    Kc = A.bf16([16, 512]); b_Kc = [Buf() for _ in range(4)]; kcsem = None
    KsT = A.bf16([4, 2048]); b_KsT = Buf()
    Vs = A.bf16([16, 8, 65]); b_Vs = [Buf() for _ in range(16)]
    LFc = A.f32([16, 8]); b_LFc = Buf()
    Cs = A.f32([16, 8]); b_Cs = Buf()
    CsN = A.f32([8], parts=16); b_CsN = Buf()
    crefS = A.f32([8]); b_crefS = Buf()
    memset(Vs[:, :, :, 64:65], 1.0, b_Vs)
    for bb in range(4):
        for q4 in range(4):
            ld(Kc[:, 4 * q4:4 * q4 + 4, :], cbk[bb, 512 * q4:512 * (q4 + 1), :].rearrange("(t p) n -> p t n", p=128),
               kcsem, [b_Kc[q4]], eng="pool")
        for t in range(16):
            ld(Vs[:, t, :, 0:64], cbv[bb, 128 * t:128 * (t + 1), :].rearrange("p (h d) -> p h d", h=8),
               kcsem, [b_Vs[t]], eng="pool")
        ld(LFc, cblf[bb].rearrange("(t p) h -> p t h", p=128), kcsem, [b_LFc])
        for t in range(16):
            pT, pTb = pT_r.next()
            transposes([(pT[:, c * 128:(c + 1) * 128], Kc[:, t, c * 128:(c + 1) * 128], identb) for c in range(4)],
                       [b_Kc[t // 4], b_id], [pTb])
            cp(KsT[:, :, t * 128:(t + 1) * 128], pT[:, 0:512].rearrange("p (c t) -> p c t", c=4), [pTb], [b_KsT])
        run_sums([(LFc[:, t, :], b_LFc, 128) for t in range(16)], Rall, b_R, True)
        cumsum_batch(LFc.rearrange("p t h -> p (t h)"), [b_LFc], Rall[:, 0:16, :].rearrange("p t h -> p (t h)"), b_R,
                     Cs.rearrange("p t h -> p (t h)"), [b_Cs], 128)
        cumsum_par(LFs[:, bb, :], b_LFs, 16, Rall[:, 16, :], b_R, CsN, b_CsN)
        bcast_row(CsN, b_CsN, 16, SEL0, crefS, b_crefS)
        bt_all, b_bt = biasT_r.next()
        for t in range(16):
            tt(bt_all[:, t, :], crefS, Cs[:, t, :], ALU.subtract, [b_crefS, b_Cs], [b_bt])
        tt(bt_all[0:16, 16, :], crefS[0:16], CsN, ALU.subtract, [b_crefS, b_CsN], [b_bt])
        for h in (0, 2, 4, 6, 1, 3, 5, 7):
            hp, hr = h // 2, (h % 2) * 64
            po, pob, off = po_slot(h, 8, 64)
            kts = []
            for t in range(16):
                kts.append(dict(kT=KsT[hr:hr + 64, hp, t * 128:(t + 1) * 128], nk=128, v=Vs[:, t, h, :],
                                rd=[b_KsT, b_Vs[t]], bias=bt_all[:, t, h:h + 1], brd=[b_bt], mask=None))
            kts.append(dict(kT=KsT_new[hr:hr + 64, hp, bb * 16:(bb + 1) * 16], nk=16, v=VsN[:, bb, h, :],
                            rd=[b_KsN, b_VsN], bias=bt_all[0:16, 16, h:h + 1], brd=[b_bt], mask=trib))
            attn_head(QsT[hr:hr + 64, hp, bb * 16:(bb + 1) * 16], 16, kts, po[0:16, off:off + 65], pob,
                      0.125, [b_QsT])
            if h >= 6:
                attn_finish_group(16, 8, 64, h % 2)
        attn_finish(16, 8, 64, osT["b"][:, :, bb * 16:(bb + 1) * 16], b_osT["b"])
        pipe_drain()
    full_barrier()
    A.top = markB

    psum_mode(False)
    oT["a"] = A.bf16([4, 2048])
    markAA = A.top
    NA = 4 + NOWN
    KaT = A.bf16([4, NA * 128]); b_KaT = [Buf() for _ in range(NA)]
    Va = A.bf16([NA, 8, 65]); b_Va = [Buf() for _ in range(NA)]
    Qa_r = Rot([(A.bf16([4, 128]), Buf()) for _ in range(2)])
    KsaN = A.bf16([4, NS]); b_KsaN = Buf()
    QsaT = A.bf16([4, NS]); b_QsaT = Buf()
    VsaN = A.bf16([4, 8, 65], parts=16); b_VsaN = Buf()
    memset(Va[:, :, :, 64:65], 1.0, b_Va)
    memset(VsaN[:, :, :, 64:65], 1.0, [b_VsaN])

    def xa_of(idx):
        if idx < 4:
            return xpT4[:, :, idx * 128:(idx + 1) * 128], b_xp4
        return xT_of(idx - 4)

    wv, wb = wload(w_in[:, C_KA:C_KA + 512], 8, 512)
    def mk_ka(idx):
        xv, xb_ = xa_of(idx)
        pz, pzb = proj_tok(xv, xb_, 128, wv, wb, 512)
        od = akp[(idx - 16) * 128:(idx - 15) * 128, :] if idx >= 16 else None
        return head_norm(pz, pzb, 128, 8, 64, gv["ka"], od, KaT[:, :, idx * 128:(idx + 1) * 128], b_KaT[idx], staged=True)
    pipelined(NA, mk_ka)
    pz, pzb = proj_tok(xsT, b_xsT, NS, wv, wb, 512)
    head_norm(pz, pzb, NS, 8, 64, gv["ka"], aks, KsaN, b_KsaN)
    wv, wb = wload(w_in[:, C_VA:C_VA + 512], 8, 512)
    for idx in range(NA):
        xv, xb_ = xa_of(idx)
        pz, pzb = proj_tok(xv, xb_, 128, wv, wb, 512)
        od = avp[(idx - 16) * 128:(idx - 15) * 128, :] if idx >= 16 else None
        v_store(pz, pzb, 128, 8, 64, od, Va[:, idx], b_Va[idx])
    for bb in range(4):
        pz, pzb = proj_tok(xsT[:, :, bb * 16:(bb + 1) * 16], b_xsT, 16, wv, wb, 512)
        v_store(pz, pzb, 16, 8, 64, avs[bb * 16:(bb + 1) * 16, :], VsaN[:, bb], b_VsaN)
    wv, wb = wload(w_in[:, C_QA:C_QA + 512], 8, 512)
    pz, pzb = proj_tok(xsT, b_xsT, NS, wv, wb, 512)
    head_norm(pz, pzb, NS, 8, 64, gv["qa"], None, QsaT, b_QsaT)
    def prep_a(i):
        xv, xb_ = xT_of(i)
        pz, pzb = proj_tok(xv, xb_, 128, wv, wb, 512)
        qv, qvb = Qa_r.next()
        head_norm(pz, pzb, 128, 8, 64, gv["qa"], None, qv, qvb, defer_T=2)
        return qv, qvb

    psum_mode(True)
    nxt = prep_a(0)
    for i in range(NOWN):
        run_deferred(True)
        qv, qvb = nxt
        if i + 1 < NOWN:
            nxt = prep_a(i + 1)
        for h in (0, 2, 4, 6, 1, 3, 5, 7):
            hp, hr = h // 2, (h % 2) * 64
            po, pob, off = po_slot(h, 8, 64)
            kts = []
            for t in range(5):
                idx = i + t
                pre_ = idx < 4
                d = dict(kT=KaT[hr:hr + 64, hp, idx * 128:(idx + 1) * 128], nk=128, v=Va[:, idx, h, :],
                         rd=[b_KaT[idx], b_Va[idx]], mask=None, brd=[b_bias, b_nf])
                if t < 3:
                    d["bias"] = (relcnf if pre_ else relc)[:, h:h + 1]
                    if t == 0:
                        d["mask"] = mask0
                else:
                    d["pre"] = (BiasA[:, h, t - 3, :], [b_bias])
                    d["bias"] = negflag[:, 0:1] if pre_ else 0.0
                kts.append(d)
            attn_head(qv[hr:hr + 64, hp, :], 128, kts, po[:, off:off + 65], pob,
                      0.125, [qvb])
            if h >= 6:
                attn_finish_group(128, 8, 64, h % 2)
        attn_finish(128, 8, 64, oT["a"][:, :, i * 128:(i + 1) * 128], b_oT["a"])
    Kca = A.bf16([4, 512]); b_Kca = Buf()
    KsaT = A.bf16([4, 512]); b_KsaT = Buf()
    Vsa = A.bf16([4, 8, 65]); b_Vsa = Buf()
    memset(Vsa[:, :, :, 64:65], 1.0, [b_Vsa])
    for bb in range(4):
        ld(Kca, cak[bb].rearrange("(t p) n -> p t n", p=128), kcsem, [b_Kca], eng="pool")
        for t in range(4):
            ld(Vsa[:, t, :, 0:64], cav[bb, 128 * t:128 * (t + 1), :].rearrange("p (h d) -> p h d", h=8),
               kcsem, [b_Vsa], eng="pool")
        for t in range(4):
            pT, pTb = pT_r.next()
            transposes([(pT[:, c * 128:(c + 1) * 128], Kca[:, t, c * 128:(c + 1) * 128], identb) for c in range(4)],
                       [b_Kca, b_id], [pTb])
            cp(KsaT[:, :, t * 128:(t + 1) * 128], pT[:, 0:512].rearrange("p (c t) -> p c t", c=4), [pTb], [b_KsaT])
        for h in (0, 2, 4, 6, 1, 3, 5, 7):
            hp, hr = h // 2, (h % 2) * 64
            po, pob, off = po_slot(h, 8, 64)
            kts = []
            for t in range(4):
                d = dict(kT=KsaT[hr:hr + 64, hp, t * 128:(t + 1) * 128], nk=128, v=Vsa[:, t, h, :],
                         rd=[b_KsaT, b_Vsa], mask=None, brd=[b_bias])
                if t < 3:
                    d["bias"] = relc[:, h:h + 1]
                else:
                    d["pre"] = (BiasA[:, h, 0, 0:16], [b_bias])
                    d["bias"] = 0.0
                kts.append(d)
            kts.append(dict(kT=KsaN[hr:hr + 64, hp, bb * 16:(bb + 1) * 16], nk=16, v=VsaN[:, bb, h, :],
                            rd=[b_KsaN, b_VsaN], mask=None, brd=[b_bias],
                            pre=(BiasA[0:16, h, 1, 0:16], [b_bias]), bias=0.0))
            attn_head(QsaT[hr:hr + 64, hp, bb * 16:(bb + 1) * 16], 16, kts, po[0:16, off:off + 65], pob,
                      0.125, [b_QsaT])
            if h >= 6:
                attn_finish_group(16, 8, 64, h % 2)
        attn_finish(16, 8, 64, osT["a"][:, :, bb * 16:(bb + 1) * 16], b_osT["a"])
        pipe_drain()
    full_barrier()
    A.top = markAA

    psum_mode(False)
    oT["m"] = A.bf16([4, 2048])
    markM = A.top
    Qm_r = Rot([(A.bf16([4, 128]), Buf()) for _ in range(2)])
    QsmT = A.bf16([4, NS]); b_QsmT = Buf()
    wv, wb = wload(w_in[:, C_QM:C_QM + 512], 8, 512)
    pz, pzb = proj_tok(xsT, b_xsT, NS, wv, wb, 512)
    head_norm(pz, pzb, NS, 4, 128, gv["qm"], None, QsmT, b_QsmT)
    sc_m = float(128 ** -0.5)
    def prep_m(i):
        xv, xb_ = xT_of(i)
        pz, pzb = proj_tok(xv, xb_, 128, wv, wb, 512)
        qv, qvb = Qm_r.next()
        head_norm(pz, pzb, 128, 4, 128, gv["qm"], None, qv, qvb, defer_T=1)
        return qv, qvb

    psum_mode(True)
    nxt = prep_m(0)
    for i in range(NOWN):
        run_deferred(True)
        qv, qvb = nxt
        if i + 1 < NOWN:
            nxt = prep_m(i + 1)
        for h in range(4):
            po, pob, off = po_slot(h, 4, 128)
            kts = [dict(kT=mkT[:, h, m * 128:(m + 1) * 128], nk=128, v=mvA[:, m, h, :], rd=[b_mkT, b_mv],
                        mask=None, bias=0.0) for m in range(2)]
            attn_head(qv[:, h, :], 128, kts, po[:, off:off + 129], pob, sc_m, [qvb])
            if (h + 1) % (4 // 2) == 0:
                attn_finish_group(128, 4, 128, h // (4 // 2))
        attn_finish(128, 4, 128, oT["m"][:, :, i * 128:(i + 1) * 128], b_oT["m"])
    Kcm = A.bf16([2, 512]); b_Kcm = Buf()
    mksT = A.bf16([4, 256]); b_mksT = Buf()
    mvs = A.bf16([2, 4, 129]); b_mvs = Buf()
    memset(mvs[:, :, :, 128:129], 1.0, [b_mvs])
    for bb in range(4):
        ld(Kcm, cmk[bb].rearrange("(t p) n -> p t n", p=128), kcsem, [b_Kcm], eng="pool")
        for t in range(2):
            ld(mvs[:, t, :, 0:128], cmv[bb, 128 * t:128 * (t + 1), :].rearrange("p (h d) -> p h d", h=4),
               kcsem, [b_mvs], eng="pool")
        for t in range(2):
            pT, pTb = pT_r.next()
            transposes([(pT[:, c * 128:(c + 1) * 128], Kcm[:, t, c * 128:(c + 1) * 128], identb) for c in range(4)],
                       [b_Kcm, b_id], [pTb])
            cp(mksT[:, :, t * 128:(t + 1) * 128], pT[:, 0:512].rearrange("p (c t) -> p c t", c=4), [pTb], [b_mksT])
        for h in range(4):
            po, pob, off = po_slot(h, 4, 128)
            kts = [dict(kT=mksT[:, h, m * 128:(m + 1) * 128], nk=128, v=mvs[:, m, h, :], rd=[b_mksT, b_mvs],
                        mask=None, bias=0.0) for m in range(2)]
            attn_head(QsmT[:, h, bb * 16:(bb + 1) * 16], 16, kts, po[0:16, off:off + 129], pob, sc_m, [b_QsmT])
            if (h + 1) % (4 // 2) == 0:
                attn_finish_group(16, 4, 128, h // (4 // 2))
        attn_finish(16, 4, 128, osT["m"][:, :, bb * 16:(bb + 1) * 16], b_osT["m"])
        pipe_drain()
    full_barrier()
    A.top = markM

    psum_mode(False)
    hT_start = A.top
    hT = A.bf16([8, 2048]); b_hT = [Buf() for _ in range(4)]
    hsT = A.bf16([8, NS]); b_hsT = Buf()
    mark_h_end = A.top
    acc_r = Rot([(A.f32([512]), Buf()) for _ in range(2)])
    wp_r = Rot([(A.bf16([4 * 384]), Buf(), P.new_dma_sem()) for _ in range(2)])
    wps = {"a": w_pa, "b": w_pb, "m": w_pm}
    BR = "abm"
    groups = [(tg * 512, 512, b_xnT[4 * tg:4 * tg + 4], b_hT[tg]) for tg in range(4)]
    for nch in range(8):
        wt, wgb, wsem = w_r.next()
        wg = wt[:, 0:8 * 384].rearrange("p (c b n) -> p c b n", c=8, b=3)
        wpt, wpb, wpsem = wp_r.next()
        wp = wpt.rearrange("p (c b n) -> p c b n", c=4, b=3)
        for bi, br in enumerate(BR):
            c0 = C_G + bi * 1024 + nch * 128
            ld(wg[:, :, bi, :], w_in[:, c0:c0 + 128].rearrange("(c p) n -> p c n", p=128), wsem, [wgb], eng="pool")
            ld(wp[:, :, bi, :], wps[br][:, nch * 128:(nch + 1) * 128].rearrange("(c p) n -> p c n", p=128),
               wpsem, [wpb], eng="pool")
        for gidx in range(5):
            if gidx < 4:
                t0, ntok, xbufs, hb = groups[gidx]
                xv = xnT[:, :, t0:t0 + ntok]
                ov = {k: oT[k][:, :, t0:t0 + ntok] for k in BR}
                obufs = b_oT
                hdst = hT[:, nch, t0:t0 + ntok]
            else:
                ntok, xbufs, hb = NS, [b_xsT], b_hsT
                xv = xsT
                ov = osT
                obufs = b_osT
                hdst = hsT[:, nch, :]
            acc, accb = acc_r.next()
            for bi, br in enumerate(BR):
                pz, pzb = pz_r.next()
                mm(pz[:, 0:ntok], [(wg[:, c, bi, :], xv[:, c, :]) for c in range(8)], [wgb] + list(xbufs), [pzb])
                sg, sgb = st_r.next()
                act(sg[:, 0:ntok], pz[:, 0:ntok], AF.Sigmoid, [pzb], [sgb])
                pz2, pz2b = pz_r.next()
                mm(pz2[:, 0:ntok], [(wp[:, c, bi, :], ov[br][:, c, :]) for c in range(4)], [wpb, obufs[br]], [pz2b])
                if bi == 0:
                    tt(acc[:, 0:ntok], sg[:, 0:ntok], pz2[:, 0:ntok], ALU.mult, [sgb, pz2b], [accb])
                elif bi == 1:
                    tt(sg[:, 0:ntok], sg[:, 0:ntok], pz2[:, 0:ntok], ALU.mult, [sgb, pz2b], [sgb])
                    tt(acc[:, 0:ntok], acc[:, 0:ntok], sg[:, 0:ntok], ALU.add, [accb, sgb], [accb])
                else:
                    tt(sg[:, 0:ntok], sg[:, 0:ntok], pz2[:, 0:ntok], ALU.mult, [sgb, pz2b], [sgb])
                    tt(hdst, acc[:, 0:ntok], sg[:, 0:ntok], ALU.add, [accb, sgb], [hb])
    full_barrier()

    A.top = mark_x
    x1 = A.f32([NOWN, 1024]); b_x1 = [Buf() for _ in range(NOWN)]
    x1s = A.f32([1024]); b_x1s = Buf()
    assert A.top <= hT_start, ("x1 overlaps hT", A.top, hT_start)
    mark_x1_end = A.top
    A.top = mark_h_end
    wo0, wo0b = wload(w_o[:, 0:512], 8, 512)
    wo1, wo1b = wload(w_o[:, 512:1024], 8, 512)
    for i in range(NOWN + 1):
        if i < NOWN:
            nt_, src, hv, hb, dst, db = 128, xown[i * 128:(i + 1) * 128, :], hT[:, :, i * 128:(i + 1) * 128], b_hT[i // 4], x1[:, i, :], b_x1[i]
        else:
            nt_, src, hv, hb, dst, db = NS, xs, hsT, b_hsT, x1s, b_x1s
        xt, xb, xsem = xt_r.next()
        ld(xt[0:nt_], src, xsem, [xb])
        for half, (wo, wob) in enumerate(((wo0, wo0b), (wo1, wo1b))):
            pz, pzb = pz_r.next()
            mm(pz[0:nt_, :], [(hv[:, c, :], wo[:, c, :]) for c in range(8)], [hb, wob], [pzb])
            tt(dst[0:nt_, half * 512:(half + 1) * 512], xt[0:nt_, half * 512:(half + 1) * 512], pz[0:nt_, :],
               ALU.add, [xb, pzb], [db])
    full_barrier()

    A.top = mark_x1_end
    gffn = A.f32([1024])
    ld(gffn, g_ffn.partition_broadcast(128), csem, [b_g])
    xn2T = A.bf16([8, 512]); b_xn2T = Buf()
    hdnT = A.bf16([32, 512]); b_hdn = Buf()
    wd_r = Rot([(A.bf16([4, 1024]), Buf(), P.new_dma_sem()) for _ in range(2)])
    for stg in range(5):
        if stg < 4:
            ntok = 512
            tiles = [(x1[:, 4 * stg + k, :], b_x1[4 * stg + k], 128, y_p[(4 * stg + k) * 128:(4 * stg + k + 1) * 128, :]) for k in range(4)]
        else:
            ntok = NS
            tiles = [(x1s, b_x1s, NS, y_s)]
        for k, (xa, xab, nt_, _) in enumerate(tiles):
            norm_T((xa, xab), nt_, gffn, xn2T[:, :, k * 128:k * 128 + nt_], b_xn2T)
        for ffb in range(8):
            wv, wb = wload(w_up[:, ffb * 512:(ffb + 1) * 512], 8, 512)
            for fc in range(4):
                pz, pzb = pz_r.next()
                mm(pz[:, 0:ntok], [(wv[:, c, fc * 128:(fc + 1) * 128], xn2T[:, c, 0:ntok]) for c in range(8)],
                   [wb, b_xn2T], [pzb])
                rl, rlb = st_r.next()
                act(rl[:, 0:ntok], pz[:, 0:ntok], AF.Relu, [pzb], [rlb])
                tt(hdnT[:, ffb * 4 + fc, 0:ntok], rl[:, 0:ntok], rl[:, 0:ntok], ALU.mult, [rlb], [b_hdn])
        for blk in range(8):
            wd, wdb, wdsem = wd_r.next()
            ld(wd, w_down[blk * 512:(blk + 1) * 512, :].rearrange("(c p) n -> p c n", p=128), wdsem, [wdb], eng="pool")
            for k, (xa, xab, nt_, _) in enumerate(tiles):
                for half in range(2):
                    pz, pzb = pz_r.next()
                    mm(pz[0:nt_, :], [(hdnT[:, blk * 4 + c, k * 128:k * 128 + nt_], wd[:, c, half * 512:(half + 1) * 512])
                                      for c in range(4)], [b_hdn, wdb], [pzb])
                    tt(xa[0:nt_, half * 512:(half + 1) * 512], xa[0:nt_, half * 512:(half + 1) * 512], pz[0:nt_, :],
                       ALU.add, [xab, pzb], [xab])
        for k, (xa, xab, nt_, yd) in enumerate(tiles):
            store(yd, xa[0:nt_], [xab])

    P.emit(final_waits=STORE_SEMS)
    st.close()
    return nc


_NC = None


def _consts():
    tri = np.triu(np.ones((128, 128), np.float32))
    ones = np.ones((128, 128), np.float32)
    selmid = np.zeros((128, 128), np.float32); selmid[64, :] = 1.0
    sel0 = np.zeros((128, 128), np.float32); sel0[0, :] = 1.0
    cm = np.stack([tri, ones, selmid, sel0], axis=1).reshape(128, 512)
    return np.ascontiguousarray(cm), np.eye(128, dtype=np.float32)


def kernel(**inp):
    global _NC
    if _NC is None:
        _NC = build_nc()
    nc = _NC
    f = lambda a: np.ascontiguousarray(np.asarray(a, dtype=np.float32))
    cm, ident = _consts()
    xp = f(inp["x_prompt"]); xsm = f(inp["x_sample"]); memp = f(inp["mem_prompt"])
    shared = {
        "g_mix": f(inp["g_mix"]), "w_in": f(inp["w_in"][0]), "b_f": f(inp["b_f"]),
        "g_qa": f(inp["g_qa"]), "g_ka": f(inp["g_ka"]), "g_qb": f(inp["g_qb"]), "g_kb": f(inp["g_kb"]),
        "g_qm": f(inp["g_qm"]), "g_km": f(inp["g_km"]), "rel_bias": f(inp["rel_bias"][0]),
        "g_mem": f(inp["g_mem"]), "w_mkv": f(inp["w_mkv"][0]),
        "w_pa": f(inp["w_pa"][0]), "w_pb": f(inp["w_pb"][0]), "w_pm": f(inp["w_pm"][0]),
        "w_o": f(inp["w_o"][0]), "g_ffn": f(inp["g_ffn"]), "w_up": f(inp["w_up"][0]), "w_down": f(inp["w_down"][0]),
        "cmat": cm, "ident": ident,
    }
    in_maps = []
    for c in range(8):
        b, hf = c // 2, c % 2
        sb = slice(4 * c, 4 * c + 4)
        m = dict(shared)
        m["xpre"] = f(xp[b, 0:2048])
        m["xown"] = f(xp[b, hf * 2048:(hf + 1) * 2048])
        m["xs"] = f(xsm[sb].reshape(64, 1024))
        m["mem"] = f(memp[b])
        m["cak"] = f(inp["cache_a_k"][0, sb].reshape(4, 512, 512))
        m["cav"] = f(inp["cache_a_v"][0, sb].reshape(4, 512, 512))
        m["cbk"] = f(inp["cache_b_k"][0, sb].reshape(4, 2048, 512))
        m["cbv"] = f(inp["cache_b_v"][0, sb].reshape(4, 2048, 512))
        m["cblf"] = f(inp["cache_b_logf"][0, sb])
        m["cmk"] = f(inp["cache_mem_k"][0, sb].reshape(4, 256, 512))
        m["cmv"] = f(inp["cache_mem_v"][0, sb].reshape(4, 256, 512))
        m["flag"] = np.full((1, 1), float(hf), np.float32)
        in_maps.append(m)
    res = run_bass_kernel_spmd(nc, in_maps, core_ids=list(range(8)))
    R = res.results
    y_p = np.zeros((4, 4096, 1024), np.float32)
    y_s = np.zeros((32, 16, 1024), np.float32)
    akp = np.zeros((1, 4, 512, 8, 64), np.float32); avp = np.zeros_like(akp)
    bkp = np.zeros((1, 4, 4096, 8, 64), np.float32); bvp = np.zeros_like(bkp)
    blfp = np.zeros((1, 4, 4096, 8), np.float32)
    mkp = np.zeros((1, 4, 256, 4, 128), np.float32); mvp = np.zeros_like(mkp)
    aks = np.zeros((1, 32, 16, 8, 64), np.float32); avs = np.zeros_like(aks)
    bks = np.zeros_like(aks); bvs = np.zeros_like(aks)
    blfs = np.zeros((1, 32, 16, 8), np.float32)
    for c in range(8):
        b, hf = c // 2, c % 2
        r = R[c]
        s = slice(hf * 2048, (hf + 1) * 2048)
        y_p[b, s] = r["y_p"]
        y_s[4 * c:4 * c + 4] = r["y_s"].reshape(4, 16, 1024)
        bkp[0, b, s] = r["bkp"].reshape(2048, 8, 64)
        bvp[0, b, s] = r["bvp"].reshape(2048, 8, 64)
        blfp[0, b, s] = r["blfp"]
        if hf == 1:
            akp[0, b] = r["akp"].reshape(512, 8, 64)
            avp[0, b] = r["avp"].reshape(512, 8, 64)
        else:
            mkp[0, b] = r["mkp"].reshape(256, 4, 128)
            mvp[0, b] = r["mvp"].reshape(256, 4, 128)
        aks[0, 4 * c:4 * c + 4] = r["aks"].reshape(4, 16, 8, 64)
        avs[0, 4 * c:4 * c + 4] = r["avs"].reshape(4, 16, 8, 64)
        bks[0, 4 * c:4 * c + 4] = r["bks"].reshape(4, 16, 8, 64)
        bvs[0, 4 * c:4 * c + 4] = r["bvs"].reshape(4, 16, 8, 64)
        blfs[0, 4 * c:4 * c + 4] = r["blfs"].reshape(4, 16, 8)
    return (y_p, y_s, akp, avp, bkp, bvp, blfp, mkp, mvp, aks, avs, bks, bvs, blfs)
```

```python
import contextlib
import numpy as np
import concourse.bass as bass
import concourse.mybir as mybir
from concourse.bass_utils import run_bass_kernel_spmd

F32 = mybir.dt.float32
BF16 = mybir.dt.bfloat16
ALU = mybir.AluOpType
AF = mybir.ActivationFunctionType
AX = mybir.AxisListType

ENGS = ("pe", "act", "dve", "pool", "sp")
NEG = -30000.0
EPS = 1e-6
LREL = 768


ALLBUFS = []


class Buf:
    __slots__ = ("name", "w", "r", "excl")

    def __init__(self, name="", excl=False):
        self.name = name
        self.w = None
        self.r = []
        self.excl = excl
        ALLBUFS.append(self)


class Prog:
    def __init__(self, nc):
        self.nc = nc
        self.ops = {e: [] for e in ENGS}
        self.n_dma_sem = 0

    def new_dma_sem(self):
        self.n_dma_sem += 1
        return {"id": self.n_dma_sem - 1, "count": 0}

    def _deps_for(self, reads, writes):
        deps = []
        for b in reads:
            if b.w is not None:
                deps.append(b.w)
        for b in writes:
            if b.w is not None:
                deps.append(b.w)
            deps.extend(b.r)
        return deps

    def _post(self, tok, reads, writes):
        for b in writes:
            b.w = tok
            b.r = []
        for b in reads:
            if b not in writes:
                if len(b.r) > 64:
                    last = {}
                    for t in b.r:
                        last[(t[0], t[1])] = t
                    b.r = list(last.values())
                b.r.append(tok)

    def op(self, eng, fn, reads=(), writes=()):
        writes = list(writes) + [b for b in reads if b.excl and b not in writes]
        deps = self._deps_for(reads, writes)
        o = {"eng": eng, "fn": fn, "deps": deps, "kind": "c", "signal": False,
             "idx": len(self.ops[eng])}
        self.ops[eng].append(o)
        self._post(("e", eng, o["idx"]), reads, writes)
        return o

    def dma(self, eng, fn, sem, reads=(), writes=()):
        deps = self._deps_for(reads, writes)
        sem["count"] += 16
        o = {"eng": eng, "fn": fn, "deps": deps, "kind": "d", "sem": sem,
             "idx": len(self.ops[eng])}
        self.ops[eng].append(o)
        self._post(("d", sem["id"], sem["count"]), reads, writes)
        return o

    def barrier(self, bufs):
        deps = []
        for b in bufs:
            if b.w is not None:
                deps.append(b.w)
            deps.extend(b.r)
        for e in ENGS:
            o = {"eng": e, "fn": None, "deps": list(deps), "kind": "n",
                 "idx": len(self.ops[e])}
            self.ops[e].append(o)

    def emit(self, final_waits=()):
        nc = self.nc
        for e in ENGS:
            for o in self.ops[e]:
                for d in o["deps"]:
                    if d[0] == "e":
                        self.ops[d[1]][d[2]]["signal"] = True
        for e in ENGS:
            c = 0
            for o in self.ops[e]:
                if o["kind"] == "c" and o["signal"]:
                    c += 1
                    o["sigval"] = c
        with contextlib.ExitStack() as st:
            esem = {e: st.enter_context(nc.semaphore("s_" + e)) for e in ENGS}
            dsem = [st.enter_context(nc.semaphore("d%d" % i)) for i in range(self.n_dma_sem)]
            block = st.enter_context(nc.Block())
            prog = self

            def run(ename, eng):
                seen_e = {e: 0 for e in ENGS}
                seen_d = {}
                for o in prog.ops[ename]:
                    need_e = {}
                    need_d = {}
                    for d in o["deps"]:
                        if d[0] == "e":
                            v = prog.ops[d[1]][d[2]]["sigval"]
                            if v > need_e.get(d[1], 0):
                                need_e[d[1]] = v
                        else:
                            if d[2] > need_d.get(d[1], 0):
                                need_d[d[1]] = d[2]
                    for e2, v in need_e.items():
                        if v > seen_e[e2]:
                            eng.wait_ge(esem[e2], v)
                            seen_e[e2] = v
                    for s, v in need_d.items():
                        if v > seen_d.get(s, 0):
                            eng.wait_ge(dsem[s], v)
                            seen_d[s] = v
                    if o["kind"] == "c":
                        ins = o["fn"](eng)
                        if o["signal"]:
                            ins.then_inc(esem[ename], 1)
                    elif o["kind"] == "d":
                        ins = o["fn"](eng)
                        ins.then_inc(dsem[o["sem"]["id"]], 16)
                if ename == "sp":
                    for s in final_waits:
                        if s["count"] > 0:
                            eng.wait_ge(dsem[s["id"]], s["count"])

            @block.tensor
            def _(eng):
                run("pe", eng)

            @block.scalar
            def _(eng):
                run("act", eng)

            @block.vector
            def _(eng):
                run("dve", eng)

            @block.gpsimd
            def _(eng):
                run("pool", eng)

            @block.sync
            def _(eng):
                run("sp", eng)


def bcl(ap, m):
    return bass.AP(tensor=ap.tensor, offset=ap.offset,
                   ap=[list(x) for x in ap.ap] + [[0, m]])


def bcm(ap, k):
    a = [list(x) for x in ap.ap]
    return bass.AP(tensor=ap.tensor, offset=ap.offset, ap=[a[0], [0, k]] + a[1:])


class Rot:
    def __init__(self, items):
        self.items = items
        self.i = 0

    def next(self):
        it = self.items[self.i % len(self.items)]
        self.i += 1
        return it


C_QA, C_KA, C_VA, C_QB, C_KB, C_VB, C_FL, C_QM, C_G = 0, 512, 1024, 1536, 2048, 2560, 3072, 3080, 3592

NPRE = 16
NOWN = 16
NT = NPRE + NOWN
NS = 64
ARENA_W = 53100


def build_nc():
    nc = bass.Bass("TRN2", target_bir_lowering=False)
    P = Prog(nc)

    def din(name, shape):
        return nc.dram_tensor(name, list(shape), F32, kind="ExternalInput").ap()

    def dout(name, shape):
        return nc.dram_tensor(name, list(shape), F32, kind="ExternalOutput").ap()

    xpre = din("xpre", [2048, 1024]); xown = din("xown", [2048, 1024]); xs = din("xs", [NS, 1024])
    mem = din("mem", [256, 1024])
    cak = din("cak", [4, 512, 512]); cav = din("cav", [4, 512, 512])
    cbk = din("cbk", [4, 2048, 512]); cbv = din("cbv", [4, 2048, 512]); cblf = din("cblf", [4, 2048, 8])
    cmk = din("cmk", [4, 256, 512]); cmv = din("cmv", [4, 256, 512])
    g_mix = din("g_mix", [1, 1024]); w_in = din("w_in", [1024, 6664]); b_f = din("b_f", [1, 8])
    g_qa = din("g_qa", [1, 64]); g_ka = din("g_ka", [1, 64]); g_qb = din("g_qb", [1, 64]); g_kb = din("g_kb", [1, 64])
    g_qm = din("g_qm", [1, 128]); g_km = din("g_km", [1, 128]); rel_bias = din("rel_bias", [257, 8])
    g_mem = din("g_mem", [1, 1024]); w_mkv = din("w_mkv", [1024, 1024])
    w_pa = din("w_pa", [512, 1024]); w_pb = din("w_pb", [512, 1024]); w_pm = din("w_pm", [512, 1024])
    w_o = din("w_o", [1024, 1024]); g_ffn = din("g_ffn", [1, 1024])
    w_up = din("w_up", [1024, 4096]); w_down = din("w_down", [4096, 1024])
    cmat = din("cmat", [128, 4 * 128]); ident = din("ident", [128, 128]); flag = din("flag", [1, 1])

    y_p = dout("y_p", [2048, 1024]); y_s = dout("y_s", [NS, 1024])
    akp = dout("akp", [512, 512]); avp = dout("avp", [512, 512])
    bkp = dout("bkp", [2048, 512]); bvp = dout("bvp", [2048, 512]); blfp = dout("blfp", [2048, 8])
    mkp = dout("mkp", [256, 512]); mvp = dout("mvp", [256, 512])
    aks = dout("aks", [NS, 512]); avs = dout("avs", [NS, 512])
    bks = dout("bks", [NS, 512]); bvs = dout("bvs", [NS, 512]); blfs = dout("blfs", [NS, 8])
    wupb = nc.dram_tensor("wupb", [1024, 4096], BF16, kind="Internal").ap()
    wdnb = nc.dram_tensor("wdnb", [4096, 1024], BF16, kind="Internal").ap()
    relE = nc.dram_tensor("relE", [8, LREL], F32, kind="Internal").ap()
    relD = nc.dram_tensor("relD", [128, 8 * LREL], F32, kind="Internal").ap()

    st = contextlib.ExitStack()
    arena = st.enter_context(nc.sbuf_tensor("arena", [128, ARENA_W], F32))
    psum = [st.enter_context(nc.psum_tensor("ps%d" % i, [128, 512], F32)) for i in range(8)]

    class Arena:
        def __init__(self):
            self.top = 0

        def f32(self, dims, parts=128):
            n = int(np.prod(dims))
            o = self.top
            self.top += n
            assert self.top <= ARENA_W, ("arena overflow", self.top)
            ap = arena[0:parts, o:o + n]
            return self._view(ap, dims)

        def bf16(self, dims, parts=128):
            n = int(np.prod(dims))
            w = (n + 1) // 2
            o = self.top
            self.top += w
            assert self.top <= ARENA_W, ("arena overflow", self.top)
            ap = arena[0:parts, o:o + w].bitcast(BF16)[:, 0:n]
            return self._view(ap, dims)

        @staticmethod
        def _view(ap, dims):
            if len(dims) == 1:
                return ap
            if len(dims) == 2:
                return ap.rearrange("p (a b) -> p a b", a=dims[0])
            if len(dims) == 3:
                return ap.rearrange("p (a b c) -> p a b c", a=dims[0], b=dims[1])
            if len(dims) == 4:
                return ap.rearrange("p (a b c d) -> p a b c d", a=dims[0], b=dims[1], c=dims[2])
            raise ValueError

    A = Arena()
    out_sems = [P.new_dma_sem() for _ in range(4)]
    osr = Rot(out_sems)

    def mm(out, pairs, reads, writes):
        pairs = list(pairs)

        def f(e):
            n = len(pairs)
            for k, (l, r) in enumerate(pairs):
                i = e.matmul(out, lhsT=l, rhs=r, start=(k == 0), stop=(k == n - 1))
            return i
        P.op("pe", f, reads, writes)

    def transposes(items, reads, writes):
        items = list(items)

        def f(e):
            for (o, i_, idn) in items:
                ins = e.transpose(out=o, in_=i_, identity=idn)
            return ins
        P.op("pe", f, reads, writes)

    def act(out, in_, func, reads, writes, bias=None, scale=None, accum=None, eng="act"):
        kw = {}
        if bias is not None:
            kw["bias"] = bias
        if scale is not None:
            kw["scale"] = scale
        if accum is not None:
            kw["accum_out"] = accum
        P.op(eng, lambda e: e.activation(out=out, in_=in_, func=func, **kw), reads, writes)

    def tt(out, in0, in1, op, reads, writes, eng="dve"):
        P.op(eng, lambda e: e.tensor_tensor(out=out, in0=in0, in1=in1, op=op), reads, writes)

    def ts(out, in0, s1, s2, op0, op1, reads, writes, eng="dve"):
        if s2 is None:
            P.op(eng, lambda e: e.tensor_scalar(out=out, in0=in0, scalar1=s1, scalar2=None, op0=op0), reads, writes)
        else:
            P.op(eng, lambda e: e.tensor_scalar(out=out, in0=in0, scalar1=s1, scalar2=s2, op0=op0, op1=op1), reads, writes)

    def stt(out, in0, scalar, in1, op0, op1, reads, writes, eng="dve"):
        P.op(eng, lambda e: e.scalar_tensor_tensor(out=out, in0=in0, scalar=scalar, in1=in1, op0=op0, op1=op1), reads, writes)

    def cp(out, in_, reads, writes, eng="dve"):
        P.op(eng, lambda e: e.tensor_copy(out=out, in_=in_), reads, writes)

    def recip(out, in_, reads, writes):
        P.op("dve", lambda e: e.reciprocal(out=out, in_=in_), reads, writes)

    def red(out, in_, reads, writes):
        P.op("dve", lambda e: e.tensor_reduce(out=out, in_=in_, axis=AX.X, op=ALU.add), reads, writes)

    def memset(ap, val, writes, eng="dve"):
        P.op(eng, lambda e: e.memset(ap, val), (), writes)

    BUFSEM = {}

    def sem_of(b):
        if id(b) not in BUFSEM:
            BUFSEM[id(b)] = P.new_dma_sem()
        return BUFSEM[id(b)]

    STORE_SEMS = []

    def ld(out, in_, sem, writes, reads=(), eng="sp", slow=False):
        sem = sem_of(writes[0])
        if slow:
            P.dma(eng, lambda e: e.dma_start(out=out, in_=in_, allow_slow_non_contiguous=True), sem, reads, writes)
        else:
            P.dma(eng, lambda e: e.dma_start(out=out, in_=in_), sem, reads, writes)

    def store(out, in_, reads):
        sem = sem_of(reads[0])
        if sem not in STORE_SEMS:
            STORE_SEMS.append(sem)
        P.dma("sp", lambda e: e.dma_start(out=out, in_=in_), sem, reads, ())

    def rstd_from_ss(ap, n, bufs):
        act(ap, ap, AF.Sqrt, list(bufs) + [b_eps], bufs, bias=EPSB[0:ap.shape[0], 0:1], scale=1.0 / n)
        recip(ap, ap, bufs, bufs)

    csem = P.new_dma_sem()
    identb = A.bf16([128]); b_id = Buf()
    ld(identb, ident, csem, [b_id], eng="pool")
    trif = A.f32([4, 128]); b_cm = Buf()
    ld(trif, cmat.rearrange("p (a b) -> p a b", a=4), csem, [b_cm])
    TRI, ONES, SELMID, SEL0 = trif[:, 0, :], trif[:, 1, :], trif[:, 2, :], trif[:, 3, :]
    trib = A.bf16([128]); b_trib = Buf()
    cp(trib, TRI, [b_cm], [b_trib])
    gmix = A.f32([1024]); b_g = Buf()
    ld(gmix, g_mix.partition_broadcast(128), csem, [b_g])
    gv = {}
    for nm, src, n in (("qa", g_qa, 64), ("ka", g_ka, 64), ("qb", g_qb, 64), ("kb", g_kb, 64),
                       ("qm", g_qm, 128), ("km", g_km, 128)):
        gv[nm] = A.f32([n])
        ld(gv[nm], src.partition_broadcast(128), csem, [b_g])
    bfb = A.f32([8])
    ld(bfb, b_f.partition_broadcast(128), csem, [b_g])
    EPSB = A.f32([1]); b_eps = Buf()
    memset(EPSB, EPS, [b_eps])
    negflag = A.f32([1]); b_nf = Buf()
    ld(negflag, flag.partition_broadcast(128), csem, [b_nf])
    ts(negflag, negflag, -1.0, -NEG, ALU.add, ALU.mult, [b_nf], [b_nf])

    b_wupb = [Buf() for _ in range(2)]
    b_wdnb = [Buf() for _ in range(2)]
    for k in range(2):
        ld(wupb[k * 512:(k + 1) * 512, :].rearrange("(c p) n -> p c n", p=128),
           w_up[k * 512:(k + 1) * 512, :].rearrange("(c p) n -> p c n", p=128), None, [b_wupb[k]], eng="pool")
    for k in range(2):
        ld(wdnb[k * 2048:(k + 1) * 2048, :].rearrange("(c p) n -> p c n", p=128),
           w_down[k * 2048:(k + 1) * 2048, :].rearrange("(c p) n -> p c n", p=128), None, [b_wdnb[k]], eng="pool")

    xt_r = Rot([(A.f32([1024]), Buf(), P.new_dma_sem()) for _ in range(2)])
    sq_r = Rot([(A.bf16([1024]), Buf()) for _ in range(1)])
    xn_r = Rot([(A.bf16([1024]), Buf()) for _ in range(2)])
    st_r = Rot([(A.f32([512]), Buf()) for _ in range(3)])
    sb_r = Rot([(A.bf16([512]), Buf()) for _ in range(2)])
    sm_r = Rot([(A.f32([16]), Buf()) for _ in range(4)])
    pt_r = Rot([(A.bf16([128]), Buf()) for _ in range(12)])
    sa_r = Rot([(A.f32([128]), Buf()) for _ in range(4)])
    fin_r = Rot([(A.bf16([512]), Buf()) for _ in range(2)])
    bias_r = Rot([(A.f32([8]), Buf()) for _ in range(4)])
    w_r = Rot([(A.bf16([4224]), Buf(), P.new_dma_sem()) for _ in range(2)])
    PB = [Buf(excl=True) for _ in range(8)]
    pz_r = Rot([(psum[0], PB[0]), (psum[1], PB[1])])
    pT_r = Rot([(psum[2].bitcast(BF16), PB[2]), (psum[3].bitcast(BF16), PB[3])])
    pS_r = Rot([(psum[4], PB[4]), (psum[5], PB[5])])
    pO = [(psum[6], PB[6]), (psum[7], PB[7])]

    PIPEQ = []
    PST = {"s": 0, "r": 0}
    PFN = {}
    PDEF = []

    def defer(fn, delay):
        PDEF.append([PST["r"] + delay, fn])

    def run_deferred(all_=False):
        while True:
            todo = [d for d in PDEF if all_ or d[0] <= PST["r"]]
            if not todo:
                break
            for d in todo:
                PDEF.remove(d)
            for d in todo:
                d[1]()

    def pipe_pump(drain):
        nb = len(pS_r.items)
        while True:
            if PST["s"] < len(PIPEQ) and PST["s"] - PST["r"] < nb:
                ent = PIPEQ[PST["s"]]
                ent["bank"] = PFN["S"](ent["q"])
                PST["s"] += 1
                continue
            if PST["r"] < PST["s"] and (drain or PST["s"] - PST["r"] >= nb):
                ent = PIPEQ[PST["r"]]
                PFN["R"](ent["q"], *ent["bank"])
                for cb in ent["after"]:
                    cb()
                ent["after"] = []
                PST["r"] += 1
                run_deferred()
                continue
            break
        if drain:
            run_deferred(True)

    def pipe_drain():
        pipe_pump(True)

    def psum_mode(attn):
        pipe_drain()
        if attn:
            pz_r.items = [(psum[0], PB[0])]
            pT_r.items = [(psum[2].bitcast(BF16), PB[2])]
            pS_r.items = [(psum[1], PB[1]), (psum[3], PB[3]), (psum[4], PB[4]), (psum[5], PB[5])]
        else:
            pz_r.items = [(psum[0], PB[0]), (psum[1], PB[1]), (psum[4], PB[4]), (psum[5], PB[5])]
            pT_r.items = [(psum[2].bitcast(BF16), PB[2]), (psum[3].bitcast(BF16), PB[3])]
            pS_r.items = [(psum[4], PB[4]), (psum[5], PB[5])]

    psum_mode(False)

    base_top = A.top

    def wload(src, kc, n):
        wt, wb, wsem = w_r.next()
        v = wt[:, 0:kc * n].rearrange("p (c n) -> p c n", c=kc)
        ld(v, src.rearrange("(c p) n -> p c n", p=128), wsem, [wb], eng="pool")
        return v, wb

    def norm_T(xrows, nt, gbc, dstT, dstbuf, x_keep=None):
        if isinstance(xrows, tuple):
            xt, xb = xrows
        else:
            xt, xb, xsem = xt_r.next()
            ld(xt[0:nt], xrows, xsem, [xb])
        sq, sqb = sq_r.next()
        sm, smb = sm_r.next()
        act(sq[0:nt], xt[0:nt], AF.Square, [xb], [sqb, smb], accum=sm[0:nt, 0:1])
        rstd_from_ss(sm[0:nt, 0:1], 1024.0, [smb])
        xn, xnb = xn_r.next()
        stt(xn[0:nt], xt[0:nt], sm[0:nt, 0:1], gbc[0:nt], ALU.mult, ALU.mult, [xb, smb, b_g], [xnb])
        pT, pTb = pT_r.next()
        transposes([(pT[:, c * 128:c * 128 + nt], xn[0:nt, c * 128:(c + 1) * 128], identb[0:nt, 0:nt])
                    for c in range(8)], [xnb, b_id], [pTb])
        cp(dstT, pT.rearrange("p (c t) -> p c t", c=8)[:, :, 0:nt], [pTb], [dstbuf])

    def norm_T_staged(xrows, nt, gbc, dstT, dstbuf):
        xt, xb, xsem = xt_r.next()
        ld(xt[0:nt], xrows, xsem, [xb])
        sq, sqb = sq_r.next()
        sm, smb = sm_r.next()
        hold = {}

        def sA():
            act(sq[0:nt], xt[0:nt], AF.Square, [xb], [sqb, smb], accum=sm[0:nt, 0:1])
            act(sm[0:nt, 0:1], sm[0:nt, 0:1], AF.Sqrt, [smb, b_eps], [smb], bias=EPSB[0:nt, 0:1], scale=1.0 / 1024.0)

        def sB():
            recip(sm[0:nt, 0:1], sm[0:nt, 0:1], [smb], [smb])
            xn, xnb = xn_r.next()
            stt(xn[0:nt], xt[0:nt], sm[0:nt, 0:1], gbc[0:nt], ALU.mult, ALU.mult, [xb, smb, b_g], [xnb])
            hold["pT"] = pT_r.next()
            pT, pTb = hold["pT"]
            transposes([(pT[:, c * 128:c * 128 + nt], xn[0:nt, c * 128:(c + 1) * 128], identb[0:nt, 0:nt])
                        for c in range(8)], [xnb, b_id], [pTb])

        def sC():
            pT, pTb = hold["pT"]
            cp(dstT, pT.rearrange("p (c t) -> p c t", c=8)[:, :, 0:nt], [pTb], [dstbuf])
        return sA, sB, sC

    def proj_tok(xT, xTb, nt, wv, wb, ncols, c0=0):
        pz, pzb = pz_r.next()
        mm(pz[0:nt, 0:ncols], [(xT[:, c, :], wv[:, c, c0:c0 + ncols]) for c in range(8)], [xTb, wb], [pzb])
        return pz, pzb

    hn_sm_r = Rot([(A.f32([16]), Buf()) for _ in range(3)])

    def head_norm(pz, pzb, nt, nh, dh, g, out_dram, dstT=None, dstTb=None, pair=True, defer_T=0, staged=False):
        n = nh * dh
        sq, sqb = sq_r.next()
        sm, smb = hn_sm_r.next() if (defer_T or staged) else sm_r.next()
        stg, stb = st_r.next()
        sbf, sbb = (None, None)
        if dstT is not None:
            sbf, sbb = sb_r.next()
        v3 = lambda a: a[0:nt, 0:n].rearrange("p (h d) -> p h d", h=nh)

        def stage_act():
            def fsq(e):
                for h in range(nh):
                    ins = e.activation(out=sq[0:nt, h * dh:(h + 1) * dh], in_=pz[0:nt, h * dh:(h + 1) * dh],
                                       func=AF.Square, accum_out=sm[0:nt, h:h + 1])
                return ins
            P.op("act", fsq, [pzb], [sqb, smb])
            act(sm[0:nt, 0:nh], sm[0:nt, 0:nh], AF.Sqrt, [smb, b_eps], [smb], bias=EPSB[0:nt, 0:1], scale=1.0 / dh)

        def stage_dve():
            recip(sm[0:nt, 0:nh], sm[0:nt, 0:nh], [smb], [smb])
            tt(v3(stg), v3(pz), bcl(sm[0:nt, 0:nh], dh), ALU.mult, [pzb, smb], [stb])
            if out_dram is not None:
                tt(v3(stg), v3(stg), bcm(g[0:nt], nh), ALU.mult, [stb, b_g], [stb])
                store(out_dram, stg[0:nt, 0:n], [stb])
                if dstT is not None:
                    cp(sbf[0:nt, 0:n], stg[0:nt, 0:n], [stb], [sbb])
            else:
                tt(v3(sbf), v3(stg), bcm(g[0:nt], nh), ALU.mult, [stb, b_g], [sbb])

        def stage_pe():
            if dstT is None:
                return
            pT, pTb = pT_r.next()
            nch = n // 128
            transposes([(pT[:, c * 128:c * 128 + nt], sbf[0:nt, c * 128:(c + 1) * 128], identb[0:nt, 0:nt])
                        for c in range(nch)], [sbb, b_id], [pTb])
            cp(dstT, pT[:, 0:nch * 128].rearrange("p (c t) -> p c t", c=nch)[:, :, 0:nt], [pTb], [dstTb])

        if staged:
            return stage_act, stage_dve, stage_pe
        if defer_T:
            defer(stage_act, defer_T)
            defer(stage_dve, 2 * defer_T)
            defer(stage_pe, 3 * defer_T)
        else:
            stage_act()
            stage_dve()
            stage_pe()

    def pipelined(n_tiles, make):
        st_ = {}
        for it in range(n_tiles + 2):
            if it < n_tiles:
                st_[it] = make(it)
                st_[it][0]()
            if 0 <= it - 1 < n_tiles:
                st_[it - 1][1]()
            if 0 <= it - 2 < n_tiles:
                st_.pop(it - 2)[2]()

    def v_store(pz, pzb, nt, nh, dh, out_dram, vaug, vb):
        n = nh * dh
        stg, stb = st_r.next()
        cp(stg[0:nt, 0:n], pz[0:nt, 0:n], [pzb], [stb])
        if out_dram is not None:
            store(out_dram, stg[0:nt, 0:n], [stb])
        act(vaug[0:nt, :, 0:dh], stg[0:nt, 0:n].rearrange("p (h d) -> p h d", h=nh), AF.Copy, [stb], [vb])

    def logf_a(xT, xTb, nt, wv, wb, c0, LFdst, lfb):
        pz, pzb = pz_r.next()
        mm(pz[0:nt, 0:8], [(xT[:, c, :], wv[:, c, c0:c0 + 8]) for c in range(8)], [xTb, wb], [pzb])
        tt(LFdst, pz[0:nt, 0:8], bfb[0:nt], ALU.add, [pzb, b_g], [lfb])

    def logf_b(LFdst, lfb):
        act(LFdst, LFdst, AF.Exp, [lfb], [lfb], scale=-1.0)
        act(LFdst, LFdst, AF.Ln, [lfb], [lfb], bias=1.0)

    def logf_c(LFdst, lfb, out_dram):
        ts(LFdst, LFdst, -1.0, None, ALU.mult, None, [lfb], [lfb])
        if out_dram is not None:
            store(out_dram, LFdst, [lfb])

    def logf_tile(xT, xTb, nt, wv, wb, c0, LFdst, lfb, out_dram):
        logf_a(xT, xTb, nt, wv, wb, c0, LFdst, lfb)
        logf_b(LFdst, lfb)
        logf_c(LFdst, lfb, out_dram)

    def cumsum_tile(LF, lfb, nt, R, rb, Cdst, cb, first):
        pz, pzb = pz_r.next()
        if first:
            mm(pz[0:nt, 0:8], [(TRI[0:nt, 0:nt], LF)], [lfb, b_cm], [pzb])
        else:
            mm(pz[0:nt, 0:8], [(TRI[0:nt, 0:nt], LF), (ONES[:, 0:nt], R)], [lfb, b_cm, rb], [pzb])
        cp(Cdst, pz[0:nt, 0:8], [pzb], [cb])
        if first:
            memset(R, 0.0, [rb])
        tt(R[0:nt], R[0:nt], LF, ALU.add, [rb, lfb], [rb])

    def run_sums(LFs_list, Rall, rb, first_is_zero):
        if first_is_zero:
            memset(Rall[:, 0, :], 0.0, [rb])
        for t, (lf, lfb, nt) in enumerate(LFs_list):
            if nt < 128:
                continue
            tt(Rall[:, t + 1, :], Rall[:, t, :], lf, ALU.add, [rb, lfb], [rb])

    def cumsum_par(LF, lfb, nt, Rt, rb, Cdst, cb):
        pz, pzb = pz_r.next()
        mm(pz[0:nt, 0:8], [(TRI[0:nt, 0:nt], LF), (ONES[:, 0:nt], Rt)], [lfb, b_cm, rb], [pzb])
        cp(Cdst, pz[0:nt, 0:8], [pzb], [cb])

    def cumsum_batch(LFv, lfbufs, Rv, rb, Cv, cbufs, ncol):
        pz, pzb = pz_r.next()
        mm(pz[:, 0:ncol], [(TRI, LFv), (ONES, Rv)], list(lfbufs) + [b_cm, rb], [pzb])
        cp(Cv, pz[:, 0:ncol], [pzb], list(cbufs))

    def bcast_row(src, sb_, nrows, SEL, dst, db):
        pz, pzb = pz_r.next()
        mm(pz[:, 0:8], [(SEL[0:nrows, :], src)], [sb_, b_cm], [pzb])
        cp(dst, pz[:, 0:8], [pzb], [db])

    def attn_head(qT, nq, ktiles, po_ap, pob, scale, first_reads):
        n = len(ktiles)
        PENDING.append([dict(kt=kt, qT=qT, nq=nq, po_ap=po_ap, pob=pob, scale=scale,
                             first_reads=list(first_reads), start=(idx == 0), stop=(idx == n - 1))
                        for idx, kt in enumerate(ktiles)])

    PENDING = []
    LA = 4

    def attn_flush():
        calls = list(PENDING)
        del PENDING[:]
        steps = [sp_ for c_ in calls for sp_ in c_]
        quads = [steps[i:i + 4] for i in range(0, len(steps), 4)]

        def uniq(bl):
            out = []
            for b in bl:
                if b not in out:
                    out.append(b)
            return out

        def emit_S(q):
            bank, bb = pS_r.next()
            q = list(q)

            def f(e):
                for k, sp_ in enumerate(q):
                    kt = sp_["kt"]
                    ins = e.matmul(bank[0:kt["nk"], k * 128:k * 128 + sp_["nq"]], lhsT=kt["kT"], rhs=sp_["qT"],
                                   start=True, stop=True)
                return ins
            rds = []
            for sp_ in q:
                rds += list(sp_["kt"]["rd"]) + sp_["first_reads"]
            P.op("pe", f, uniq(rds), [bb])
            return bank, bb

        def emit_rest(q, bank, bb):
            q = list(q)
            pts = [pt_r.next() for _ in q]
            sas = {}
            pre_items = []
            for k, sp_ in enumerate(q):
                if sp_["kt"].get("pre") is not None:
                    sas[k] = sa_r.next()
                    pre_items.append(k)
            if pre_items:
                def fd(e):
                    for k in pre_items:
                        sp_ = q[k]
                        kt = sp_["kt"]
                        nk, nq = kt["nk"], sp_["nq"]
                        ins = e.scalar_tensor_tensor(out=sas[k][0][0:nk, 0:nq], in0=bank[0:nk, k * 128:k * 128 + nq],
                                                     scalar=sp_["scale"], in1=kt["pre"][0], op0=ALU.mult, op1=ALU.add)
                    return ins
                rds = [bb]
                for k in pre_items:
                    rds += list(q[k]["kt"]["pre"][1])
                P.op("dve", fd, uniq(rds), [sas[k][1] for k in pre_items])

            def fa(e):
                for k, sp_ in enumerate(q):
                    kt = sp_["kt"]
                    nk, nq = kt["nk"], sp_["nq"]
                    if k in sas:
                        ins = e.activation(out=pts[k][0][0:nk, 0:nq], in_=sas[k][0][0:nk, 0:nq], func=AF.Exp,
                                           bias=kt.get("bias", 0.0))
                    else:
                        ins = e.activation(out=pts[k][0][0:nk, 0:nq], in_=bank[0:nk, k * 128:k * 128 + nq], func=AF.Exp,
                                           bias=kt.get("bias", 0.0), scale=sp_["scale"])
                return ins
            rds = [bb] + [sas[k][1] for k in pre_items]
            for sp_ in q:
                rds += list(sp_["kt"].get("brd", []))
            P.op("act", fa, uniq(rds), [p_[1] for p_ in pts])
            for k, sp_ in enumerate(q):
                kt = sp_["kt"]
                if kt.get("mask") is not None:
                    nk, nq = kt["nk"], sp_["nq"]
                    tt(pts[k][0][0:nk, 0:nq], pts[k][0][0:nk, 0:nq], kt["mask"][0:nk, 0:nq], ALU.mult,
                       [pts[k][1], b_trib, b_mask0], [pts[k][1]], eng="pool")
            def fp(e):
                for k, sp_ in enumerate(q):
                    kt = sp_["kt"]
                    nk, nq = kt["nk"], sp_["nq"]
                    ins = e.matmul(sp_["po_ap"], lhsT=pts[k][0][0:nk, 0:nq], rhs=kt["v"],
                                   start=sp_["start"], stop=sp_["stop"])
                return ins
            rds = [p_[1] for p_ in pts]
            for sp_ in q:
                rds += list(sp_["kt"]["rd"])
            P.op("pe", fp, uniq(rds), uniq([sp_["pob"] for sp_ in q]))

        for q in quads:
            PIPEQ.append({"q": q, "bank": None, "after": []})
        PFN["S"], PFN["R"] = emit_S, emit_rest
        pipe_pump(False)


    def mm_acc(out, l, r, start, stop, reads, writes):
        P.op("pe", lambda e: e.matmul(out, lhsT=l, rhs=r, start=start, stop=stop), reads, writes)

    FIN = {}

    def attn_finish_group(nq, nh, dh, half):
        attn_flush()
        PIPEQ[-1]["after"].append(lambda: fin_group(nq, nh, dh, half))
        if PST["r"] >= len(PIPEQ):
            raise RuntimeError("pipeline ran ahead of finish")

    def fin_group(nq, nh, dh, half):
        hpb = nh // 2
        if half == 0:
            FIN["sbf"] = fin_r.next()
        sbf, sbb = FIN["sbf"]
        po, pob = pO[half]
        pv = po[0:nq, 0:hpb * (dh + 1)].rearrange("p (h d) -> p h d", h=hpb)
        sm, smb = sm_r.next()
        recip(sm[0:nq, 0:hpb], pv[:, :, dh], [pob], [smb])
        if dh == 64:
            ov_ = sbf[0:nq, 0:512].rearrange("p (hp two d) -> p hp two d", two=2, d=64)[:, :, half, :]
        else:
            ov_ = sbf[0:nq, half * hpb * dh:(half + 1) * hpb * dh].rearrange("p (h d) -> p h d", h=hpb)
        tt(ov_, pv[:, :, 0:dh], bcl(sm[0:nq, 0:hpb], dh), ALU.mult, [pob, smb], [sbb])

    def attn_finish(nq, nh, dh, dstT, dstTb):
        PIPEQ[-1]["after"].append(lambda: fin_tile(nq, dstT, dstTb))

    def fin_tile(nq, dstT, dstTb):
        sbf, sbb = FIN["sbf"]

        def part2():
            pT, pTb = pT_r.next()
            transposes([(pT[:, c * 128:c * 128 + nq], sbf[0:nq, c * 128:(c + 1) * 128], identb[0:nq, 0:nq])
                        for c in range(4)], [sbb, b_id], [pTb])
            cp(dstT, pT[:, 0:512].rearrange("p (c t) -> p c t", c=4)[:, :, 0:nq], [pTb], [dstTb])
        defer(part2, 2)

    def po_slot(h, nh, dh):
        hpb = nh // 2
        if dh == 64:
            po, pob = pO[h % 2]
            hh = h // 2
        else:
            po, pob = pO[h // hpb]
            hh = h % hpb
        return po, pob, hh * (dh + 1)

    BiasA = A.f32([8, 2, 128]); b_bias = Buf()
    relc = A.f32([8]); relcnf = A.f32([8])
    ld(relc, rel_bias[256:257, :].partition_broadcast(128), csem, [b_bias])
    ts(relcnf, relc, negflag[:, 0:1], None, ALU.add, None, [b_bias, b_nf], [b_bias])
    mask0 = A.bf16([128]); b_mask0 = Buf()
    memset(mask0, 1.0, [b_mask0])
    memset(mask0[0:64, 64:128], 0.0, [b_mask0])
    mark0 = A.top
    relx = A.f32([LREL], parts=8); b_relx = Buf()
    rsem = P.new_dma_sem()
    memset(relx, 0.0, [b_relx])
    sm8 = A.f32([1], parts=8)
    ld(sm8, rel_bias[256:257, :].rearrange("i h -> h i"), rsem, [b_relx], slow=True)
    ts(relx, relx, sm8[:, 0:1], None, ALU.add, None, [b_relx], [b_relx])
    ld(relx[:, 0:257], rel_bias.rearrange("i h -> h i"), rsem, [b_relx], slow=True)
    b_relE = Buf()
    ld(relE, relx, rsem, [b_relE], reads=[b_relx])
    relrep = A.f32([8, LREL]); b_rr = Buf()
    ld(relrep, relE.partition_broadcast(128), rsem, [b_rr], reads=[b_relE])
    b_relD = Buf()
    ld(relD, relrep.rearrange("p a b -> p (a b)"), rsem, [b_relD], reads=[b_rr])
    for t in (3, 4):
        src = bass.AP(tensor=relD.tensor, offset=640 - 128 * t,
                      ap=[[8 * LREL - 1, 128], [LREL, 8], [1, 128]])
        ld(BiasA[:, :, t - 3, :], src, rsem, [b_bias], reads=[b_relD])
    memset(BiasA[64:128, :, 1, 0:64], NEG, [b_bias])
    P.barrier(list(ALLBUFS))
    A.top = mark0

    mkT = A.bf16([4, 256]); b_mkT = Buf()
    mvA = A.bf16([2, 4, 129]); b_mv = Buf()
    memset(mvA[:, :, :, 128:129], 1.0, [b_mv])
    mark1 = A.top
    gmem = A.f32([1024]); b_gm = Buf()
    ld(gmem, g_mem.partition_broadcast(128), csem, [b_g])
    xmT = A.bf16([8, 256]); b_xmT = Buf()
    for t in range(2):
        norm_T(mem[t * 128:(t + 1) * 128, :], 128, gmem, xmT[:, :, t * 128:(t + 1) * 128], b_xmT)
    wv, wb = wload(w_mkv[:, 0:512], 8, 512)
    for t in range(2):
        pz, pzb = proj_tok(xmT[:, :, t * 128:(t + 1) * 128], b_xmT, 128, wv, wb, 512)
        head_norm(pz, pzb, 128, 4, 128, gv["km"], mkp[t * 128:(t + 1) * 128, :],
                  mkT[:, :, t * 128:(t + 1) * 128], b_mkT)
    wv, wb = wload(w_mkv[:, 512:1024], 8, 512)
    for t in range(2):
        pz, pzb = proj_tok(xmT[:, :, t * 128:(t + 1) * 128], b_xmT, 128, wv, wb, 512)
        v_store(pz, pzb, 128, 4, 128, mvp[t * 128:(t + 1) * 128, :], mvA[:, t], b_mv)
    P.barrier(list(ALLBUFS))
    A.top = mark1

    def full_barrier():
        pipe_drain()
        P.barrier(list(ALLBUFS))

    mark_x = A.top
    xnT = A.bf16([8, 2048]); b_xnT = [Buf() for _ in range(NOWN)]
    xpT4 = A.bf16([8, 512]); b_xp4 = Buf()
    xsT = A.bf16([8, NS]); b_xsT = Buf()
    osT = {k: A.bf16([4, NS]) for k in "bam"}
    b_osT = {k: Buf() for k in "bam"}
    oT = {}
    b_oT = {k: Buf() for k in "bam"}
    KsT_new = A.bf16([4, NS]); b_KsN = Buf()
    QsT = A.bf16([4, NS]); b_QsT = Buf()
    VsN = A.bf16([4, 8, 65], parts=16); b_VsN = Buf()
    LFs = A.f32([4, 8], parts=16); b_LFs = Buf()
    memset(VsN[:, :, :, 64:65], 1.0, [b_VsN])
    oT["b"] = A.bf16([4, 2048])

    def xT_of(i):
        return xnT[:, :, i * 128:(i + 1) * 128], b_xnT[i]

    markB = A.top
    KbT = A.bf16([4, NT * 128]); b_KbT = [Buf() for _ in range(NT)]
    Vb = A.bf16([NT, 8, 65]); b_Vb = [Buf() for _ in range(NT)]
    Qb_r = Rot([(A.bf16([4, 128]), Buf()) for _ in range(2)])
    LF = A.f32([NT, 8]); b_LF = [Buf() for _ in range(NT)]
    _sh = P.new_dma_sem()
    for b_ in b_LF:
        BUFSEM[id(b_)] = _sh
    Cc = A.f32([NT, 8]); b_C = [Buf() for _ in range(NT)]
    crefs = A.f32([NOWN, 8]); b_cref = [Buf() for _ in range(NOWN)]
    Rall = A.f32([NT + 1, 8]); b_R = Buf()
    biasT_r = Rot([(A.f32([NT + 1, 8]), Buf()) for _ in range(3)])
    memset(Vb[:, :, :, 64:65], 1.0, b_Vb)
    def mk_nt_pre(i):
        xv, xb_ = xT_of(i)
        return norm_T_staged(xpre[i * 128:(i + 1) * 128, :], 128, gmix, xv, xb_)
    pipelined(NPRE, mk_nt_pre)
    for i in range(4):
        xv, xb_ = xT_of(12 + i)
        cp(xpT4[:, :, i * 128:(i + 1) * 128], xv, [xb_], [b_xp4])
    wv, wb = wload(w_in[:, C_KB:C_KB + 512], 8, 512)
    def mk_kb_pre(i):
        xv, xb_ = xT_of(i)
        pz, pzb = proj_tok(xv, xb_, 128, wv, wb, 512)
        return head_norm(pz, pzb, 128, 8, 64, gv["kb"], None, KbT[:, :, i * 128:(i + 1) * 128], b_KbT[i], staged=True)
    pipelined(NPRE, mk_kb_pre)
    wv, wb = wload(w_in[:, C_VB:C_VB + 520], 8, 520)
    for i in range(NPRE):
        xv, xb_ = xT_of(i)
        pz, pzb = proj_tok(xv, xb_, 128, wv, wb, 512)
        v_store(pz, pzb, 128, 8, 64, None, Vb[:, i], b_Vb[i])
        logf_a(xv, xb_, 128, wv, wb, 512, LF[:, i, :], b_LF[i])
    for i in range(NPRE):
        logf_b(LF[:, i, :], b_LF[i])
    for i in range(NPRE):
        logf_c(LF[:, i, :], b_LF[i], None)
    run_sums([(LF[:, i, :], b_LF[i], 128) for i in range(NPRE)], Rall, b_R, True)
    cumsum_batch(LF[:, 0:NPRE, :].rearrange("p t h -> p (t h)"), b_LF[0:NPRE],
                 Rall[:, 0:NPRE, :].rearrange("p t h -> p (t h)"), b_R,
                 Cc[:, 0:NPRE, :].rearrange("p t h -> p (t h)"), b_C[0:NPRE], NPRE * 8)
    def mk_nt_own(i):
        xv, xb_ = xT_of(i)
        return norm_T_staged(xown[i * 128:(i + 1) * 128, :], 128, gmix, xv, xb_)
    pipelined(NOWN, mk_nt_own)
    norm_T(xs, NS, gmix, xsT, b_xsT)
    wv, wb = wload(w_in[:, C_KB:C_KB + 512], 8, 512)
    def mk_kb_own(i):
        xv, xb_ = xT_of(i)
        pz, pzb = proj_tok(xv, xb_, 128, wv, wb, 512)
        return head_norm(pz, pzb, 128, 8, 64, gv["kb"], bkp[i * 128:(i + 1) * 128, :],
                         KbT[:, :, (NPRE + i) * 128:(NPRE + i + 1) * 128], b_KbT[NPRE + i], staged=True)
    pipelined(NOWN, mk_kb_own)
    pz, pzb = proj_tok(xsT, b_xsT, NS, wv, wb, 512)
    head_norm(pz, pzb, NS, 8, 64, gv["kb"], bks, KsT_new, b_KsN)
    wv, wb = wload(w_in[:, C_VB:C_VB + 520], 8, 520)
    for i in range(NOWN):
        xv, xb_ = xT_of(i)
        j = NPRE + i
        pz, pzb = proj_tok(xv, xb_, 128, wv, wb, 512)
        v_store(pz, pzb, 128, 8, 64, bvp[i * 128:(i + 1) * 128, :], Vb[:, j], b_Vb[j])
        logf_a(xv, xb_, 128, wv, wb, 512, LF[:, j, :], b_LF[j])
    for i in range(NOWN):
        logf_b(LF[:, NPRE + i, :], b_LF[NPRE + i])
    for i in range(NOWN):
        logf_c(LF[:, NPRE + i, :], b_LF[NPRE + i], blfp[i * 128:(i + 1) * 128, :])
    for i in range(NOWN):
        j = NPRE + i
        tt(Rall[:, j + 1, :], Rall[:, j, :], LF[:, j, :], ALU.add, [b_R, b_LF[j]], [b_R])
    cumsum_batch(LF[:, NPRE:NT, :].rearrange("p t h -> p (t h)"), b_LF[NPRE:NT],
                 Rall[:, NPRE:NT, :].rearrange("p t h -> p (t h)"), b_R,
                 Cc[:, NPRE:NT, :].rearrange("p t h -> p (t h)"), b_C[NPRE:NT], NOWN * 8)
    pz, pzb = pz_r.next()
    mm(pz[:, 0:NOWN * 8], [(SELMID, Cc[:, NPRE:NT, :].rearrange("p t h -> p (t h)"))], b_C[NPRE:NT] + [b_cm], [pzb])
    cp(crefs.rearrange("p t h -> p (t h)"), pz[:, 0:NOWN * 8], [pzb], b_cref)
    for bb in range(4):
        xv = xsT[:, :, bb * 16:(bb + 1) * 16]
        pz, pzb = proj_tok(xv, b_xsT, 16, wv, wb, 512)
        v_store(pz, pzb, 16, 8, 64, bvs[bb * 16:(bb + 1) * 16, :], VsN[:, bb], b_VsN)
        logf_tile(xv, b_xsT, 16, wv, wb, 512, LFs[:, bb, :], b_LFs, blfs[bb * 16:(bb + 1) * 16, :])
    wv, wb = wload(w_in[:, C_QB:C_QB + 512], 8, 512)
    pz, pzb = proj_tok(xsT, b_xsT, NS, wv, wb, 512)
    head_norm(pz, pzb, NS, 8, 64, gv["qb"], None, QsT, b_QsT)

    psum_mode(True)
    def prep_b(i):
        gi = NPRE + i
        xv, xb_ = xT_of(i)
        pz, pzb = proj_tok(xv, xb_, 128, wv, wb, 512)
        qv, qvb = Qb_r.next()
        head_norm(pz, pzb, 128, 8, 64, gv["qb"], None, qv, qvb, defer_T=2)
        bt_all, b_bt = biasT_r.next()
        for j in range(gi + 1):
            tt(bt_all[:, j, :], crefs[:, i, :], Cc[:, j, :], ALU.subtract, [b_cref[i], b_C[j]], [b_bt])
        ts(bt_all[:, 0:NPRE, :], bt_all[:, 0:NPRE, :], negflag[:, 0:1], None, ALU.add, None, [b_bt, b_nf], [b_bt])
        return qv, qvb, bt_all, b_bt

    nxt = prep_b(0)
    for i in range(NOWN):
        gi = NPRE + i
        run_deferred(True)
        qv, qvb, bt_all, b_bt = nxt
        if i + 1 < NOWN:
            nxt = prep_b(i + 1)
        for h in (0, 2, 4, 6, 1, 3, 5, 7):
            hp, hr = h // 2, (h % 2) * 64
            po, pob, off = po_slot(h, 8, 64)
            kts = []
            for j in range(gi + 1):
                kts.append(dict(kT=KbT[hr:hr + 64, hp, j * 128:(j + 1) * 128], nk=128,
                                v=Vb[:, j, h, :], rd=[b_KbT[j], b_Vb[j]],
                                bias=bt_all[:, j, h:h + 1], brd=[b_bt],
                                mask=(trib if j == gi else None)))
            attn_head(qv[hr:hr + 64, hp, :], 128, kts, po[:, off:off + 65], pob,
                      0.125, [qvb])
            if h >= 6:
                attn_finish_group(128, 8, 64, h % 2)
        attn_finish(128, 8, 64, oT["b"][:, :, i * 128:(i + 1) * 128], b_oT["b"])
    full_barrier()
    A.top = markB

    Kc = A.bf16([16, 512]); b_Kc = [Buf() for _ in range(4)]; kcsem = None
    KsT = A.bf16([4, 2048]); b_KsT = Buf()
    Vs = A.bf16([16, 8, 65]); b_Vs = [Buf() for _ in range(16)]
    LFc = A.f32([16, 8]); b_LFc = Buf()
    Cs = A.f32([16, 8]); b_Cs = Buf()
    CsN = A.f32([8], parts=16); b_CsN = Buf()
    crefS = A.f32([8]); b_crefS = Buf()
    memset(Vs[:, :, :, 64:65], 1.0, b_Vs)
    for bb in range(4):
        for q4 in range(4):
            ld(Kc[:, 4 * q4:4 * q4 + 4, :], cbk[bb, 512 * q4:512 * (q4 + 1), :].rearrange("(t p) n -> p t n", p=128),
               kcsem, [b_Kc[q4]], eng="pool")
        for t in range(16):
            ld(Vs[:, t, :, 0:64], cbv[bb, 128 * t:128 * (t + 1), :].rearrange("p (h d) -> p h d", h=8),
               kcsem, [b_Vs[t]], eng="pool")
        ld(LFc, cblf[bb].rearrange("(t p) h -> p t h", p=128), kcsem, [b_LFc])
        for t in range(16):
            pT, pTb = pT_r.next()
            transposes([(pT[:, c * 128:(c + 1) * 128], Kc[:, t, c * 128:(c + 1) * 128], identb) for c in range(4)],
                       [b_Kc[t // 4], b_id], [pTb])
            cp(KsT[:, :, t * 128:(t + 1) * 128], pT[:, 0:512].rearrange("p (c t) -> p c t", c=4), [pTb], [b_KsT])
        run_sums([(LFc[:, t, :], b_LFc, 128) for t in range(16)], Rall, b_R, True)
        cumsum_batch(LFc.rearrange("p t h -> p (t h)"), [b_LFc], Rall[:, 0:16, :].rearrange("p t h -> p (t h)"), b_R,
                     Cs.rearrange("p t h -> p (t h)"), [b_Cs], 128)
        cumsum_par(LFs[:, bb, :], b_LFs, 16, Rall[:, 16, :], b_R, CsN, b_CsN)
        bcast_row(CsN, b_CsN, 16, SEL0, crefS, b_crefS)
        bt_all, b_bt = biasT_r.next()
        for t in range(16):
            tt(bt_all[:, t, :], crefS, Cs[:, t, :], ALU.subtract, [b_crefS, b_Cs], [b_bt])
        tt(bt_all[0:16, 16, :], crefS[0:16], CsN, ALU.subtract, [b_crefS, b_CsN], [b_bt])
        for h in (0, 2, 4, 6, 1, 3, 5, 7):
            hp, hr = h // 2, (h % 2) * 64
            po, pob, off = po_slot(h, 8, 64)
            kts = []
            for t in range(16):
                kts.append(dict(kT=KsT[hr:hr + 64, hp, t * 128:(t + 1) * 128], nk=128, v=Vs[:, t, h, :],
                                rd=[b_KsT, b_Vs[t]], bias=bt_all[:, t, h:h + 1], brd=[b_bt], mask=None))
            kts.append(dict(kT=KsT_new[hr:hr + 64, hp, bb * 16:(bb + 1) * 16], nk=16, v=VsN[:, bb, h, :],
                            rd=[b_KsN, b_VsN], bias=bt_all[0:16, 16, h:h + 1], brd=[b_bt], mask=trib))
            attn_head(QsT[hr:hr + 64, hp, bb * 16:(bb + 1) * 16], 16, kts, po[0:16, off:off + 65], pob,
                      0.125, [b_QsT])
            if h >= 6:
                attn_finish_group(16, 8, 64, h % 2)
        attn_finish(16, 8, 64, osT["b"][:, :, bb * 16:(bb + 1) * 16], b_osT["b"])
        pipe_drain()
    full_barrier()
    A.top = markB

    psum_mode(False)
    oT["a"] = A.bf16([4, 2048])
    markAA = A.top
    NA = 4 + NOWN
    KaT = A.bf16([4, NA * 128]); b_KaT = [Buf() for _ in range(NA)]
    Va = A.bf16([NA, 8, 65]); b_Va = [Buf() for _ in range(NA)]
    Qa_r = Rot([(A.bf16([4, 128]), Buf()) for _ in range(2)])
    KsaN = A.bf16([4, NS]); b_KsaN = Buf()
    QsaT = A.bf16([4, NS]); b_QsaT = Buf()
    VsaN = A.bf16([4, 8, 65], parts=16); b_VsaN = Buf()
    memset(Va[:, :, :, 64:65], 1.0, b_Va)
    memset(VsaN[:, :, :, 64:65], 1.0, [b_VsaN])

    def xa_of(idx):
        if idx < 4:
            return xpT4[:, :, idx * 128:(idx + 1) * 128], b_xp4
        return xT_of(idx - 4)

    wv, wb = wload(w_in[:, C_KA:C_KA + 512], 8, 512)
    def mk_ka(idx):
        xv, xb_ = xa_of(idx)
        pz, pzb = proj_tok(xv, xb_, 128, wv, wb, 512)
        od = akp[(idx - 16) * 128:(idx - 15) * 128, :] if idx >= 16 else None
        return head_norm(pz, pzb, 128, 8, 64, gv["ka"], od, KaT[:, :, idx * 128:(idx + 1) * 128], b_KaT[idx], staged=True)
    pipelined(NA, mk_ka)
    pz, pzb = proj_tok(xsT, b_xsT, NS, wv, wb, 512)
    head_norm(pz, pzb, NS, 8, 64, gv["ka"], aks, KsaN, b_KsaN)
    wv, wb = wload(w_in[:, C_VA:C_VA + 512], 8, 512)
    for idx in range(NA):
        xv, xb_ = xa_of(idx)
        pz, pzb = proj_tok(xv, xb_, 128, wv, wb, 512)
        od = avp[(idx - 16) * 128:(idx - 15) * 128, :] if idx >= 16 else None
        v_store(pz, pzb, 128, 8, 64, od, Va[:, idx], b_Va[idx])
    for bb in range(4):
        pz, pzb = proj_tok(xsT[:, :, bb * 16:(bb + 1) * 16], b_xsT, 16, wv, wb, 512)
        v_store(pz, pzb, 16, 8, 64, avs[bb * 16:(bb + 1) * 16, :], VsaN[:, bb], b_VsaN)
    wv, wb = wload(w_in[:, C_QA:C_QA + 512], 8, 512)
    pz, pzb = proj_tok(xsT, b_xsT, NS, wv, wb, 512)
    head_norm(pz, pzb, NS, 8, 64, gv["qa"], None, QsaT, b_QsaT)
    def prep_a(i):
        xv, xb_ = xT_of(i)
        pz, pzb = proj_tok(xv, xb_, 128, wv, wb, 512)
        qv, qvb = Qa_r.next()
        head_norm(pz, pzb, 128, 8, 64, gv["qa"], None, qv, qvb, defer_T=2)
        return qv, qvb

    psum_mode(True)
    nxt = prep_a(0)
    for i in range(NOWN):
        run_deferred(True)
        qv, qvb = nxt
        if i + 1 < NOWN:
            nxt = prep_a(i + 1)
        for h in (0, 2, 4, 6, 1, 3, 5, 7):
            hp, hr = h // 2, (h % 2) * 64
            po, pob, off = po_slot(h, 8, 64)
            kts = []
            for t in range(5):
                idx = i + t
                pre_ = idx < 4
                d = dict(kT=KaT[hr:hr + 64, hp, idx * 128:(idx + 1) * 128], nk=128, v=Va[:, idx, h, :],
                         rd=[b_KaT[idx], b_Va[idx]], mask=None, brd=[b_bias, b_nf])
                if t < 3:
                    d["bias"] = (relcnf if pre_ else relc)[:, h:h + 1]
                    if t == 0:
                        d["mask"] = mask0
                else:
                    d["pre"] = (BiasA[:, h, t - 3, :], [b_bias])
                    d["bias"] = negflag[:, 0:1] if pre_ else 0.0
                kts.append(d)
            attn_head(qv[hr:hr + 64, hp, :], 128, kts, po[:, off:off + 65], pob,
                      0.125, [qvb])
            if h >= 6:
                attn_finish_group(128, 8, 64, h % 2)
        attn_finish(128, 8, 64, oT["a"][:, :, i * 128:(i + 1) * 128], b_oT["a"])
    Kca = A.bf16([4, 512]); b_Kca = Buf()
    KsaT = A.bf16([4, 512]); b_KsaT = Buf()
    Vsa = A.bf16([4, 8, 65]); b_Vsa = Buf()
    memset(Vsa[:, :, :, 64:65], 1.0, [b_Vsa])
    for bb in range(4):
        ld(Kca, cak[bb].rearrange("(t p) n -> p t n", p=128), kcsem, [b_Kca], eng="pool")
        for t in range(4):
            ld(Vsa[:, t, :, 0:64], cav[bb, 128 * t:128 * (t + 1), :].rearrange("p (h d) -> p h d", h=8),
               kcsem, [b_Vsa], eng="pool")
        for t in range(4):
            pT, pTb = pT_r.next()
            transposes([(pT[:, c * 128:(c + 1) * 128], Kca[:, t, c * 128:(c + 1) * 128], identb) for c in range(4)],
                       [b_Kca, b_id], [pTb])
            cp(KsaT[:, :, t * 128:(t + 1) * 128], pT[:, 0:512].rearrange("p (c t) -> p c t", c=4), [pTb], [b_KsaT])
        for h in (0, 2, 4, 6, 1, 3, 5, 7):
            hp, hr = h // 2, (h % 2) * 64
            po, pob, off = po_slot(h, 8, 64)
            kts = []
            for t in range(4):
                d = dict(kT=KsaT[hr:hr + 64, hp, t * 128:(t + 1) * 128], nk=128, v=Vsa[:, t, h, :],
                         rd=[b_KsaT, b_Vsa], mask=None, brd=[b_bias])
                if t < 3:
                    d["bias"] = relc[:, h:h + 1]
                else:
                    d["pre"] = (BiasA[:, h, 0, 0:16], [b_bias])
                    d["bias"] = 0.0
                kts.append(d)
            kts.append(dict(kT=KsaN[hr:hr + 64, hp, bb * 16:(bb + 1) * 16], nk=16, v=VsaN[:, bb, h, :],
                            rd=[b_KsaN, b_VsaN], mask=None, brd=[b_bias],
                            pre=(BiasA[0:16, h, 1, 0:16], [b_bias]), bias=0.0))
            attn_head(QsaT[hr:hr + 64, hp, bb * 16:(bb + 1) * 16], 16, kts, po[0:16, off:off + 65], pob,
                      0.125, [b_QsaT])
            if h >= 6:
                attn_finish_group(16, 8, 64, h % 2)
        attn_finish(16, 8, 64, osT["a"][:, :, bb * 16:(bb + 1) * 16], b_osT["a"])
        pipe_drain()
    full_barrier()
    A.top = markAA

    psum_mode(False)
    oT["m"] = A.bf16([4, 2048])
    markM = A.top
    Qm_r = Rot([(A.bf16([4, 128]), Buf()) for _ in range(2)])
    QsmT = A.bf16([4, NS]); b_QsmT = Buf()
    wv, wb = wload(w_in[:, C_QM:C_QM + 512], 8, 512)
    pz, pzb = proj_tok(xsT, b_xsT, NS, wv, wb, 512)
    head_norm(pz, pzb, NS, 4, 128, gv["qm"], None, QsmT, b_QsmT)
    sc_m = float(128 ** -0.5)
    def prep_m(i):
        xv, xb_ = xT_of(i)
        pz, pzb = proj_tok(xv, xb_, 128, wv, wb, 512)
        qv, qvb = Qm_r.next()
        head_norm(pz, pzb, 128, 4, 128, gv["qm"], None, qv, qvb, defer_T=1)
        return qv, qvb

    psum_mode(True)
    nxt = prep_m(0)
    for i in range(NOWN):
        run_deferred(True)
        qv, qvb = nxt
        if i + 1 < NOWN:
            nxt = prep_m(i + 1)
        for h in range(4):
            po, pob, off = po_slot(h, 4, 128)
            kts = [dict(kT=mkT[:, h, m * 128:(m + 1) * 128], nk=128, v=mvA[:, m, h, :], rd=[b_mkT, b_mv],
                        mask=None, bias=0.0) for m in range(2)]
            attn_head(qv[:, h, :], 128, kts, po[:, off:off + 129], pob, sc_m, [qvb])
            if (h + 1) % (4 // 2) == 0:
                attn_finish_group(128, 4, 128, h // (4 // 2))
        attn_finish(128, 4, 128, oT["m"][:, :, i * 128:(i + 1) * 128], b_oT["m"])
    Kcm = A.bf16([2, 512]); b_Kcm = Buf()
    mksT = A.bf16([4, 256]); b_mksT = Buf()
    mvs = A.bf16([2, 4, 129]); b_mvs = Buf()
    memset(mvs[:, :, :, 128:129], 1.0, [b_mvs])
    for bb in range(4):
        ld(Kcm, cmk[bb].rearrange("(t p) n -> p t n", p=128), kcsem, [b_Kcm], eng="pool")
        for t in range(2):
            ld(mvs[:, t, :, 0:128], cmv[bb, 128 * t:128 * (t + 1), :].rearrange("p (h d) -> p h d", h=4),
               kcsem, [b_mvs], eng="pool")
        for t in range(2):
            pT, pTb = pT_r.next()
            transposes([(pT[:, c * 128:(c + 1) * 128], Kcm[:, t, c * 128:(c + 1) * 128], identb) for c in range(4)],
                       [b_Kcm, b_id], [pTb])
            cp(mksT[:, :, t * 128:(t + 1) * 128], pT[:, 0:512].rearrange("p (c t) -> p c t", c=4), [pTb], [b_mksT])
        for h in range(4):
            po, pob, off = po_slot(h, 4, 128)
            kts = [dict(kT=mksT[:, h, m * 128:(m + 1) * 128], nk=128, v=mvs[:, m, h, :], rd=[b_mksT, b_mvs],
                        mask=None, bias=0.0) for m in range(2)]
            attn_head(QsmT[:, h, bb * 16:(bb + 1) * 16], 16, kts, po[0:16, off:off + 129], pob, sc_m, [b_QsmT])
            if (h + 1) % (4 // 2) == 0:
                attn_finish_group(16, 4, 128, h // (4 // 2))
        attn_finish(16, 4, 128, osT["m"][:, :, bb * 16:(bb + 1) * 16], b_osT["m"])
        pipe_drain()
    full_barrier()
    A.top = markM

    psum_mode(False)
    hT_start = A.top
    hT = A.bf16([8, 2048]); b_hT = [Buf() for _ in range(4)]
    hsT = A.bf16([8, NS]); b_hsT = Buf()
    mark_h_end = A.top
    acc_r = Rot([(A.f32([512]), Buf()) for _ in range(2)])
    wp_r = Rot([(A.bf16([4 * 384]), Buf(), P.new_dma_sem()) for _ in range(2)])
    wps = {"a": w_pa, "b": w_pb, "m": w_pm}
    BR = "abm"
    groups = [(tg * 512, 512, b_xnT[4 * tg:4 * tg + 4], b_hT[tg]) for tg in range(4)]
    for nch in range(8):
        wt, wgb, wsem = w_r.next()
        wg = wt[:, 0:8 * 384].rearrange("p (c b n) -> p c b n", c=8, b=3)
        wpt, wpb, wpsem = wp_r.next()
        wp = wpt.rearrange("p (c b n) -> p c b n", c=4, b=3)
        for bi, br in enumerate(BR):
            c0 = C_G + bi * 1024 + nch * 128
            ld(wg[:, :, bi, :], w_in[:, c0:c0 + 128].rearrange("(c p) n -> p c n", p=128), wsem, [wgb], eng="pool")
            ld(wp[:, :, bi, :], wps[br][:, nch * 128:(nch + 1) * 128].rearrange("(c p) n -> p c n", p=128),
               wpsem, [wpb], eng="pool")
        for gidx in range(5):
            if gidx < 4:
                t0, ntok, xbufs, hb = groups[gidx]
                xv = xnT[:, :, t0:t0 + ntok]
                ov = {k: oT[k][:, :, t0:t0 + ntok] for k in BR}
                obufs = b_oT
                hdst = hT[:, nch, t0:t0 + ntok]
            else:
                ntok, xbufs, hb = NS, [b_xsT], b_hsT
                xv = xsT
                ov = osT
                obufs = b_osT
                hdst = hsT[:, nch, :]
            acc, accb = acc_r.next()
            for bi, br in enumerate(BR):
                pz, pzb = pz_r.next()
                mm(pz[:, 0:ntok], [(wg[:, c, bi, :], xv[:, c, :]) for c in range(8)], [wgb] + list(xbufs), [pzb])
                sg, sgb = st_r.next()
                act(sg[:, 0:ntok], pz[:, 0:ntok], AF.Sigmoid, [pzb], [sgb])
                pz2, pz2b = pz_r.next()
                mm(pz2[:, 0:ntok], [(wp[:, c, bi, :], ov[br][:, c, :]) for c in range(4)], [wpb, obufs[br]], [pz2b])
                if bi == 0:
                    tt(acc[:, 0:ntok], sg[:, 0:ntok], pz2[:, 0:ntok], ALU.mult, [sgb, pz2b], [accb])
                elif bi == 1:
                    tt(sg[:, 0:ntok], sg[:, 0:ntok], pz2[:, 0:ntok], ALU.mult, [sgb, pz2b], [sgb])
                    tt(acc[:, 0:ntok], acc[:, 0:ntok], sg[:, 0:ntok], ALU.add, [accb, sgb], [accb])
                else:
                    tt(sg[:, 0:ntok], sg[:, 0:ntok], pz2[:, 0:ntok], ALU.mult, [sgb, pz2b], [sgb])
                    tt(hdst, acc[:, 0:ntok], sg[:, 0:ntok], ALU.add, [accb, sgb], [hb])
    full_barrier()

    A.top = mark_x
    x1 = A.f32([NOWN, 1024]); b_x1 = [Buf() for _ in range(NOWN)]
    _sh2 = P.new_dma_sem()
    for b_ in b_x1:
        BUFSEM[id(b_)] = _sh2
    x1s = A.f32([1024]); b_x1s = Buf()
    assert A.top <= hT_start, ("x1 overlaps hT", A.top, hT_start)
    mark_x1_end = A.top
    A.top = mark_h_end
    wo0, wo0b = wload(w_o[:, 0:512], 8, 512)
    wo1, wo1b = wload(w_o[:, 512:1024], 8, 512)
    for i in range(NOWN + 1):
        if i < NOWN:
            nt_, src, hv, hb, dst, db = 128, xown[i * 128:(i + 1) * 128, :], hT[:, :, i * 128:(i + 1) * 128], b_hT[i // 4], x1[:, i, :], b_x1[i]
        else:
            nt_, src, hv, hb, dst, db = NS, xs, hsT, b_hsT, x1s, b_x1s
        xt, xb, xsem = xt_r.next()
        ld(xt[0:nt_], src, xsem, [xb])
        for half, (wo, wob) in enumerate(((wo0, wo0b), (wo1, wo1b))):
            pz, pzb = pz_r.next()
            mm(pz[0:nt_, :], [(hv[:, c, :], wo[:, c, :]) for c in range(8)], [hb, wob], [pzb])
            tt(dst[0:nt_, half * 512:(half + 1) * 512], xt[0:nt_, half * 512:(half + 1) * 512], pz[0:nt_, :],
               ALU.add, [xb, pzb], [db])
    full_barrier()

    A.top = mark_x1_end
    gffn = A.f32([1024])
    ld(gffn, g_ffn.partition_broadcast(128), csem, [b_g])
    xn2T = A.bf16([8, 512]); b_xn2T = Buf()
    hdnT = A.bf16([32, 512]); b_hdn = Buf()
    wd_r = Rot([(A.bf16([4, 1024]), Buf(), P.new_dma_sem()) for _ in range(2)])
    for stg in range(5):
        if stg < 4:
            ntok = 512
            tiles = [(x1[:, 4 * stg + k, :], b_x1[4 * stg + k], 128, y_p[(4 * stg + k) * 128:(4 * stg + k + 1) * 128, :]) for k in range(4)]
        else:
            ntok = NS
            tiles = [(x1s, b_x1s, NS, y_s)]
        for k, (xa, xab, nt_, _) in enumerate(tiles):
            norm_T((xa, xab), nt_, gffn, xn2T[:, :, k * 128:k * 128 + nt_], b_xn2T)
        for ffb in range(8):
            wt_, wb, wsem_ = w_r.next()
            wv = wt_[:, 0:8 * 512].rearrange("p (c n) -> p c n", c=8)
            ld(wv, wupb[:, ffb * 512:(ffb + 1) * 512].rearrange("(c p) n -> p c n", p=128), None, [wb],
               reads=b_wupb, eng="pool")
            for fc in range(4):
                pz, pzb = pz_r.next()
                mm(pz[:, 0:ntok], [(wv[:, c, fc * 128:(fc + 1) * 128], xn2T[:, c, 0:ntok]) for c in range(8)],
                   [wb, b_xn2T], [pzb])
                rl, rlb = st_r.next()
                act(rl[:, 0:ntok], pz[:, 0:ntok], AF.Relu, [pzb], [rlb])
                tt(hdnT[:, ffb * 4 + fc, 0:ntok], rl[:, 0:ntok], rl[:, 0:ntok], ALU.mult, [rlb], [b_hdn])
        for blk in range(8):
            wd, wdb, wdsem = wd_r.next()
            ld(wd, wdnb[blk * 512:(blk + 1) * 512, :].rearrange("(c p) n -> p c n", p=128), wdsem, [wdb],
               reads=[b_wdnb[blk // 4]], eng="pool")
            for k, (xa, xab, nt_, _) in enumerate(tiles):
                for half in range(2):
                    pz, pzb = pz_r.next()
                    mm(pz[0:nt_, :], [(hdnT[:, blk * 4 + c, k * 128:k * 128 + nt_], wd[:, c, half * 512:(half + 1) * 512])
                                      for c in range(4)], [b_hdn, wdb], [pzb])
                    tt(xa[0:nt_, half * 512:(half + 1) * 512], xa[0:nt_, half * 512:(half + 1) * 512], pz[0:nt_, :],
                       ALU.add, [xab, pzb], [xab])
        for k, (xa, xab, nt_, yd) in enumerate(tiles):
            store(yd, xa[0:nt_], [xab])

    P.emit(final_waits=STORE_SEMS)
    st.close()
    return nc


_NC = None


def _consts():
    tri = np.triu(np.ones((128, 128), np.float32))
    ones = np.ones((128, 128), np.float32)
    selmid = np.zeros((128, 128), np.float32); selmid[64, :] = 1.0
    sel0 = np.zeros((128, 128), np.float32); sel0[0, :] = 1.0
    cm = np.stack([tri, ones, selmid, sel0], axis=1).reshape(128, 512)
    return np.ascontiguousarray(cm), np.eye(128, dtype=np.float32)


def kernel(**inp):
    global _NC
    if _NC is None:
        _NC = build_nc()
    nc = _NC
    f = lambda a: np.ascontiguousarray(np.asarray(a, dtype=np.float32))
    cm, ident = _consts()
    xp = f(inp["x_prompt"]); xsm = f(inp["x_sample"]); memp = f(inp["mem_prompt"])
    shared = {
        "g_mix": f(inp["g_mix"]), "w_in": f(inp["w_in"][0]), "b_f": f(inp["b_f"]),
        "g_qa": f(inp["g_qa"]), "g_ka": f(inp["g_ka"]), "g_qb": f(inp["g_qb"]), "g_kb": f(inp["g_kb"]),
        "g_qm": f(inp["g_qm"]), "g_km": f(inp["g_km"]), "rel_bias": f(inp["rel_bias"][0]),
        "g_mem": f(inp["g_mem"]), "w_mkv": f(inp["w_mkv"][0]),
        "w_pa": f(inp["w_pa"][0]), "w_pb": f(inp["w_pb"][0]), "w_pm": f(inp["w_pm"][0]),
        "w_o": f(inp["w_o"][0]), "g_ffn": f(inp["g_ffn"]), "w_up": f(inp["w_up"][0]), "w_down": f(inp["w_down"][0]),
        "cmat": cm, "ident": ident,
    }
    in_maps = []
    for c in range(8):
        b, hf = c // 2, c % 2
        sb = slice(4 * c, 4 * c + 4)
        m = dict(shared)
        m["xpre"] = f(xp[b, 0:2048])
        m["xown"] = f(xp[b, hf * 2048:(hf + 1) * 2048])
        m["xs"] = f(xsm[sb].reshape(64, 1024))
        m["mem"] = f(memp[b])
        m["cak"] = f(inp["cache_a_k"][0, sb].reshape(4, 512, 512))
        m["cav"] = f(inp["cache_a_v"][0, sb].reshape(4, 512, 512))
        m["cbk"] = f(inp["cache_b_k"][0, sb].reshape(4, 2048, 512))
        m["cbv"] = f(inp["cache_b_v"][0, sb].reshape(4, 2048, 512))
        m["cblf"] = f(inp["cache_b_logf"][0, sb])
        m["cmk"] = f(inp["cache_mem_k"][0, sb].reshape(4, 256, 512))
        m["cmv"] = f(inp["cache_mem_v"][0, sb].reshape(4, 256, 512))
        m["flag"] = np.full((1, 1), float(hf), np.float32)
        in_maps.append(m)
    res = run_bass_kernel_spmd(nc, in_maps, core_ids=list(range(8)))
    R = res.results
    y_p = np.zeros((4, 4096, 1024), np.float32)
    y_s = np.zeros((32, 16, 1024), np.float32)
    akp = np.zeros((1, 4, 512, 8, 64), np.float32); avp = np.zeros_like(akp)
    bkp = np.zeros((1, 4, 4096, 8, 64), np.float32); bvp = np.zeros_like(bkp)
    blfp = np.zeros((1, 4, 4096, 8), np.float32)
    mkp = np.zeros((1, 4, 256, 4, 128), np.float32); mvp = np.zeros_like(mkp)
    aks = np.zeros((1, 32, 16, 8, 64), np.float32); avs = np.zeros_like(aks)
    bks = np.zeros_like(aks); bvs = np.zeros_like(aks)
    blfs = np.zeros((1, 32, 16, 8), np.float32)
    for c in range(8):
        b, hf = c // 2, c % 2
        r = R[c]
        s = slice(hf * 2048, (hf + 1) * 2048)
        y_p[b, s] = r["y_p"]
        y_s[4 * c:4 * c + 4] = r["y_s"].reshape(4, 16, 1024)
        bkp[0, b, s] = r["bkp"].reshape(2048, 8, 64)
        bvp[0, b, s] = r["bvp"].reshape(2048, 8, 64)
        blfp[0, b, s] = r["blfp"]
        if hf == 1:
            akp[0, b] = r["akp"].reshape(512, 8, 64)
            avp[0, b] = r["avp"].reshape(512, 8, 64)
        else:
            mkp[0, b] = r["mkp"].reshape(256, 4, 128)
            mvp[0, b] = r["mvp"].reshape(256, 4, 128)
        aks[0, 4 * c:4 * c + 4] = r["aks"].reshape(4, 16, 8, 64)
        avs[0, 4 * c:4 * c + 4] = r["avs"].reshape(4, 16, 8, 64)
        bks[0, 4 * c:4 * c + 4] = r["bks"].reshape(4, 16, 8, 64)
        bvs[0, 4 * c:4 * c + 4] = r["bvs"].reshape(4, 16, 8, 64)
        blfs[0, 4 * c:4 * c + 4] = r["blfs"].reshape(4, 16, 8)
    return (y_p, y_s, akp, avp, bkp, bvp, blfp, mkp, mvp, aks, avs, bks, bvs, blfs)
```

```python
import contextlib
import numpy as np
import concourse.bass as bass
import concourse.mybir as mybir
from concourse.bass_utils import run_bass_kernel_spmd

F32 = mybir.dt.float32
BF16 = mybir.dt.bfloat16
ALU = mybir.AluOpType
AF = mybir.ActivationFunctionType
AX = mybir.AxisListType

ENGS = ("pe", "act", "dve", "pool", "sp")
NEG = -30000.0
EPS = 1e-6
LREL = 768


ALLBUFS = []


class Buf:
    __slots__ = ("name", "w", "r", "excl")

    def __init__(self, name="", excl=False):
        self.name = name
        self.w = None
        self.r = []
        self.excl = excl
        ALLBUFS.append(self)


class Prog:
    def __init__(self, nc):
        self.nc = nc
        self.ops = {e: [] for e in ENGS}
        self.n_dma_sem = 0

    def new_dma_sem(self):
        self.n_dma_sem += 1
        return {"id": self.n_dma_sem - 1, "count": 0}

    def _deps_for(self, reads, writes):
        deps = []
        for b in reads:
            if b.w is not None:
                deps.append(b.w)
        for b in writes:
            if b.w is not None:
                deps.append(b.w)
            deps.extend(b.r)
        return deps

    def _post(self, tok, reads, writes):
        for b in writes:
            b.w = tok
            b.r = []
        for b in reads:
            if b not in writes:
                if len(b.r) > 64:
                    last = {}
                    for t in b.r:
                        last[(t[0], t[1])] = t
                    b.r = list(last.values())
                b.r.append(tok)

    def op(self, eng, fn, reads=(), writes=()):
        writes = list(writes) + [b for b in reads if b.excl and b not in writes]
        deps = self._deps_for(reads, writes)
        o = {"eng": eng, "fn": fn, "deps": deps, "kind": "c", "signal": False,
             "idx": len(self.ops[eng])}
        self.ops[eng].append(o)
        self._post(("e", eng, o["idx"]), reads, writes)
        return o

    def dma(self, eng, fn, sem, reads=(), writes=()):
        deps = self._deps_for(reads, writes)
        sem["count"] += 16
        o = {"eng": eng, "fn": fn, "deps": deps, "kind": "d", "sem": sem,
             "idx": len(self.ops[eng])}
        self.ops[eng].append(o)
        self._post(("d", sem["id"], sem["count"]), reads, writes)
        return o

    def barrier(self, bufs):
        deps = []
        for b in bufs:
            if b.w is not None:
                deps.append(b.w)
            deps.extend(b.r)
        for e in ENGS:
            o = {"eng": e, "fn": None, "deps": list(deps), "kind": "n",
                 "idx": len(self.ops[e])}
            self.ops[e].append(o)

    def emit(self, final_waits=()):
        nc = self.nc
        for e in ENGS:
            for o in self.ops[e]:
                for d in o["deps"]:
                    if d[0] == "e":
                        self.ops[d[1]][d[2]]["signal"] = True
        for e in ENGS:
            c = 0
            for o in self.ops[e]:
                if o["kind"] == "c" and o["signal"]:
                    c += 1
                    o["sigval"] = c
        with contextlib.ExitStack() as st:
            esem = {e: st.enter_context(nc.semaphore("s_" + e)) for e in ENGS}
            dsem = [st.enter_context(nc.semaphore("d%d" % i)) for i in range(self.n_dma_sem)]
            block = st.enter_context(nc.Block())
            prog = self

            def run(ename, eng):
                seen_e = {e: 0 for e in ENGS}
                seen_d = {}
                for o in prog.ops[ename]:
                    need_e = {}
                    need_d = {}
                    for d in o["deps"]:
                        if d[0] == "e":
                            v = prog.ops[d[1]][d[2]]["sigval"]
                            if v > need_e.get(d[1], 0):
                                need_e[d[1]] = v
                        else:
                            if d[2] > need_d.get(d[1], 0):
                                need_d[d[1]] = d[2]
                    for e2, v in need_e.items():
                        if v > seen_e[e2]:
                            eng.wait_ge(esem[e2], v)
                            seen_e[e2] = v
                    for s, v in need_d.items():
                        if v > seen_d.get(s, 0):
                            eng.wait_ge(dsem[s], v)
                            seen_d[s] = v
                    if o["kind"] == "c":
                        ins = o["fn"](eng)
                        if o["signal"]:
                            ins.then_inc(esem[ename], 1)
                    elif o["kind"] == "d":
                        ins = o["fn"](eng)
                        ins.then_inc(dsem[o["sem"]["id"]], 16)
                if ename == "sp":
                    for s in final_waits:
                        if s["count"] > 0:
                            eng.wait_ge(dsem[s["id"]], s["count"])

            @block.tensor
            def _(eng):
                run("pe", eng)

            @block.scalar
            def _(eng):
                run("act", eng)

            @block.vector
            def _(eng):
                run("dve", eng)

            @block.gpsimd
            def _(eng):
                run("pool", eng)

            @block.sync
            def _(eng):
                run("sp", eng)


def bcl(ap, m):
    return bass.AP(tensor=ap.tensor, offset=ap.offset,
                   ap=[list(x) for x in ap.ap] + [[0, m]])


def bcm(ap, k):
    a = [list(x) for x in ap.ap]
    return bass.AP(tensor=ap.tensor, offset=ap.offset, ap=[a[0], [0, k]] + a[1:])


class Rot:
    def __init__(self, items):
        self.items = items
        self.i = 0

    def next(self):
        it = self.items[self.i % len(self.items)]
        self.i += 1
        return it


C_QA, C_KA, C_VA, C_QB, C_KB, C_VB, C_FL, C_QM, C_G = 0, 512, 1024, 1536, 2048, 2560, 3072, 3080, 3592

NPRE = 16
NOWN = 16
NT = NPRE + NOWN
NS = 64
ARENA_W = 53100


def build_nc():
    nc = bass.Bass("TRN2", target_bir_lowering=False)
    P = Prog(nc)

    def din(name, shape):
        return nc.dram_tensor(name, list(shape), F32, kind="ExternalInput").ap()

    def dout(name, shape):
        return nc.dram_tensor(name, list(shape), F32, kind="ExternalOutput").ap()

    xpre = din("xpre", [2048, 1024]); xown = din("xown", [2048, 1024]); xs = din("xs", [NS, 1024])
    mem = din("mem", [256, 1024])
    cak = din("cak", [4, 512, 512]); cav = din("cav", [4, 512, 512])
    cbk = din("cbk", [4, 2048, 512]); cbv = din("cbv", [4, 2048, 512]); cblf = din("cblf", [4, 2048, 8])
    cmk = din("cmk", [4, 256, 512]); cmv = din("cmv", [4, 256, 512])
    g_mix = din("g_mix", [1, 1024]); w_in = din("w_in", [1024, 6664]); b_f = din("b_f", [1, 8])
    g_qa = din("g_qa", [1, 64]); g_ka = din("g_ka", [1, 64]); g_qb = din("g_qb", [1, 64]); g_kb = din("g_kb", [1, 64])
    g_qm = din("g_qm", [1, 128]); g_km = din("g_km", [1, 128]); rel_bias = din("rel_bias", [257, 8])
    g_mem = din("g_mem", [1, 1024]); w_mkv = din("w_mkv", [1024, 1024])
    w_pa = din("w_pa", [512, 1024]); w_pb = din("w_pb", [512, 1024]); w_pm = din("w_pm", [512, 1024])
    w_o = din("w_o", [1024, 1024]); g_ffn = din("g_ffn", [1, 1024])
    w_up = din("w_up", [1024, 4096]); w_down = din("w_down", [4096, 1024])
    cmat = din("cmat", [128, 4 * 128]); ident = din("ident", [128, 128]); flag = din("flag", [1, 1])

    y_p = dout("y_p", [2048, 1024]); y_s = dout("y_s", [NS, 1024])
    akp = dout("akp", [512, 512]); avp = dout("avp", [512, 512])
    bkp = dout("bkp", [2048, 512]); bvp = dout("bvp", [2048, 512]); blfp = dout("blfp", [2048, 8])
    mkp = dout("mkp", [256, 512]); mvp = dout("mvp", [256, 512])
    aks = dout("aks", [NS, 512]); avs = dout("avs", [NS, 512])
    bks = dout("bks", [NS, 512]); bvs = dout("bvs", [NS, 512]); blfs = dout("blfs", [NS, 8])
    wupb = nc.dram_tensor("wupb", [1024, 4096], BF16, kind="Internal").ap()
    wdnb = nc.dram_tensor("wdnb", [4096, 1024], BF16, kind="Internal").ap()
    relE = nc.dram_tensor("relE", [8, LREL], F32, kind="Internal").ap()
    relD = nc.dram_tensor("relD", [128, 8 * LREL], F32, kind="Internal").ap()

    st = contextlib.ExitStack()
    arena = st.enter_context(nc.sbuf_tensor("arena", [128, ARENA_W], F32))
    psum = [st.enter_context(nc.psum_tensor("ps%d" % i, [128, 512], F32)) for i in range(8)]

    class Arena:
        def __init__(self):
            self.top = 0

        def f32(self, dims, parts=128):
            n = int(np.prod(dims))
            o = self.top
            self.top += n
            assert self.top <= ARENA_W, ("arena overflow", self.top)
            ap = arena[0:parts, o:o + n]
            return self._view(ap, dims)

        def bf16(self, dims, parts=128):
            n = int(np.prod(dims))
            w = (n + 1) // 2
            o = self.top
            self.top += w
            assert self.top <= ARENA_W, ("arena overflow", self.top)
            ap = arena[0:parts, o:o + w].bitcast(BF16)[:, 0:n]
            return self._view(ap, dims)

        @staticmethod
        def _view(ap, dims):
            if len(dims) == 1:
                return ap
            if len(dims) == 2:
                return ap.rearrange("p (a b) -> p a b", a=dims[0])
            if len(dims) == 3:
                return ap.rearrange("p (a b c) -> p a b c", a=dims[0], b=dims[1])
            if len(dims) == 4:
                return ap.rearrange("p (a b c d) -> p a b c d", a=dims[0], b=dims[1], c=dims[2])
            raise ValueError

    A = Arena()
    out_sems = [P.new_dma_sem() for _ in range(4)]
    osr = Rot(out_sems)

    def mm(out, pairs, reads, writes):
        pairs = list(pairs)

        def f(e):
            n = len(pairs)
            for k, (l, r) in enumerate(pairs):
                i = e.matmul(out, lhsT=l, rhs=r, start=(k == 0), stop=(k == n - 1))
            return i
        P.op("pe", f, reads, writes)

    def transposes(items, reads, writes):
        items = list(items)

        def f(e):
            for (o, i_, idn) in items:
                ins = e.transpose(out=o, in_=i_, identity=idn)
            return ins
        P.op("pe", f, reads, writes)

    def act(out, in_, func, reads, writes, bias=None, scale=None, accum=None, eng="act"):
        kw = {}
        if bias is not None:
            kw["bias"] = bias
        if scale is not None:
            kw["scale"] = scale
        if accum is not None:
            kw["accum_out"] = accum
        P.op(eng, lambda e: e.activation(out=out, in_=in_, func=func, **kw), reads, writes)

    def tt(out, in0, in1, op, reads, writes, eng="dve"):
        P.op(eng, lambda e: e.tensor_tensor(out=out, in0=in0, in1=in1, op=op), reads, writes)

    def ts(out, in0, s1, s2, op0, op1, reads, writes, eng="dve"):
        if s2 is None:
            P.op(eng, lambda e: e.tensor_scalar(out=out, in0=in0, scalar1=s1, scalar2=None, op0=op0), reads, writes)
        else:
            P.op(eng, lambda e: e.tensor_scalar(out=out, in0=in0, scalar1=s1, scalar2=s2, op0=op0, op1=op1), reads, writes)

    def stt(out, in0, scalar, in1, op0, op1, reads, writes, eng="dve"):
        P.op(eng, lambda e: e.scalar_tensor_tensor(out=out, in0=in0, scalar=scalar, in1=in1, op0=op0, op1=op1), reads, writes)

    def cp(out, in_, reads, writes, eng="dve"):
        P.op(eng, lambda e: e.tensor_copy(out=out, in_=in_), reads, writes)

    def recip(out, in_, reads, writes):
        P.op("dve", lambda e: e.reciprocal(out=out, in_=in_), reads, writes)

    def red(out, in_, reads, writes):
        P.op("dve", lambda e: e.tensor_reduce(out=out, in_=in_, axis=AX.X, op=ALU.add), reads, writes)

    def memset(ap, val, writes, eng="dve"):
        P.op(eng, lambda e: e.memset(ap, val), (), writes)

    BUFSEM = {}

    def sem_of(b):
        if id(b) not in BUFSEM:
            BUFSEM[id(b)] = P.new_dma_sem()
        return BUFSEM[id(b)]

    STORE_SEMS = []

    def ld(out, in_, sem, writes, reads=(), eng="sp", slow=False):
        sem = sem_of(writes[0])
        if slow:
            P.dma(eng, lambda e: e.dma_start(out=out, in_=in_, allow_slow_non_contiguous=True), sem, reads, writes)
        else:
            P.dma(eng, lambda e: e.dma_start(out=out, in_=in_), sem, reads, writes)

    def store(out, in_, reads):
        sem = sem_of(reads[0])
        if sem not in STORE_SEMS:
            STORE_SEMS.append(sem)
        P.dma("sp", lambda e: e.dma_start(out=out, in_=in_), sem, reads, ())

    def rstd_from_ss(ap, n, bufs):
        act(ap, ap, AF.Sqrt, list(bufs) + [b_eps], bufs, bias=EPSB[0:ap.shape[0], 0:1], scale=1.0 / n)
        recip(ap, ap, bufs, bufs)

    csem = P.new_dma_sem()
    identb = A.bf16([128]); b_id = Buf()
    ld(identb, ident, csem, [b_id], eng="pool")
    trif = A.f32([4, 128]); b_cm = Buf()
    ld(trif, cmat.rearrange("p (a b) -> p a b", a=4), csem, [b_cm])
    TRI, ONES, SELMID, SEL0 = trif[:, 0, :], trif[:, 1, :], trif[:, 2, :], trif[:, 3, :]
    trib = A.bf16([128]); b_trib = Buf()
    cp(trib, TRI, [b_cm], [b_trib])
    gmix = A.f32([1024]); b_g = Buf()
    ld(gmix, g_mix.partition_broadcast(128), csem, [b_g])
    gv = {}
    for nm, src, n in (("qa", g_qa, 64), ("ka", g_ka, 64), ("qb", g_qb, 64), ("kb", g_kb, 64),
                       ("qm", g_qm, 128), ("km", g_km, 128)):
        gv[nm] = A.f32([n])
        ld(gv[nm], src.partition_broadcast(128), csem, [b_g])
    bfb = A.f32([8])
    ld(bfb, b_f.partition_broadcast(128), csem, [b_g])
    EPSB = A.f32([1]); b_eps = Buf()
    memset(EPSB, EPS, [b_eps])
    negflag = A.f32([1]); b_nf = Buf()
    ld(negflag, flag.partition_broadcast(128), csem, [b_nf])
    ts(negflag, negflag, -1.0, -NEG, ALU.add, ALU.mult, [b_nf], [b_nf])

    xt_r = Rot([(A.f32([1024]), Buf(), P.new_dma_sem()) for _ in range(2)])
    sq_r = Rot([(A.bf16([1024]), Buf()) for _ in range(1)])
    xn_r = Rot([(A.bf16([1024]), Buf()) for _ in range(2)])
    st_r = Rot([(A.f32([512]), Buf()) for _ in range(3)])
    sb_r = Rot([(A.bf16([512]), Buf()) for _ in range(2)])
    sm_r = Rot([(A.f32([16]), Buf()) for _ in range(4)])
    pt_r = Rot([(A.bf16([128]), Buf()) for _ in range(12)])
    sa_r = Rot([(A.f32([128]), Buf()) for _ in range(4)])
    fin_r = Rot([(A.bf16([512]), Buf()) for _ in range(2)])
    bias_r = Rot([(A.f32([8]), Buf()) for _ in range(4)])
    w_r = Rot([(A.bf16([4224]), Buf(), P.new_dma_sem()) for _ in range(2)])
    PB = [Buf(excl=True) for _ in range(8)]
    pz_r = Rot([(psum[0], PB[0]), (psum[1], PB[1])])
    pT_r = Rot([(psum[2].bitcast(BF16), PB[2]), (psum[3].bitcast(BF16), PB[3])])
    pS_r = Rot([(psum[4], PB[4]), (psum[5], PB[5])])
    pO = [(psum[6], PB[6]), (psum[7], PB[7])]

    PIPEQ = []
    PST = {"s": 0, "r": 0}
    PFN = {}
    PDEF = []

    def defer(fn, delay):
        PDEF.append([PST["r"] + delay, fn])

    def run_deferred(all_=False):
        while True:
            todo = [d for d in PDEF if all_ or d[0] <= PST["r"]]
            if not todo:
                break
            for d in todo:
                PDEF.remove(d)
            for d in todo:
                d[1]()

    def pipe_pump(drain):
        nb = len(pS_r.items)
        while True:
            if PST["s"] < len(PIPEQ) and PST["s"] - PST["r"] < nb:
                ent = PIPEQ[PST["s"]]
                ent["bank"] = PFN["S"](ent["q"])
                PST["s"] += 1
                continue
            if PST["r"] < PST["s"] and (drain or PST["s"] - PST["r"] >= nb):
                ent = PIPEQ[PST["r"]]
                PFN["R"](ent["q"], *ent["bank"])
                for cb in ent["after"]:
                    cb()
                ent["after"] = []
                PST["r"] += 1
                run_deferred()
                continue
            break
        if drain:
            run_deferred(True)

    def pipe_drain():
        pipe_pump(True)

    def psum_mode(attn):
        pipe_drain()
        if attn:
            pz_r.items = [(psum[0], PB[0])]
            pT_r.items = [(psum[2].bitcast(BF16), PB[2])]
            pS_r.items = [(psum[1], PB[1]), (psum[3], PB[3]), (psum[4], PB[4]), (psum[5], PB[5])]
        else:
            pz_r.items = [(psum[0], PB[0]), (psum[1], PB[1]), (psum[4], PB[4]), (psum[5], PB[5])]
            pT_r.items = [(psum[2].bitcast(BF16), PB[2]), (psum[3].bitcast(BF16), PB[3])]
            pS_r.items = [(psum[4], PB[4]), (psum[5], PB[5])]

    psum_mode(False)

    base_top = A.top

    def wload(src, kc, n):
        wt, wb, wsem = w_r.next()
        v = wt[:, 0:kc * n].rearrange("p (c n) -> p c n", c=kc)
        ld(v, src.rearrange("(c p) n -> p c n", p=128), wsem, [wb], eng="pool")
        return v, wb

    def norm_T(xrows, nt, gbc, dstT, dstbuf, x_keep=None):
        if isinstance(xrows, tuple):
            xt, xb = xrows
        else:
            xt, xb, xsem = xt_r.next()
            ld(xt[0:nt], xrows, xsem, [xb])
        sq, sqb = sq_r.next()
        sm, smb = sm_r.next()
        act(sq[0:nt], xt[0:nt], AF.Square, [xb], [sqb, smb], accum=sm[0:nt, 0:1])
        rstd_from_ss(sm[0:nt, 0:1], 1024.0, [smb])
        xn, xnb = xn_r.next()
        stt(xn[0:nt], xt[0:nt], sm[0:nt, 0:1], gbc[0:nt], ALU.mult, ALU.mult, [xb, smb, b_g], [xnb])
        pT, pTb = pT_r.next()
        transposes([(pT[:, c * 128:c * 128 + nt], xn[0:nt, c * 128:(c + 1) * 128], identb[0:nt, 0:nt])
                    for c in range(8)], [xnb, b_id], [pTb])
        cp(dstT, pT.rearrange("p (c t) -> p c t", c=8)[:, :, 0:nt], [pTb], [dstbuf])

    def norm_T_staged(xrows, nt, gbc, dstT, dstbuf):
        xt, xb, xsem = xt_r.next()
        ld(xt[0:nt], xrows, xsem, [xb])
        sq, sqb = sq_r.next()
        sm, smb = sm_r.next()
        hold = {}

        def sA():
            act(sq[0:nt], xt[0:nt], AF.Square, [xb], [sqb, smb], accum=sm[0:nt, 0:1])
            act(sm[0:nt, 0:1], sm[0:nt, 0:1], AF.Sqrt, [smb, b_eps], [smb], bias=EPSB[0:nt, 0:1], scale=1.0 / 1024.0)

        def sB():
            recip(sm[0:nt, 0:1], sm[0:nt, 0:1], [smb], [smb])
            xn, xnb = xn_r.next()
            stt(xn[0:nt], xt[0:nt], sm[0:nt, 0:1], gbc[0:nt], ALU.mult, ALU.mult, [xb, smb, b_g], [xnb])
            hold["pT"] = pT_r.next()
            pT, pTb = hold["pT"]
            transposes([(pT[:, c * 128:c * 128 + nt], xn[0:nt, c * 128:(c + 1) * 128], identb[0:nt, 0:nt])
                        for c in range(8)], [xnb, b_id], [pTb])

        def sC():
            pT, pTb = hold["pT"]
            cp(dstT, pT.rearrange("p (c t) -> p c t", c=8)[:, :, 0:nt], [pTb], [dstbuf])
        return sA, sB, sC

    def proj_tok(xT, xTb, nt, wv, wb, ncols, c0=0):
        pz, pzb = pz_r.next()
        mm(pz[0:nt, 0:ncols], [(xT[:, c, :], wv[:, c, c0:c0 + ncols]) for c in range(8)], [xTb, wb], [pzb])
        return pz, pzb

    hn_sm_r = Rot([(A.f32([16]), Buf()) for _ in range(3)])

    def head_norm(pz, pzb, nt, nh, dh, g, out_dram, dstT=None, dstTb=None, pair=True, defer_T=0, staged=False):
        n = nh * dh
        sq, sqb = sq_r.next()
        sm, smb = hn_sm_r.next() if (defer_T or staged) else sm_r.next()
        stg, stb = st_r.next()
        sbf, sbb = (None, None)
        if dstT is not None:
            sbf, sbb = sb_r.next()
        v3 = lambda a: a[0:nt, 0:n].rearrange("p (h d) -> p h d", h=nh)

        def stage_act():
            def fsq(e):
                for h in range(nh):
                    ins = e.activation(out=sq[0:nt, h * dh:(h + 1) * dh], in_=pz[0:nt, h * dh:(h + 1) * dh],
                                       func=AF.Square, accum_out=sm[0:nt, h:h + 1])
                return ins
            P.op("act", fsq, [pzb], [sqb, smb])
            act(sm[0:nt, 0:nh], sm[0:nt, 0:nh], AF.Sqrt, [smb, b_eps], [smb], bias=EPSB[0:nt, 0:1], scale=1.0 / dh)

        def stage_dve():
            recip(sm[0:nt, 0:nh], sm[0:nt, 0:nh], [smb], [smb])
            tt(v3(stg), v3(pz), bcl(sm[0:nt, 0:nh], dh), ALU.mult, [pzb, smb], [stb])
            if out_dram is not None:
                tt(v3(stg), v3(stg), bcm(g[0:nt], nh), ALU.mult, [stb, b_g], [stb])
                store(out_dram, stg[0:nt, 0:n], [stb])
                if dstT is not None:
                    cp(sbf[0:nt, 0:n], stg[0:nt, 0:n], [stb], [sbb])
            else:
                tt(v3(sbf), v3(stg), bcm(g[0:nt], nh), ALU.mult, [stb, b_g], [sbb])

        def stage_pe():
            if dstT is None:
                return
            pT, pTb = pT_r.next()
            nch = n // 128
            transposes([(pT[:, c * 128:c * 128 + nt], sbf[0:nt, c * 128:(c + 1) * 128], identb[0:nt, 0:nt])
                        for c in range(nch)], [sbb, b_id], [pTb])
            cp(dstT, pT[:, 0:nch * 128].rearrange("p (c t) -> p c t", c=nch)[:, :, 0:nt], [pTb], [dstTb])

        if staged:
            return stage_act, stage_dve, stage_pe
        if defer_T:
            defer(stage_act, defer_T)
            defer(stage_dve, 2 * defer_T)
            defer(stage_pe, 3 * defer_T)
        else:
            stage_act()
            stage_dve()
            stage_pe()

    def pipelined(n_tiles, make):
        st_ = {}
        for it in range(n_tiles + 2):
            if it < n_tiles:
                st_[it] = make(it)
                st_[it][0]()
            if 0 <= it - 1 < n_tiles:
                st_[it - 1][1]()
            if 0 <= it - 2 < n_tiles:
                st_.pop(it - 2)[2]()

    def v_store(pz, pzb, nt, nh, dh, out_dram, vaug, vb):
        n = nh * dh
        stg, stb = st_r.next()
        cp(stg[0:nt, 0:n], pz[0:nt, 0:n], [pzb], [stb])
        if out_dram is not None:
            store(out_dram, stg[0:nt, 0:n], [stb])
        act(vaug[0:nt, :, 0:dh], stg[0:nt, 0:n].rearrange("p (h d) -> p h d", h=nh), AF.Copy, [stb], [vb])

    def logf_a(xT, xTb, nt, wv, wb, c0, LFdst, lfb):
        pz, pzb = pz_r.next()
        mm(pz[0:nt, 0:8], [(xT[:, c, :], wv[:, c, c0:c0 + 8]) for c in range(8)], [xTb, wb], [pzb])
        tt(LFdst, pz[0:nt, 0:8], bfb[0:nt], ALU.add, [pzb, b_g], [lfb])

    def logf_b(LFdst, lfb):
        act(LFdst, LFdst, AF.Exp, [lfb], [lfb], scale=-1.0)
        act(LFdst, LFdst, AF.Ln, [lfb], [lfb], bias=1.0)

    def logf_c(LFdst, lfb, out_dram):
        ts(LFdst, LFdst, -1.0, None, ALU.mult, None, [lfb], [lfb])
        if out_dram is not None:
            store(out_dram, LFdst, [lfb])

    def logf_tile(xT, xTb, nt, wv, wb, c0, LFdst, lfb, out_dram):
        logf_a(xT, xTb, nt, wv, wb, c0, LFdst, lfb)
        logf_b(LFdst, lfb)
        logf_c(LFdst, lfb, out_dram)

    def cumsum_tile(LF, lfb, nt, R, rb, Cdst, cb, first):
        pz, pzb = pz_r.next()
        if first:
            mm(pz[0:nt, 0:8], [(TRI[0:nt, 0:nt], LF)], [lfb, b_cm], [pzb])
        else:
            mm(pz[0:nt, 0:8], [(TRI[0:nt, 0:nt], LF), (ONES[:, 0:nt], R)], [lfb, b_cm, rb], [pzb])
        cp(Cdst, pz[0:nt, 0:8], [pzb], [cb])
        if first:
            memset(R, 0.0, [rb])
        tt(R[0:nt], R[0:nt], LF, ALU.add, [rb, lfb], [rb])

    def run_sums(LFs_list, Rall, rb, first_is_zero):
        if first_is_zero:
            memset(Rall[:, 0, :], 0.0, [rb])
        for t, (lf, lfb, nt) in enumerate(LFs_list):
            if nt < 128:
                continue
            tt(Rall[:, t + 1, :], Rall[:, t, :], lf, ALU.add, [rb, lfb], [rb])

    def cumsum_par(LF, lfb, nt, Rt, rb, Cdst, cb):
        pz, pzb = pz_r.next()
        mm(pz[0:nt, 0:8], [(TRI[0:nt, 0:nt], LF), (ONES[:, 0:nt], Rt)], [lfb, b_cm, rb], [pzb])
        cp(Cdst, pz[0:nt, 0:8], [pzb], [cb])

    def cumsum_batch(LFv, lfbufs, Rv, rb, Cv, cbufs, ncol):
        pz, pzb = pz_r.next()
        mm(pz[:, 0:ncol], [(TRI, LFv), (ONES, Rv)], list(lfbufs) + [b_cm, rb], [pzb])
        cp(Cv, pz[:, 0:ncol], [pzb], list(cbufs))

    def bcast_row(src, sb_, nrows, SEL, dst, db):
        pz, pzb = pz_r.next()
        mm(pz[:, 0:8], [(SEL[0:nrows, :], src)], [sb_, b_cm], [pzb])
        cp(dst, pz[:, 0:8], [pzb], [db])

    def attn_head(qT, nq, ktiles, po_ap, pob, scale, first_reads):
        n = len(ktiles)
        PENDING.append([dict(kt=kt, qT=qT, nq=nq, po_ap=po_ap, pob=pob, scale=scale,
                             first_reads=list(first_reads), start=(idx == 0), stop=(idx == n - 1))
                        for idx, kt in enumerate(ktiles)])

    PENDING = []
    LA = 4

    def attn_flush():
        calls = list(PENDING)
        del PENDING[:]
        steps = [sp_ for c_ in calls for sp_ in c_]
        quads = [steps[i:i + 4] for i in range(0, len(steps), 4)]

        def uniq(bl):
            out = []
            for b in bl:
                if b not in out:
                    out.append(b)
            return out

        def emit_S(q):
            bank, bb = pS_r.next()
            q = list(q)

            def f(e):
                for k, sp_ in enumerate(q):
                    kt = sp_["kt"]
                    ins = e.matmul(bank[0:kt["nk"], k * 128:k * 128 + sp_["nq"]], lhsT=kt["kT"], rhs=sp_["qT"],
                                   start=True, stop=True)
                return ins
            rds = []
            for sp_ in q:
                rds += list(sp_["kt"]["rd"]) + sp_["first_reads"]
            P.op("pe", f, uniq(rds), [bb])
            return bank, bb

        def emit_rest(q, bank, bb):
            q = list(q)
            pts = [pt_r.next() for _ in q]
            sas = {}
            pre_items = []
            for k, sp_ in enumerate(q):
                if sp_["kt"].get("pre") is not None:
                    sas[k] = sa_r.next()
                    pre_items.append(k)
            if pre_items:
                def fd(e):
                    for k in pre_items:
                        sp_ = q[k]
                        kt = sp_["kt"]
                        nk, nq = kt["nk"], sp_["nq"]
                        ins = e.scalar_tensor_tensor(out=sas[k][0][0:nk, 0:nq], in0=bank[0:nk, k * 128:k * 128 + nq],
                                                     scalar=sp_["scale"], in1=kt["pre"][0], op0=ALU.mult, op1=ALU.add)
                    return ins
                rds = [bb]
                for k in pre_items:
                    rds += list(q[k]["kt"]["pre"][1])
                P.op("dve", fd, uniq(rds), [sas[k][1] for k in pre_items])

            def fa(e):
                for k, sp_ in enumerate(q):
                    kt = sp_["kt"]
                    nk, nq = kt["nk"], sp_["nq"]
                    if k in sas:
                        ins = e.activation(out=pts[k][0][0:nk, 0:nq], in_=sas[k][0][0:nk, 0:nq], func=AF.Exp,
                                           bias=kt.get("bias", 0.0))
                    else:
                        ins = e.activation(out=pts[k][0][0:nk, 0:nq], in_=bank[0:nk, k * 128:k * 128 + nq], func=AF.Exp,
                                           bias=kt.get("bias", 0.0), scale=sp_["scale"])
                return ins
            rds = [bb] + [sas[k][1] for k in pre_items]
            for sp_ in q:
                rds += list(sp_["kt"].get("brd", []))
            P.op("act", fa, uniq(rds), [p_[1] for p_ in pts])
            for k, sp_ in enumerate(q):
                kt = sp_["kt"]
                if kt.get("mask") is not None:
                    nk, nq = kt["nk"], sp_["nq"]
                    tt(pts[k][0][0:nk, 0:nq], pts[k][0][0:nk, 0:nq], kt["mask"][0:nk, 0:nq], ALU.mult,
                       [pts[k][1], b_trib, b_mask0], [pts[k][1]], eng="pool")
            def fp(e):
                for k, sp_ in enumerate(q):
                    kt = sp_["kt"]
                    nk, nq = kt["nk"], sp_["nq"]
                    ins = e.matmul(sp_["po_ap"], lhsT=pts[k][0][0:nk, 0:nq], rhs=kt["v"],
                                   start=sp_["start"], stop=sp_["stop"])
                return ins
            rds = [p_[1] for p_ in pts]
            for sp_ in q:
                rds += list(sp_["kt"]["rd"])
            P.op("pe", fp, uniq(rds), uniq([sp_["pob"] for sp_ in q]))

        for q in quads:
            PIPEQ.append({"q": q, "bank": None, "after": []})
        PFN["S"], PFN["R"] = emit_S, emit_rest
        pipe_pump(False)


    def mm_acc(out, l, r, start, stop, reads, writes):
        P.op("pe", lambda e: e.matmul(out, lhsT=l, rhs=r, start=start, stop=stop), reads, writes)

    FIN = {}

    def attn_finish_group(nq, nh, dh, half):
        attn_flush()
        PIPEQ[-1]["after"].append(lambda: fin_group(nq, nh, dh, half))
        if PST["r"] >= len(PIPEQ):
            raise RuntimeError("pipeline ran ahead of finish")

    def fin_group(nq, nh, dh, half):
        hpb = nh // 2
        if half == 0:
            FIN["sbf"] = fin_r.next()
        sbf, sbb = FIN["sbf"]
        po, pob = pO[half]
        pv = po[0:nq, 0:hpb * (dh + 1)].rearrange("p (h d) -> p h d", h=hpb)
        sm, smb = sm_r.next()
        recip(sm[0:nq, 0:hpb], pv[:, :, dh], [pob], [smb])
        if dh == 64:
            ov_ = sbf[0:nq, 0:512].rearrange("p (hp two d) -> p hp two d", two=2, d=64)[:, :, half, :]
        else:
            ov_ = sbf[0:nq, half * hpb * dh:(half + 1) * hpb * dh].rearrange("p (h d) -> p h d", h=hpb)
        tt(ov_, pv[:, :, 0:dh], bcl(sm[0:nq, 0:hpb], dh), ALU.mult, [pob, smb], [sbb])

    def attn_finish(nq, nh, dh, dstT, dstTb):
        PIPEQ[-1]["after"].append(lambda: fin_tile(nq, dstT, dstTb))

    def fin_tile(nq, dstT, dstTb):
        sbf, sbb = FIN["sbf"]

        def part2():
            pT, pTb = pT_r.next()
            transposes([(pT[:, c * 128:c * 128 + nq], sbf[0:nq, c * 128:(c + 1) * 128], identb[0:nq, 0:nq])
                        for c in range(4)], [sbb, b_id], [pTb])
            cp(dstT, pT[:, 0:512].rearrange("p (c t) -> p c t", c=4)[:, :, 0:nq], [pTb], [dstTb])
        defer(part2, 2)

    def po_slot(h, nh, dh):
        hpb = nh // 2
        if dh == 64:
            po, pob = pO[h % 2]
            hh = h // 2
        else:
            po, pob = pO[h // hpb]
            hh = h % hpb
        return po, pob, hh * (dh + 1)

    BiasA = A.f32([8, 2, 128]); b_bias = Buf()
    relc = A.f32([8]); relcnf = A.f32([8])
    ld(relc, rel_bias[256:257, :].partition_broadcast(128), csem, [b_bias])
    ts(relcnf, relc, negflag[:, 0:1], None, ALU.add, None, [b_bias, b_nf], [b_bias])
    mask0 = A.bf16([128]); b_mask0 = Buf()
    memset(mask0, 1.0, [b_mask0])
    memset(mask0[0:64, 64:128], 0.0, [b_mask0])
    mark0 = A.top
    relx = A.f32([LREL], parts=8); b_relx = Buf()
    rsem = P.new_dma_sem()
    memset(relx, 0.0, [b_relx])
    sm8 = A.f32([1], parts=8)
    ld(sm8, rel_bias[256:257, :].rearrange("i h -> h i"), rsem, [b_relx], slow=True)
    ts(relx, relx, sm8[:, 0:1], None, ALU.add, None, [b_relx], [b_relx])
    ld(relx[:, 0:257], rel_bias.rearrange("i h -> h i"), rsem, [b_relx], slow=True)
    b_relE = Buf()
    ld(relE, relx, rsem, [b_relE], reads=[b_relx])
    relrep = A.f32([8, LREL]); b_rr = Buf()
    ld(relrep, relE.partition_broadcast(128), rsem, [b_rr], reads=[b_relE])
    b_relD = Buf()
    ld(relD, relrep.rearrange("p a b -> p (a b)"), rsem, [b_relD], reads=[b_rr])
    for t in (3, 4):
        src = bass.AP(tensor=relD.tensor, offset=640 - 128 * t,
                      ap=[[8 * LREL - 1, 128], [LREL, 8], [1, 128]])
        ld(BiasA[:, :, t - 3, :], src, rsem, [b_bias], reads=[b_relD])
    memset(BiasA[64:128, :, 1, 0:64], NEG, [b_bias])
    P.barrier(list(ALLBUFS))
    A.top = mark0

    mkT = A.bf16([4, 256]); b_mkT = Buf()
    mvA = A.bf16([2, 4, 129]); b_mv = Buf()
    memset(mvA[:, :, :, 128:129], 1.0, [b_mv])
    mark1 = A.top
    gmem = A.f32([1024]); b_gm = Buf()
    ld(gmem, g_mem.partition_broadcast(128), csem, [b_g])
    xmT = A.bf16([8, 256]); b_xmT = Buf()
    for t in range(2):
        norm_T(mem[t * 128:(t + 1) * 128, :], 128, gmem, xmT[:, :, t * 128:(t + 1) * 128], b_xmT)
    wv, wb = wload(w_mkv[:, 0:512], 8, 512)
    for t in range(2):
        pz, pzb = proj_tok(xmT[:, :, t * 128:(t + 1) * 128], b_xmT, 128, wv, wb, 512)
        head_norm(pz, pzb, 128, 4, 128, gv["km"], mkp[t * 128:(t + 1) * 128, :],
                  mkT[:, :, t * 128:(t + 1) * 128], b_mkT)
    wv, wb = wload(w_mkv[:, 512:1024], 8, 512)
    for t in range(2):
        pz, pzb = proj_tok(xmT[:, :, t * 128:(t + 1) * 128], b_xmT, 128, wv, wb, 512)
        v_store(pz, pzb, 128, 4, 128, mvp[t * 128:(t + 1) * 128, :], mvA[:, t], b_mv)
    P.barrier(list(ALLBUFS))
    A.top = mark1

    def full_barrier():
        pipe_drain()
        P.barrier(list(ALLBUFS))

    mark_x = A.top
    xnT = A.bf16([8, 2048]); b_xnT = [Buf() for _ in range(NOWN)]
    xpT4 = A.bf16([8, 512]); b_xp4 = Buf()
    xsT = A.bf16([8, NS]); b_xsT = Buf()
    osT = {k: A.bf16([4, NS]) for k in "bam"}
    b_osT = {k: Buf() for k in "bam"}
    oT = {}
    b_oT = {k: Buf() for k in "bam"}
    KsT_new = A.bf16([4, NS]); b_KsN = Buf()
    QsT = A.bf16([4, NS]); b_QsT = Buf()
    VsN = A.bf16([4, 8, 65], parts=16); b_VsN = Buf()
    LFs = A.f32([4, 8], parts=16); b_LFs = Buf()
    memset(VsN[:, :, :, 64:65], 1.0, [b_VsN])
    oT["b"] = A.bf16([4, 2048])

    def xT_of(i):
        return xnT[:, :, i * 128:(i + 1) * 128], b_xnT[i]

    markB = A.top
    KbT = A.bf16([4, NT * 128]); b_KbT = [Buf() for _ in range(NT)]
    Vb = A.bf16([NT, 8, 65]); b_Vb = [Buf() for _ in range(NT)]
    Qb_r = Rot([(A.bf16([4, 128]), Buf()) for _ in range(2)])
    LF = A.f32([NT, 8]); b_LF = [Buf() for _ in range(NT)]
    _sh = P.new_dma_sem()
    for b_ in b_LF:
        BUFSEM[id(b_)] = _sh
    Cc = A.f32([NT, 8]); b_C = [Buf() for _ in range(NT)]
    crefs = A.f32([NOWN, 8]); b_cref = [Buf() for _ in range(NOWN)]
    Rall = A.f32([NT + 1, 8]); b_R = Buf()
    biasT_r = Rot([(A.f32([NT + 1, 8]), Buf()) for _ in range(3)])
    memset(Vb[:, :, :, 64:65], 1.0, b_Vb)
    def mk_nt_pre(i):
        xv, xb_ = xT_of(i)
        return norm_T_staged(xpre[i * 128:(i + 1) * 128, :], 128, gmix, xv, xb_)
    pipelined(NPRE, mk_nt_pre)
    for i in range(4):
        xv, xb_ = xT_of(12 + i)
        cp(xpT4[:, :, i * 128:(i + 1) * 128], xv, [xb_], [b_xp4])
    wv, wb = wload(w_in[:, C_KB:C_KB + 512], 8, 512)
    def mk_kb_pre(i):
        xv, xb_ = xT_of(i)
        pz, pzb = proj_tok(xv, xb_, 128, wv, wb, 512)
        return head_norm(pz, pzb, 128, 8, 64, gv["kb"], None, KbT[:, :, i * 128:(i + 1) * 128], b_KbT[i], staged=True)
    pipelined(NPRE, mk_kb_pre)
    wv, wb = wload(w_in[:, C_VB:C_VB + 520], 8, 520)
    for i in range(NPRE):
        xv, xb_ = xT_of(i)
        pz, pzb = proj_tok(xv, xb_, 128, wv, wb, 512)
        v_store(pz, pzb, 128, 8, 64, None, Vb[:, i], b_Vb[i])
        logf_a(xv, xb_, 128, wv, wb, 512, LF[:, i, :], b_LF[i])
    for i in range(NPRE):
        logf_b(LF[:, i, :], b_LF[i])
    for i in range(NPRE):
        logf_c(LF[:, i, :], b_LF[i], None)
    run_sums([(LF[:, i, :], b_LF[i], 128) for i in range(NPRE)], Rall, b_R, True)
    cumsum_batch(LF[:, 0:NPRE, :].rearrange("p t h -> p (t h)"), b_LF[0:NPRE],
                 Rall[:, 0:NPRE, :].rearrange("p t h -> p (t h)"), b_R,
                 Cc[:, 0:NPRE, :].rearrange("p t h -> p (t h)"), b_C[0:NPRE], NPRE * 8)
    def mk_nt_own(i):
        xv, xb_ = xT_of(i)
        return norm_T_staged(xown[i * 128:(i + 1) * 128, :], 128, gmix, xv, xb_)
    pipelined(NOWN, mk_nt_own)
    norm_T(xs, NS, gmix, xsT, b_xsT)
    wv, wb = wload(w_in[:, C_KB:C_KB + 512], 8, 512)
    def mk_kb_own(i):
        xv, xb_ = xT_of(i)
        pz, pzb = proj_tok(xv, xb_, 128, wv, wb, 512)
        return head_norm(pz, pzb, 128, 8, 64, gv["kb"], bkp[i * 128:(i + 1) * 128, :],
                         KbT[:, :, (NPRE + i) * 128:(NPRE + i + 1) * 128], b_KbT[NPRE + i], staged=True)
    pipelined(NOWN, mk_kb_own)
    pz, pzb = proj_tok(xsT, b_xsT, NS, wv, wb, 512)
    head_norm(pz, pzb, NS, 8, 64, gv["kb"], bks, KsT_new, b_KsN)
    wv, wb = wload(w_in[:, C_VB:C_VB + 520], 8, 520)
    for i in range(NOWN):
        xv, xb_ = xT_of(i)
        j = NPRE + i
        pz, pzb = proj_tok(xv, xb_, 128, wv, wb, 512)
        v_store(pz, pzb, 128, 8, 64, bvp[i * 128:(i + 1) * 128, :], Vb[:, j], b_Vb[j])
        logf_a(xv, xb_, 128, wv, wb, 512, LF[:, j, :], b_LF[j])
    for i in range(NOWN):
        logf_b(LF[:, NPRE + i, :], b_LF[NPRE + i])
    for i in range(NOWN):
        logf_c(LF[:, NPRE + i, :], b_LF[NPRE + i], blfp[i * 128:(i + 1) * 128, :])
    for i in range(NOWN):
        j = NPRE + i
        tt(Rall[:, j + 1, :], Rall[:, j, :], LF[:, j, :], ALU.add, [b_R, b_LF[j]], [b_R])
    cumsum_batch(LF[:, NPRE:NT, :].rearrange("p t h -> p (t h)"), b_LF[NPRE:NT],
                 Rall[:, NPRE:NT, :].rearrange("p t h -> p (t h)"), b_R,
                 Cc[:, NPRE:NT, :].rearrange("p t h -> p (t h)"), b_C[NPRE:NT], NOWN * 8)
    pz, pzb = pz_r.next()
    mm(pz[:, 0:NOWN * 8], [(SELMID, Cc[:, NPRE:NT, :].rearrange("p t h -> p (t h)"))], b_C[NPRE:NT] + [b_cm], [pzb])
    cp(crefs.rearrange("p t h -> p (t h)"), pz[:, 0:NOWN * 8], [pzb], b_cref)
    for bb in range(4):
        xv = xsT[:, :, bb * 16:(bb + 1) * 16]
        pz, pzb = proj_tok(xv, b_xsT, 16, wv, wb, 512)
        v_store(pz, pzb, 16, 8, 64, bvs[bb * 16:(bb + 1) * 16, :], VsN[:, bb], b_VsN)
        logf_tile(xv, b_xsT, 16, wv, wb, 512, LFs[:, bb, :], b_LFs, blfs[bb * 16:(bb + 1) * 16, :])
    wv, wb = wload(w_in[:, C_QB:C_QB + 512], 8, 512)
    pz, pzb = proj_tok(xsT, b_xsT, NS, wv, wb, 512)
    head_norm(pz, pzb, NS, 8, 64, gv["qb"], None, QsT, b_QsT)

    b_wupb = [Buf() for _ in range(2)]
    b_wdnb = [Buf() for _ in range(2)]
    for k in range(2):
        ld(wupb[k * 512:(k + 1) * 512, :].rearrange("(c p) n -> p c n", p=128),
           w_up[k * 512:(k + 1) * 512, :].rearrange("(c p) n -> p c n", p=128), None, [b_wupb[k]], eng="pool")
    for k in range(2):
        ld(wdnb[k * 2048:(k + 1) * 2048, :].rearrange("(c p) n -> p c n", p=128),
           w_down[k * 2048:(k + 1) * 2048, :].rearrange("(c p) n -> p c n", p=128), None, [b_wdnb[k]], eng="pool")

    psum_mode(True)
    def prep_b(i):
        gi = NPRE + i
        xv, xb_ = xT_of(i)
        pz, pzb = proj_tok(xv, xb_, 128, wv, wb, 512)
        qv, qvb = Qb_r.next()
        head_norm(pz, pzb, 128, 8, 64, gv["qb"], None, qv, qvb, defer_T=2)
        bt_all, b_bt = biasT_r.next()
        for j in range(gi + 1):
            tt(bt_all[:, j, :], crefs[:, i, :], Cc[:, j, :], ALU.subtract, [b_cref[i], b_C[j]], [b_bt])
        ts(bt_all[:, 0:NPRE, :], bt_all[:, 0:NPRE, :], negflag[:, 0:1], None, ALU.add, None, [b_bt, b_nf], [b_bt])
        return qv, qvb, bt_all, b_bt

    nxt = prep_b(0)
    for i in range(NOWN):
        gi = NPRE + i
        run_deferred(True)
        qv, qvb, bt_all, b_bt = nxt
        if i + 1 < NOWN:
            nxt = prep_b(i + 1)
        for h in (0, 2, 4, 6, 1, 3, 5, 7):
            hp, hr = h // 2, (h % 2) * 64
            po, pob, off = po_slot(h, 8, 64)
            kts = []
            for j in range(gi + 1):
                kts.append(dict(kT=KbT[hr:hr + 64, hp, j * 128:(j + 1) * 128], nk=128,
                                v=Vb[:, j, h, :], rd=[b_KbT[j], b_Vb[j]],
                                bias=bt_all[:, j, h:h + 1], brd=[b_bt],
                                mask=(trib if j == gi else None)))
            attn_head(qv[hr:hr + 64, hp, :], 128, kts, po[:, off:off + 65], pob,
                      0.125, [qvb])
            if h >= 6:
                attn_finish_group(128, 8, 64, h % 2)
        attn_finish(128, 8, 64, oT["b"][:, :, i * 128:(i + 1) * 128], b_oT["b"])
    full_barrier()
    A.top = markB

    Kc = A.bf16([16, 512]); b_Kc = [Buf() for _ in range(4)]; kcsem = None
    KsT = A.bf16([4, 2048]); b_KsT = Buf()
    Vs = A.bf16([16, 8, 65]); b_Vs = [Buf() for _ in range(16)]
    LFc = A.f32([16, 8]); b_LFc = Buf()
    Cs = A.f32([16, 8]); b_Cs = Buf()
    CsN = A.f32([8], parts=16); b_CsN = Buf()
    crefS = A.f32([8]); b_crefS = Buf()
    memset(Vs[:, :, :, 64:65], 1.0, b_Vs)
    for bb in range(4):
        for q4 in range(4):
            ld(Kc[:, 4 * q4:4 * q4 + 4, :], cbk[bb, 512 * q4:512 * (q4 + 1), :].rearrange("(t p) n -> p t n", p=128),
               kcsem, [b_Kc[q4]], eng="pool")
        for t in range(16):
            ld(Vs[:, t, :, 0:64], cbv[bb, 128 * t:128 * (t + 1), :].rearrange("p (h d) -> p h d", h=8),
               kcsem, [b_Vs[t]], eng="pool")
        ld(LFc, cblf[bb].rearrange("(t p) h -> p t h", p=128), kcsem, [b_LFc])
        for t in range(16):
            pT, pTb = pT_r.next()
            transposes([(pT[:, c * 128:(c + 1) * 128], Kc[:, t, c * 128:(c + 1) * 128], identb) for c in range(4)],
                       [b_Kc[t // 4], b_id], [pTb])
            cp(KsT[:, :, t * 128:(t + 1) * 128], pT[:, 0:512].rearrange("p (c t) -> p c t", c=4), [pTb], [b_KsT])
        run_sums([(LFc[:, t, :], b_LFc, 128) for t in range(16)], Rall, b_R, True)
        cumsum_batch(LFc.rearrange("p t h -> p (t h)"), [b_LFc], Rall[:, 0:16, :].rearrange("p t h -> p (t h)"), b_R,
                     Cs.rearrange("p t h -> p (t h)"), [b_Cs], 128)
        cumsum_par(LFs[:, bb, :], b_LFs, 16, Rall[:, 16, :], b_R, CsN, b_CsN)
        bcast_row(CsN, b_CsN, 16, SEL0, crefS, b_crefS)
        bt_all, b_bt = biasT_r.next()
        for t in range(16):
            tt(bt_all[:, t, :], crefS, Cs[:, t, :], ALU.subtract, [b_crefS, b_Cs], [b_bt])
        tt(bt_all[0:16, 16, :], crefS[0:16], CsN, ALU.subtract, [b_crefS, b_CsN], [b_bt])
        for h in (0, 2, 4, 6, 1, 3, 5, 7):
            hp, hr = h // 2, (h % 2) * 64
            po, pob, off = po_slot(h, 8, 64)
            kts = []
            for t in range(16):
                kts.append(dict(kT=KsT[hr:hr + 64, hp, t * 128:(t + 1) * 128], nk=128, v=Vs[:, t, h, :],
                                rd=[b_KsT, b_Vs[t]], bias=bt_all[:, t, h:h + 1], brd=[b_bt], mask=None))
            kts.append(dict(kT=KsT_new[hr:hr + 64, hp, bb * 16:(bb + 1) * 16], nk=16, v=VsN[:, bb, h, :],
                            rd=[b_KsN, b_VsN], bias=bt_all[0:16, 16, h:h + 1], brd=[b_bt], mask=trib))
            attn_head(QsT[hr:hr + 64, hp, bb * 16:(bb + 1) * 16], 16, kts, po[0:16, off:off + 65], pob,
                      0.125, [b_QsT])
            if h >= 6:
                attn_finish_group(16, 8, 64, h % 2)
        attn_finish(16, 8, 64, osT["b"][:, :, bb * 16:(bb + 1) * 16], b_osT["b"])
        pipe_drain()
    full_barrier()
    A.top = markB

    psum_mode(False)
    oT["a"] = A.bf16([4, 2048])
    markAA = A.top
    NA = 4 + NOWN
    KaT = A.bf16([4, NA * 128]); b_KaT = [Buf() for _ in range(NA)]
    Va = A.bf16([NA, 8, 65]); b_Va = [Buf() for _ in range(NA)]
    Qa_r = Rot([(A.bf16([4, 128]), Buf()) for _ in range(2)])
    KsaN = A.bf16([4, NS]); b_KsaN = Buf()
    QsaT = A.bf16([4, NS]); b_QsaT = Buf()
    VsaN = A.bf16([4, 8, 65], parts=16); b_VsaN = Buf()
    memset(Va[:, :, :, 64:65], 1.0, b_Va)
    memset(VsaN[:, :, :, 64:65], 1.0, [b_VsaN])

    def xa_of(idx):
        if idx < 4:
            return xpT4[:, :, idx * 128:(idx + 1) * 128], b_xp4
        return xT_of(idx - 4)

    wv, wb = wload(w_in[:, C_KA:C_KA + 512], 8, 512)
    def mk_ka(idx):
        xv, xb_ = xa_of(idx)
        pz, pzb = proj_tok(xv, xb_, 128, wv, wb, 512)
        od = akp[(idx - 16) * 128:(idx - 15) * 128, :] if idx >= 16 else None
        return head_norm(pz, pzb, 128, 8, 64, gv["ka"], od, KaT[:, :, idx * 128:(idx + 1) * 128], b_KaT[idx], staged=True)
    pipelined(NA, mk_ka)
    pz, pzb = proj_tok(xsT, b_xsT, NS, wv, wb, 512)
    head_norm(pz, pzb, NS, 8, 64, gv["ka"], aks, KsaN, b_KsaN)
    wv, wb = wload(w_in[:, C_VA:C_VA + 512], 8, 512)
    for idx in range(NA):
        xv, xb_ = xa_of(idx)
        pz, pzb = proj_tok(xv, xb_, 128, wv, wb, 512)
        od = avp[(idx - 16) * 128:(idx - 15) * 128, :] if idx >= 16 else None
        v_store(pz, pzb, 128, 8, 64, od, Va[:, idx], b_Va[idx])
    for bb in range(4):
        pz, pzb = proj_tok(xsT[:, :, bb * 16:(bb + 1) * 16], b_xsT, 16, wv, wb, 512)
        v_store(pz, pzb, 16, 8, 64, avs[bb * 16:(bb + 1) * 16, :], VsaN[:, bb], b_VsaN)
    wv, wb = wload(w_in[:, C_QA:C_QA + 512], 8, 512)
    pz, pzb = proj_tok(xsT, b_xsT, NS, wv, wb, 512)
    head_norm(pz, pzb, NS, 8, 64, gv["qa"], None, QsaT, b_QsaT)
    def prep_a(i):
        xv, xb_ = xT_of(i)
        pz, pzb = proj_tok(xv, xb_, 128, wv, wb, 512)
        qv, qvb = Qa_r.next()
        head_norm(pz, pzb, 128, 8, 64, gv["qa"], None, qv, qvb, defer_T=2)
        return qv, qvb

    psum_mode(True)
    nxt = prep_a(0)
    for i in range(NOWN):
        run_deferred(True)
        qv, qvb = nxt
        if i + 1 < NOWN:
            nxt = prep_a(i + 1)
        for h in (0, 2, 4, 6, 1, 3, 5, 7):
            hp, hr = h // 2, (h % 2) * 64
            po, pob, off = po_slot(h, 8, 64)
            kts = []
            for t in range(5):
                idx = i + t
                pre_ = idx < 4
                d = dict(kT=KaT[hr:hr + 64, hp, idx * 128:(idx + 1) * 128], nk=128, v=Va[:, idx, h, :],
                         rd=[b_KaT[idx], b_Va[idx]], mask=None, brd=[b_bias, b_nf])
                if t < 3:
                    d["bias"] = (relcnf if pre_ else relc)[:, h:h + 1]
                    if t == 0:
                        d["mask"] = mask0
                else:
                    d["pre"] = (BiasA[:, h, t - 3, :], [b_bias])
                    d["bias"] = negflag[:, 0:1] if pre_ else 0.0
                kts.append(d)
            attn_head(qv[hr:hr + 64, hp, :], 128, kts, po[:, off:off + 65], pob,
                      0.125, [qvb])
            if h >= 6:
                attn_finish_group(128, 8, 64, h % 2)
        attn_finish(128, 8, 64, oT["a"][:, :, i * 128:(i + 1) * 128], b_oT["a"])
    Kca = A.bf16([4, 512]); b_Kca = Buf()
    KsaT = A.bf16([4, 512]); b_KsaT = Buf()
    Vsa = A.bf16([4, 8, 65]); b_Vsa = Buf()
    memset(Vsa[:, :, :, 64:65], 1.0, [b_Vsa])
    for bb in range(4):
        ld(Kca, cak[bb].rearrange("(t p) n -> p t n", p=128), kcsem, [b_Kca], eng="pool")
        for t in range(4):
            ld(Vsa[:, t, :, 0:64], cav[bb, 128 * t:128 * (t + 1), :].rearrange("p (h d) -> p h d", h=8),
               kcsem, [b_Vsa], eng="pool")
        for t in range(4):
            pT, pTb = pT_r.next()
            transposes([(pT[:, c * 128:(c + 1) * 128], Kca[:, t, c * 128:(c + 1) * 128], identb) for c in range(4)],
                       [b_Kca, b_id], [pTb])
            cp(KsaT[:, :, t * 128:(t + 1) * 128], pT[:, 0:512].rearrange("p (c t) -> p c t", c=4), [pTb], [b_KsaT])
        for h in (0, 2, 4, 6, 1, 3, 5, 7):
            hp, hr = h // 2, (h % 2) * 64
            po, pob, off = po_slot(h, 8, 64)
            kts = []
            for t in range(4):
                d = dict(kT=KsaT[hr:hr + 64, hp, t * 128:(t + 1) * 128], nk=128, v=Vsa[:, t, h, :],
                         rd=[b_KsaT, b_Vsa], mask=None, brd=[b_bias])
                if t < 3:
                    d["bias"] = relc[:, h:h + 1]
                else:
                    d["pre"] = (BiasA[:, h, 0, 0:16], [b_bias])
                    d["bias"] = 0.0
                kts.append(d)
            kts.append(dict(kT=KsaN[hr:hr + 64, hp, bb * 16:(bb + 1) * 16], nk=16, v=VsaN[:, bb, h, :],
                            rd=[b_KsaN, b_VsaN], mask=None, brd=[b_bias],
                            pre=(BiasA[0:16, h, 1, 0:16], [b_bias]), bias=0.0))
            attn_head(QsaT[hr:hr + 64, hp, bb * 16:(bb + 1) * 16], 16, kts, po[0:16, off:off + 65], pob,
                      0.125, [b_QsaT])
            if h >= 6:
                attn_finish_group(16, 8, 64, h % 2)
        attn_finish(16, 8, 64, osT["a"][:, :, bb * 16:(bb + 1) * 16], b_osT["a"])
        pipe_drain()
    full_barrier()
    A.top = markAA

    psum_mode(False)
    oT["m"] = A.bf16([4, 2048])
    markM = A.top
    Qm_r = Rot([(A.bf16([4, 128]), Buf()) for _ in range(2)])
    QsmT = A.bf16([4, NS]); b_QsmT = Buf()
    wv, wb = wload(w_in[:, C_QM:C_QM + 512], 8, 512)
    pz, pzb = proj_tok(xsT, b_xsT, NS, wv, wb, 512)
    head_norm(pz, pzb, NS, 4, 128, gv["qm"], None, QsmT, b_QsmT)
    sc_m = float(128 ** -0.5)
    def prep_m(i):
        xv, xb_ = xT_of(i)
        pz, pzb = proj_tok(xv, xb_, 128, wv, wb, 512)
        qv, qvb = Qm_r.next()
        head_norm(pz, pzb, 128, 4, 128, gv["qm"], None, qv, qvb, defer_T=1)
        return qv, qvb

    psum_mode(True)
    nxt = prep_m(0)
    for i in range(NOWN):
        run_deferred(True)
        qv, qvb = nxt
        if i + 1 < NOWN:
            nxt = prep_m(i + 1)
        for h in range(4):
            po, pob, off = po_slot(h, 4, 128)
            kts = [dict(kT=mkT[:, h, m * 128:(m + 1) * 128], nk=128, v=mvA[:, m, h, :], rd=[b_mkT, b_mv],
                        mask=None, bias=0.0) for m in range(2)]
            attn_head(qv[:, h, :], 128, kts, po[:, off:off + 129], pob, sc_m, [qvb])
            if (h + 1) % (4 // 2) == 0:
                attn_finish_group(128, 4, 128, h // (4 // 2))
        attn_finish(128, 4, 128, oT["m"][:, :, i * 128:(i + 1) * 128], b_oT["m"])
    Kcm = A.bf16([2, 512]); b_Kcm = Buf()
    mksT = A.bf16([4, 256]); b_mksT = Buf()
    mvs = A.bf16([2, 4, 129]); b_mvs = Buf()
    memset(mvs[:, :, :, 128:129], 1.0, [b_mvs])
    for bb in range(4):
        ld(Kcm, cmk[bb].rearrange("(t p) n -> p t n", p=128), kcsem, [b_Kcm], eng="pool")
        for t in range(2):
            ld(mvs[:, t, :, 0:128], cmv[bb, 128 * t:128 * (t + 1), :].rearrange("p (h d) -> p h d", h=4),
               kcsem, [b_mvs], eng="pool")
        for t in range(2):
            pT, pTb = pT_r.next()
            transposes([(pT[:, c * 128:(c + 1) * 128], Kcm[:, t, c * 128:(c + 1) * 128], identb) for c in range(4)],
                       [b_Kcm, b_id], [pTb])
            cp(mksT[:, :, t * 128:(t + 1) * 128], pT[:, 0:512].rearrange("p (c t) -> p c t", c=4), [pTb], [b_mksT])
        for h in range(4):
            po, pob, off = po_slot(h, 4, 128)
            kts = [dict(kT=mksT[:, h, m * 128:(m + 1) * 128], nk=128, v=mvs[:, m, h, :], rd=[b_mksT, b_mvs],
                        mask=None, bias=0.0) for m in range(2)]
            attn_head(QsmT[:, h, bb * 16:(bb + 1) * 16], 16, kts, po[0:16, off:off + 129], pob, sc_m, [b_QsmT])
            if (h + 1) % (4 // 2) == 0:
                attn_finish_group(16, 4, 128, h // (4 // 2))
        attn_finish(16, 4, 128, osT["m"][:, :, bb * 16:(bb + 1) * 16], b_osT["m"])
        pipe_drain()
    full_barrier()
    A.top = markM

    psum_mode(False)
    hT_start = A.top
    hT = A.bf16([8, 2048]); b_hT = [Buf() for _ in range(4)]
    hsT = A.bf16([8, NS]); b_hsT = Buf()
    mark_h_end = A.top
    acc_r = Rot([(A.f32([512]), Buf()) for _ in range(2)])
    wp_r = Rot([(A.bf16([4 * 384]), Buf(), P.new_dma_sem()) for _ in range(2)])
    wps = {"a": w_pa, "b": w_pb, "m": w_pm}
    BR = "abm"
    groups = [(tg * 512, 512, b_xnT[4 * tg:4 * tg + 4], b_hT[tg]) for tg in range(4)]
    for nch in range(8):
        wt, wgb, wsem = w_r.next()
        wg = wt[:, 0:8 * 384].rearrange("p (c b n) -> p c b n", c=8, b=3)
        wpt, wpb, wpsem = wp_r.next()
        wp = wpt.rearrange("p (c b n) -> p c b n", c=4, b=3)
        for bi, br in enumerate(BR):
            c0 = C_G + bi * 1024 + nch * 128
            ld(wg[:, :, bi, :], w_in[:, c0:c0 + 128].rearrange("(c p) n -> p c n", p=128), wsem, [wgb], eng="pool")
            ld(wp[:, :, bi, :], wps[br][:, nch * 128:(nch + 1) * 128].rearrange("(c p) n -> p c n", p=128),
               wpsem, [wpb], eng="pool")
        for gidx in range(5):
            if gidx < 4:
                t0, ntok, xbufs, hb = groups[gidx]
                xv = xnT[:, :, t0:t0 + ntok]
                ov = {k: oT[k][:, :, t0:t0 + ntok] for k in BR}
                obufs = b_oT
                hdst = hT[:, nch, t0:t0 + ntok]
            else:
                ntok, xbufs, hb = NS, [b_xsT], b_hsT
                xv = xsT
                ov = osT
                obufs = b_osT
                hdst = hsT[:, nch, :]
            acc, accb = acc_r.next()
            for bi, br in enumerate(BR):
                pz, pzb = pz_r.next()
                mm(pz[:, 0:ntok], [(wg[:, c, bi, :], xv[:, c, :]) for c in range(8)], [wgb] + list(xbufs), [pzb])
                sg, sgb = st_r.next()
                act(sg[:, 0:ntok], pz[:, 0:ntok], AF.Sigmoid, [pzb], [sgb])
                pz2, pz2b = pz_r.next()
                mm(pz2[:, 0:ntok], [(wp[:, c, bi, :], ov[br][:, c, :]) for c in range(4)], [wpb, obufs[br]], [pz2b])
                if bi == 0:
                    tt(acc[:, 0:ntok], sg[:, 0:ntok], pz2[:, 0:ntok], ALU.mult, [sgb, pz2b], [accb])
                elif bi == 1:
                    tt(sg[:, 0:ntok], sg[:, 0:ntok], pz2[:, 0:ntok], ALU.mult, [sgb, pz2b], [sgb])
                    tt(acc[:, 0:ntok], acc[:, 0:ntok], sg[:, 0:ntok], ALU.add, [accb, sgb], [accb])
                else:
                    tt(sg[:, 0:ntok], sg[:, 0:ntok], pz2[:, 0:ntok], ALU.mult, [sgb, pz2b], [sgb])
                    tt(hdst, acc[:, 0:ntok], sg[:, 0:ntok], ALU.add, [accb, sgb], [hb])
    full_barrier()

    A.top = mark_x
    x1 = A.f32([NOWN, 1024]); b_x1 = [Buf() for _ in range(NOWN)]
    _sh2 = P.new_dma_sem()
    for b_ in b_x1:
        BUFSEM[id(b_)] = _sh2
    x1s = A.f32([1024]); b_x1s = Buf()
    assert A.top <= hT_start, ("x1 overlaps hT", A.top, hT_start)
    mark_x1_end = A.top
    A.top = mark_h_end
    wo0, wo0b = wload(w_o[:, 0:512], 8, 512)
    wo1, wo1b = wload(w_o[:, 512:1024], 8, 512)
    for i in range(NOWN + 1):
        if i < NOWN:
            nt_, src, hv, hb, dst, db = 128, xown[i * 128:(i + 1) * 128, :], hT[:, :, i * 128:(i + 1) * 128], b_hT[i // 4], x1[:, i, :], b_x1[i]
        else:
            nt_, src, hv, hb, dst, db = NS, xs, hsT, b_hsT, x1s, b_x1s
        xt, xb, xsem = xt_r.next()
        ld(xt[0:nt_], src, xsem, [xb])
        for half, (wo, wob) in enumerate(((wo0, wo0b), (wo1, wo1b))):
            pz, pzb = pz_r.next()
            mm(pz[0:nt_, :], [(hv[:, c, :], wo[:, c, :]) for c in range(8)], [hb, wob], [pzb])
            tt(dst[0:nt_, half * 512:(half + 1) * 512], xt[0:nt_, half * 512:(half + 1) * 512], pz[0:nt_, :],
               ALU.add, [xb, pzb], [db])
    full_barrier()

    A.top = mark_x1_end
    gffn = A.f32([1024])
    ld(gffn, g_ffn.partition_broadcast(128), csem, [b_g])
    xn2T = A.bf16([8, 512]); b_xn2T = Buf()
    hdnT = A.bf16([32, 512]); b_hdn = Buf()
    wd_r = Rot([(A.bf16([4, 1024]), Buf(), P.new_dma_sem()) for _ in range(2)])
    for stg in range(5):
        if stg < 4:
            ntok = 512
            tiles = [(x1[:, 4 * stg + k, :], b_x1[4 * stg + k], 128, y_p[(4 * stg + k) * 128:(4 * stg + k + 1) * 128, :]) for k in range(4)]
        else:
            ntok = NS
            tiles = [(x1s, b_x1s, NS, y_s)]
        for k, (xa, xab, nt_, _) in enumerate(tiles):
            norm_T((xa, xab), nt_, gffn, xn2T[:, :, k * 128:k * 128 + nt_], b_xn2T)
        for ffb in range(8):
            wt_, wb, wsem_ = w_r.next()
            wv = wt_[:, 0:8 * 512].rearrange("p (c n) -> p c n", c=8)
            ld(wv, wupb[:, ffb * 512:(ffb + 1) * 512].rearrange("(c p) n -> p c n", p=128), None, [wb],
               reads=b_wupb, eng="pool")
            for fc in range(4):
                pz, pzb = pz_r.next()
                mm(pz[:, 0:ntok], [(wv[:, c, fc * 128:(fc + 1) * 128], xn2T[:, c, 0:ntok]) for c in range(8)],
                   [wb, b_xn2T], [pzb])
                rl, rlb = st_r.next()
                act(rl[:, 0:ntok], pz[:, 0:ntok], AF.Relu, [pzb], [rlb])
                tt(hdnT[:, ffb * 4 + fc, 0:ntok], rl[:, 0:ntok], rl[:, 0:ntok], ALU.mult, [rlb], [b_hdn])
        for blk in range(8):
            wd, wdb, wdsem = wd_r.next()
            ld(wd, wdnb[blk * 512:(blk + 1) * 512, :].rearrange("(c p) n -> p c n", p=128), wdsem, [wdb],
               reads=[b_wdnb[blk // 4]], eng="pool")
            for k, (xa, xab, nt_, _) in enumerate(tiles):
                for half in range(2):
                    pz, pzb = pz_r.next()
                    mm(pz[0:nt_, :], [(hdnT[:, blk * 4 + c, k * 128:k * 128 + nt_], wd[:, c, half * 512:(half + 1) * 512])
                                      for c in range(4)], [b_hdn, wdb], [pzb])
                    tt(xa[0:nt_, half * 512:(half + 1) * 512], xa[0:nt_, half * 512:(half + 1) * 512], pz[0:nt_, :],
                       ALU.add, [xab, pzb], [xab])
        for k, (xa, xab, nt_, yd) in enumerate(tiles):
            store(yd, xa[0:nt_], [xab])

    P.emit(final_waits=STORE_SEMS)
    st.close()
    return nc


_NC = None


def _consts():
    tri = np.triu(np.ones((128, 128), np.float32))
    ones = np.ones((128, 128), np.float32)
    selmid = np.zeros((128, 128), np.float32); selmid[64, :] = 1.0
    sel0 = np.zeros((128, 128), np.float32); sel0[0, :] = 1.0
    cm = np.stack([tri, ones, selmid, sel0], axis=1).reshape(128, 512)
    return np.ascontiguousarray(cm), np.eye(128, dtype=np.float32)


def kernel(**inp):
    global _NC
    if _NC is None:
        _NC = build_nc()
    nc = _NC
    f = lambda a: np.ascontiguousarray(np.asarray(a, dtype=np.float32))
    cm, ident = _consts()
    xp = f(inp["x_prompt"]); xsm = f(inp["x_sample"]); memp = f(inp["mem_prompt"])
    shared = {
        "g_mix": f(inp["g_mix"]), "w_in": f(inp["w_in"][0]), "b_f": f(inp["b_f"]),
        "g_qa": f(inp["g_qa"]), "g_ka": f(inp["g_ka"]), "g_qb": f(inp["g_qb"]), "g_kb": f(inp["g_kb"]),
        "g_qm": f(inp["g_qm"]), "g_km": f(inp["g_km"]), "rel_bias": f(inp["rel_bias"][0]),
        "g_mem": f(inp["g_mem"]), "w_mkv": f(inp["w_mkv"][0]),
        "w_pa": f(inp["w_pa"][0]), "w_pb": f(inp["w_pb"][0]), "w_pm": f(inp["w_pm"][0]),
        "w_o": f(inp["w_o"][0]), "g_ffn": f(inp["g_ffn"]), "w_up": f(inp["w_up"][0]), "w_down": f(inp["w_down"][0]),
        "cmat": cm, "ident": ident,
    }
    in_maps = []
    for c in range(8):
        b, hf = c // 2, c % 2
        sb = slice(4 * c, 4 * c + 4)
        m = dict(shared)
        m["xpre"] = f(xp[b, 0:2048])
        m["xown"] = f(xp[b, hf * 2048:(hf + 1) * 2048])
        m["xs"] = f(xsm[sb].reshape(64, 1024))
        m["mem"] = f(memp[b])
        m["cak"] = f(inp["cache_a_k"][0, sb].reshape(4, 512, 512))
        m["cav"] = f(inp["cache_a_v"][0, sb].reshape(4, 512, 512))
        m["cbk"] = f(inp["cache_b_k"][0, sb].reshape(4, 2048, 512))
        m["cbv"] = f(inp["cache_b_v"][0, sb].reshape(4, 2048, 512))
        m["cblf"] = f(inp["cache_b_logf"][0, sb])
        m["cmk"] = f(inp["cache_mem_k"][0, sb].reshape(4, 256, 512))
        m["cmv"] = f(inp["cache_mem_v"][0, sb].reshape(4, 256, 512))
        m["flag"] = np.full((1, 1), float(hf), np.float32)
        in_maps.append(m)
    res = run_bass_kernel_spmd(nc, in_maps, core_ids=list(range(8)))
    R = res.results
    y_p = np.zeros((4, 4096, 1024), np.float32)
    y_s = np.zeros((32, 16, 1024), np.float32)
    akp = np.zeros((1, 4, 512, 8, 64), np.float32); avp = np.zeros_like(akp)
    bkp = np.zeros((1, 4, 4096, 8, 64), np.float32); bvp = np.zeros_like(bkp)
    blfp = np.zeros((1, 4, 4096, 8), np.float32)
    mkp = np.zeros((1, 4, 256, 4, 128), np.float32); mvp = np.zeros_like(mkp)
    aks = np.zeros((1, 32, 16, 8, 64), np.float32); avs = np.zeros_like(aks)
    bks = np.zeros_like(aks); bvs = np.zeros_like(aks)
    blfs = np.zeros((1, 32, 16, 8), np.float32)
    for c in range(8):
        b, hf = c // 2, c % 2
        r = R[c]
        s = slice(hf * 2048, (hf + 1) * 2048)
        y_p[b, s] = r["y_p"]
        y_s[4 * c:4 * c + 4] = r["y_s"].reshape(4, 16, 1024)
        bkp[0, b, s] = r["bkp"].reshape(2048, 8, 64)
        bvp[0, b, s] = r["bvp"].reshape(2048, 8, 64)
        blfp[0, b, s] = r["blfp"]
        if hf == 1:
            akp[0, b] = r["akp"].reshape(512, 8, 64)
            avp[0, b] = r["avp"].reshape(512, 8, 64)
        else:
            mkp[0, b] = r["mkp"].reshape(256, 4, 128)
            mvp[0, b] = r["mvp"].reshape(256, 4, 128)
        aks[0, 4 * c:4 * c + 4] = r["aks"].reshape(4, 16, 8, 64)
        avs[0, 4 * c:4 * c + 4] = r["avs"].reshape(4, 16, 8, 64)
        bks[0, 4 * c:4 * c + 4] = r["bks"].reshape(4, 16, 8, 64)
        bvs[0, 4 * c:4 * c + 4] = r["bvs"].reshape(4, 16, 8, 64)
        blfs[0, 4 * c:4 * c + 4] = r["blfs"].reshape(4, 16, 8)
    return (y_p, y_s, akp, avp, bkp, bvp, blfp, mkp, mvp, aks, avs, bks, bvs, blfs)
```
